# Optimizing a Trainium2 kernel written in Bass

```python
import jax
import jax.numpy as jnp
from jax import lax
import numpy as np

D_MODEL = 1024
BATCH = 16
SEQ = 256
DEPTH = 4
DEC_BATCH = 2
DEC_SEQ = 1024
PAST_LEN = 512

GRID_W = 64
N_MIXERS = 4
N_LAYERS_A = (DEPTH + 3) // 4
N_LAYERS_B = (DEPTH + 2) // 4
N_LAYERS_C = (DEPTH + 1) // 4
N_LAYERS_D = DEPTH // 4

N_HEADS = 16
QK_NOPE = 64
QK_ROPE = 32
V_HEAD = 64
Q_LORA = 384
KV_LORA = 256
ROPE_THETA = 10000.0
Q_BLOCK = 128
DENSE_KEY_LIMIT = 2048

POOL_WINDOWS = (2, 4, 8, 16)
POOL_GROUP = D_MODEL // len(POOL_WINDOWS)

N_FFT_GROUPS = 4
FFT_GROUP = D_MODEL // N_FFT_GROUPS

D_FF = 2816
CONV_W = 3

EPS = 1e-6

kernel_name = "hybrid_dit_mla_pool_fnet_shortconv_step"


def _rmsnorm(x, w):
    xf = x.astype(jnp.float32)
    y = xf * lax.rsqrt(jnp.mean(xf * xf, axis=-1, keepdims=True) + EPS)
    return (y * w.astype(jnp.float32)).astype(x.dtype)


def _modulate(h, shift, scale):
    return h * (1 + scale[:, None, :]) + shift[:, None, :]


def _dwconv3(x, w):
    xp = jnp.pad(x, ((0, 0), (1, 1), (0, 0)))
    return xp[:, :-2] * w[0] + xp[:, 1:-1] * w[1] + xp[:, 2:] * w[2]


def _axial_rope_tables(rows):
    r, col = jnp.meshgrid(jnp.arange(rows), jnp.arange(GRID_W), indexing="ij")
    r = r.reshape(-1).astype(jnp.float32)
    col = col.reshape(-1).astype(jnp.float32)
    n_freq = QK_ROPE // 4
    inv = ROPE_THETA ** (-jnp.arange(n_freq, dtype=jnp.float32) / n_freq)
    ang = jnp.concatenate([r[:, None] * inv, col[:, None] * inv], axis=-1)
    return jnp.cos(ang), jnp.sin(ang)


def _rope(x, cos, sin):
    xr = x.reshape(x.shape[:-1] + (QK_ROPE // 2, 2))
    x1, x2 = xr[..., 0], xr[..., 1]
    cos = cos.astype(x.dtype)
    sin = sin.astype(x.dtype)
    out = jnp.stack([x1 * cos - x2 * sin, x1 * sin + x2 * cos], axis=-1)
    return out.reshape(x.shape)


def _mla_project(h, P, j):
    B, T, _ = h.shape
    cq = _rmsnorm(h @ P["mla_wdq"][j], P["mla_q_norm"][j])
    q = (cq @ P["mla_wuq"][j]).reshape(B, T, N_HEADS, QK_NOPE + QK_ROPE)
    kv = h @ P["mla_wdkv"][j]
    ckv = _rmsnorm(kv[..., :KV_LORA], P["mla_kv_norm"][j])
    return q[..., :QK_NOPE], q[..., QK_NOPE:], ckv, kv[..., KV_LORA:]


def _mla_expand(ckv, w_ukv):
    B, S, _ = ckv.shape
    kv = (ckv @ w_ukv).reshape(B, S, N_HEADS, QK_NOPE + V_HEAD)
    return kv[..., :QK_NOPE], kv[..., QK_NOPE:]


def _attend_block(q_nope, q_rope, k_nope, k_rope, v):
    s = jnp.einsum("bqhd,bkhd->bhqk", q_nope, k_nope, preferred_element_type=jnp.float32)
    s = s + jnp.einsum("bqhr,bkr->bhqk", q_rope, k_rope, preferred_element_type=jnp.float32)
    p = jax.nn.softmax(s * (1.0 / np.sqrt(QK_NOPE + QK_ROPE)), axis=-1)
    return jnp.einsum("bhqk,bkhd->bqhd", p.astype(v.dtype), v)


def _attention(q_nope, q_rope, k_nope, k_rope, v):
    B, T = q_nope.shape[:2]
    if k_nope.shape[1] < DENSE_KEY_LIMIT or T % Q_BLOCK:
        return _attend_block(q_nope, q_rope, k_nope, k_rope, v)
    nb = T // Q_BLOCK
    qn = q_nope.reshape(B, nb, Q_BLOCK, N_HEADS, QK_NOPE).swapaxes(0, 1)
    qr = q_rope.reshape(B, nb, Q_BLOCK, N_HEADS, QK_ROPE).swapaxes(0, 1)
    out = lax.map(lambda qs: _attend_block(qs[0], qs[1], k_nope, k_rope, v), (qn, qr))
    return out.swapaxes(0, 1).reshape(B, T, N_HEADS, V_HEAD)


def _mla_context(h, P, j):
    B, T, _ = h.shape
    q_nope, q_rope, ckv, k_rope = _mla_project(h, P, j)
    k_nope, v = _mla_expand(ckv, P["mla_wukv"][j])
    o = _attention(q_nope, q_rope, k_nope, k_rope, v)
    return o.reshape(B, T, N_HEADS * V_HEAD) @ P["mla_wo"][j], ckv, k_rope


def _mla_latent(h, P, j, ctx_ckv, ctx_krope, cos, sin):
    B, T, _ = h.shape
    q_nope, q_rope, ckv, k_rope = _mla_project(h, P, j)
    q_rope = _rope(q_rope, cos[:, None, :], sin[:, None, :])
    k_rope = _rope(k_rope, cos, sin)
    ckv_all = jnp.concatenate([ctx_ckv.astype(ckv.dtype), ckv], axis=1)
    krope_all = jnp.concatenate([ctx_krope.astype(k_rope.dtype), k_rope], axis=1)
    k_nope, v = _mla_expand(ckv_all, P["mla_wukv"][j])
    o = _attention(q_nope, q_rope, k_nope, krope_all, v)
    return o.reshape(B, T, N_HEADS * V_HEAD) @ P["mla_wo"][j]


def _pool_mixer(h, w_groups, scale):
    B, T, D = h.shape
    hf = h.astype(jnp.float32)
    cs = jnp.concatenate([jnp.zeros((B, 1, D), jnp.float32), jnp.cumsum(hf, axis=1)], axis=1)
    t = jnp.arange(T)
    outs = []
    for g, w in enumerate(POOL_WINDOWS):
        lo = jnp.clip(t - w // 2, 0, T)
        hi = jnp.clip(t + w - w // 2, 0, T)
        sl = slice(g * POOL_GROUP, (g + 1) * POOL_GROUP)
        seg = cs[:, :, sl]
        cnt = (hi - lo).astype(jnp.float32)[None, :, None]
        outs.append((seg[:, hi] - seg[:, lo]) / cnt - hf[:, :, sl])
    pooled = jnp.stack(outs, axis=2).astype(h.dtype)
    y = jnp.einsum("btgc,gcd->btgd", pooled, w_groups).reshape(B, T, D)
    return y * scale


def _fourier_mixer(h, w_out):
    B, T, D = h.shape
    hg = h.astype(jnp.float32).reshape(B, T, N_FFT_GROUPS, FFT_GROUP)
    f = jnp.fft.fft2(hg, axes=(1, 3), norm="ortho").real
    return f.reshape(B, T, D).astype(h.dtype) @ w_out


def _shortconv_mixer(h, w_in, w_conv, w_out):
    z = h @ w_in
    gb, gc, u = jnp.split(z, 3, axis=-1)
    return (gb * _dwconv3(gc * u, w_conv)) @ w_out


def _conv_ffn(h, w_up, conv_w, conv_b, w_down):
    g, u = jnp.split(h @ w_up, 2, axis=-1)
    g = _dwconv3(g, conv_w) + conv_b
    return (jax.nn.gelu(g, approximate=False) * u) @ w_down


def _layer(P, li, x, cond, latent, ctx_cache, cos, sin):
    mod = jax.nn.silu(cond.astype(jnp.float32)).astype(x.dtype) @ P["ada_w"][li] + P["ada_b"][li]
    sh1, sc1, g1, sh2, sc2, g2 = jnp.split(mod, 6, axis=-1)
    h = _modulate(_rmsnorm(x, P["norm1_w"][li]), sh1, sc1)
    kind, j = li % N_MIXERS, li // N_MIXERS
    new_cache = None
    if kind == 0:
        if latent:
            y = _mla_latent(h, P, j, ctx_cache[0], ctx_cache[1], cos, sin)
        else:
            y, ckv, krope = _mla_context(h, P, j)
            new_cache = (ckv, krope)
    elif kind == 1:
        y = _pool_mixer(h, P["pool_w"][j], P["pool_scale"][j])
    elif kind == 2:
        y = _fourier_mixer(h, P["fnet_w"][j])
    else:
        y = _shortconv_mixer(h, P["sconv_win"][j], P["sconv_conv"][j], P["sconv_wout"][j])
    x = x + g1[:, None, :] * y
    h = _modulate(_rmsnorm(x, P["norm2_w"][li]), sh2, sc2)
    x = x + g2[:, None, :] * _conv_ffn(h, P["ffn_up"][li], P["ffn_conv_w"][li],
                                        P["ffn_conv_b"][li], P["ffn_down"][li])
    return x, new_cache


def setup_inputs(seed: int = 0) -> dict:
    key = jax.random.key(seed)
    ks = iter(jax.random.split(key, 40))

    def nrm(shape, scale):
        return jax.random.normal(next(ks), shape, jnp.float32) * scale

    def gain(shape):
        return 1.0 + nrm(shape, 0.05)

    D = D_MODEL
    H = N_HEADS
    return {
        "x_prompt": nrm((BATCH, SEQ, D), 1.0),
        "x_sample": nrm((DEC_BATCH, DEC_SEQ, D), 1.0),
        "cache_ckv": nrm((DEC_BATCH, N_LAYERS_A, PAST_LEN, KV_LORA), 1.0),
        "cache_krope": nrm((DEC_BATCH, N_LAYERS_A, PAST_LEN, QK_ROPE), 1.0),
        "c": nrm((DEC_BATCH, D), 1.0),
        "c_ctx": nrm((D,), 1.0),
        "norm1_w": gain((DEPTH, D)),
        "norm2_w": gain((DEPTH, D)),
        "ada_w": nrm((DEPTH, D, 6 * D), 0.5 * D ** -0.5),
        "ada_b": nrm((DEPTH, 6 * D), 0.01),
        "mla_wdq": nrm((N_LAYERS_A, D, Q_LORA), D ** -0.5),
        "mla_q_norm": gain((N_LAYERS_A, Q_LORA)),
        "mla_wuq": nrm((N_LAYERS_A, Q_LORA, H * (QK_NOPE + QK_ROPE)), Q_LORA ** -0.5),
        "mla_wdkv": nrm((N_LAYERS_A, D, KV_LORA + QK_ROPE), D ** -0.5),
        "mla_kv_norm": gain((N_LAYERS_A, KV_LORA)),
        "mla_wukv": nrm((N_LAYERS_A, KV_LORA, H * (QK_NOPE + V_HEAD)), KV_LORA ** -0.5),
        "mla_wo": nrm((N_LAYERS_A, H * V_HEAD, D), (H * V_HEAD) ** -0.5),
        "pool_w": nrm((N_LAYERS_B, len(POOL_WINDOWS), POOL_GROUP, POOL_GROUP), POOL_GROUP ** -0.5),
        "pool_scale": gain((N_LAYERS_B, D)),
        "fnet_w": nrm((N_LAYERS_C, D, D), D ** -0.5),
        "sconv_win": nrm((N_LAYERS_D, D, 3 * D), D ** -0.5),
        "sconv_conv": nrm((N_LAYERS_D, CONV_W, D), CONV_W ** -0.5),
        "sconv_wout": nrm((N_LAYERS_D, D, D), D ** -0.5),
        "ffn_up": nrm((DEPTH, D, 2 * D_FF), D ** -0.5),
        "ffn_conv_w": nrm((DEPTH, CONV_W, D_FF), CONV_W ** -0.5),
        "ffn_conv_b": nrm((DEPTH, D_FF), 0.01),
        "ffn_down": nrm((DEPTH, D_FF, D), D_FF ** -0.5),
        "final_norm_w": gain((D,)),
    }


def reference(x_prompt, x_sample, cache_ckv, cache_krope, c, c_ctx,
              norm1_w, norm2_w, ada_w, ada_b,
              mla_wdq, mla_q_norm, mla_wuq, mla_wdkv, mla_kv_norm, mla_wukv, mla_wo,
              pool_w, pool_scale, fnet_w, sconv_win, sconv_conv, sconv_wout,
              ffn_up, ffn_conv_w, ffn_conv_b, ffn_down, final_norm_w):
    P = dict(norm1_w=norm1_w, norm2_w=norm2_w, ada_w=ada_w, ada_b=ada_b,
             mla_wdq=mla_wdq, mla_q_norm=mla_q_norm, mla_wuq=mla_wuq, mla_wdkv=mla_wdkv,
             mla_kv_norm=mla_kv_norm, mla_wukv=mla_wukv, mla_wo=mla_wo,
             pool_w=pool_w, pool_scale=pool_scale, fnet_w=fnet_w,
             sconv_win=sconv_win, sconv_conv=sconv_conv, sconv_wout=sconv_wout,
             ffn_up=ffn_up, ffn_conv_w=ffn_conv_w, ffn_conv_b=ffn_conv_b, ffn_down=ffn_down)

    cond_ctx = jnp.broadcast_to(c_ctx[None, :], (x_prompt.shape[0], D_MODEL))
    xp = x_prompt
    ckv_list, krope_list = [], []
    for li in range(DEPTH):
        xp, cache = _layer(P, li, xp, cond_ctx, False, None, None, None)
        if cache is not None:
            ckv_list.append(cache[0])
            krope_list.append(cache[1])
    y_prompt = _rmsnorm(xp, final_norm_w)
    new_cache_ckv = jnp.stack(ckv_list, axis=1)
    new_cache_krope = jnp.stack(krope_list, axis=1)

    rows = x_sample.shape[1] // GRID_W
    cos, sin = _axial_rope_tables(rows)
    xs = x_sample
    for li in range(DEPTH):
        j = li // N_MIXERS
        ctx = (cache_ckv[:, j], cache_krope[:, j]) if li % N_MIXERS == 0 else None
        xs, _ = _layer(P, li, xs, c, True, ctx, cos, sin)
    y_sample = _rmsnorm(xs, final_norm_w)

    return (y_prompt, y_sample, new_cache_ckv, new_cache_krope)
```

```python
import numpy as np
from contextlib import ExitStack
import ml_dtypes

import concourse.bass as bass
import concourse.mybir as mybir
from concourse.bass_utils import run_bass_kernel_spmd

F32 = mybir.dt.float32
BF16 = mybir.dt.bfloat16
AF = mybir.ActivationFunctionType
ALU = mybir.AluOpType

D = 1024
T = 1024
KC = 8
DFF = 2816
FC = 22
DEPTH = 4
EPS = 1e-6
NCORES = 8


class Buf:
    __slots__ = ("name", "w", "r", "sem", "cum", "excl")

    def __init__(self, name):
        self.name = name
        self.excl = False
        self.w = None
        self.r = {}
        self.sem = None
        self.cum = 0


class Q:
    def __init__(self, fw, name, own_wait=True):
        self.fw = fw
        self.name = name
        self.thunks = []
        self.sem = fw.new_sem("q_" + name)
        self.cnt = 0
        self.known = {}
        self.own_wait = own_wait


class FW:
    def __init__(self, nc, es):
        self.nc = nc
        self.es = es
        self.nsem = 0
        self.pe = Q(self, "pe", own_wait=False)
        self.act = Q(self, "act")
        self.dve = Q(self, "dve")
        self.pool = Q(self, "pool")
        self.sp = Q(self, "sp")
        self.out_recs = []
        self.dry = False

    def new_sem(self, name):
        self.nsem += 1
        return self.es.enter_context(self.nc.semaphore(f"s{self.nsem}_{name}"))

    def buf(self, name, dma=False):
        b = Buf(name)
        if dma:
            b.sem = self.new_sem("d_" + name)
        return b

    def _collect(self, q, reads, writes):
        waits = {}

        def need(rec):
            if rec is None:
                return
            sem, val = rec
            if sem is q.sem and not q.own_wait:
                return
            if q.known.get(sem, 0) >= val:
                return
            if waits.get(sem, 0) < val:
                waits[sem] = val

        for b in reads:
            need(b.w)
            if b.excl:
                for sem, val in b.r.items():
                    if sem is not q.sem:
                        need((sem, val))
        for b in writes:
            need(b.w)
            for sem, val in b.r.items():
                need((sem, val))
        for sem, val in waits.items():
            q.known[sem] = val
        return list(waits.items())

    @staticmethod
    def _commit(rec, reads, writes):
        sem, val = rec
        for b in reads:
            if b.r.get(sem, 0) < val:
                b.r[sem] = val
        for b in writes:
            b.w = rec
            b.r = {}

    def op(self, q, fn, reads=(), writes=()):
        if self.dry:
            return None
        wl = self._collect(q, reads, writes)
        q.cnt += 1
        rec = (q.sem, q.cnt)
        q.thunks.append((wl, fn, rec, 1))
        self._commit(rec, reads, writes)
        return rec

    def dma(self, q, out_ap, in_ap, dst=None, src=(), reads=(), kw=None):
        if self.dry:
            return None
        kw = kw or {}
        writes = [dst] if dst is not None else []
        rds = list(src) + list(reads)
        wl = self._collect(q, rds, writes)
        owner = dst if dst is not None else src[0]
        owner.cum += 16
        rec = (owner.sem, owner.cum)

        def fn(e, out_ap=out_ap, in_ap=in_ap, kw=kw):
            return e.dma_start(out=out_ap, in_=in_ap, **kw)

        q.thunks.append((wl, fn, rec, 16))
        self._commit(rec, rds, writes)
        if dst is None:
            self.out_recs.append(rec)
        return rec

    def barrier_bufs(self, bufs):
        allq = [self.pe, self.act, self.dve, self.pool]
        for b in bufs:
            for q in allq:
                if q.cnt > 0:
                    if b.r.get(q.sem, 0) < q.cnt:
                        b.r[q.sem] = q.cnt

    def replay(self, q, eng):
        for wl, fn, rec, inc in q.thunks:
            for sem, val in wl:
                eng.wait_ge(sem, val)
            ins = fn(eng)
            if isinstance(ins, (list, tuple)):
                ins = ins[-1]
            ins.then_inc(rec[0], inc)


class WStream:
    def __init__(self, fw, q, slots, bufs, slot_elems):
        self.fw = fw
        self.q = q
        self.slots = slots
        self.bufs = bufs
        self.n = len(slots)
        self.slot_elems = slot_elems
        self.specs = []
        self.reset()

    def reset(self):
        self.issued = 0
        self.consumed = 0
        self.done_flags = []

    def _view(self, k, shape):
        t = self.slots[k % self.n]
        n = int(np.prod(shape))
        assert n <= self.slot_elems, shape
        if len(shape) == 1:
            return t[:, 0:n]
        if len(shape) == 2:
            return t[:, 0:n].rearrange("p (a b) -> p a b", a=shape[0], b=shape[1])
        return t[:, 0:n].rearrange("p (a b c) -> p a b c", a=shape[0], b=shape[1], c=shape[2])

    def _pump(self):
        while self.issued < len(self.specs):
            k = self.issued
            if k >= self.n and not (k - self.n < len(self.done_flags) and self.done_flags[k - self.n]):
                break
            dram_ap, shape = self.specs[k]
            self.fw.dma(self.q, self._view(k, shape), dram_ap, dst=self.bufs[k % self.n])
            self.issued += 1

    def next(self, dram_ap, shape):
        shape = tuple(shape)
        if self.fw.dry:
            self.specs.append((dram_ap, shape))
            return self._view(0, shape), self.bufs[0], None
        k = self.consumed
        self.consumed += 1
        assert self.specs[k][1] == shape, (k, self.specs[k][1], shape)
        self.done_flags.append(False)
        self._pump()
        assert self.issued > k, (k, self.issued)
        return self._view(k, shape), self.bufs[k % self.n], k

    def done(self, k):
        if self.fw.dry:
            return
        self.done_flags[k] = True
        self._pump()


def _pvec_map():
    m = {}
    o = 0

    def add(name, n):
        nonlocal o
        m[name] = (o, n)
        o += n

    for l in range(DEPTH):
        add(f"n1w{l}", KC)
        add(f"n2w{l}", KC)
        add(f"fw0_{l}", FC)
        add(f"fw1_{l}", FC)
        add(f"fw2_{l}", FC)
        add(f"fb_{l}", FC)
        add(f"adab{l}", 48)
    add("fnw", KC)
    add("cond", KC)
    add("flagneg", 1)
    add("qnw", 3)
    add("kvnw", 2)
    add("pscale", KC)
    add("sw0", KC)
    add("sw1", KC)
    add("sw2", KC)
    m["_n"] = o
    return m


PV = _pvec_map()
NPV = PV["_n"]

STAGE = 4
MLA_SUB = 4.0
NSLOT = 6
SLOT_ELEMS = 4096


def build_program(stage=STAGE):
    nc = bass.Bass("TRN2", target_bir_lowering=False)

    def din(name, shape, dt=F32):
        return nc.dram_tensor(name, list(shape), dt, kind="ExternalInput").ap()

    def dout(name, shape, dt=F32):
        return nc.dram_tensor(name, list(shape), dt, kind="ExternalOutput").ap()

    xin = din("xin", [T, D])
    pvec_d = din("pvec", [128, NPV])
    ident_d = din("ident", [128, 128])
    ada_w = din("ada_w", [DEPTH, D, 6 * D])
    ffn_up = din("ffn_up", [DEPTH, D, 2 * DFF])
    ffn_down = din("ffn_down", [DEPTH, DFF, D])
    pool_w = din("pool_w", [4, 256, 256])
    poolA = din("poolA", [128, 4, 8 * 3 * 128], BF16)
    fnet_w = din("fnet_w", [D, D])
    dftC = din("dftC", [128, 8, T], BF16)
    dftS = din("dftS", [128, 8, T], BF16)
    dftCS = din("dftCS", [128, 2, 512], BF16)
    identb_d = din("identb", [128, 128], BF16)
    sconv_win = din("sconv_win", [D, 3 * D])
    sconv_wout = din("sconv_wout", [D, D])
    wdq = din("wdq", [D, 384])
    wdkv_aug = din("wdkv_aug", [D, 448])
    wuq_aug = din("wuq_aug", [384, 16 * 192])
    wukv = din("wukv", [256, 2048])
    wo = din("wo", [D, D])
    ropeCS_d = din("ropeCS", [128, 2, T])
    cacheT_d = din("cacheT", [128, 2, 512])
    krmask_d = din("krmask", [128, 1536])
    eq_d = din("eq", [128, T])
    yout = dout("yout", [T, D])
    ckv_o = dout("ckv_o", [T, 256])
    kr_o = dout("kr_o", [T, 32])

    es = ExitStack()
    with es:
        fw = FW(nc, es)
        PE, ACT, DVE, POOL, SP = fw.pe, fw.act, fw.dve, fw.pool, fw.sp

        def sb(name, shape, dt):
            return es.enter_context(nc.sbuf_tensor(name, list(shape), dt))

        x_t = sb("x", [128, KC, T], F32)
        xb = [[fw.buf(f"x{c}_{tt}") for tt in range(2)] for c in range(KC)]
        h_t = sb("h", [128, KC, T], BF16)
        hb = [[fw.buf(f"h{c}_{tt}") for tt in range(2)] for c in range(KC)]
        a_t = sb("a", [128, 26, T], BF16)
        ab = [fw.buf(f"a{j}") for j in range(26)]
        pvec = sb("pvec_sb", [128, NPV], F32)
        pvec_b = fw.buf("pvec", dma=True)
        ident = sb("ident_sb", [128, 128], F32)
        ident_b = fw.buf("ident", dma=True)
        ones_bf = sb("ones_bf", [128, 128], BF16)
        one_f = sb("one_f", [128, 1], F32)
        eps_t = sb("eps", [128, 1], F32)
        const_b = fw.buf("consts")
        stg_t = [sb(f"stg{i}", [128, D], F32) for i in range(2)]
        stg_b = [fw.buf(f"stg{i}", dma=True) for i in range(2)]
        tA_t = [sb(f"tA{i}", [128, T], F32) for i in range(2)]
        tA_b = [fw.buf(f"tA{i}") for i in range(2)]
        sq_t = [sb(f"sq{i}", [128, 512], BF16) for i in range(2)]
        sq_b = [fw.buf(f"sq{i}") for i in range(2)]
        rstd_t = sb("rstd", [128, 512], F32)
        rstd_b = fw.buf("rstd")
        xn_t = [sb(f"xn{i}", [128, 512], F32) for i in range(2)]
        xn_b = [fw.buf(f"xn{i}") for i in range(2)]
        scond = sb("scond", [128, KC], BF16)
        scond_b = fw.buf("scond")
        row_t = [sb(f"row{i}", [1, 512], F32) for i in range(2)]
        row_b = [fw.buf(f"row{i}") for i in range(2)]
        mod_t = [sb(f"mod{l}", [128, 48], F32) for l in range(DEPTH)]
        mod_b = [fw.buf(f"mod{l}") for l in range(DEPTH)]
        col_t = [sb(f"cols{l}", [128, 16 + 2 * FC], F32) for l in range(DEPTH)]
        col_b = [fw.buf(f"cols{l}") for l in range(DEPTH)]
        identb = sb("identb_sb", [128, 128], BF16)
        identb_b = fw.buf("identb", dma=True)
        cs_t = sb("dftcs_sb", [128, 2, 512], BF16)
        cs_b = fw.buf("dftcs", dma=True)
        ropecs = sb("ropecs", [128, 2, T], F32)
        ropecs_b = fw.buf("ropecs", dma=True)
        onesf = sb("onesf", [128, 128], F32)
        sel_t = sb("sel", [128, 2, 128], F32)
        ckvst = sb("ckvst", [128, 8, 256], F32)
        ckvst_b = fw.buf("ckvst", dma=True)
        krst = sb("krst", [128, 8, 32], F32)
        krst_b = fw.buf("krst", dma=True)
        ckvall_b = [fw.buf(f"ckvall{c}", dma=True) for c in range(2)]
        kr_b = fw.buf("KR", dma=True)
        qt_b = [fw.buf(f"QT{i}", dma=True) for i in range(3)]
        kt_b = [fw.buf(f"KT{i}") for i in range(3)]
        vp_b = [fw.buf(f"VP{i}") for i in range(2)]
        pt_b = [fw.buf(f"PT{i}") for i in range(4)]
        cq_b = [fw.buf(f"cq{i}") for i in range(3)]
        mcol_t = sb("mcols", [128, 32], F32)
        mcol_b = fw.buf("mcols")
        slots = [sb(f"wslot{i}", [128, SLOT_ELEMS], BF16) for i in range(NSLOT)]
        slot_b = [fw.buf(f"wslot{i}", dma=True) for i in range(NSLOT)]
        ws = WStream(fw, POOL, slots, slot_b, SLOT_ELEMS)

        pd_t = [es.enter_context(nc.psum_tensor(f"pd{i}", [128, 1024], F32)) for i in range(4)]
        ps_b = [fw.buf(f"ps{i}") for i in range(8)]
        for b in ps_b:
            b.excl = True

        pdb_t = [t.bitcast(BF16) for t in pd_t]

        def ps(bank):
            return pd_t[bank // 2][:, (bank % 2) * 512:(bank % 2) * 512 + 512]

        def psb(bank):
            return pdb_t[bank // 2][:, (bank % 2) * 1024:(bank % 2) * 1024 + 1024]

        def pv(name, j=0, n=1):
            o, _ = PV[name]
            return pvec[:, o + j:o + j + n]

        def tsl(tt):
            return slice(tt * 512, (tt + 1) * 512)

        def emit():
            fw.dma(SP, pvec[:], pvec_d, dst=pvec_b)
            fw.dma(SP, ident[:], ident_d, dst=ident_b)
            fw.dma(SP, identb[:], identb_d, dst=identb_b)
            fw.dma(SP, cs_t[:], dftCS, dst=cs_b)
            fw.op(DVE, lambda e: e.memset(ones_bf[:], 1.0), writes=[const_b])
            fw.op(DVE, lambda e: e.memset(eps_t[:], EPS), writes=[const_b])
            fw.op(DVE, lambda e: e.memset(one_f[:], 1.0), writes=[const_b])
            fw.op(DVE, lambda e: e.memset(onesf[:], 1.0), writes=[const_b])
            fw.op(DVE, lambda e: e.memset(sel_t[:], 0.0), writes=[const_b])
            fw.op(DVE, lambda e: e.memset(sel_t[64:65, 0, :], 1.0), writes=[const_b])
            fw.op(DVE, lambda e: e.memset(sel_t[0:1, 1, :], 1.0), writes=[const_b])

            for i in range(8):
                s = i % 2
                tt = i // 4
                fw.dma(SP, stg_t[s][:], xin[i * 128:(i + 1) * 128, :], dst=stg_b[s])
                for half in range(2):
                    bank = (i * 2 + half) % 8

                    def mm(e, s=s, half=half, bank=bank):
                        r = None
                        for cc in range(4):
                            c = half * 4 + cc
                            r = e.transpose(ps(bank)[:, cc * 128:(cc + 1) * 128],
                                            stg_t[s][:, c * 128:(c + 1) * 128], ident[:])
                        return r
                    fw.op(PE, mm, reads=[stg_b[s], ident_b], writes=[ps_b[bank]])
                    wr = [xb[c][tt] for c in range(half * 4, half * 4 + 4)]
                    if half == 0:
                        fw.op(DVE, lambda e, half=half, bank=bank, i=i: e.tensor_copy(
                            out=x_t[:, half * 4:half * 4 + 4, i * 128:(i + 1) * 128],
                            in_=ps(bank).rearrange("p (c t) -> p c t", c=4)),
                            reads=[ps_b[bank]], writes=wr)
                    else:
                        fw.op(ACT, lambda e, half=half, bank=bank, i=i: e.activation(
                            out=x_t[:, half * 4:half * 4 + 4, i * 128:(i + 1) * 128],
                            in_=ps(bank).rearrange("p (c t) -> p c t", c=4), func=AF.Copy),
                            reads=[ps_b[bank]], writes=wr)

            fw.op(ACT, lambda e: e.activation(out=scond[:], in_=pv("cond", 0, KC), func=AF.Silu),
                  reads=[pvec_b], writes=[scond_b])

            def rms_stats(srcs, nch, inv_n, bank):
                for c, (ap, bufs) in enumerate(srcs):
                    s = c % 2
                    fw.op(ACT, lambda e, ap=ap, s=s: e.activation(out=sq_t[s][:], in_=ap,
                                                                   func=AF.Square),
                          reads=bufs, writes=[sq_b[s]])
                    fw.op(PE, lambda e, c=c, s=s: e.matmul(ps(bank), ones_bf[:], sq_t[s][:],
                                                           start=(c == 0), stop=(c == nch - 1)),
                          reads=[const_b, sq_b[s]], writes=[ps_b[bank]])
                fw.op(ACT, lambda e: e.activation(out=rstd_t[:], in_=ps(bank), func=AF.Sqrt,
                                                  bias=eps_t[:, 0:1], scale=inv_n),
                      reads=[ps_b[bank], const_b], writes=[rstd_b])
                fw.op(DVE, lambda e: e.reciprocal(out=rstd_t[:], in_=rstd_t[:]),
                      reads=[rstd_b], writes=[rstd_b])

            def norm_mod(l, which):
                ao = 0 if which == 1 else 8
                bo = 0 if which == 1 else 24
                for tt in range(2):
                    rms_stats([(x_t[:, c, tsl(tt)], [xb[c][tt]]) for c in range(KC)], KC, 1.0 / D,
                              bank=tt)
                    for c in range(KC):
                        s = c % 2
                        fw.op(DVE, lambda e, c=c, s=s, tt=tt: e.tensor_tensor(
                            out=xn_t[s][:], in0=x_t[:, c, tsl(tt)], in1=rstd_t[:], op=ALU.mult),
                            reads=[xb[c][tt], rstd_b], writes=[xn_b[s]])
                        fw.op(ACT, lambda e, c=c, s=s, tt=tt: e.activation(
                            out=h_t[:, c, tsl(tt)], in_=xn_t[s][:], func=AF.Identity,
                            bias=mod_t[l][:, bo + c:bo + c + 1],
                            scale=col_t[l][:, ao + c:ao + c + 1]),
                            reads=[xn_b[s], mod_b[l], col_b[l]], writes=[hb[c][tt]])

            def ada_block(l, nb, b0=0, b1=1):
                wv = ada_w[l].rearrange("(k p) n -> p k n", p=128)
                w, wb, wk = ws.next(wv[:, :, nb * 512:(nb + 1) * 512], (KC, 512))

                def mm(e, w=w):
                    r = None
                    for k in range(KC):
                        r = e.matmul(ps(b0)[0:1, :], scond[:, k:k + 1], w[:, k, :],
                                     start=(k == 0), stop=(k == KC - 1))
                    return r
                fw.op(PE, mm, reads=[scond_b, wb], writes=[ps_b[b0]])
                ws.done(wk)
                s = nb % 2
                fw.op(ACT, lambda e, s=s: e.activation(out=row_t[s][:], in_=ps(b0)[0:1, :], func=AF.Copy),
                      reads=[ps_b[b0]], writes=[row_b[s]])

                def mt(e, s=s):
                    r = None
                    for j in range(4):
                        r = e.matmul(ps(b1)[:, j:j + 1], row_t[s][0:1, j * 128:(j + 1) * 128],
                                     one_f[0:1, 0:1], start=True, stop=True)
                    return r
                fw.op(PE, mt, reads=[row_b[s], const_b], writes=[ps_b[b1]])
                fw.op(DVE, lambda e: e.tensor_tensor(out=mod_t[l][:, nb * 4:nb * 4 + 4], in0=ps(b1)[:, 0:4],
                                                     in1=pv(f"adab{l}", nb * 4, 4), op=ALU.add),
                      reads=[ps_b[b1], pvec_b], writes=[mod_b[l]])

            def ada_finish(l):
                fw.op(DVE, lambda e: e.scalar_tensor_tensor(
                    out=col_t[l][:, 0:8], in0=mod_t[l][:, 8:16], scalar=1.0,
                    in1=pv(f"n1w{l}", 0, KC), op0=ALU.add, op1=ALU.mult),
                    reads=[mod_b[l], pvec_b], writes=[col_b[l]])
                fw.op(DVE, lambda e: e.scalar_tensor_tensor(
                    out=col_t[l][:, 8:16], in0=mod_t[l][:, 32:40], scalar=1.0,
                    in1=pv(f"n2w{l}", 0, KC), op0=ALU.add, op1=ALU.mult),
                    reads=[mod_b[l], pvec_b], writes=[col_b[l]])
                fw.op(DVE, lambda e: e.tensor_scalar(
                    out=col_t[l][:, 16:16 + FC], in0=pv(f"fw0_{l}", 0, FC),
                    scalar1=pv("flagneg"), scalar2=None, op0=ALU.mult),
                    reads=[pvec_b], writes=[col_b[l]])
                fw.op(DVE, lambda e: e.tensor_scalar(
                    out=col_t[l][:, 16 + FC:16 + 2 * FC], in0=pv(f"fw2_{l}", 0, FC),
                    scalar1=pv("flagneg"), scalar2=None, op0=ALU.mult),
                    reads=[pvec_b], writes=[col_b[l]])

            def conv_fix(eng, t_ap, src_ap, nf0, nf2, reads, writes):
                fw.op(eng, lambda e: e.scalar_tensor_tensor(
                    out=t_ap[:, 256:1024:256], in0=src_ap[:, 255:1023:256], scalar=nf0,
                    in1=t_ap[:, 256:1024:256], op0=ALU.mult, op1=ALU.add),
                    reads=reads, writes=writes)
                fw.op(eng, lambda e: e.scalar_tensor_tensor(
                    out=t_ap[:, 255:1023:256], in0=src_ap[:, 256:1024:256], scalar=nf2,
                    in1=t_ap[:, 255:1023:256], op0=ALU.mult, op1=ALU.add),
                    reads=reads, writes=writes)

            def ffn(l, mid_hook=None):
                upv = ffn_up[l].rearrange("(k p) n -> p k n", p=128)
                dnv = ffn_down[l].rearrange("(k p) n -> p k n", p=128)
                j = 0
                for jg in range(6):
                    ncol = 512 if jg < 5 else 256
                    gw, gwb, gk = ws.next(upv[:, :, jg * 512:jg * 512 + ncol], (KC, ncol))
                    uw, uwb, uk = ws.next(upv[:, :, DFF + jg * 512:DFF + jg * 512 + ncol],
                                          (KC, ncol))
                    for jj in range(ncol // 128):
                        dbl = 2 * (j % 2)
                        for which, (w, wb) in enumerate(((gw, gwb), (uw, uwb))):
                            for tt in range(2):
                                bank = (dbl + which) * 2 + tt

                                def mm(e, w=w, jj=jj, tt=tt, bank=bank):
                                    r = None
                                    for k in range(KC):
                                        r = e.matmul(ps(bank), w[:, k, jj * 128:(jj + 1) * 128],
                                                     h_t[:, k, tsl(tt)],
                                                     start=(k == 0), stop=(k == KC - 1))
                                    return r
                                fw.op(PE, mm, reads=[wb] + [hb[k][tt] for k in range(KC)],
                                      writes=[ps_b[bank]])
                        g_ap = pd_t[dbl][:]
                        u_ap = pd_t[dbl + 1][:]
                        gB = [ps_b[dbl * 2], ps_b[dbl * 2 + 1]]
                        uB = [ps_b[dbl * 2 + 2], ps_b[dbl * 2 + 3]]
                        s = j % 2
                        t = tA_t[s]
                        tb = tA_b[s]
                        fw.op(ACT, lambda e, t=t, g_ap=g_ap, j=j: e.activation(
                            out=t[:], in_=g_ap, func=AF.Identity, bias=pv(f"fb_{l}", j),
                            scale=pv(f"fw1_{l}", j)),
                            reads=gB + [pvec_b], writes=[tb])
                        fw.op(DVE, lambda e, t=t, g_ap=g_ap, j=j: e.scalar_tensor_tensor(
                            out=t[:, 1:T], in0=g_ap[:, 0:T - 1], scalar=pv(f"fw0_{l}", j),
                            in1=t[:, 1:T], op0=ALU.mult, op1=ALU.add),
                            reads=gB + [pvec_b, tb], writes=[tb])
                        fw.op(DVE, lambda e, t=t, g_ap=g_ap, j=j: e.scalar_tensor_tensor(
                            out=t[:, 0:T - 1], in0=g_ap[:, 1:T], scalar=pv(f"fw2_{l}", j),
                            in1=t[:, 0:T - 1], op0=ALU.mult, op1=ALU.add),
                            reads=gB + [pvec_b, tb], writes=[tb])
                        conv_fix(DVE, t, g_ap, col_t[l][:, 16 + j:17 + j],
                                 col_t[l][:, 16 + FC + j:17 + FC + j],
                                 reads=gB + [col_b[l], tb], writes=[tb])
                        fw.op(ACT, lambda e, t=t: e.activation(out=t[:], in_=t[:], func=AF.Gelu),
                              reads=[tb], writes=[tb])
                        fw.op(DVE, lambda e, t=t, u_ap=u_ap, j=j: e.tensor_tensor(
                            out=a_t[:, j, :], in0=t[:], in1=u_ap, op=ALU.mult),
                            reads=[tb] + uB, writes=[ab[j]])
                        j += 1
                    ws.done(gk)
                    ws.done(uk)
                    if mid_hook is not None:
                        mid_hook(jg)
                for oc in range(KC):
                    w, wb, wk = ws.next(dnv[:, :, oc * 128:(oc + 1) * 128], (FC, 128))
                    for tt in range(2):
                        bank = (oc * 2 + tt) % 8

                        def mm(e, w=w, tt=tt, bank=bank):
                            r = None
                            for k in range(FC):
                                r = e.matmul(ps(bank), w[:, k, :], a_t[:, k, tsl(tt)],
                                             start=(k == 0), stop=(k == FC - 1))
                            return r
                        fw.op(PE, mm, reads=[wb] + ab[0:FC], writes=[ps_b[bank]])
                        fw.op(DVE, lambda e, oc=oc, tt=tt, bank=bank: e.scalar_tensor_tensor(
                            out=x_t[:, oc, tsl(tt)], in0=ps(bank), scalar=mod_t[l][:, 40 + oc:41 + oc],
                            in1=x_t[:, oc, tsl(tt)], op0=ALU.mult, op1=ALU.add),
                            reads=[ps_b[bank], mod_b[l], xb[oc][tt]], writes=[xb[oc][tt]])
                    ws.done(wk)


            def evac(i, out_ap, in_ap, reads, writes):
                if i % 2 == 0:
                    fw.op(DVE, lambda e: e.tensor_copy(out=out_ap, in_=in_ap), reads=reads, writes=writes)
                else:
                    fw.op(ACT, lambda e: e.activation(out=out_ap, in_=in_ap, func=AF.Copy),
                          reads=reads, writes=writes)

            def out_linear(l, wdram, src_ap, src_bufs, gcol):
                wv = wdram.rearrange("(k p) n -> p k n", p=128)
                n = 0
                for nb in range(2):
                    w, wb, wk = ws.next(wv[:, :, nb * 512:(nb + 1) * 512], (KC, 512))
                    for o4 in range(4):
                        oc = nb * 4 + o4
                        for tt in range(2):
                            bank = n % 8
                            n += 1

                            def mm(e, w=w, o4=o4, tt=tt, bank=bank):
                                r = None
                                for kk in range(KC):
                                    r = e.matmul(ps(bank), w[:, kk, o4 * 128:(o4 + 1) * 128],
                                                 src_ap(kk, tt), start=(kk == 0), stop=(kk == KC - 1))
                                return r
                            fw.op(PE, mm, reads=[wb] + src_bufs(tt), writes=[ps_b[bank]])
                            fw.op(DVE, lambda e, oc=oc, tt=tt, bank=bank: e.scalar_tensor_tensor(
                                out=x_t[:, oc, tsl(tt)], in0=ps(bank), scalar=gcol(oc),
                                in1=x_t[:, oc, tsl(tt)], op0=ALU.mult, op1=ALU.add),
                                reads=[ps_b[bank], mod_b[l], mcol_b, xb[oc][tt]], writes=[xb[oc][tt]])
                    ws.done(wk)

            def mixer_pool(l):
                fw.barrier_bufs(ab)
                fw.op(DVE, lambda e: e.tensor_tensor(out=mcol_t[:, 0:8], in0=pv("pscale", 0, KC),
                                                     in1=mod_t[l][:, 16:24], op=ALU.mult),
                      reads=[pvec_b, mod_b[l]], writes=[mcol_b])
                for i in range(8):
                    bank = i % 8

                    def tr(e, i=i, bank=bank):
                        r = None
                        for c in range(KC):
                            r = e.transpose(psb(bank)[:, c * 128:(c + 1) * 128],
                                            h_t[:, c, i * 128:(i + 1) * 128], identb[:])
                        return r
                    fw.op(PE, tr, reads=[hb[c][i // 4] for c in range(KC)] + [identb_b],
                          writes=[ps_b[bank]])
                    evac(i, a_t[:, i, :], psb(bank), [ps_b[bank]], [ab[i]])
                aw = ab_k = None
                for cc in range(8):
                    g = cc // 2
                    if cc % 2 == 0:
                        aw, awb, ak = ws.next(poolA[:, g, :].rearrange("p (a b c) -> p a b c", a=8, b=3, c=128), (8, 3, 128))
                    dbl = cc % 4

                    def mm(e, cc=cc, aw=aw, dbl=dbl):
                        r = None
                        for i in range(8):
                            ds = [d for d in range(3) if 0 <= i + d - 1 < 8]
                            for n, d in enumerate(ds):
                                r = e.matmul(pd_t[dbl][:, i * 128:(i + 1) * 128],
                                             a_t[:, i + d - 1, cc * 128:(cc + 1) * 128],
                                             aw[:, i, d, :], start=(n == 0), stop=(n == len(ds) - 1))
                        return r
                    fw.op(PE, mm, reads=ab[0:8] + [awb], writes=[ps_b[2 * dbl], ps_b[2 * dbl + 1]])
                    evac(cc, a_t[:, 8 + cc, :], pd_t[dbl][:], [ps_b[2 * dbl], ps_b[2 * dbl + 1]],
                         [ab[8 + cc]])
                    if cc % 2 == 1:
                        ws.done(ak)
                pw, pwb, pk = ws.next(pool_w.rearrange("g (kk p) d -> p g kk d", p=128), (4, 2, 256))
                n = 0
                for dc in range(8):
                    g = dc // 2
                    for tt in range(2):
                        bank = n % 8
                        n += 1

                        def mm2(e, dc=dc, g=g, tt=tt, bank=bank):
                            r = None
                            for kk in range(2):
                                r = e.matmul(ps(bank), pw[:, g, kk, (dc % 2) * 128:(dc % 2) * 128 + 128],
                                             a_t[:, 8 + g * 2 + kk, tsl(tt)], start=(kk == 0), stop=(kk == 1))
                            return r
                        fw.op(PE, mm2, reads=[pwb, ab[8 + g * 2], ab[9 + g * 2]], writes=[ps_b[bank]])
                        fw.op(DVE, lambda e, dc=dc, tt=tt, bank=bank: e.scalar_tensor_tensor(
                            out=x_t[:, dc, tsl(tt)], in0=ps(bank), scalar=mcol_t[:, dc:dc + 1],
                            in1=x_t[:, dc, tsl(tt)], op0=ALU.mult, op1=ALU.add),
                            reads=[ps_b[bank], mcol_b, xb[dc][tt]], writes=[xb[dc][tt]])
                ws.done(pk)

            def mixer_fnet(l):
                fw.barrier_bufs(ab)

                def pq(i, g):
                    return a_t[:, 2 * i + g // 2, (g % 2) * 512:(g % 2) * 512 + 512]
                n = 0
                for i in range(8):
                    for g in range(4):
                        bank = n % 8

                        def mm(e, i=i, g=g, bank=bank):
                            r = None
                            for kk in range(2):
                                r = e.matmul(ps(bank), h_t[:, g * 2 + kk, i * 128:(i + 1) * 128],
                                             cs_t[:, kk, :], start=(kk == 0), stop=(kk == 1))
                            return r
                        fw.op(PE, mm, reads=[hb[g * 2][i // 4], hb[g * 2 + 1][i // 4], cs_b],
                              writes=[ps_b[bank]])
                        evac(n, pq(i, g), ps(bank), [ps_b[bank]], [ab[2 * i + g // 2]])
                        n += 1
                for tt in range(2):
                    cw, cwb, ck = ws.next(dftC[:, :, tsl(tt)], (8, 512))
                    sw, swb, sk = ws.next(dftS[:, :, tsl(tt)], (8, 512))
                    for mc in range(8):
                        g, m2 = mc // 2, mc % 2
                        bank = n % 8

                        def mm(e, g=g, m2=m2, bank=bank, cw=cw, sw=sw):
                            r = None
                            for i in range(8):
                                r = e.matmul(ps(bank), pq(i, g)[:, m2 * 128:(m2 + 1) * 128], cw[:, i, :],
                                             start=(i == 0), stop=False)
                                r = e.matmul(ps(bank), pq(i, g)[:, 256 + m2 * 128:256 + (m2 + 1) * 128],
                                             sw[:, i, :], start=False, stop=(i == 7))
                            return r
                        fw.op(PE, mm, reads=ab[0:16] + [cwb, swb], writes=[ps_b[bank]])
                        evac(n, a_t[:, 16 + mc, tsl(tt)], ps(bank), [ps_b[bank]], [ab[16 + mc]])
                        n += 1
                    ws.done(ck)
                    ws.done(sk)
                out_linear(l, fnet_w, lambda kk, tt: a_t[:, 16 + kk, tsl(tt)],
                           lambda tt: ab[16:24], lambda oc: mod_t[l][:, 16 + oc:17 + oc])

            def mixer_sconv(l):
                fw.barrier_bufs(ab)
                fw.op(DVE, lambda e: e.tensor_scalar(out=mcol_t[:, 8:16], in0=pv("sw0", 0, KC),
                                                     scalar1=pv("flagneg"), scalar2=None, op0=ALU.mult),
                      reads=[pvec_b], writes=[mcol_b])
                fw.op(DVE, lambda e: e.tensor_scalar(out=mcol_t[:, 16:24], in0=pv("sw2", 0, KC),
                                                     scalar1=pv("flagneg"), scalar2=None, op0=ALU.mult),
                      reads=[pvec_b], writes=[mcol_b])
                wv = sconv_win.rearrange("(k p) n -> p k n", p=128)
                nd = 0
                for jg in range(2):
                    wl = [ws.next(wv[:, :, part * D + jg * 512:part * D + jg * 512 + 512], (KC, 512))
                          for part in range(3)]
                    for jj in range(4):
                        j = jg * 4 + jj
                        dbls = []
                        for part in range(3):
                            dbl = nd % 4
                            nd += 1
                            dbls.append(dbl)
                            w, wb, _ = wl[part]
                            for tt in range(2):
                                bank = 2 * dbl + tt

                                def mm(e, w=w, jj=jj, tt=tt, bank=bank):
                                    r = None
                                    for kk in range(KC):
                                        r = e.matmul(ps(bank), w[:, kk, jj * 128:(jj + 1) * 128],
                                                     h_t[:, kk, tsl(tt)], start=(kk == 0), stop=(kk == KC - 1))
                                    return r
                                fw.op(PE, mm, reads=[wb] + [hb[kk][tt] for kk in range(KC)],
                                      writes=[ps_b[bank]])
                        gbB = [ps_b[2 * dbls[0]], ps_b[2 * dbls[0] + 1]]
                        gcB = [ps_b[2 * dbls[1]], ps_b[2 * dbls[1] + 1]]
                        uB = [ps_b[2 * dbls[2]], ps_b[2 * dbls[2] + 1]]
                        s = j % 2
                        v, vb = tA_t[s], tA_b[s]
                        t2, t2b = stg_t[s], stg_b[s]
                        fw.op(ACT, lambda e, v=v, d=dbls[2]: e.activation(out=v[:], in_=pd_t[d][:], func=AF.Copy),
                              reads=uB, writes=[vb])
                        fw.op(DVE, lambda e, v=v, d=dbls[1]: e.tensor_tensor(out=v[:], in0=pd_t[d][:], in1=v[:],
                                                                             op=ALU.mult),
                              reads=gcB + [vb], writes=[vb])
                        fw.op(ACT, lambda e, v=v, t2=t2, j=j: e.activation(out=t2[:], in_=v[:], func=AF.Copy,
                                                                             scale=pv("sw1", j)),
                              reads=[vb, pvec_b], writes=[t2b])
                        fw.op(DVE, lambda e, v=v, t2=t2, j=j: e.scalar_tensor_tensor(
                            out=t2[:, 1:T], in0=v[:, 0:T - 1], scalar=pv("sw0", j), in1=t2[:, 1:T],
                            op0=ALU.mult, op1=ALU.add), reads=[vb, pvec_b, t2b], writes=[t2b])
                        fw.op(DVE, lambda e, v=v, t2=t2, j=j: e.scalar_tensor_tensor(
                            out=t2[:, 0:T - 1], in0=v[:, 1:T], scalar=pv("sw2", j), in1=t2[:, 0:T - 1],
                            op0=ALU.mult, op1=ALU.add), reads=[vb, pvec_b, t2b], writes=[t2b])
                        conv_fix(DVE, t2, v, mcol_t[:, 8 + j:9 + j], mcol_t[:, 16 + j:17 + j],
                                 reads=[vb, mcol_b, t2b], writes=[t2b])
                        fw.op(DVE, lambda e, t2=t2, j=j, d=dbls[0]: e.tensor_tensor(
                            out=a_t[:, j, :], in0=pd_t[d][:], in1=t2[:], op=ALU.mult),
                            reads=gbB + [t2b], writes=[ab[j]])
                    for _, _, wk in wl:
                        ws.done(wk)
                out_linear(l, sconv_wout, lambda kk, tt: a_t[:, kk, tsl(tt)],
                           lambda tt: ab[0:8], lambda oc: mod_t[l][:, 16 + oc:17 + oc])


            def mixer_mla(l):
                SC = 1.0 / float(np.sqrt(96.0))
                arena = a_t[:].rearrange("p a b -> p (a b)")
                cqT = lambda c: a_t[:, c, :]
                ckvall = lambda c: arena[:, 3 * T + c * 1536:3 * T + (c + 1) * 1536]
                KR = arena[:, 6 * T:6 * T + 1536]
                QT = lambda r: a_t[:, 8 + r, :]
                KT = lambda r: arena[:, (11 + 2 * r) * T:(11 + 2 * r) * T + 1536]
                VP = lambda r: arena[:, (17 + 3 * r) * T:(17 + 3 * r) * T + 3072].rearrange(
                    "p (k x) -> p k x", k=12, x=256)
                PT = lambda r: arena[:, 23 * T + r * 512:23 * T + (r + 1) * 512]
                for i in range(2):
                    rr = fw.dma(SP, ropecs[:, i, :], ropeCS_d[:, i, :], dst=ropecs_b)
                    if MLA_SUB == 0.11 and not fw.dry:
                        fw.out_recs.append(rr)
                for c in range(2):
                    fw.dma(POOL, ckvall(c)[:, 0:512], cacheT_d[:, c, :], dst=ckvall_b[c])
                fw.dma(POOL, KR, krmask_d, dst=kr_b)
                for r in range(3):
                    fw.dma(POOL, QT(r), eq_d, dst=qt_b[r])
                for r in range(2):
                    fw.op(DVE, lambda e, r=r: e.memset(VP(r), 0.0), writes=[vp_b[r]])
                    fw.op(DVE, lambda e, r=r: e.memset(VP(r)[:, :, 64:65], 1.0), writes=[vp_b[r]])
                    fw.op(DVE, lambda e, r=r: e.memset(VP(r)[:, :, 128:129], 1.0), writes=[vp_b[r]])

                if MLA_SUB < 0.2:
                    return
                w, wb, wk = ws.next(wdq.rearrange("(k p) n -> p k n", p=128), (KC, 384))
                for tt in range(2):
                    banks = [(4 * tt + c) % 8 for c in range(3)]
                    for c in range(3):
                        def mm(e, c=c, tt=tt, bank=banks[c], w=w):
                            r = None
                            for kk in range(KC):
                                r = e.matmul(ps(bank), w[:, kk, c * 128:(c + 1) * 128], h_t[:, kk, tsl(tt)],
                                             start=(kk == 0), stop=(kk == KC - 1))
                            return r
                        fw.op(PE, mm, reads=[wb] + [hb[kk][tt] for kk in range(KC)], writes=[ps_b[banks[c]]])
                    rms_stats([(ps(banks[c]), [ps_b[banks[c]]]) for c in range(3)], 3, 1.0 / 384,
                              bank=(4 * tt + 3) % 8)
                    for c in range(3):
                        s = c % 2
                        fw.op(DVE, lambda e, s=s, bank=banks[c]: e.tensor_tensor(
                            out=xn_t[s][:], in0=ps(bank), in1=rstd_t[:], op=ALU.mult),
                            reads=[ps_b[banks[c]], rstd_b], writes=[xn_b[s]])
                        fw.op(ACT, lambda e, s=s, c=c, tt=tt: e.activation(
                            out=cqT(c)[:, tsl(tt)], in_=xn_t[s][:], func=AF.Copy, scale=pv("qnw", c)),
                            reads=[xn_b[s], pvec_b], writes=[cq_b[c]])
                ws.done(wk)

                if MLA_SUB < 0.5:
                    return
                w, wb, wk = ws.next(wdkv_aug.rearrange("(k p) n -> p k n", p=128), (KC, 448))
                for tt in range(2):
                    banks = [(5 * tt + i) % 8 for i in range(4)]
                    cols = [(0, 128), (128, 128), (256, 96), (352, 96)]
                    for i in range(4):
                        c0, m = cols[i]

                        def mm(e, c0=c0, m=m, tt=tt, bank=banks[i], w=w):
                            r = None
                            for kk in range(KC):
                                r = e.matmul(ps(bank)[0:m, :], w[:, kk, c0:c0 + m], h_t[:, kk, tsl(tt)],
                                             start=(kk == 0), stop=(kk == KC - 1))
                            return r
                        fw.op(PE, mm, reads=[wb] + [hb[kk][tt] for kk in range(KC)], writes=[ps_b[banks[i]]])
                    if MLA_SUB < 0.51:
                        continue
                    rms_stats([(ps(banks[c]), [ps_b[banks[c]]]) for c in range(2)], 2, 1.0 / 256,
                              bank=(5 * tt + 4) % 8)
                    if MLA_SUB < 0.52:
                        continue
                    for c in range(2):
                        s = c % 2
                        fw.op(DVE, lambda e, s=s, bank=banks[c]: e.tensor_tensor(
                            out=xn_t[s][:], in0=ps(bank), in1=rstd_t[:], op=ALU.mult),
                            reads=[ps_b[banks[c]], rstd_b], writes=[xn_b[s]])
                        fw.op(ACT, lambda e, s=s, c=c, tt=tt: e.activation(
                            out=ckvall(c)[:, 512 + tt * 512:1024 + tt * 512], in_=xn_t[s][:], func=AF.Copy,
                            scale=pv("kvnw", c)),
                            reads=[xn_b[s], pvec_b], writes=[ckvall_b[c]])
                        fw.op(DVE, lambda e, s=s, c=c, tt=tt: e.tensor_scalar(
                            out=tA_t[c][:, tsl(tt)], in0=xn_t[s][:], scalar1=pv("kvnw", c), scalar2=None,
                            op0=ALU.mult),
                            reads=[xn_b[s], pvec_b], writes=[tA_b[c]])
                    if MLA_SUB < 0.55:
                        continue
                    bA, bB = banks[2], banks[3]
                    fw.op(ACT, lambda e, tt=tt, bA=bA: e.activation(
                        out=stg_t[0][64:96, tsl(tt)], in_=ps(bA)[64:96, :], func=AF.Copy),
                        reads=[ps_b[bA]], writes=[stg_b[0]])
                    if MLA_SUB < 0.56:
                        continue
                    fw.op(DVE, lambda e, tt=tt, bA=bA: e.tensor_tensor(
                        out=xn_t[0][64:96, :], in0=ps(bA)[64:96, :], in1=ropecs[64:96, 0, tsl(tt)], op=ALU.mult),
                        reads=[ps_b[bA], ropecs_b], writes=[xn_b[0]])
                    if MLA_SUB < 0.57:
                        continue
                    fw.op(DVE, lambda e, tt=tt, bB=bB: e.tensor_tensor(
                        out=xn_t[1][64:96, :], in0=ps(bB)[64:96, :], in1=ropecs[64:96, 1, tsl(tt)], op=ALU.mult),
                        reads=[ps_b[bB], ropecs_b], writes=[xn_b[1]])
                    if MLA_SUB < 0.58:
                        continue
                    fw.op(DVE, lambda e, tt=tt: e.tensor_tensor(
                        out=KR[64:96, 512 + tt * 512:1024 + tt * 512], in0=xn_t[0][64:96, :],
                        in1=xn_t[1][64:96, :], op=ALU.add),
                        reads=[xn_b[0], xn_b[1]], writes=[kr_b])
                ws.done(wk)

                if MLA_SUB < 0.7:
                    return
                for i in range(8):
                    bank = i % 8

                    def tr(e, i=i, bank=bank):
                        r = None
                        for c in range(2):
                            r = e.transpose(ps(bank)[:, c * 128:(c + 1) * 128],
                                            tA_t[c][:, i * 128:(i + 1) * 128], ident[:])
                        r = e.transpose(ps(bank)[:, 256:384], stg_t[0][:, i * 128:(i + 1) * 128], ident[:])
                        return r
                    fw.op(PE, tr, reads=[tA_b[0], tA_b[1], stg_b[0], ident_b], writes=[ps_b[bank]])
                    fw.op(DVE, lambda e, i=i, bank=bank: e.tensor_copy(out=ckvst[:, i, :], in_=ps(bank)[:, 0:256]),
                          reads=[ps_b[bank]], writes=[ckvst_b])
                    fw.op(ACT, lambda e, i=i, bank=bank: e.activation(out=krst[:, i, :], in_=ps(bank)[:, 320:352],
                                                                     func=AF.Copy),
                          reads=[ps_b[bank]], writes=[krst_b])
                fw.dma(SP, ckv_o.rearrange("(i p) c -> p i c", p=128), ckvst[:], dst=None, src=[ckvst_b])
                fw.dma(SP, kr_o.rearrange("(i p) c -> p i c", p=128), krst[:], dst=None, src=[krst_b])

                for r in range(3):
                    fw.op(DVE, lambda e, r=r: e.tensor_copy(out=KT(r)[64:128, :], in_=KR[64:128, :]),
                          reads=[kr_b], writes=[kt_b[r]])

                fw.op(DVE, lambda e: e.memset(stg_t[1][:], 0.0), writes=[stg_b[1]])
                if MLA_SUB < 2:
                    return
                ukv, ukvb, ukvk = ws.next(wukv.rearrange("(k p) n -> p k n", p=128), (2, 2048))
                wqv = wuq_aug.rearrange("(k p) n -> p k n", p=128)
                st = {'qw': None, 'n_o': 0, 'n_s': 0, 'n_p': 0}

                def prep_V(p):
                    vr = p % 2
                    for ktg in range(3):
                        bank = 6 + ktg % 2

                        def mmv(e, ktg=ktg, bank=bank, p=p):
                            r = None
                            for j in range(4):
                                kt = ktg * 4 + j
                                for kc in range(2):
                                    rhs = ukv[:, kc, p * 256:(p + 1) * 256].rearrange("p (h x) -> p h x", h=2)[:, :, 64:128]
                                    r = e.matmul(ps(bank)[:, j * 128:(j + 1) * 128].rearrange("p (h x) -> p h x", h=2),
                                                 ckvall(kc)[:, kt * 128:(kt + 1) * 128], rhs,
                                                 start=(kc == 0), stop=(kc == 1))
                            return r
                        fw.op(PE, mmv, reads=[ukvb, ckvall_b[0], ckvall_b[1]], writes=[ps_b[bank]])
                        src = ps(bank).rearrange("p (j h x) -> p j h x", j=4, h=2, x=64)
                        fw.op(DVE, lambda e, src=src, ktg=ktg, vr=vr: e.tensor_copy(
                            out=VP(vr)[:, ktg * 4:ktg * 4 + 4, 0:64], in_=src[:, :, 0, :]),
                            reads=[ps_b[bank]], writes=[vp_b[vr]])
                        fw.op(ACT, lambda e, src=src, ktg=ktg, vr=vr: e.activation(
                            out=VP(vr)[:, ktg * 4:ktg * 4 + 4, 192:256], in_=src[:, :, 1, :], func=AF.Copy),
                            reads=[ps_b[bank]], writes=[vp_b[vr]])

                def prep_KQ(h):
                    r3 = h % 3
                    if h % 4 == 0:
                        if st['qw'] is not None:
                            ws.done(st['qw'][2])
                        st['qw'] = ws.next(wqv[:, :, (h // 4) * 768:(h // 4 + 1) * 768], (3, 768))
                    qw = st['qw']
                    for kt5 in range(3):
                        bank = 6 + kt5 % 2

                        def mmk(e, h=h, kt5=kt5, bank=bank):
                            r = None
                            for kc in range(2):
                                r = e.matmul(ps(bank)[0:64, :], ukv[:, kc, h * 128:h * 128 + 64],
                                             ckvall(kc)[:, kt5 * 512:(kt5 + 1) * 512],
                                             start=(kc == 0), stop=(kc == 1))
                            return r
                        fw.op(PE, mmk, reads=[ukvb, ckvall_b[0], ckvall_b[1]], writes=[ps_b[bank]])
                        evac(kt5, KT(r3)[0:64, kt5 * 512:(kt5 + 1) * 512], ps(bank)[0:64, :],
                             [ps_b[bank]], [kt_b[r3]])
                    qcol = (h % 4) * 192
                    for tt in range(2):
                        for which in range(2):
                            bank = 6 + which

                            def mmq(e, which=which, tt=tt, bank=bank, qcol=qcol, qwv=qw[0]):
                                r = None
                                for kc in range(3):
                                    r = e.matmul(ps(bank)[0:96, :],
                                                 qwv[:, kc, qcol + which * 96:qcol + which * 96 + 96],
                                                 cqT(kc)[:, tsl(tt)], start=(kc == 0), stop=(kc == 2))
                                return r
                            fw.op(PE, mmq, reads=[qw[1]] + cq_b, writes=[ps_b[bank]])
                        fw.op(ACT, lambda e, r3=r3, tt=tt: e.activation(
                            out=QT(r3)[0:64, tsl(tt)], in_=ps(6)[0:64, :], func=AF.Copy),
                            reads=[ps_b[6]], writes=[qt_b[r3]])
                        fw.op(DVE, lambda e, tt=tt: e.tensor_tensor(
                            out=xn_t[0][64:96, :], in0=ps(6)[64:96, :], in1=ropecs[64:96, 0, tsl(tt)],
                            op=ALU.mult), reads=[ps_b[6], ropecs_b], writes=[xn_b[0]])
                        fw.op(DVE, lambda e, tt=tt: e.tensor_tensor(
                            out=xn_t[1][64:96, :], in0=ps(7)[64:96, :], in1=ropecs[64:96, 1, tsl(tt)],
                            op=ALU.mult), reads=[ps_b[7], ropecs_b], writes=[xn_b[1]])
                        fw.op(DVE, lambda e, r3=r3, tt=tt: e.tensor_tensor(
                            out=QT(r3)[64:96, tsl(tt)], in0=xn_t[0][64:96, :], in1=xn_t[1][64:96, :],
                            op=ALU.add), reads=[xn_b[0], xn_b[1]], writes=[qt_b[r3]])

                def attend(h, tts, pending=None):
                    p, hh = h // 2, h % 2
                    vr = p % 2
                    r3 = h % 3
                    if hh == 0:
                        vcols, orows, srow, om = (0, 65), (0, 64), 64, 65
                    else:
                        vcols, orows, srow, om = (128, 256), (64, 128), 0, 128
                    for tt in (tts if MLA_SUB >= 3 else ()):
                        ob = 2 + st['n_o'] % 3
                        st['n_o'] += 1
                        pend = []
                        for kt in range(14):
                            if kt < 12:
                                sb_ = (0, 1, 5)[st['n_s'] % 3]
                                st['n_s'] += 1
                                fw.op(PE, lambda e, sb_=sb_, kt=kt, r3=r3, tt=tt: e.matmul(
                                    ps(sb_), KT(r3)[:, kt * 128:(kt + 1) * 128], QT(r3)[:, tsl(tt)],
                                    start=True, stop=True),
                                    reads=[kt_b[r3], qt_b[r3]], writes=[ps_b[sb_]])
                                pi = st['n_p'] % 4
                                st['n_p'] += 1
                                fw.op(ACT, lambda e, sb_=sb_, pi=pi: e.activation(
                                    out=PT(pi), in_=ps(sb_), func=AF.Exp, scale=SC),
                                    reads=[ps_b[sb_]], writes=[pt_b[pi]])
                            if kt >= 2:
                                pkt, ppi = pend.pop(0)
                                fw.op(PE, lambda e, ob=ob, om=om, vr=vr, pkt=pkt, ppi=ppi, vcols=vcols: e.matmul(
                                    ps(ob)[0:om, :], VP(vr)[:, pkt, vcols[0]:vcols[1]], PT(ppi),
                                    start=(pkt == 0), stop=(pkt == 11)),
                                    reads=[vp_b[vr], pt_b[ppi]], writes=[ps_b[ob]])
                            if kt < 12:
                                pend.append((kt, pi))
                            if kt == 8 and pending is not None:
                                pending()
                                pending = None
                        if MLA_SUB < 4:
                            continue
                        rs = stg_t[1][srow:srow + 1, 0:512] if hh == 0 else stg_t[1][srow:srow + 1, 512:1024]
                        fw.op(DVE, lambda e, rs=rs, ob=ob, srow=srow: e.reciprocal(
                            out=rs, in_=ps(ob)[srow:srow + 1, :]), reads=[ps_b[ob]], writes=[stg_b[1]])
                        return lambda ob=ob, tt=tt: norm_o(h, tt, ob)
                    return None

                def norm_o(h, tt, ob):
                    p, hh = h // 2, h % 2
                    if hh == 0:
                        orows, srow = (0, 64), 64
                    else:
                        orows, srow = (64, 128), 0
                    if True:
                        bb = 6 + st['n_o'] % 2
                        fw.op(PE, lambda e, bb=bb, hh=hh: e.matmul(
                            ps(bb), sel_t[:, hh, :], stg_t[1][:, hh * 512:(hh + 1) * 512],
                            start=True, stop=True),
                            reads=[stg_b[1], const_b], writes=[ps_b[bb]])
                        fw.op(ACT, lambda e, bb=bb: e.activation(out=rstd_t[:], in_=ps(bb), func=AF.Copy),
                              reads=[ps_b[bb]], writes=[rstd_b])
                        fw.op(DVE, lambda e, ob=ob, orows=orows, p=p, tt=tt: e.tensor_tensor(
                            out=h_t[orows[0]:orows[1], p, tsl(tt)], in0=ps(ob)[orows[0]:orows[1], :],
                            in1=rstd_t[orows[0]:orows[1], :], op=ALU.mult),
                            reads=[ps_b[ob], rstd_b], writes=[hb[p][tt]])

                prep_V(0)
                prep_KQ(0)
                pnd = None
                for h in range(16):
                    pnd = attend(h, (0,), pnd)
                    if h + 1 < 16:
                        if (h + 1) % 2 == 0:
                            prep_V((h + 1) // 2)
                        prep_KQ(h + 1)
                    pnd = attend(h, (1,), pnd)
                if pnd is not None:
                    pnd()
                qw = st['qw']
                ws.done(qw[2])
                ws.done(ukvk)
                out_linear(l, wo, lambda kk, tt: h_t[:, kk, tsl(tt)],
                           lambda tt: [hb[kk][tt] for kk in range(KC)], lambda oc: mod_t[l][:, 16 + oc:17 + oc])

            def mixer(l):
                norm_mod(l, 1)
                if l == 0 and stage >= 4:
                    mixer_mla(l)
                elif l == 1 and stage >= 3:
                    mixer_pool(l)
                elif l == 2 and stage >= 3:
                    mixer_fnet(l)
                elif l == 3 and stage >= 3:
                    mixer_sconv(l)
                fw.barrier_bufs(ab)

            if stage >= 2:
                for nb in range(12):
                    ada_block(0, nb, 4 + nb % 2, 6 + nb % 2)
                ada_finish(0)
                for l in range(DEPTH):
                    mixer(l)
                    norm_mod(l, 2)
                    if l + 1 < DEPTH:
                        def hook(jg, l=l):
                            ada_block(l + 1, 2 * jg, 0, 1)
                            ada_block(l + 1, 2 * jg + 1, 2, 3)
                            if jg == 5:
                                ada_finish(l + 1)
                        ffn(l, hook)
                    else:
                        ffn(l)

            fo, _ = PV["fnw"]
            for tt in range(2):
                rms_stats([(x_t[:, c, tsl(tt)], [xb[c][tt]]) for c in range(KC)], KC, 1.0 / D,
                          bank=tt)
                for c in range(KC):
                    fw.op(DVE, lambda e, c=c, tt=tt: e.scalar_tensor_tensor(
                        out=x_t[:, c, tsl(tt)], in0=x_t[:, c, tsl(tt)],
                        scalar=pvec[:, fo + c:fo + c + 1], in1=rstd_t[:],
                        op0=ALU.mult, op1=ALU.mult),
                        reads=[xb[c][tt], rstd_b, pvec_b], writes=[xb[c][tt]])
            for i in range(8):
                s = i % 2
                tt = i // 4
                for half in range(2):
                    bank = 2 + (i * 2 + half) % 6

                    def mm(e, half=half, bank=bank, i=i):
                        r = None
                        for cc in range(4):
                            c = half * 4 + cc
                            r = e.transpose(ps(bank)[:, cc * 128:(cc + 1) * 128],
                                            x_t[:, c, i * 128:(i + 1) * 128], ident[:])
                        return r
                    fw.op(PE, mm, reads=[xb[c][tt] for c in range(half * 4, half * 4 + 4)] + [ident_b],
                          writes=[ps_b[bank]])
                    if half == 0:
                        fw.op(DVE, lambda e, s=s, bank=bank: e.tensor_copy(
                            out=stg_t[s][:, 0:512], in_=ps(bank)),
                            reads=[ps_b[bank]], writes=[stg_b[s]])
                    else:
                        fw.op(ACT, lambda e, s=s, bank=bank: e.activation(
                            out=stg_t[s][:, 512:1024], in_=ps(bank), func=AF.Copy),
                            reads=[ps_b[bank]], writes=[stg_b[s]])
                fw.dma(SP, yout[i * 128:(i + 1) * 128, :], stg_t[s][:], dst=None, src=[stg_b[s]])

        fw.dry = True
        emit()
        fw.dry = False
        ws.reset()
        emit()
        assert ws.consumed == len(ws.specs)

        final_waits = {}
        for sem, val in fw.out_recs:
            final_waits[sem] = max(final_waits.get(sem, 0), val)

        with nc.Block() as block:
            @block.sync
            def _(e):
                fw.replay(SP, e)
                for sem, val in final_waits.items():
                    e.wait_ge(sem, val)

            @block.tensor
            def _(e):
                fw.replay(PE, e)

            @block.scalar
            def _(e):
                fw.replay(ACT, e)

            @block.vector
            def _(e):
                fw.replay(DVE, e)

            @block.gpsimd
            def _(e):
                fw.replay(POOL, e)
    return nc


def _cols(v):
    v = np.asarray(v, np.float32)
    return np.ascontiguousarray(v.reshape(-1, 128).T)


def _make_pvec(inp, cond_vec, flagneg):
    pv = np.zeros((128, NPV), np.float32)

    def put(name, arr):
        o, n = PV[name]
        assert arr.shape == (128, n), (name, arr.shape, n)
        pv[:, o:o + n] = arr

    for l in range(DEPTH):
        put(f"n1w{l}", _cols(inp["norm1_w"][l]))
        put(f"n2w{l}", _cols(inp["norm2_w"][l]))
        put(f"fw0_{l}", _cols(inp["ffn_conv_w"][l, 0]))
        put(f"fw1_{l}", _cols(inp["ffn_conv_w"][l, 1]))
        put(f"fw2_{l}", _cols(inp["ffn_conv_w"][l, 2]))
        put(f"fb_{l}", _cols(inp["ffn_conv_b"][l]))
        put(f"adab{l}", _cols(inp["ada_b"][l]))
    put("fnw", _cols(inp["final_norm_w"]))
    put("cond", _cols(cond_vec))
    pv[:, PV["flagneg"][0]] = flagneg
    put("qnw", _cols(inp["mla_q_norm"][0]))
    put("kvnw", _cols(inp["mla_kv_norm"][0]))
    put("pscale", _cols(inp["pool_scale"][0]))
    put("sw0", _cols(inp["sconv_conv"][0, 0]))
    put("sw1", _cols(inp["sconv_conv"][0, 1]))
    put("sw2", _cols(inp["sconv_conv"][0, 2]))
    return pv


def _pool_tables(L):
    A = np.zeros((4, T, T), np.float64)
    wins = (2, 4, 8, 16)
    for g, w in enumerate(wins):
        for t in range(T):
            s0 = (t // L) * L
            tl = t - s0
            lo = max(tl - w // 2, 0)
            hi = min(tl + w - w // 2, L)
            A[g, t, s0 + lo:s0 + hi] = 1.0 / (hi - lo)
            A[g, t, t] -= 1.0
    out = np.zeros((128, 4, 8, 3, 128), np.float32)
    for g in range(4):
        for i in range(8):
            for d in range(3):
                ip = i + d - 1
                if 0 <= ip < 8:
                    out[:, g, i, d, :] = A[g, i * 128:(i + 1) * 128, ip * 128:(ip + 1) * 128].T
    return out.reshape(128, 4, 8 * 3 * 128).astype(ml_dtypes.bfloat16)


def _dft_tables(L):
    t = np.arange(T)
    same = (t[:, None] // L) == (t[None, :] // L)
    ang = 2.0 * np.pi * ((t[:, None] % L) * (t[None, :] % L) % L) / L
    nrm = 1.0 / np.sqrt(L * 256.0)
    C = np.where(same, np.cos(ang), 0.0) * nrm
    S = np.where(same, -np.sin(ang), 0.0) * nrm

    def lay(M):
        return np.ascontiguousarray(M.reshape(8, 128, T).transpose(1, 0, 2)).astype(ml_dtypes.bfloat16)
    c = np.arange(256)
    a2 = 2.0 * np.pi * ((c[:, None] * c[None, :]) % 256) / 256.0
    CS = np.concatenate([np.cos(a2), np.sin(a2)], axis=1)
    CS = np.ascontiguousarray(CS.reshape(2, 128, 512).transpose(1, 0, 2)).astype(ml_dtypes.bfloat16)
    return lay(C), lay(S), CS


_PERM = np.concatenate([np.arange(0, 32, 2), np.arange(1, 32, 2)])
_PERM_SW = np.concatenate([np.arange(1, 32, 2), np.arange(0, 32, 2)])


def _mla_weights(inp):
    wdkv = np.asarray(inp["mla_wdkv"][0], np.float32)
    aug = np.zeros((D, 448), np.float32)
    aug[:, 0:256] = wdkv[:, 0:256]
    aug[:, 320:352] = wdkv[:, 256 + _PERM]
    aug[:, 416:448] = wdkv[:, 256 + _PERM_SW]
    wuq = np.asarray(inp["mla_wuq"][0], np.float32).reshape(384, 16, 96)
    qa = np.zeros((384, 16, 192), np.float32)
    qa[:, :, 0:64] = wuq[:, :, 0:64]
    qa[:, :, 64:96] = wuq[:, :, 64 + _PERM]
    qa[:, :, 160:192] = wuq[:, :, 64 + _PERM_SW]
    return aug, np.ascontiguousarray(qa.reshape(384, 16 * 192))


def _rope_tables(kind):
    cs = np.zeros((128, 2, T), np.float32)
    if kind == "p":
        cs[64:96, 0, :] = 1.0
        return cs
    t = np.arange(T)
    r = (t // 64).astype(np.float32)
    col = (t % 64).astype(np.float32)
    inv = (np.float32(10000.0) ** (-np.arange(8, dtype=np.float32) / np.float32(8))).astype(np.float32)
    ang = np.concatenate([r[:, None] * inv, col[:, None] * inv], axis=-1).astype(np.float32)
    c, s = np.cos(ang).T, np.sin(ang).T
    cs[64:80, 0, :] = c
    cs[80:96, 0, :] = c
    cs[64:80, 1, :] = -s
    cs[80:96, 1, :] = s
    return cs


def _mask_tables(kind):
    NEG = -30000.0
    eq = np.zeros((128, T), np.float32)
    ek = np.zeros((128, 1536), np.float32)
    if kind == "p":
        seq = np.arange(T) // 256
        for r in range(4):
            eq[96 + r, :] = (seq == r)
            ek[96 + r, 0:512] = NEG
            ek[96 + r, 512:] = np.where(seq == r, 0.0, NEG)
    return eq, ek


def _core_roles():
    return [("s", 0), ("s", 1), ("p", 0), ("p", 1), ("p", 2), ("p", 3), ("p", 3), ("p", 3)]


_NC_CACHE = {}


def make_in_maps(inp):
    roles = _core_roles()
    ident = np.eye(128, dtype=np.float32)
    shared = {
        "ident": ident,
        "ada_w": np.ascontiguousarray(inp["ada_w"], dtype=np.float32),
        "ffn_up": np.ascontiguousarray(inp["ffn_up"], dtype=np.float32),
        "ffn_down": np.ascontiguousarray(inp["ffn_down"], dtype=np.float32),
        "pool_w": np.ascontiguousarray(inp["pool_w"][0], dtype=np.float32),
        "fnet_w": np.ascontiguousarray(inp["fnet_w"][0], dtype=np.float32),
        "sconv_win": np.ascontiguousarray(inp["sconv_win"][0], dtype=np.float32),
        "sconv_wout": np.ascontiguousarray(inp["sconv_wout"][0], dtype=np.float32),
        "identb": ident.astype(ml_dtypes.bfloat16),
        "wdq": np.ascontiguousarray(inp["mla_wdq"][0], dtype=np.float32),
        "wukv": np.ascontiguousarray(inp["mla_wukv"][0], dtype=np.float32),
        "wo": np.ascontiguousarray(inp["mla_wo"][0], dtype=np.float32),
    }
    shared["wdkv_aug"], shared["wuq_aug"] = _mla_weights(inp)
    tabs = {}
    for kind, L in (("s", 1024), ("p", 256)):
        C, S, CS = _dft_tables(L)
        eq, ek = _mask_tables(kind)
        tabs[kind] = {"poolA": _pool_tables(L), "dftC": C, "dftS": S, "dftCS": CS,
                      "ropeCS": _rope_tables(kind), "eq": eq, "_ek": ek}
    in_maps = []
    for kind, idx in roles:
        if kind == "s":
            xc = inp["x_sample"][idx]
            cond = inp["c"][idx]
            flag = 0.0
        else:
            xc = inp["x_prompt"][4 * idx:4 * idx + 4].reshape(T, D)
            cond = inp["c_ctx"]
            flag = -1.0
        m = dict(shared)
        m.update({a: b for a, b in tabs[kind].items() if not a.startswith("_")})
        krm = tabs[kind]["_ek"].copy()
        cT = np.zeros((128, 2, 512), np.float32)
        if kind == "s":
            cT[:] = np.asarray(inp["cache_ckv"][idx, 0], np.float32).T.reshape(2, 128, 512).transpose(1, 0, 2)
            krm[64:96, 0:512] = np.asarray(inp["cache_krope"][idx, 0], np.float32)[:, _PERM].T
        m["cacheT"] = cT
        m["krmask"] = krm
        m["xin"] = np.ascontiguousarray(xc, dtype=np.float32)
        m["pvec"] = _make_pvec(inp, cond, flag)
        in_maps.append(m)
    return in_maps


def kernel(**inputs):
    inp = {k: np.asarray(v) for k, v in inputs.items()}
    in_maps = make_in_maps(inp)
    if "nc" not in _NC_CACHE:
        _NC_CACHE["nc"] = build_program(STAGE)
    nc = _NC_CACHE["nc"]
    res = run_bass_kernel_spmd(nc, in_maps, core_ids=list(range(NCORES)))
    outs = res.results
    y_sample = np.stack([outs[0]["yout"], outs[1]["yout"]], axis=0).astype(np.float32)
    y_prompt = np.concatenate([outs[2 + g]["yout"].reshape(4, 256, D) for g in range(4)], axis=0)
    y_prompt = y_prompt.astype(np.float32)
    new_ckv = np.concatenate([outs[2 + g]["ckv_o"].reshape(4, 1, 256, 256) for g in range(4)], axis=0)
    krp = np.concatenate([outs[2 + g]["kr_o"].reshape(4, 1, 256, 32) for g in range(4)], axis=0)
    new_kr = np.empty_like(krp)
    new_kr[..., _PERM] = krp
    new_ckv = new_ckv.astype(np.float32)
    new_kr = new_kr.astype(np.float32)
    return (y_prompt, y_sample, new_ckv, new_kr)
```

```python
import numpy as np
from contextlib import ExitStack
import ml_dtypes

import concourse.bass as bass
import concourse.mybir as mybir
from concourse.bass_utils import run_bass_kernel_spmd

F32 = mybir.dt.float32
BF16 = mybir.dt.bfloat16
AF = mybir.ActivationFunctionType
ALU = mybir.AluOpType

D = 1024
T = 1024
KC = 8
DFF = 2816
FC = 22
DEPTH = 4
EPS = 1e-6
NCORES = 8


class Buf:
    __slots__ = ("name", "w", "r", "sem", "cum", "excl")

    def __init__(self, name):
        self.name = name
        self.excl = False
        self.w = None
        self.r = {}
        self.sem = None
        self.cum = 0


class Q:
    def __init__(self, fw, name, own_wait=True):
        self.fw = fw
        self.name = name
        self.thunks = []
        self.sem = fw.new_sem("q_" + name)
        self.cnt = 0
        self.known = {}
        self.own_wait = own_wait


class FW:
    def __init__(self, nc, es):
        self.nc = nc
        self.es = es
        self.nsem = 0
        self.pe = Q(self, "pe", own_wait=False)
        self.act = Q(self, "act")
        self.dve = Q(self, "dve")
        self.pool = Q(self, "pool")
        self.sp = Q(self, "sp")
        self.out_recs = []
        self.dry = False

    def new_sem(self, name):
        self.nsem += 1
        return self.es.enter_context(self.nc.semaphore(f"s{self.nsem}_{name}"))

    def buf(self, name, dma=False):
        b = Buf(name)
        if dma:
            b.sem = self.new_sem("d_" + name)
        return b

    def _collect(self, q, reads, writes):
        waits = {}

        def need(rec):
            if rec is None:
                return
            sem, val = rec
            if sem is q.sem and not q.own_wait:
                return
            if q.known.get(sem, 0) >= val:
                return
            if waits.get(sem, 0) < val:
                waits[sem] = val

        for b in reads:
            need(b.w)
            if b.excl:
                for sem, val in b.r.items():
                    if sem is not q.sem:
                        need((sem, val))
        for b in writes:
            need(b.w)
            for sem, val in b.r.items():
                need((sem, val))
        for sem, val in waits.items():
            q.known[sem] = val
        return list(waits.items())

    @staticmethod
    def _commit(rec, reads, writes):
        sem, val = rec
        for b in reads:
            if b.r.get(sem, 0) < val:
                b.r[sem] = val
        for b in writes:
            b.w = rec
            b.r = {}

    def op(self, q, fn, reads=(), writes=()):
        if self.dry:
            return None
        wl = self._collect(q, reads, writes)
        q.cnt += 1
        rec = (q.sem, q.cnt)
        q.thunks.append((wl, fn, rec, 1))
        self._commit(rec, reads, writes)
        return rec

    def dma(self, q, out_ap, in_ap, dst=None, src=(), reads=(), kw=None):
        if self.dry:
            return None
        kw = kw or {}
        writes = [dst] if dst is not None else []
        rds = list(src) + list(reads)
        wl = self._collect(q, rds, writes)
        owner = dst if dst is not None else src[0]
        owner.cum += 16
        rec = (owner.sem, owner.cum)

        def fn(e, out_ap=out_ap, in_ap=in_ap, kw=kw):
            return e.dma_start(out=out_ap, in_=in_ap, **kw)

        q.thunks.append((wl, fn, rec, 16))
        self._commit(rec, rds, writes)
        if dst is None:
            self.out_recs.append(rec)
        return rec

    def barrier_bufs(self, bufs):
        allq = [self.pe, self.act, self.dve, self.pool]
        for b in bufs:
            for q in allq:
                if q.cnt > 0:
                    if b.r.get(q.sem, 0) < q.cnt:
                        b.r[q.sem] = q.cnt

    def replay(self, q, eng):
        for wl, fn, rec, inc in q.thunks:
            for sem, val in wl:
                eng.wait_ge(sem, val)
            ins = fn(eng)
            if isinstance(ins, (list, tuple)):
                ins = ins[-1]
            ins.then_inc(rec[0], inc)


class WStream:
    def __init__(self, fw, q, slots, bufs, slot_elems):
        self.fw = fw
        self.q = q
        self.slots = slots
        self.bufs = bufs
        self.n = len(slots)
        self.slot_elems = slot_elems
        self.specs = []
        self.reset()

    def reset(self):
        self.issued = 0
        self.consumed = 0
        self.done_flags = []

    def _view(self, k, shape):
        t = self.slots[k % self.n]
        n = int(np.prod(shape))
        assert n <= self.slot_elems, shape
        if len(shape) == 1:
            return t[:, 0:n]
        if len(shape) == 2:
            return t[:, 0:n].rearrange("p (a b) -> p a b", a=shape[0], b=shape[1])
        return t[:, 0:n].rearrange("p (a b c) -> p a b c", a=shape[0], b=shape[1], c=shape[2])

    def _pump(self):
        while self.issued < len(self.specs):
            k = self.issued
            if k >= self.n and not (k - self.n < len(self.done_flags) and self.done_flags[k - self.n]):
                break
            dram_ap, shape = self.specs[k]
            self.fw.dma(self.q, self._view(k, shape), dram_ap, dst=self.bufs[k % self.n])
            self.issued += 1

    def next(self, dram_ap, shape):
        shape = tuple(shape)
        if self.fw.dry:
            self.specs.append((dram_ap, shape))
            return self._view(0, shape), self.bufs[0], None
        k = self.consumed
        self.consumed += 1
        assert self.specs[k][1] == shape, (k, self.specs[k][1], shape)
        self.done_flags.append(False)
        self._pump()
        assert self.issued > k, (k, self.issued)
        return self._view(k, shape), self.bufs[k % self.n], k

    def done(self, k):
        if self.fw.dry:
            return
        self.done_flags[k] = True
        self._pump()


def _pvec_map():
    m = {}
    o = 0

    def add(name, n):
        nonlocal o
        m[name] = (o, n)
        o += n

    for l in range(DEPTH):
        add(f"n1w{l}", KC)
        add(f"n2w{l}", KC)
        add(f"fw0_{l}", FC)
        add(f"fw1_{l}", FC)
        add(f"fw2_{l}", FC)
        add(f"fb_{l}", FC)
        add(f"adab{l}", 48)
    add("fnw", KC)
    add("cond", KC)
    add("flagneg", 1)
    add("qnw", 3)
    add("kvnw", 2)
    add("pscale", KC)
    add("sw0", KC)
    add("sw1", KC)
    add("sw2", KC)
    m["_n"] = o
    return m


PV = _pvec_map()
NPV = PV["_n"]

STAGE = 4
MLA_SUB = 4.0
NSLOT = 6
SLOT_ELEMS = 4096


def build_program(stage=STAGE):
    nc = bass.Bass("TRN2", target_bir_lowering=False)

    def din(name, shape, dt=F32):
        return nc.dram_tensor(name, list(shape), dt, kind="ExternalInput").ap()

    def dout(name, shape, dt=F32):
        return nc.dram_tensor(name, list(shape), dt, kind="ExternalOutput").ap()

    xin = din("xin", [T, D])
    pvec_d = din("pvec", [128, NPV])
    ident_d = din("ident", [128, 128])
    ada_w = din("ada_w", [DEPTH, D, 6 * D])
    ffn_up = din("ffn_up", [DEPTH, D, 2 * DFF])
    ffn_down = din("ffn_down", [DEPTH, DFF, D])
    pool_w = din("pool_w", [4, 256, 256])
    poolA = din("poolA", [128, 4, 8 * 3 * 128], BF16)
    fnet_w = din("fnet_w", [D, D])
    dftC = din("dftC", [128, 8, T], BF16)
    dftS = din("dftS", [128, 8, T], BF16)
    dftCS = din("dftCS", [128, 2, 512], BF16)
    identb_d = din("identb", [128, 128], BF16)
    sconv_win = din("sconv_win", [D, 3 * D])
    sconv_wout = din("sconv_wout", [D, D])
    wdq = din("wdq", [D, 384])
    wdkv_aug = din("wdkv_aug", [D, 448])
    wuq_aug = din("wuq_aug", [384, 16 * 192])
    wukv = din("wukv", [256, 2048])
    wo = din("wo", [D, D])
    ropeCS_d = din("ropeCS", [128, 2, T])
    cacheT_d = din("cacheT", [128, 2, 512])
    krmask_d = din("krmask", [128, 1536])
    eq_d = din("eq", [128, T])
    yout = dout("yout", [T, D])
    ckv_o = dout("ckv_o", [T, 256])
    kr_o = dout("kr_o", [T, 32])

    es = ExitStack()
    with es:
        fw = FW(nc, es)
        PE, ACT, DVE, POOL, SP = fw.pe, fw.act, fw.dve, fw.pool, fw.sp

        def sb(name, shape, dt):
            return es.enter_context(nc.sbuf_tensor(name, list(shape), dt))

        x_t = sb("x", [128, KC, T], F32)
        xb = [[fw.buf(f"x{c}_{tt}") for tt in range(2)] for c in range(KC)]
        h_t = sb("h", [128, KC, T], BF16)
        hb = [[fw.buf(f"h{c}_{tt}") for tt in range(2)] for c in range(KC)]
        a_t = sb("a", [128, 25, T], BF16)
        ab = [fw.buf(f"a{j}") for j in range(25)]
        pvec = sb("pvec_sb", [128, NPV], F32)
        pvec_b = fw.buf("pvec", dma=True)
        ident = sb("ident_sb", [128, 128], F32)
        ident_b = fw.buf("ident", dma=True)
        ones_bf = sb("ones_bf", [128, 128], BF16)
        one_f = sb("one_f", [128, 1], F32)
        eps_t = sb("eps", [128, 1], F32)
        const_b = fw.buf("consts")
        stg_t = [sb(f"stg{i}", [128, D], F32) for i in range(2)]
        stg_b = [fw.buf(f"stg{i}", dma=True) for i in range(2)]
        tA_t = [sb(f"tA{i}", [128, T], F32) for i in range(2)]
        tA_b = [fw.buf(f"tA{i}") for i in range(2)]
        sq_t = [sb(f"sq{i}", [128, 512], BF16) for i in range(4)]
        sq_b = [fw.buf(f"sq{i}") for i in range(4)]
        rstd_t = sb("rstd", [128, 512], F32)
        rstd_b = fw.buf("rstd")
        rstd1_t = sb("rstd1", [128, 512], F32)
        rstd1_b = fw.buf("rstd1")
        xn_t = [sb(f"xn{i}", [128, 512], F32) for i in range(4)]
        xn_b = [fw.buf(f"xn{i}") for i in range(4)]
        scond = sb("scond", [128, KC], BF16)
        scond_b = fw.buf("scond")
        row_t = [sb(f"row{i}", [1, 512], F32) for i in range(2)]
        row_b = [fw.buf(f"row{i}") for i in range(2)]
        mod_t = [sb(f"mod{l}", [128, 48], F32) for l in range(DEPTH)]
        mod_b = [fw.buf(f"mod{l}") for l in range(DEPTH)]
        col_t = [sb(f"cols{l}", [128, 16 + 2 * FC], F32) for l in range(DEPTH)]
        col_b = [fw.buf(f"cols{l}") for l in range(DEPTH)]
        identb = sb("identb_sb", [128, 128], BF16)
        identb_b = fw.buf("identb", dma=True)
        cs_t = sb("dftcs_sb", [128, 2, 512], BF16)
        cs_b = fw.buf("dftcs", dma=True)
        ropecs = sb("ropecs", [128, 2, T], F32)
        ropecs_b = fw.buf("ropecs", dma=True)
        sel_t = sb("sel", [128, 2, 128], F32)
        ckvst = sb("ckvst", [128, 8, 256], F32)
        ckvst_b = fw.buf("ckvst", dma=True)
        krst = sb("krst", [128, 8, 32], F32)
        krst_b = fw.buf("krst", dma=True)
        ckvall_b = [fw.buf(f"ckvall{c}", dma=True) for c in range(2)]
        kr_b = fw.buf("KR", dma=True)
        qt_b = [fw.buf(f"QT{i}", dma=True) for i in range(3)]
        kt_b = [fw.buf(f"KT{i}") for i in range(3)]
        vp_b = [fw.buf(f"VP{i}") for i in range(2)]
        pt_b = [fw.buf(f"PT{i}") for i in range(4)]
        cq_b = [fw.buf(f"cq{i}") for i in range(3)]
        mcol_t = sb("mcols", [128, 32], F32)
        mcol_b = fw.buf("mcols")
        slots = [sb(f"wslot{i}", [128, SLOT_ELEMS], BF16) for i in range(NSLOT)]
        slot_b = [fw.buf(f"wslot{i}", dma=True) for i in range(NSLOT)]
        ws = WStream(fw, POOL, slots, slot_b, SLOT_ELEMS)

        pd_t = [es.enter_context(nc.psum_tensor(f"pd{i}", [128, 1024], F32)) for i in range(4)]
        ps_b = [fw.buf(f"ps{i}") for i in range(8)]
        for b in ps_b:
            b.excl = True

        pdb_t = [t.bitcast(BF16) for t in pd_t]

        def ps(bank):
            return pd_t[bank // 2][:, (bank % 2) * 512:(bank % 2) * 512 + 512]

        def psb(bank):
            return pdb_t[bank // 2][:, (bank % 2) * 1024:(bank % 2) * 1024 + 1024]

        def pv(name, j=0, n=1):
            o, _ = PV[name]
            return pvec[:, o + j:o + j + n]

        def tsl(tt):
            return slice(tt * 512, (tt + 1) * 512)

        def emit():
            fw.dma(SP, pvec[:], pvec_d, dst=pvec_b)
            fw.dma(SP, ident[:], ident_d, dst=ident_b)
            fw.dma(SP, identb[:], identb_d, dst=identb_b)
            fw.dma(SP, cs_t[:], dftCS, dst=cs_b)
            fw.op(DVE, lambda e: e.memset(ones_bf[:], 1.0), writes=[const_b])
            fw.op(DVE, lambda e: e.memset(eps_t[:], EPS), writes=[const_b])
            fw.op(DVE, lambda e: e.memset(one_f[:], 1.0), writes=[const_b])
            fw.op(DVE, lambda e: e.memset(sel_t[:], 0.0), writes=[const_b])
            fw.op(DVE, lambda e: e.memset(sel_t[64:65, 0, :], 1.0), writes=[const_b])
            fw.op(DVE, lambda e: e.memset(sel_t[0:1, 1, :], 1.0), writes=[const_b])

            for i in range(8):
                s = i % 2
                tt = i // 4
                fw.dma(SP, stg_t[s][:], xin[i * 128:(i + 1) * 128, :], dst=stg_b[s])
                for half in range(2):
                    bank = (i * 2 + half) % 8

                    def mm(e, s=s, half=half, bank=bank):
                        r = None
                        for cc in range(4):
                            c = half * 4 + cc
                            r = e.transpose(ps(bank)[:, cc * 128:(cc + 1) * 128],
                                            stg_t[s][:, c * 128:(c + 1) * 128], ident[:])
                        return r
                    fw.op(PE, mm, reads=[stg_b[s], ident_b], writes=[ps_b[bank]])
                    wr = [xb[c][tt] for c in range(half * 4, half * 4 + 4)]
                    if half == 0:
                        fw.op(DVE, lambda e, half=half, bank=bank, i=i: e.tensor_copy(
                            out=x_t[:, half * 4:half * 4 + 4, i * 128:(i + 1) * 128],
                            in_=ps(bank).rearrange("p (c t) -> p c t", c=4)),
                            reads=[ps_b[bank]], writes=wr)
                    else:
                        fw.op(ACT, lambda e, half=half, bank=bank, i=i: e.activation(
                            out=x_t[:, half * 4:half * 4 + 4, i * 128:(i + 1) * 128],
                            in_=ps(bank).rearrange("p (c t) -> p c t", c=4), func=AF.Copy),
                            reads=[ps_b[bank]], writes=wr)

            fw.op(ACT, lambda e: e.activation(out=scond[:], in_=pv("cond", 0, KC), func=AF.Silu),
                  reads=[pvec_b], writes=[scond_b])

            sqn = [0]

            def rms_stats(srcs, nch, inv_n, bank, rt=None, rb=None):
                rt = rstd_t if rt is None else rt
                rb = rstd_b if rb is None else rb
                for c, (ap, bufs) in enumerate(srcs):
                    s = sqn[0] % 4
                    sqn[0] += 1
                    fw.op(ACT, lambda e, ap=ap, s=s: e.activation(out=sq_t[s][:], in_=ap,
                                                                   func=AF.Square),
                          reads=bufs, writes=[sq_b[s]])
                    fw.op(PE, lambda e, c=c, s=s: e.matmul(ps(bank), ones_bf[:], sq_t[s][:],
                                                           start=(c == 0), stop=(c == nch - 1)),
                          reads=[const_b, sq_b[s]], writes=[ps_b[bank]])
                fw.op(ACT, lambda e: e.activation(out=rt[:], in_=ps(bank), func=AF.Ln,
                                                  bias=eps_t[:, 0:1], scale=inv_n),
                      reads=[ps_b[bank], const_b], writes=[rb])
                fw.op(ACT, lambda e: e.activation(out=rt[:], in_=rt[:], func=AF.Exp, scale=-0.5),
                      reads=[rb], writes=[rb])

            def norm_mod(l, which):
                ao = 0 if which == 1 else 8
                bo = 0 if which == 1 else 24
                rts = [(rstd_t, rstd_b), (rstd1_t, rstd1_b)]
                for tt in range(2):
                    rms_stats([(x_t[:, c, tsl(tt)], [xb[c][tt]]) for c in range(KC)], KC, 1.0 / D,
                              bank=tt, rt=rts[tt][0], rb=rts[tt][1])
                n = 0
                for tt in range(2):
                    rt, rb = rts[tt]
                    for c in range(KC):
                        s = n % 4
                        n += 1
                        fw.op(DVE, lambda e, c=c, s=s, tt=tt, rt=rt: e.tensor_tensor(
                            out=xn_t[s][:], in0=x_t[:, c, tsl(tt)], in1=rt[:], op=ALU.mult),
                            reads=[xb[c][tt], rb], writes=[xn_b[s]])
                        if c % 2 == 0:
                            fw.op(ACT, lambda e, c=c, s=s, tt=tt: e.activation(
                                out=h_t[:, c, tsl(tt)], in_=xn_t[s][:], func=AF.Identity,
                                bias=mod_t[l][:, bo + c:bo + c + 1],
                                scale=col_t[l][:, ao + c:ao + c + 1]),
                                reads=[xn_b[s], mod_b[l], col_b[l]], writes=[hb[c][tt]])
                        else:
                            fw.op(DVE, lambda e, c=c, s=s, tt=tt: e.tensor_scalar(
                                out=h_t[:, c, tsl(tt)], in0=xn_t[s][:],
                                scalar1=col_t[l][:, ao + c:ao + c + 1],
                                scalar2=mod_t[l][:, bo + c:bo + c + 1], op0=ALU.mult, op1=ALU.add),
                                reads=[xn_b[s], mod_b[l], col_b[l]], writes=[hb[c][tt]])

            def ada_block(l, nb, b0=0, b1=1):
                wv = ada_w[l].rearrange("(k p) n -> p k n", p=128)
                w, wb, wk = ws.next(wv[:, :, nb * 512:(nb + 1) * 512], (KC, 512))

                def mm(e, w=w):
                    r = None
                    for k in range(KC):
                        r = e.matmul(ps(b0)[0:1, :], scond[:, k:k + 1], w[:, k, :],
                                     start=(k == 0), stop=(k == KC - 1))
                    return r
                fw.op(PE, mm, reads=[scond_b, wb], writes=[ps_b[b0]])
                ws.done(wk)
                s = nb % 2
                fw.op(ACT, lambda e, s=s: e.activation(out=row_t[s][:], in_=ps(b0)[0:1, :], func=AF.Copy),
                      reads=[ps_b[b0]], writes=[row_b[s]])

                def mt(e, s=s):
                    r = None
                    for j in range(4):
                        r = e.matmul(ps(b1)[:, j:j + 1], row_t[s][0:1, j * 128:(j + 1) * 128],
                                     one_f[0:1, 0:1], start=True, stop=True)
                    return r
                fw.op(PE, mt, reads=[row_b[s], const_b], writes=[ps_b[b1]])
                fw.op(DVE, lambda e: e.tensor_tensor(out=mod_t[l][:, nb * 4:nb * 4 + 4], in0=ps(b1)[:, 0:4],
                                                     in1=pv(f"adab{l}", nb * 4, 4), op=ALU.add),
                      reads=[ps_b[b1], pvec_b], writes=[mod_b[l]])

            def ada_finish(l):
                fw.op(DVE, lambda e: e.scalar_tensor_tensor(
                    out=col_t[l][:, 0:8], in0=mod_t[l][:, 8:16], scalar=1.0,
                    in1=pv(f"n1w{l}", 0, KC), op0=ALU.add, op1=ALU.mult),
                    reads=[mod_b[l], pvec_b], writes=[col_b[l]])
                fw.op(DVE, lambda e: e.scalar_tensor_tensor(
                    out=col_t[l][:, 8:16], in0=mod_t[l][:, 32:40], scalar=1.0,
                    in1=pv(f"n2w{l}", 0, KC), op0=ALU.add, op1=ALU.mult),
                    reads=[mod_b[l], pvec_b], writes=[col_b[l]])
                fw.op(DVE, lambda e: e.tensor_scalar(
                    out=col_t[l][:, 16:16 + FC], in0=pv(f"fw0_{l}", 0, FC),
                    scalar1=pv("flagneg"), scalar2=None, op0=ALU.mult),
                    reads=[pvec_b], writes=[col_b[l]])
                fw.op(DVE, lambda e: e.tensor_scalar(
                    out=col_t[l][:, 16 + FC:16 + 2 * FC], in0=pv(f"fw2_{l}", 0, FC),
                    scalar1=pv("flagneg"), scalar2=None, op0=ALU.mult),
                    reads=[pvec_b], writes=[col_b[l]])

            def conv_fix(eng, t_ap, src_ap, nf0, nf2, reads, writes):
                fw.op(eng, lambda e: e.scalar_tensor_tensor(
                    out=t_ap[:, 256:1024:256], in0=src_ap[:, 255:1023:256], scalar=nf0,
                    in1=t_ap[:, 256:1024:256], op0=ALU.mult, op1=ALU.add),
                    reads=reads, writes=writes)
                fw.op(eng, lambda e: e.scalar_tensor_tensor(
                    out=t_ap[:, 255:1023:256], in0=src_ap[:, 256:1024:256], scalar=nf2,
                    in1=t_ap[:, 255:1023:256], op0=ALU.mult, op1=ALU.add),
                    reads=reads, writes=writes)

            def ffn(l, mid_hook=None):
                upv = ffn_up[l].rearrange("(k p) n -> p k n", p=128)
                dnv = ffn_down[l].rearrange("(k p) n -> p k n", p=128)
                j = 0
                for jg in range(6):
                    ncol = 512 if jg < 5 else 256
                    gw, gwb, gk = ws.next(upv[:, :, jg * 512:jg * 512 + ncol], (KC, ncol))
                    uw, uwb, uk = ws.next(upv[:, :, DFF + jg * 512:DFF + jg * 512 + ncol],
                                          (KC, ncol))
                    for jj in range(ncol // 128):
                        dbl = 2 * (j % 2)
                        for which, (w, wb) in enumerate(((gw, gwb), (uw, uwb))):
                            for tt in range(2):
                                bank = (dbl + which) * 2 + tt

                                def mm(e, w=w, jj=jj, tt=tt, bank=bank):
                                    r = None
                                    for k in range(KC):
                                        r = e.matmul(ps(bank), w[:, k, jj * 128:(jj + 1) * 128],
                                                     h_t[:, k, tsl(tt)],
                                                     start=(k == 0), stop=(k == KC - 1))
                                    return r
                                fw.op(PE, mm, reads=[wb] + [hb[k][tt] for k in range(KC)],
                                      writes=[ps_b[bank]])
                        g_ap = pd_t[dbl][:]
                        u_ap = pd_t[dbl + 1][:]
                        gB = [ps_b[dbl * 2], ps_b[dbl * 2 + 1]]
                        uB = [ps_b[dbl * 2 + 2], ps_b[dbl * 2 + 3]]
                        s = j % 2
                        t = tA_t[s]
                        tb = tA_b[s]
                        fw.op(ACT, lambda e, t=t, g_ap=g_ap, j=j: e.activation(
                            out=t[:], in_=g_ap, func=AF.Identity, bias=pv(f"fb_{l}", j),
                            scale=pv(f"fw1_{l}", j)),
                            reads=gB + [pvec_b], writes=[tb])
                        fw.op(DVE, lambda e, t=t, g_ap=g_ap, j=j: e.scalar_tensor_tensor(
                            out=t[:, 1:T], in0=g_ap[:, 0:T - 1], scalar=pv(f"fw0_{l}", j),
                            in1=t[:, 1:T], op0=ALU.mult, op1=ALU.add),
                            reads=gB + [pvec_b, tb], writes=[tb])
                        fw.op(DVE, lambda e, t=t, g_ap=g_ap, j=j: e.scalar_tensor_tensor(
                            out=t[:, 0:T - 1], in0=g_ap[:, 1:T], scalar=pv(f"fw2_{l}", j),
                            in1=t[:, 0:T - 1], op0=ALU.mult, op1=ALU.add),
                            reads=gB + [pvec_b, tb], writes=[tb])
                        conv_fix(DVE, t, g_ap, col_t[l][:, 16 + j:17 + j],
                                 col_t[l][:, 16 + FC + j:17 + FC + j],
                                 reads=gB + [col_b[l], tb], writes=[tb])
                        fw.op(ACT, lambda e, t=t: e.activation(out=t[:], in_=t[:], func=AF.Gelu),
                              reads=[tb], writes=[tb])
                        fw.op(DVE, lambda e, t=t, u_ap=u_ap, j=j: e.tensor_tensor(
                            out=a_t[:, j, :], in0=t[:], in1=u_ap, op=ALU.mult),
                            reads=[tb] + uB, writes=[ab[j]])
                        j += 1
                    ws.done(gk)
                    ws.done(uk)
                    if mid_hook is not None:
                        mid_hook(jg)
                for oc in range(KC):
                    w, wb, wk = ws.next(dnv[:, :, oc * 128:(oc + 1) * 128], (FC, 128))
                    for tt in range(2):
                        bank = (oc * 2 + tt) % 8

                        def mm(e, w=w, tt=tt, bank=bank):
                            r = None
                            for k in range(FC):
                                r = e.matmul(ps(bank), w[:, k, :], a_t[:, k, tsl(tt)],
                                             start=(k == 0), stop=(k == FC - 1))
                            return r
                        fw.op(PE, mm, reads=[wb] + ab[0:FC], writes=[ps_b[bank]])
                        fw.op(DVE, lambda e, oc=oc, tt=tt, bank=bank: e.scalar_tensor_tensor(
                            out=x_t[:, oc, tsl(tt)], in0=ps(bank), scalar=mod_t[l][:, 40 + oc:41 + oc],
                            in1=x_t[:, oc, tsl(tt)], op0=ALU.mult, op1=ALU.add),
                            reads=[ps_b[bank], mod_b[l], xb[oc][tt]], writes=[xb[oc][tt]])
                    ws.done(wk)


            def evac(i, out_ap, in_ap, reads, writes):
                if i % 2 == 0:
                    fw.op(DVE, lambda e: e.tensor_copy(out=out_ap, in_=in_ap), reads=reads, writes=writes)
                else:
                    fw.op(ACT, lambda e: e.activation(out=out_ap, in_=in_ap, func=AF.Copy),
                          reads=reads, writes=writes)

            def out_linear(l, wdram, src_ap, src_bufs, gcol):
                wv = wdram.rearrange("(k p) n -> p k n", p=128)
                n = 0
                for nb in range(2):
                    w, wb, wk = ws.next(wv[:, :, nb * 512:(nb + 1) * 512], (KC, 512))
                    for o4 in range(4):
                        oc = nb * 4 + o4
                        for tt in range(2):
                            bank = n % 8
                            n += 1

                            def mm(e, w=w, o4=o4, tt=tt, bank=bank):
                                r = None
                                for kk in range(KC):
                                    r = e.matmul(ps(bank), w[:, kk, o4 * 128:(o4 + 1) * 128],
                                                 src_ap(kk, tt), start=(kk == 0), stop=(kk == KC - 1))
                                return r
                            fw.op(PE, mm, reads=[wb] + src_bufs(tt), writes=[ps_b[bank]])
                            fw.op(DVE, lambda e, oc=oc, tt=tt, bank=bank: e.scalar_tensor_tensor(
                                out=x_t[:, oc, tsl(tt)], in0=ps(bank), scalar=gcol(oc),
                                in1=x_t[:, oc, tsl(tt)], op0=ALU.mult, op1=ALU.add),
                                reads=[ps_b[bank], mod_b[l], mcol_b, xb[oc][tt]], writes=[xb[oc][tt]])
                    ws.done(wk)

            def mixer_pool(l):
                fw.barrier_bufs(ab)
                fw.op(DVE, lambda e: e.tensor_tensor(out=mcol_t[:, 0:8], in0=pv("pscale", 0, KC),
                                                     in1=mod_t[l][:, 16:24], op=ALU.mult),
                      reads=[pvec_b, mod_b[l]], writes=[mcol_b])
                for i in range(8):
                    bank = i % 8

                    def tr(e, i=i, bank=bank):
                        r = None
                        for c in range(KC):
                            r = e.transpose(psb(bank)[:, c * 128:(c + 1) * 128],
                                            h_t[:, c, i * 128:(i + 1) * 128], identb[:])
                        return r
                    fw.op(PE, tr, reads=[hb[c][i // 4] for c in range(KC)] + [identb_b],
                          writes=[ps_b[bank]])
                    evac(i, a_t[:, i, :], psb(bank), [ps_b[bank]], [ab[i]])
                aw = ab_k = None
                for cc in range(8):
                    g = cc // 2
                    if cc % 2 == 0:
                        aw, awb, ak = ws.next(poolA[:, g, :].rearrange("p (a b c) -> p a b c", a=8, b=3, c=128), (8, 3, 128))
                    dbl = cc % 4

                    def mm(e, cc=cc, aw=aw, dbl=dbl):
                        r = None
                        for i in range(8):
                            ds = [d for d in range(3) if 0 <= i + d - 1 < 8]
                            for n, d in enumerate(ds):
                                r = e.matmul(pd_t[dbl][:, i * 128:(i + 1) * 128],
                                             a_t[:, i + d - 1, cc * 128:(cc + 1) * 128],
                                             aw[:, i, d, :], start=(n == 0), stop=(n == len(ds) - 1))
                        return r
                    fw.op(PE, mm, reads=ab[0:8] + [awb], writes=[ps_b[2 * dbl], ps_b[2 * dbl + 1]])
                    evac(cc, a_t[:, 8 + cc, :], pd_t[dbl][:], [ps_b[2 * dbl], ps_b[2 * dbl + 1]],
                         [ab[8 + cc]])
                    if cc % 2 == 1:
                        ws.done(ak)
                pw, pwb, pk = ws.next(pool_w.rearrange("g (kk p) d -> p g kk d", p=128), (4, 2, 256))
                n = 0
                for dc in range(8):
                    g = dc // 2
                    for tt in range(2):
                        bank = n % 8
                        n += 1

                        def mm2(e, dc=dc, g=g, tt=tt, bank=bank):
                            r = None
                            for kk in range(2):
                                r = e.matmul(ps(bank), pw[:, g, kk, (dc % 2) * 128:(dc % 2) * 128 + 128],
                                             a_t[:, 8 + g * 2 + kk, tsl(tt)], start=(kk == 0), stop=(kk == 1))
                            return r
                        fw.op(PE, mm2, reads=[pwb, ab[8 + g * 2], ab[9 + g * 2]], writes=[ps_b[bank]])
                        fw.op(DVE, lambda e, dc=dc, tt=tt, bank=bank: e.scalar_tensor_tensor(
                            out=x_t[:, dc, tsl(tt)], in0=ps(bank), scalar=mcol_t[:, dc:dc + 1],
                            in1=x_t[:, dc, tsl(tt)], op0=ALU.mult, op1=ALU.add),
                            reads=[ps_b[bank], mcol_b, xb[dc][tt]], writes=[xb[dc][tt]])
                ws.done(pk)

            def mixer_fnet(l):
                fw.barrier_bufs(ab)

                def pq(i, g):
                    return a_t[:, 2 * i + g // 2, (g % 2) * 512:(g % 2) * 512 + 512]
                n = 0
                for i in range(8):
                    for g in range(4):
                        bank = n % 8

                        def mm(e, i=i, g=g, bank=bank):
                            r = None
                            for kk in range(2):
                                r = e.matmul(ps(bank), h_t[:, g * 2 + kk, i * 128:(i + 1) * 128],
                                             cs_t[:, kk, :], start=(kk == 0), stop=(kk == 1))
                            return r
                        fw.op(PE, mm, reads=[hb[g * 2][i // 4], hb[g * 2 + 1][i // 4], cs_b],
                              writes=[ps_b[bank]])
                        evac(n, pq(i, g), ps(bank), [ps_b[bank]], [ab[2 * i + g // 2]])
                        n += 1
                for tt in range(2):
                    cw, cwb, ck = ws.next(dftC[:, :, tsl(tt)], (8, 512))
                    sw, swb, sk = ws.next(dftS[:, :, tsl(tt)], (8, 512))
                    for mc in range(8):
                        g, m2 = mc // 2, mc % 2
                        bank = n % 8

                        def mm(e, g=g, m2=m2, bank=bank, cw=cw, sw=sw):
                            r = None
                            for i in range(8):
                                r = e.matmul(ps(bank), pq(i, g)[:, m2 * 128:(m2 + 1) * 128], cw[:, i, :],
                                             start=(i == 0), stop=False)
                                r = e.matmul(ps(bank), pq(i, g)[:, 256 + m2 * 128:256 + (m2 + 1) * 128],
                                             sw[:, i, :], start=False, stop=(i == 7))
                            return r
                        fw.op(PE, mm, reads=ab[0:16] + [cwb, swb], writes=[ps_b[bank]])
                        evac(n, a_t[:, 16 + mc, tsl(tt)], ps(bank), [ps_b[bank]], [ab[16 + mc]])
                        n += 1
                    ws.done(ck)
                    ws.done(sk)
                out_linear(l, fnet_w, lambda kk, tt: a_t[:, 16 + kk, tsl(tt)],
                           lambda tt: ab[16:24], lambda oc: mod_t[l][:, 16 + oc:17 + oc])

            def mixer_sconv(l):
                fw.barrier_bufs(ab)
                fw.op(DVE, lambda e: e.tensor_scalar(out=mcol_t[:, 8:16], in0=pv("sw0", 0, KC),
                                                     scalar1=pv("flagneg"), scalar2=None, op0=ALU.mult),
                      reads=[pvec_b], writes=[mcol_b])
                fw.op(DVE, lambda e: e.tensor_scalar(out=mcol_t[:, 16:24], in0=pv("sw2", 0, KC),
                                                     scalar1=pv("flagneg"), scalar2=None, op0=ALU.mult),
                      reads=[pvec_b], writes=[mcol_b])
                wv = sconv_win.rearrange("(k p) n -> p k n", p=128)
                nd = 0
                for jg in range(2):
                    wl = [ws.next(wv[:, :, part * D + jg * 512:part * D + jg * 512 + 512], (KC, 512))
                          for part in range(3)]
                    for jj in range(4):
                        j = jg * 4 + jj
                        dbls = []
                        for part in range(3):
                            dbl = nd % 4
                            nd += 1
                            dbls.append(dbl)
                            w, wb, _ = wl[part]
                            for tt in range(2):
                                bank = 2 * dbl + tt

                                def mm(e, w=w, jj=jj, tt=tt, bank=bank):
                                    r = None
                                    for kk in range(KC):
                                        r = e.matmul(ps(bank), w[:, kk, jj * 128:(jj + 1) * 128],
                                                     h_t[:, kk, tsl(tt)], start=(kk == 0), stop=(kk == KC - 1))
                                    return r
                                fw.op(PE, mm, reads=[wb] + [hb[kk][tt] for kk in range(KC)],
                                      writes=[ps_b[bank]])
                        gbB = [ps_b[2 * dbls[0]], ps_b[2 * dbls[0] + 1]]
                        gcB = [ps_b[2 * dbls[1]], ps_b[2 * dbls[1] + 1]]
                        uB = [ps_b[2 * dbls[2]], ps_b[2 * dbls[2] + 1]]
                        s = j % 2
                        v, vb = tA_t[s], tA_b[s]
                        t2, t2b = stg_t[s], stg_b[s]
                        fw.op(ACT, lambda e, v=v, d=dbls[2]: e.activation(out=v[:], in_=pd_t[d][:], func=AF.Copy),
                              reads=uB, writes=[vb])
                        fw.op(DVE, lambda e, v=v, d=dbls[1]: e.tensor_tensor(out=v[:], in0=pd_t[d][:], in1=v[:],
                                                                             op=ALU.mult),
                              reads=gcB + [vb], writes=[vb])
                        fw.op(ACT, lambda e, v=v, t2=t2, j=j: e.activation(out=t2[:], in_=v[:], func=AF.Copy,
                                                                             scale=pv("sw1", j)),
                              reads=[vb, pvec_b], writes=[t2b])
                        fw.op(DVE, lambda e, v=v, t2=t2, j=j: e.scalar_tensor_tensor(
                            out=t2[:, 1:T], in0=v[:, 0:T - 1], scalar=pv("sw0", j), in1=t2[:, 1:T],
                            op0=ALU.mult, op1=ALU.add), reads=[vb, pvec_b, t2b], writes=[t2b])
                        fw.op(DVE, lambda e, v=v, t2=t2, j=j: e.scalar_tensor_tensor(
                            out=t2[:, 0:T - 1], in0=v[:, 1:T], scalar=pv("sw2", j), in1=t2[:, 0:T - 1],
                            op0=ALU.mult, op1=ALU.add), reads=[vb, pvec_b, t2b], writes=[t2b])
                        conv_fix(DVE, t2, v, mcol_t[:, 8 + j:9 + j], mcol_t[:, 16 + j:17 + j],
                                 reads=[vb, mcol_b, t2b], writes=[t2b])
                        fw.op(DVE, lambda e, t2=t2, j=j, d=dbls[0]: e.tensor_tensor(
                            out=a_t[:, j, :], in0=pd_t[d][:], in1=t2[:], op=ALU.mult),
                            reads=gbB + [t2b], writes=[ab[j]])
                    for _, _, wk in wl:
                        ws.done(wk)
                out_linear(l, sconv_wout, lambda kk, tt: a_t[:, kk, tsl(tt)],
                           lambda tt: ab[0:8], lambda oc: mod_t[l][:, 16 + oc:17 + oc])


            def mixer_mla(l):
                SC = 1.0 / float(np.sqrt(96.0))
                arena = a_t[:].rearrange("p a b -> p (a b)")
                cqT = lambda c: a_t[:, c, :]
                ckvall = lambda c: arena[:, 3 * T + c * 1536:3 * T + (c + 1) * 1536]
                KR = arena[:, 6 * T:6 * T + 1536]
                QT = lambda r: a_t[:, 8 + r, :]
                KT = lambda r: arena[:, (11 + 2 * r) * T:(11 + 2 * r) * T + 1536]
                VP = lambda r: arena[:, (17 + 3 * r) * T:(17 + 3 * r) * T + 3072].rearrange(
                    "p (k x) -> p k x", k=12, x=256)
                PT = lambda r: arena[:, 23 * T + r * 512:23 * T + (r + 1) * 512]
                for i in range(2):
                    rr = fw.dma(SP, ropecs[:, i, :], ropeCS_d[:, i, :], dst=ropecs_b)
                    if MLA_SUB == 0.11 and not fw.dry:
                        fw.out_recs.append(rr)
                for c in range(2):
                    fw.dma(POOL, ckvall(c)[:, 0:512], cacheT_d[:, c, :], dst=ckvall_b[c])
                fw.dma(POOL, KR, krmask_d, dst=kr_b)
                for r in range(3):
                    fw.dma(POOL, QT(r), eq_d, dst=qt_b[r])
                for r in range(2):
                    fw.op(DVE, lambda e, r=r: e.memset(VP(r), 0.0), writes=[vp_b[r]])
                    fw.op(DVE, lambda e, r=r: e.memset(VP(r)[:, :, 64:65], 1.0), writes=[vp_b[r]])
                    fw.op(DVE, lambda e, r=r: e.memset(VP(r)[:, :, 128:129], 1.0), writes=[vp_b[r]])

                if MLA_SUB < 0.2:
                    return
                w, wb, wk = ws.next(wdq.rearrange("(k p) n -> p k n", p=128), (KC, 384))
                for tt in range(2):
                    banks = [(4 * tt + c) % 8 for c in range(3)]
                    for c in range(3):
                        def mm(e, c=c, tt=tt, bank=banks[c], w=w):
                            r = None
                            for kk in range(KC):
                                r = e.matmul(ps(bank), w[:, kk, c * 128:(c + 1) * 128], h_t[:, kk, tsl(tt)],
                                             start=(kk == 0), stop=(kk == KC - 1))
                            return r
                        fw.op(PE, mm, reads=[wb] + [hb[kk][tt] for kk in range(KC)], writes=[ps_b[banks[c]]])
                    rms_stats([(ps(banks[c]), [ps_b[banks[c]]]) for c in range(3)], 3, 1.0 / 384,
                              bank=(4 * tt + 3) % 8)
                    for c in range(3):
                        s = c % 2
                        fw.op(DVE, lambda e, s=s, bank=banks[c]: e.tensor_tensor(
                            out=xn_t[s][:], in0=ps(bank), in1=rstd_t[:], op=ALU.mult),
                            reads=[ps_b[banks[c]], rstd_b], writes=[xn_b[s]])
                        fw.op(ACT, lambda e, s=s, c=c, tt=tt: e.activation(
                            out=cqT(c)[:, tsl(tt)], in_=xn_t[s][:], func=AF.Copy, scale=pv("qnw", c)),
                            reads=[xn_b[s], pvec_b], writes=[cq_b[c]])
                ws.done(wk)

                if MLA_SUB < 0.5:
                    return
                w, wb, wk = ws.next(wdkv_aug.rearrange("(k p) n -> p k n", p=128), (KC, 448))
                for tt in range(2):
                    banks = [(5 * tt + i) % 8 for i in range(4)]
                    cols = [(0, 128), (128, 128), (256, 96), (352, 96)]
                    for i in range(4):
                        c0, m = cols[i]

                        def mm(e, c0=c0, m=m, tt=tt, bank=banks[i], w=w):
                            r = None
                            for kk in range(KC):
                                r = e.matmul(ps(bank)[0:m, :], w[:, kk, c0:c0 + m], h_t[:, kk, tsl(tt)],
                                             start=(kk == 0), stop=(kk == KC - 1))
                            return r
                        fw.op(PE, mm, reads=[wb] + [hb[kk][tt] for kk in range(KC)], writes=[ps_b[banks[i]]])
                    if MLA_SUB < 0.51:
                        continue
                    rms_stats([(ps(banks[c]), [ps_b[banks[c]]]) for c in range(2)], 2, 1.0 / 256,
                              bank=(5 * tt + 4) % 8)
                    if MLA_SUB < 0.52:
                        continue
                    for c in range(2):
                        s = c % 2
                        fw.op(DVE, lambda e, s=s, bank=banks[c]: e.tensor_tensor(
                            out=xn_t[s][:], in0=ps(bank), in1=rstd_t[:], op=ALU.mult),
                            reads=[ps_b[banks[c]], rstd_b], writes=[xn_b[s]])
                        fw.op(ACT, lambda e, s=s, c=c, tt=tt: e.activation(
                            out=ckvall(c)[:, 512 + tt * 512:1024 + tt * 512], in_=xn_t[s][:], func=AF.Copy,
                            scale=pv("kvnw", c)),
                            reads=[xn_b[s], pvec_b], writes=[ckvall_b[c]])
                        fw.op(DVE, lambda e, s=s, c=c, tt=tt: e.tensor_scalar(
                            out=tA_t[c][:, tsl(tt)], in0=xn_t[s][:], scalar1=pv("kvnw", c), scalar2=None,
                            op0=ALU.mult),
                            reads=[xn_b[s], pvec_b], writes=[tA_b[c]])
                    if MLA_SUB < 0.55:
                        continue
                    bA, bB = banks[2], banks[3]
                    fw.op(ACT, lambda e, tt=tt, bA=bA: e.activation(
                        out=stg_t[0][64:96, tsl(tt)], in_=ps(bA)[64:96, :], func=AF.Copy),
                        reads=[ps_b[bA]], writes=[stg_b[0]])
                    if MLA_SUB < 0.56:
                        continue
                    fw.op(DVE, lambda e, tt=tt, bA=bA: e.tensor_tensor(
                        out=xn_t[0][64:96, :], in0=ps(bA)[64:96, :], in1=ropecs[64:96, 0, tsl(tt)], op=ALU.mult),
                        reads=[ps_b[bA], ropecs_b], writes=[xn_b[0]])
                    if MLA_SUB < 0.57:
                        continue
                    fw.op(DVE, lambda e, tt=tt, bB=bB: e.tensor_tensor(
                        out=xn_t[1][64:96, :], in0=ps(bB)[64:96, :], in1=ropecs[64:96, 1, tsl(tt)], op=ALU.mult),
                        reads=[ps_b[bB], ropecs_b], writes=[xn_b[1]])
                    if MLA_SUB < 0.58:
                        continue
                    fw.op(DVE, lambda e, tt=tt: e.tensor_tensor(
                        out=KR[64:96, 512 + tt * 512:1024 + tt * 512], in0=xn_t[0][64:96, :],
                        in1=xn_t[1][64:96, :], op=ALU.add),
                        reads=[xn_b[0], xn_b[1]], writes=[kr_b])
                ws.done(wk)

                if MLA_SUB < 0.7:
                    return
                for i in range(8):
                    bank = i % 8

                    def tr(e, i=i, bank=bank):
                        r = None
                        for c in range(2):
                            r = e.transpose(ps(bank)[:, c * 128:(c + 1) * 128],
                                            tA_t[c][:, i * 128:(i + 1) * 128], ident[:])
                        r = e.transpose(ps(bank)[:, 256:384], stg_t[0][:, i * 128:(i + 1) * 128], ident[:])
                        return r
                    fw.op(PE, tr, reads=[tA_b[0], tA_b[1], stg_b[0], ident_b], writes=[ps_b[bank]])
                    fw.op(DVE, lambda e, i=i, bank=bank: e.tensor_copy(out=ckvst[:, i, :], in_=ps(bank)[:, 0:256]),
                          reads=[ps_b[bank]], writes=[ckvst_b])
                    fw.op(ACT, lambda e, i=i, bank=bank: e.activation(out=krst[:, i, :], in_=ps(bank)[:, 320:352],
                                                                     func=AF.Copy),
                          reads=[ps_b[bank]], writes=[krst_b])
                fw.dma(SP, ckv_o.rearrange("(i p) c -> p i c", p=128), ckvst[:], dst=None, src=[ckvst_b])
                fw.dma(SP, kr_o.rearrange("(i p) c -> p i c", p=128), krst[:], dst=None, src=[krst_b])

                for r in range(3):
                    fw.op(DVE, lambda e, r=r: e.tensor_copy(out=KT(r)[64:128, :], in_=KR[64:128, :]),
                          reads=[kr_b], writes=[kt_b[r]])

                fw.op(DVE, lambda e: e.memset(stg_t[1][:], 0.0), writes=[stg_b[1]])
                if MLA_SUB < 2:
                    return
                ukv, ukvb, ukvk = ws.next(wukv.rearrange("(k p) n -> p k n", p=128), (2, 2048))
                wqv = wuq_aug.rearrange("(k p) n -> p k n", p=128)
                st = {'qw': None, 'n_o': 0, 'n_s': 0, 'n_p': 0}

                def prep_V(p):
                    vr = p % 2
                    for ktg in range(3):
                        bank = 6 + ktg % 2

                        def mmv(e, ktg=ktg, bank=bank, p=p):
                            r = None
                            for j in range(4):
                                kt = ktg * 4 + j
                                for kc in range(2):
                                    rhs = ukv[:, kc, p * 256:(p + 1) * 256].rearrange("p (h x) -> p h x", h=2)[:, :, 64:128]
                                    r = e.matmul(ps(bank)[:, j * 128:(j + 1) * 128].rearrange("p (h x) -> p h x", h=2),
                                                 ckvall(kc)[:, kt * 128:(kt + 1) * 128], rhs,
                                                 start=(kc == 0), stop=(kc == 1))
                            return r
                        fw.op(PE, mmv, reads=[ukvb, ckvall_b[0], ckvall_b[1]], writes=[ps_b[bank]])
                        src = ps(bank).rearrange("p (j h x) -> p j h x", j=4, h=2, x=64)
                        fw.op(DVE, lambda e, src=src, ktg=ktg, vr=vr: e.tensor_copy(
                            out=VP(vr)[:, ktg * 4:ktg * 4 + 4, 0:64], in_=src[:, :, 0, :]),
                            reads=[ps_b[bank]], writes=[vp_b[vr]])
                        fw.op(ACT, lambda e, src=src, ktg=ktg, vr=vr: e.activation(
                            out=VP(vr)[:, ktg * 4:ktg * 4 + 4, 192:256], in_=src[:, :, 1, :], func=AF.Copy),
                            reads=[ps_b[bank]], writes=[vp_b[vr]])

                def prep_KQ(h):
                    r3 = h % 3
                    if h % 4 == 0:
                        if st['qw'] is not None:
                            ws.done(st['qw'][2])
                        st['qw'] = ws.next(wqv[:, :, (h // 4) * 768:(h // 4 + 1) * 768], (3, 768))
                    qw = st['qw']
                    for kt5 in range(3):
                        bank = 6 + kt5 % 2

                        def mmk(e, h=h, kt5=kt5, bank=bank):
                            r = None
                            for kc in range(2):
                                r = e.matmul(ps(bank)[0:64, :], ukv[:, kc, h * 128:h * 128 + 64],
                                             ckvall(kc)[:, kt5 * 512:(kt5 + 1) * 512],
                                             start=(kc == 0), stop=(kc == 1))
                            return r
                        fw.op(PE, mmk, reads=[ukvb, ckvall_b[0], ckvall_b[1]], writes=[ps_b[bank]])
                        evac(kt5, KT(r3)[0:64, kt5 * 512:(kt5 + 1) * 512], ps(bank)[0:64, :],
                             [ps_b[bank]], [kt_b[r3]])
                    qcol = (h % 4) * 192
                    for tt in range(2):
                        for which in range(2):
                            bank = 6 + which

                            def mmq(e, which=which, tt=tt, bank=bank, qcol=qcol, qwv=qw[0]):
                                r = None
                                for kc in range(3):
                                    r = e.matmul(ps(bank)[0:96, :],
                                                 qwv[:, kc, qcol + which * 96:qcol + which * 96 + 96],
                                                 cqT(kc)[:, tsl(tt)], start=(kc == 0), stop=(kc == 2))
                                return r
                            fw.op(PE, mmq, reads=[qw[1]] + cq_b, writes=[ps_b[bank]])
                        fw.op(ACT, lambda e, r3=r3, tt=tt: e.activation(
                            out=QT(r3)[0:64, tsl(tt)], in_=ps(6)[0:64, :], func=AF.Copy),
                            reads=[ps_b[6]], writes=[qt_b[r3]])
                        fw.op(DVE, lambda e, tt=tt: e.tensor_tensor(
                            out=xn_t[0][64:96, :], in0=ps(6)[64:96, :], in1=ropecs[64:96, 0, tsl(tt)],
                            op=ALU.mult), reads=[ps_b[6], ropecs_b], writes=[xn_b[0]])
                        fw.op(DVE, lambda e, tt=tt: e.tensor_tensor(
                            out=xn_t[1][64:96, :], in0=ps(7)[64:96, :], in1=ropecs[64:96, 1, tsl(tt)],
                            op=ALU.mult), reads=[ps_b[7], ropecs_b], writes=[xn_b[1]])
                        fw.op(DVE, lambda e, r3=r3, tt=tt: e.tensor_tensor(
                            out=QT(r3)[64:96, tsl(tt)], in0=xn_t[0][64:96, :], in1=xn_t[1][64:96, :],
                            op=ALU.add), reads=[xn_b[0], xn_b[1]], writes=[qt_b[r3]])

                def attend(h, tts, pending=None):
                    p, hh = h // 2, h % 2
                    vr = p % 2
                    r3 = h % 3
                    if hh == 0:
                        vcols, orows, srow, om = (0, 65), (0, 64), 64, 65
                    else:
                        vcols, orows, srow, om = (128, 256), (64, 128), 0, 128
                    for tt in (tts if MLA_SUB >= 3 else ()):
                        ob = 2 + st['n_o'] % 3
                        st['n_o'] += 1
                        pend = []
                        for kt in range(14):
                            if kt < 12:
                                sb_ = (0, 1, 5)[st['n_s'] % 3]
                                st['n_s'] += 1
                                fw.op(PE, lambda e, sb_=sb_, kt=kt, r3=r3, tt=tt: e.matmul(
                                    ps(sb_), KT(r3)[:, kt * 128:(kt + 1) * 128], QT(r3)[:, tsl(tt)],
                                    start=True, stop=True),
                                    reads=[kt_b[r3], qt_b[r3]], writes=[ps_b[sb_]])
                                pi = st['n_p'] % 4
                                st['n_p'] += 1
                                fw.op(ACT, lambda e, sb_=sb_, pi=pi: e.activation(
                                    out=PT(pi), in_=ps(sb_), func=AF.Exp, scale=SC),
                                    reads=[ps_b[sb_]], writes=[pt_b[pi]])
                            if kt >= 2:
                                pkt, ppi = pend.pop(0)
                                fw.op(PE, lambda e, ob=ob, om=om, vr=vr, pkt=pkt, ppi=ppi, vcols=vcols: e.matmul(
                                    ps(ob)[0:om, :], VP(vr)[:, pkt, vcols[0]:vcols[1]], PT(ppi),
                                    start=(pkt == 0), stop=(pkt == 11)),
                                    reads=[vp_b[vr], pt_b[ppi]], writes=[ps_b[ob]])
                            if kt < 12:
                                pend.append((kt, pi))
                            if kt == 8 and pending is not None:
                                pending()
                                pending = None
                        if MLA_SUB < 4:
                            continue
                        return lambda ob=ob, tt=tt: norm_o(h, tt, ob)
                    return None

                def norm_o(h, tt, ob):
                    p, hh = h // 2, h % 2
                    if hh == 0:
                        orows, srow = (0, 64), 64
                    else:
                        orows, srow = (64, 128), 0
                    if True:
                        rs = stg_t[1][srow:srow + 1, 0:512] if hh == 0 else stg_t[1][srow:srow + 1, 512:1024]
                        fw.op(DVE, lambda e, rs=rs, ob=ob, srow=srow: e.reciprocal(
                            out=rs, in_=ps(ob)[srow:srow + 1, :]), reads=[ps_b[ob]], writes=[stg_b[1]])
                        bb = 6 + st['n_o'] % 2
                        fw.op(PE, lambda e, bb=bb, hh=hh: e.matmul(
                            ps(bb), sel_t[:, hh, :], stg_t[1][:, hh * 512:(hh + 1) * 512],
                            start=True, stop=True),
                            reads=[stg_b[1], const_b], writes=[ps_b[bb]])
                        fw.op(ACT, lambda e, bb=bb: e.activation(out=rstd_t[:], in_=ps(bb), func=AF.Copy),
                              reads=[ps_b[bb]], writes=[rstd_b])
                        fw.op(DVE, lambda e, ob=ob, orows=orows, p=p, tt=tt: e.tensor_tensor(
                            out=h_t[orows[0]:orows[1], p, tsl(tt)], in0=ps(ob)[orows[0]:orows[1], :],
                            in1=rstd_t[orows[0]:orows[1], :], op=ALU.mult),
                            reads=[ps_b[ob], rstd_b], writes=[hb[p][tt]])

                prep_V(0)
                prep_KQ(0)
                pnd = None
                for h in range(16):
                    pnd = attend(h, (0,), pnd)
                    if h + 1 < 16:
                        if (h + 1) % 2 == 0:
                            prep_V((h + 1) // 2)
                        prep_KQ(h + 1)
                    pnd = attend(h, (1,), pnd)
                if pnd is not None:
                    pnd()
                qw = st['qw']
                ws.done(qw[2])
                ws.done(ukvk)
                out_linear(l, wo, lambda kk, tt: h_t[:, kk, tsl(tt)],
                           lambda tt: [hb[kk][tt] for kk in range(KC)], lambda oc: mod_t[l][:, 16 + oc:17 + oc])

            def mixer(l):
                norm_mod(l, 1)
                if l == 0 and stage >= 4:
                    mixer_mla(l)
                elif l == 1 and stage >= 3:
                    mixer_pool(l)
                elif l == 2 and stage >= 3:
                    mixer_fnet(l)
                elif l == 3 and stage >= 3:
                    mixer_sconv(l)
                fw.barrier_bufs(ab)

            if stage >= 2:
                for nb in range(12):
                    ada_block(0, nb, 4 + nb % 2, 6 + nb % 2)
                ada_finish(0)
                for l in range(DEPTH):
                    mixer(l)
                    norm_mod(l, 2)
                    if l + 1 < DEPTH:
                        def hook(jg, l=l):
                            ada_block(l + 1, 2 * jg, 0, 1)
                            ada_block(l + 1, 2 * jg + 1, 2, 3)
                            if jg == 5:
                                ada_finish(l + 1)
                        ffn(l, hook)
                    else:
                        ffn(l)

            fo, _ = PV["fnw"]
            for tt in range(2):
                rms_stats([(x_t[:, c, tsl(tt)], [xb[c][tt]]) for c in range(KC)], KC, 1.0 / D,
                          bank=tt)
                for c in range(KC):
                    fw.op(DVE, lambda e, c=c, tt=tt: e.scalar_tensor_tensor(
                        out=x_t[:, c, tsl(tt)], in0=x_t[:, c, tsl(tt)],
                        scalar=pvec[:, fo + c:fo + c + 1], in1=rstd_t[:],
                        op0=ALU.mult, op1=ALU.mult),
                        reads=[xb[c][tt], rstd_b, pvec_b], writes=[xb[c][tt]])
            for i in range(8):
                s = i % 2
                tt = i // 4
                for half in range(2):
                    bank = 2 + (i * 2 + half) % 6

                    def mm(e, half=half, bank=bank, i=i):
                        r = None
                        for cc in range(4):
                            c = half * 4 + cc
                            r = e.transpose(ps(bank)[:, cc * 128:(cc + 1) * 128],
                                            x_t[:, c, i * 128:(i + 1) * 128], ident[:])
                        return r
                    fw.op(PE, mm, reads=[xb[c][tt] for c in range(half * 4, half * 4 + 4)] + [ident_b],
                          writes=[ps_b[bank]])
                    if half == 0:
                        fw.op(DVE, lambda e, s=s, bank=bank: e.tensor_copy(
                            out=stg_t[s][:, 0:512], in_=ps(bank)),
                            reads=[ps_b[bank]], writes=[stg_b[s]])
                    else:
                        fw.op(ACT, lambda e, s=s, bank=bank: e.activation(
                            out=stg_t[s][:, 512:1024], in_=ps(bank), func=AF.Copy),
                            reads=[ps_b[bank]], writes=[stg_b[s]])
                fw.dma(SP, yout[i * 128:(i + 1) * 128, :], stg_t[s][:], dst=None, src=[stg_b[s]])

        fw.dry = True
        emit()
        fw.dry = False
        ws.reset()
        emit()
        assert ws.consumed == len(ws.specs)

        final_waits = {}
        for sem, val in fw.out_recs:
            final_waits[sem] = max(final_waits.get(sem, 0), val)

        with nc.Block() as block:
            @block.sync
            def _(e):
                fw.replay(SP, e)
                for sem, val in final_waits.items():
                    e.wait_ge(sem, val)

            @block.tensor
            def _(e):
                fw.replay(PE, e)

            @block.scalar
            def _(e):
                fw.replay(ACT, e)

            @block.vector
            def _(e):
                fw.replay(DVE, e)

            @block.gpsimd
            def _(e):
                fw.replay(POOL, e)
    return nc


def _cols(v):
    v = np.asarray(v, np.float32)
    return np.ascontiguousarray(v.reshape(-1, 128).T)


def _make_pvec(inp, cond_vec, flagneg):
    pv = np.zeros((128, NPV), np.float32)

    def put(name, arr):
        o, n = PV[name]
        assert arr.shape == (128, n), (name, arr.shape, n)
        pv[:, o:o + n] = arr

    for l in range(DEPTH):
        put(f"n1w{l}", _cols(inp["norm1_w"][l]))
        put(f"n2w{l}", _cols(inp["norm2_w"][l]))
        put(f"fw0_{l}", _cols(inp["ffn_conv_w"][l, 0]))
        put(f"fw1_{l}", _cols(inp["ffn_conv_w"][l, 1]))
        put(f"fw2_{l}", _cols(inp["ffn_conv_w"][l, 2]))
        put(f"fb_{l}", _cols(inp["ffn_conv_b"][l]))
        put(f"adab{l}", _cols(inp["ada_b"][l]))
    put("fnw", _cols(inp["final_norm_w"]))
    put("cond", _cols(cond_vec))
    pv[:, PV["flagneg"][0]] = flagneg
    put("qnw", _cols(inp["mla_q_norm"][0]))
    put("kvnw", _cols(inp["mla_kv_norm"][0]))
    put("pscale", _cols(inp["pool_scale"][0]))
    put("sw0", _cols(inp["sconv_conv"][0, 0]))
    put("sw1", _cols(inp["sconv_conv"][0, 1]))
    put("sw2", _cols(inp["sconv_conv"][0, 2]))
    return pv


def _pool_tables(L):
    A = np.zeros((4, T, T), np.float64)
    wins = (2, 4, 8, 16)
    for g, w in enumerate(wins):
        for t in range(T):
            s0 = (t // L) * L
            tl = t - s0
            lo = max(tl - w // 2, 0)
            hi = min(tl + w - w // 2, L)
            A[g, t, s0 + lo:s0 + hi] = 1.0 / (hi - lo)
            A[g, t, t] -= 1.0
    out = np.zeros((128, 4, 8, 3, 128), np.float32)
    for g in range(4):
        for i in range(8):
            for d in range(3):
                ip = i + d - 1
                if 0 <= ip < 8:
                    out[:, g, i, d, :] = A[g, i * 128:(i + 1) * 128, ip * 128:(ip + 1) * 128].T
    return out.reshape(128, 4, 8 * 3 * 128).astype(ml_dtypes.bfloat16)


def _dft_tables(L):
    t = np.arange(T)
    same = (t[:, None] // L) == (t[None, :] // L)
    ang = 2.0 * np.pi * ((t[:, None] % L) * (t[None, :] % L) % L) / L
    nrm = 1.0 / np.sqrt(L * 256.0)
    C = np.where(same, np.cos(ang), 0.0) * nrm
    S = np.where(same, -np.sin(ang), 0.0) * nrm

    def lay(M):
        return np.ascontiguousarray(M.reshape(8, 128, T).transpose(1, 0, 2)).astype(ml_dtypes.bfloat16)
    c = np.arange(256)
    a2 = 2.0 * np.pi * ((c[:, None] * c[None, :]) % 256) / 256.0
    CS = np.concatenate([np.cos(a2), np.sin(a2)], axis=1)
    CS = np.ascontiguousarray(CS.reshape(2, 128, 512).transpose(1, 0, 2)).astype(ml_dtypes.bfloat16)
    return lay(C), lay(S), CS


_PERM = np.concatenate([np.arange(0, 32, 2), np.arange(1, 32, 2)])
_PERM_SW = np.concatenate([np.arange(1, 32, 2), np.arange(0, 32, 2)])


def _mla_weights(inp):
    wdkv = np.asarray(inp["mla_wdkv"][0], np.float32)
    aug = np.zeros((D, 448), np.float32)
    aug[:, 0:256] = wdkv[:, 0:256]
    aug[:, 320:352] = wdkv[:, 256 + _PERM]
    aug[:, 416:448] = wdkv[:, 256 + _PERM_SW]
    wuq = np.asarray(inp["mla_wuq"][0], np.float32).reshape(384, 16, 96)
    qa = np.zeros((384, 16, 192), np.float32)
    qa[:, :, 0:64] = wuq[:, :, 0:64]
    qa[:, :, 64:96] = wuq[:, :, 64 + _PERM]
    qa[:, :, 160:192] = wuq[:, :, 64 + _PERM_SW]
    return aug, np.ascontiguousarray(qa.reshape(384, 16 * 192))


def _rope_tables(kind):
    cs = np.zeros((128, 2, T), np.float32)
    if kind == "p":
        cs[64:96, 0, :] = 1.0
        return cs
    t = np.arange(T)
    r = (t // 64).astype(np.float32)
    col = (t % 64).astype(np.float32)
    inv = (np.float32(10000.0) ** (-np.arange(8, dtype=np.float32) / np.float32(8))).astype(np.float32)
    ang = np.concatenate([r[:, None] * inv, col[:, None] * inv], axis=-1).astype(np.float32)
    c, s = np.cos(ang).T, np.sin(ang).T
    cs[64:80, 0, :] = c
    cs[80:96, 0, :] = c
    cs[64:80, 1, :] = -s
    cs[80:96, 1, :] = s
    return cs


def _mask_tables(kind):
    NEG = -30000.0
    eq = np.zeros((128, T), np.float32)
    ek = np.zeros((128, 1536), np.float32)
    if kind == "p":
        seq = np.arange(T) // 256
        for r in range(4):
            eq[96 + r, :] = (seq == r)
            ek[96 + r, 0:512] = NEG
            ek[96 + r, 512:] = np.where(seq == r, 0.0, NEG)
    return eq, ek


def _core_roles():
    return [("s", 0), ("s", 1), ("p", 0), ("p", 1), ("p", 2), ("p", 3), ("p", 3), ("p", 3)]


_NC_CACHE = {}


def make_in_maps(inp):
    roles = _core_roles()
    ident = np.eye(128, dtype=np.float32)
    shared = {
        "ident": ident,
        "ada_w": np.ascontiguousarray(inp["ada_w"], dtype=np.float32),
        "ffn_up": np.ascontiguousarray(inp["ffn_up"], dtype=np.float32),
        "ffn_down": np.ascontiguousarray(inp["ffn_down"], dtype=np.float32),
        "pool_w": np.ascontiguousarray(inp["pool_w"][0], dtype=np.float32),
        "fnet_w": np.ascontiguousarray(inp["fnet_w"][0], dtype=np.float32),
        "sconv_win": np.ascontiguousarray(inp["sconv_win"][0], dtype=np.float32),
        "sconv_wout": np.ascontiguousarray(inp["sconv_wout"][0], dtype=np.float32),
        "identb": ident.astype(ml_dtypes.bfloat16),
        "wdq": np.ascontiguousarray(inp["mla_wdq"][0], dtype=np.float32),
        "wukv": np.ascontiguousarray(inp["mla_wukv"][0], dtype=np.float32),
        "wo": np.ascontiguousarray(inp["mla_wo"][0], dtype=np.float32),
    }
    shared["wdkv_aug"], shared["wuq_aug"] = _mla_weights(inp)
    tabs = {}
    for kind, L in (("s", 1024), ("p", 256)):
        C, S, CS = _dft_tables(L)
        eq, ek = _mask_tables(kind)
        tabs[kind] = {"poolA": _pool_tables(L), "dftC": C, "dftS": S, "dftCS": CS,
                      "ropeCS": _rope_tables(kind), "eq": eq, "_ek": ek}
    in_maps = []
    for kind, idx in roles:
        if kind == "s":
            xc = inp["x_sample"][idx]
            cond = inp["c"][idx]
            flag = 0.0
        else:
            xc = inp["x_prompt"][4 * idx:4 * idx + 4].reshape(T, D)
            cond = inp["c_ctx"]
            flag = -1.0
        m = dict(shared)
        m.update({a: b for a, b in tabs[kind].items() if not a.startswith("_")})
        krm = tabs[kind]["_ek"].copy()
        cT = np.zeros((128, 2, 512), np.float32)
        if kind == "s":
            cT[:] = np.asarray(inp["cache_ckv"][idx, 0], np.float32).T.reshape(2, 128, 512).transpose(1, 0, 2)
            krm[64:96, 0:512] = np.asarray(inp["cache_krope"][idx, 0], np.float32)[:, _PERM].T
        m["cacheT"] = cT
        m["krmask"] = krm
        m["xin"] = np.ascontiguousarray(xc, dtype=np.float32)
        m["pvec"] = _make_pvec(inp, cond, flag)
        in_maps.append(m)
    return in_maps


def kernel(**inputs):
    inp = {k: np.asarray(v) for k, v in inputs.items()}
    in_maps = make_in_maps(inp)
    if "nc" not in _NC_CACHE:
        _NC_CACHE["nc"] = build_program(STAGE)
    nc = _NC_CACHE["nc"]
    res = run_bass_kernel_spmd(nc, in_maps, core_ids=list(range(NCORES)))
    outs = res.results
    y_sample = np.stack([outs[0]["yout"], outs[1]["yout"]], axis=0).astype(np.float32)
    y_prompt = np.concatenate([outs[2 + g]["yout"].reshape(4, 256, D) for g in range(4)], axis=0)
    y_prompt = y_prompt.astype(np.float32)
    new_ckv = np.concatenate([outs[2 + g]["ckv_o"].reshape(4, 1, 256, 256) for g in range(4)], axis=0)
    krp = np.concatenate([outs[2 + g]["kr_o"].reshape(4, 1, 256, 32) for g in range(4)], axis=0)
    new_kr = np.empty_like(krp)
    new_kr[..., _PERM] = krp
    new_ckv = new_ckv.astype(np.float32)
    new_kr = new_kr.astype(np.float32)
    return (y_prompt, y_sample, new_ckv, new_kr)
```

```python
import numpy as np
from contextlib import ExitStack
import ml_dtypes

import concourse.bass as bass
import concourse.mybir as mybir
from concourse.bass_utils import run_bass_kernel_spmd

F32 = mybir.dt.float32
BF16 = mybir.dt.bfloat16
AF = mybir.ActivationFunctionType
ALU = mybir.AluOpType

D = 1024
T = 1024
KC = 8
DFF = 2816
FC = 22
DEPTH = 4
EPS = 1e-6
NCORES = 8


class Buf:
    __slots__ = ("name", "w", "r", "sem", "cum", "excl")

    def __init__(self, name):
        self.name = name
        self.excl = False
        self.w = None
        self.r = {}
        self.sem = None
        self.cum = 0


class Q:
    def __init__(self, fw, name, own_wait=True):
        self.fw = fw
        self.name = name
        self.thunks = []
        self.sem = fw.new_sem("q_" + name)
        self.cnt = 0
        self.known = {}
        self.own_wait = own_wait


class FW:
    def __init__(self, nc, es):
        self.nc = nc
        self.es = es
        self.nsem = 0
        self.pe = Q(self, "pe", own_wait=False)
        self.act = Q(self, "act")
        self.dve = Q(self, "dve")
        self.pool = Q(self, "pool")
        self.sp = Q(self, "sp")
        self.out_recs = []
        self.dry = False

    def new_sem(self, name):
        self.nsem += 1
        return self.es.enter_context(self.nc.semaphore(f"s{self.nsem}_{name}"))

    def buf(self, name, dma=False):
        b = Buf(name)
        if dma:
            b.sem = self.new_sem("d_" + name)
        return b

    def _collect(self, q, reads, writes):
        waits = {}

        def need(rec):
            if rec is None:
                return
            sem, val = rec
            if sem is q.sem and not q.own_wait:
                return
            if q.known.get(sem, 0) >= val:
                return
            if waits.get(sem, 0) < val:
                waits[sem] = val

        for b in reads:
            need(b.w)
            if b.excl:
                for sem, val in b.r.items():
                    if sem is not q.sem:
                        need((sem, val))
        for b in writes:
            need(b.w)
            for sem, val in b.r.items():
                need((sem, val))
        for sem, val in waits.items():
            q.known[sem] = val
        return list(waits.items())

    @staticmethod
    def _commit(rec, reads, writes):
        sem, val = rec
        for b in reads:
            if b.r.get(sem, 0) < val:
                b.r[sem] = val
        for b in writes:
            b.w = rec
            b.r = {}

    def op(self, q, fn, reads=(), writes=()):
        if self.dry:
            return None
        wl = self._collect(q, reads, writes)
        q.cnt += 1
        rec = (q.sem, q.cnt)
        q.thunks.append((wl, fn, rec, 1))
        self._commit(rec, reads, writes)
        return rec

    def dma(self, q, out_ap, in_ap, dst=None, src=(), reads=(), kw=None):
        if self.dry:
            return None
        kw = kw or {}
        writes = [dst] if dst is not None else []
        rds = list(src) + list(reads)
        wl = self._collect(q, rds, writes)
        owner = dst if dst is not None else src[0]
        owner.cum += 16
        rec = (owner.sem, owner.cum)

        def fn(e, out_ap=out_ap, in_ap=in_ap, kw=kw):
            return e.dma_start(out=out_ap, in_=in_ap, **kw)

        q.thunks.append((wl, fn, rec, 16))
        self._commit(rec, rds, writes)
        if dst is None:
            self.out_recs.append(rec)
        return rec

    def barrier_bufs(self, bufs):
        allq = [self.pe, self.act, self.dve, self.pool]
        for b in bufs:
            for q in allq:
                if q.cnt > 0:
                    if b.r.get(q.sem, 0) < q.cnt:
                        b.r[q.sem] = q.cnt

    def replay(self, q, eng):
        for wl, fn, rec, inc in q.thunks:
            for sem, val in wl:
                eng.wait_ge(sem, val)
            ins = fn(eng)
            if isinstance(ins, (list, tuple)):
                ins = ins[-1]
            ins.then_inc(rec[0], inc)


class WStream:
    def __init__(self, fw, q, slots, bufs, slot_elems):
        self.fw = fw
        self.q = q
        self.slots = slots
        self.bufs = bufs
        self.n = len(slots)
        self.slot_elems = slot_elems
        self.specs = []
        self.reset()

    def reset(self):
        self.issued = 0
        self.consumed = 0
        self.done_flags = []

    def _view(self, k, shape):
        t = self.slots[k % self.n]
        n = int(np.prod(shape))
        assert n <= self.slot_elems, shape
        if len(shape) == 1:
            return t[:, 0:n]
        if len(shape) == 2:
            return t[:, 0:n].rearrange("p (a b) -> p a b", a=shape[0], b=shape[1])
        return t[:, 0:n].rearrange("p (a b c) -> p a b c", a=shape[0], b=shape[1], c=shape[2])

    def _pump(self):
        while self.issued < len(self.specs):
            k = self.issued
            if k >= self.n and not (k - self.n < len(self.done_flags) and self.done_flags[k - self.n]):
                break
            dram_ap, shape = self.specs[k]
            self.fw.dma(self.q, self._view(k, shape), dram_ap, dst=self.bufs[k % self.n])
            self.issued += 1

    def next(self, dram_ap, shape):
        shape = tuple(shape)
        if self.fw.dry:
            self.specs.append((dram_ap, shape))
            return self._view(0, shape), self.bufs[0], None
        k = self.consumed
        self.consumed += 1
        assert self.specs[k][1] == shape, (k, self.specs[k][1], shape)
        self.done_flags.append(False)
        self._pump()
        assert self.issued > k, (k, self.issued)
        return self._view(k, shape), self.bufs[k % self.n], k

    def done(self, k):
        if self.fw.dry:
            return
        self.done_flags[k] = True
        self._pump()


def _pvec_map():
    m = {}
    o = 0

    def add(name, n):
        nonlocal o
        m[name] = (o, n)
        o += n

    for l in range(DEPTH):
        add(f"n1w{l}", KC)
        add(f"n2w{l}", KC)
        add(f"fw0_{l}", FC)
        add(f"fw1_{l}", FC)
        add(f"fw2_{l}", FC)
        add(f"fb_{l}", FC)
        add(f"adab{l}", 48)
    add("fnw", KC)
    add("cond", KC)
    add("flagneg", 1)
    add("qnw", 3)
    add("kvnw", 2)
    add("pscale", KC)
    add("sw0", KC)
    add("sw1", KC)
    add("sw2", KC)
    m["_n"] = o
    return m


PV = _pvec_map()
NPV = PV["_n"]

STAGE = 4
MLA_SUB = 4.0
NSLOT = 6
SLOT_ELEMS = 4096


def build_program(stage=STAGE):
    nc = bass.Bass("TRN2", target_bir_lowering=False)

    def din(name, shape, dt=F32):
        return nc.dram_tensor(name, list(shape), dt, kind="ExternalInput").ap()

    def dout(name, shape, dt=F32):
        return nc.dram_tensor(name, list(shape), dt, kind="ExternalOutput").ap()

    xin = din("xin", [T, D])
    pvec_d = din("pvec", [128, NPV])
    ident_d = din("ident", [128, 128])
    ada_w = din("ada_w", [DEPTH, D, 6 * D])
    ffn_up = din("ffn_up", [DEPTH, D, 2 * DFF])
    ffn_down = din("ffn_down", [DEPTH, DFF, D])
    pool_w = din("pool_w", [4, 256, 256])
    poolA = din("poolA", [128, 4, 8 * 3 * 128], BF16)
    fnet_w = din("fnet_w", [D, D])
    dftC = din("dftC", [128, 8, T], BF16)
    dftS = din("dftS", [128, 8, T], BF16)
    dftCS = din("dftCS", [128, 2, 512], BF16)
    identb_d = din("identb", [128, 128], BF16)
    sconv_win = din("sconv_win", [D, 3 * D])
    sconv_wout = din("sconv_wout", [D, D])
    wdq = din("wdq", [D, 384])
    wdkv_aug = din("wdkv_aug", [D, 448])
    wuq_aug = din("wuq_aug", [384, 16 * 192])
    wukv = din("wukv", [256, 2048])
    wo = din("wo", [D, D])
    ropeCS_d = din("ropeCS", [128, 2, T])
    cacheT_d = din("cacheT", [128, 2, 512])
    krmask_d = din("krmask", [128, 1536])
    eq_d = din("eq", [128, T])
    yout = dout("yout", [T, D])
    ckv_o = dout("ckv_o", [T, 256])
    kr_o = dout("kr_o", [T, 32])

    es = ExitStack()
    with es:
        fw = FW(nc, es)
        PE, ACT, DVE, POOL, SP = fw.pe, fw.act, fw.dve, fw.pool, fw.sp

        def sb(name, shape, dt):
            return es.enter_context(nc.sbuf_tensor(name, list(shape), dt))

        x_t = sb("x", [128, KC, T], F32)
        xb = [[fw.buf(f"x{c}_{tt}") for tt in range(2)] for c in range(KC)]
        h_t = sb("h", [128, KC, T], BF16)
        hb = [[fw.buf(f"h{c}_{tt}") for tt in range(2)] for c in range(KC)]
        a_t = sb("a", [128, 25, T], BF16)
        ab = [fw.buf(f"a{j}") for j in range(25)]
        pvec = sb("pvec_sb", [128, NPV], F32)
        pvec_b = fw.buf("pvec", dma=True)
        ident = sb("ident_sb", [128, 128], F32)
        ident_b = fw.buf("ident", dma=True)
        ones_bf = sb("ones_bf", [128, 128], BF16)
        one_f = sb("one_f", [128, 1], F32)
        eps_t = sb("eps", [128, 1], F32)
        const_b = fw.buf("consts")
        stg_t = [sb(f"stg{i}", [128, D], F32) for i in range(2)]
        stg_b = [fw.buf(f"stg{i}", dma=True) for i in range(2)]
        tA_t = [sb(f"tA{i}", [128, T], F32) for i in range(2)]
        tA_b = [fw.buf(f"tA{i}") for i in range(2)]
        sq_t = [sb(f"sq{i}", [128, 512], BF16) for i in range(4)]
        sq_b = [fw.buf(f"sq{i}") for i in range(4)]
        rstd_t = sb("rstd", [128, 512], F32)
        rstd_b = fw.buf("rstd")
        rstd1_t = sb("rstd1", [128, 512], F32)
        rstd1_b = fw.buf("rstd1")
        xn_t = [sb(f"xn{i}", [128, 512], F32) for i in range(4)]
        xn_b = [fw.buf(f"xn{i}") for i in range(4)]
        scond = sb("scond", [128, KC], BF16)
        scond_b = fw.buf("scond")
        row_t = [sb(f"row{i}", [1, 512], F32) for i in range(2)]
        row_b = [fw.buf(f"row{i}") for i in range(2)]
        mod_t = [sb(f"mod{l}", [128, 48], F32) for l in range(DEPTH)]
        mod_b = [fw.buf(f"mod{l}") for l in range(DEPTH)]
        col_t = [sb(f"cols{l}", [128, 16 + 2 * FC], F32) for l in range(DEPTH)]
        col_b = [fw.buf(f"cols{l}") for l in range(DEPTH)]
        identb = sb("identb_sb", [128, 128], BF16)
        identb_b = fw.buf("identb", dma=True)
        cs_t = sb("dftcs_sb", [128, 2, 512], BF16)
        cs_b = fw.buf("dftcs", dma=True)
        ropecs = sb("ropecs", [128, 2, T], F32)
        ropecs_b = fw.buf("ropecs", dma=True)
        sel_t = sb("sel", [128, 2, 128], F32)
        ckvst = sb("ckvst", [128, 8, 256], F32)
        ckvst_b = fw.buf("ckvst", dma=True)
        krst = sb("krst", [128, 8, 32], F32)
        krst_b = fw.buf("krst", dma=True)
        ckvall_b = [fw.buf(f"ckvall{c}", dma=True) for c in range(2)]
        kr_b = fw.buf("KR", dma=True)
        qt_b = [fw.buf(f"QT{i}", dma=True) for i in range(3)]
        kt_b = [fw.buf(f"KT{i}") for i in range(3)]
        vp_b = [fw.buf(f"VP{i}") for i in range(2)]
        pt_b = [fw.buf(f"PT{i}") for i in range(4)]
        cq_b = [fw.buf(f"cq{i}") for i in range(3)]
        mcol_t = sb("mcols", [128, 32], F32)
        mcol_b = fw.buf("mcols")
        slots = [sb(f"wslot{i}", [128, SLOT_ELEMS], BF16) for i in range(NSLOT)]
        slot_b = [fw.buf(f"wslot{i}", dma=True) for i in range(NSLOT)]
        ws = WStream(fw, POOL, slots, slot_b, SLOT_ELEMS)

        pd_t = [es.enter_context(nc.psum_tensor(f"pd{i}", [128, 1024], F32)) for i in range(4)]
        ps_b = [fw.buf(f"ps{i}") for i in range(8)]
        for b in ps_b:
            b.excl = True

        pdb_t = [t.bitcast(BF16) for t in pd_t]

        def ps(bank):
            return pd_t[bank // 2][:, (bank % 2) * 512:(bank % 2) * 512 + 512]

        def psb(bank):
            return pdb_t[bank // 2][:, (bank % 2) * 1024:(bank % 2) * 1024 + 1024]

        def pv(name, j=0, n=1):
            o, _ = PV[name]
            return pvec[:, o + j:o + j + n]

        def tsl(tt):
            return slice(tt * 512, (tt + 1) * 512)

        def emit():
            fw.dma(SP, pvec[:], pvec_d, dst=pvec_b)
            fw.dma(SP, ident[:], ident_d, dst=ident_b)
            fw.dma(SP, identb[:], identb_d, dst=identb_b)
            fw.dma(SP, cs_t[:], dftCS, dst=cs_b)
            fw.op(DVE, lambda e: e.memset(ones_bf[:], 1.0), writes=[const_b])
            fw.op(DVE, lambda e: e.memset(eps_t[:], EPS), writes=[const_b])
            fw.op(DVE, lambda e: e.memset(one_f[:], 1.0), writes=[const_b])
            fw.op(DVE, lambda e: e.memset(sel_t[:], 0.0), writes=[const_b])
            fw.op(DVE, lambda e: e.memset(sel_t[64:65, 0, :], 1.0), writes=[const_b])
            fw.op(DVE, lambda e: e.memset(sel_t[0:1, 1, :], 1.0), writes=[const_b])

            for i in range(8):
                s = i % 2
                tt = i // 4
                fw.dma(SP, stg_t[s][:], xin[i * 128:(i + 1) * 128, :], dst=stg_b[s])
                for half in range(2):
                    bank = (i * 2 + half) % 8

                    def mm(e, s=s, half=half, bank=bank):
                        r = None
                        for cc in range(4):
                            c = half * 4 + cc
                            r = e.transpose(ps(bank)[:, cc * 128:(cc + 1) * 128],
                                            stg_t[s][:, c * 128:(c + 1) * 128], ident[:])
                        return r
                    fw.op(PE, mm, reads=[stg_b[s], ident_b], writes=[ps_b[bank]])
                    wr = [xb[c][tt] for c in range(half * 4, half * 4 + 4)]
                    if half == 0:
                        fw.op(DVE, lambda e, half=half, bank=bank, i=i: e.tensor_copy(
                            out=x_t[:, half * 4:half * 4 + 4, i * 128:(i + 1) * 128],
                            in_=ps(bank).rearrange("p (c t) -> p c t", c=4)),
                            reads=[ps_b[bank]], writes=wr)
                    else:
                        fw.op(ACT, lambda e, half=half, bank=bank, i=i: e.activation(
                            out=x_t[:, half * 4:half * 4 + 4, i * 128:(i + 1) * 128],
                            in_=ps(bank).rearrange("p (c t) -> p c t", c=4), func=AF.Copy),
                            reads=[ps_b[bank]], writes=wr)

            fw.op(ACT, lambda e: e.activation(out=scond[:], in_=pv("cond", 0, KC), func=AF.Silu),
                  reads=[pvec_b], writes=[scond_b])

            sqn = [0]

            def rms_stats(srcs, nch, inv_n, bank, rt=None, rb=None):
                rt = rstd_t if rt is None else rt
                rb = rstd_b if rb is None else rb
                for c, (ap, bufs) in enumerate(srcs):
                    s = sqn[0] % 4
                    sqn[0] += 1
                    fw.op(ACT, lambda e, ap=ap, s=s: e.activation(out=sq_t[s][:], in_=ap,
                                                                   func=AF.Square),
                          reads=bufs, writes=[sq_b[s]])
                    fw.op(PE, lambda e, c=c, s=s: e.matmul(ps(bank), ones_bf[:], sq_t[s][:],
                                                           start=(c == 0), stop=(c == nch - 1)),
                          reads=[const_b, sq_b[s]], writes=[ps_b[bank]])
                fw.op(ACT, lambda e: e.activation(out=rt[:], in_=ps(bank), func=AF.Ln,
                                                  bias=eps_t[:, 0:1], scale=inv_n),
                      reads=[ps_b[bank], const_b], writes=[rb])
                fw.op(ACT, lambda e: e.activation(out=rt[:], in_=rt[:], func=AF.Exp, scale=-0.5),
                      reads=[rb], writes=[rb])

            def norm_mod(l, which):
                ao = 0 if which == 1 else 8
                bo = 0 if which == 1 else 24
                rts = [(rstd_t, rstd_b), (rstd1_t, rstd1_b)]
                for tt in range(2):
                    rms_stats([(x_t[:, c, tsl(tt)], [xb[c][tt]]) for c in range(KC)], KC, 1.0 / D,
                              bank=tt, rt=rts[tt][0], rb=rts[tt][1])
                n = 0
                for tt in range(2):
                    rt, rb = rts[tt]
                    for c in range(KC):
                        s = n % 4
                        n += 1
                        fw.op(DVE, lambda e, c=c, s=s, tt=tt, rt=rt: e.tensor_tensor(
                            out=xn_t[s][:], in0=x_t[:, c, tsl(tt)], in1=rt[:], op=ALU.mult),
                            reads=[xb[c][tt], rb], writes=[xn_b[s]])
                        if c % 2 == 0:
                            fw.op(ACT, lambda e, c=c, s=s, tt=tt: e.activation(
                                out=h_t[:, c, tsl(tt)], in_=xn_t[s][:], func=AF.Identity,
                                bias=mod_t[l][:, bo + c:bo + c + 1],
                                scale=col_t[l][:, ao + c:ao + c + 1]),
                                reads=[xn_b[s], mod_b[l], col_b[l]], writes=[hb[c][tt]])
                        else:
                            fw.op(DVE, lambda e, c=c, s=s, tt=tt: e.tensor_scalar(
                                out=h_t[:, c, tsl(tt)], in0=xn_t[s][:],
                                scalar1=col_t[l][:, ao + c:ao + c + 1],
                                scalar2=mod_t[l][:, bo + c:bo + c + 1], op0=ALU.mult, op1=ALU.add),
                                reads=[xn_b[s], mod_b[l], col_b[l]], writes=[hb[c][tt]])

            def ada_block(l, nb, b0=0, b1=1):
                wv = ada_w[l].rearrange("(k p) n -> p k n", p=128)
                w, wb, wk = ws.next(wv[:, :, nb * 512:(nb + 1) * 512], (KC, 512))

                def mm(e, w=w):
                    r = None
                    for k in range(KC):
                        r = e.matmul(ps(b0)[0:1, :], scond[:, k:k + 1], w[:, k, :],
                                     start=(k == 0), stop=(k == KC - 1))
                    return r
                fw.op(PE, mm, reads=[scond_b, wb], writes=[ps_b[b0]])
                ws.done(wk)
                s = nb % 2
                fw.op(ACT, lambda e, s=s: e.activation(out=row_t[s][:], in_=ps(b0)[0:1, :], func=AF.Copy),
                      reads=[ps_b[b0]], writes=[row_b[s]])

                def mt(e, s=s):
                    r = None
                    for j in range(4):
                        r = e.matmul(ps(b1)[:, j:j + 1], row_t[s][0:1, j * 128:(j + 1) * 128],
                                     one_f[0:1, 0:1], start=True, stop=True)
                    return r
                fw.op(PE, mt, reads=[row_b[s], const_b], writes=[ps_b[b1]])
                fw.op(DVE, lambda e: e.tensor_tensor(out=mod_t[l][:, nb * 4:nb * 4 + 4], in0=ps(b1)[:, 0:4],
                                                     in1=pv(f"adab{l}", nb * 4, 4), op=ALU.add),
                      reads=[ps_b[b1], pvec_b], writes=[mod_b[l]])

            def ada_finish1(l):
                fw.op(DVE, lambda e: e.scalar_tensor_tensor(
                    out=col_t[l][:, 0:8], in0=mod_t[l][:, 8:16], scalar=1.0,
                    in1=pv(f"n1w{l}", 0, KC), op0=ALU.add, op1=ALU.mult),
                    reads=[mod_b[l], pvec_b], writes=[col_b[l]])
                fw.op(DVE, lambda e: e.tensor_scalar(
                    out=col_t[l][:, 16:16 + FC], in0=pv(f"fw0_{l}", 0, FC),
                    scalar1=pv("flagneg"), scalar2=None, op0=ALU.mult),
                    reads=[pvec_b], writes=[col_b[l]])
                fw.op(DVE, lambda e: e.tensor_scalar(
                    out=col_t[l][:, 16 + FC:16 + 2 * FC], in0=pv(f"fw2_{l}", 0, FC),
                    scalar1=pv("flagneg"), scalar2=None, op0=ALU.mult),
                    reads=[pvec_b], writes=[col_b[l]])


            def ada_finish2(l):
                fw.op(DVE, lambda e: e.scalar_tensor_tensor(
                    out=col_t[l][:, 8:16], in0=mod_t[l][:, 32:40], scalar=1.0,
                    in1=pv(f"n2w{l}", 0, KC), op0=ALU.add, op1=ALU.mult),
                    reads=[mod_b[l], pvec_b], writes=[col_b[l]])

            def ada_finish(l):
                ada_finish1(l)
                ada_finish2(l)

            def conv_fix(eng, t_ap, src_ap, nf0, nf2, reads, writes):
                fw.op(eng, lambda e: e.scalar_tensor_tensor(
                    out=t_ap[:, 256:1024:256], in0=src_ap[:, 255:1023:256], scalar=nf0,
                    in1=t_ap[:, 256:1024:256], op0=ALU.mult, op1=ALU.add),
                    reads=reads, writes=writes)
                fw.op(eng, lambda e: e.scalar_tensor_tensor(
                    out=t_ap[:, 255:1023:256], in0=src_ap[:, 256:1024:256], scalar=nf2,
                    in1=t_ap[:, 255:1023:256], op0=ALU.mult, op1=ALU.add),
                    reads=reads, writes=writes)

            def ffn(l, mid_hook=None):
                upv = ffn_up[l].rearrange("(k p) n -> p k n", p=128)
                dnv = ffn_down[l].rearrange("(k p) n -> p k n", p=128)
                j = 0
                for jg in range(6):
                    ncol = 512 if jg < 5 else 256
                    gw, gwb, gk = ws.next(upv[:, :, jg * 512:jg * 512 + ncol], (KC, ncol))
                    uw, uwb, uk = ws.next(upv[:, :, DFF + jg * 512:DFF + jg * 512 + ncol],
                                          (KC, ncol))
                    for jj in range(ncol // 128):
                        dbl = 2 * (j % 2)
                        for which, (w, wb) in enumerate(((gw, gwb), (uw, uwb))):
                            for tt in range(2):
                                bank = (dbl + which) * 2 + tt

                                def mm(e, w=w, jj=jj, tt=tt, bank=bank):
                                    r = None
                                    for k in range(KC):
                                        r = e.matmul(ps(bank), w[:, k, jj * 128:(jj + 1) * 128],
                                                     h_t[:, k, tsl(tt)],
                                                     start=(k == 0), stop=(k == KC - 1))
                                    return r
                                fw.op(PE, mm, reads=[wb] + [hb[k][tt] for k in range(KC)],
                                      writes=[ps_b[bank]])
                        g_ap = pd_t[dbl][:]
                        u_ap = pd_t[dbl + 1][:]
                        gB = [ps_b[dbl * 2], ps_b[dbl * 2 + 1]]
                        uB = [ps_b[dbl * 2 + 2], ps_b[dbl * 2 + 3]]
                        s = j % 2
                        t = tA_t[s]
                        tb = tA_b[s]
                        fw.op(ACT, lambda e, t=t, g_ap=g_ap, j=j: e.activation(
                            out=t[:], in_=g_ap, func=AF.Identity, bias=pv(f"fb_{l}", j),
                            scale=pv(f"fw1_{l}", j)),
                            reads=gB + [pvec_b], writes=[tb])
                        fw.op(DVE, lambda e, t=t, g_ap=g_ap, j=j: e.scalar_tensor_tensor(
                            out=t[:, 1:T], in0=g_ap[:, 0:T - 1], scalar=pv(f"fw0_{l}", j),
                            in1=t[:, 1:T], op0=ALU.mult, op1=ALU.add),
                            reads=gB + [pvec_b, tb], writes=[tb])
                        fw.op(DVE, lambda e, t=t, g_ap=g_ap, j=j: e.scalar_tensor_tensor(
                            out=t[:, 0:T - 1], in0=g_ap[:, 1:T], scalar=pv(f"fw2_{l}", j),
                            in1=t[:, 0:T - 1], op0=ALU.mult, op1=ALU.add),
                            reads=gB + [pvec_b, tb], writes=[tb])
                        conv_fix(DVE, t, g_ap, col_t[l][:, 16 + j:17 + j],
                                 col_t[l][:, 16 + FC + j:17 + FC + j],
                                 reads=gB + [col_b[l], tb], writes=[tb])
                        fw.op(ACT, lambda e, t=t: e.activation(out=t[:], in_=t[:], func=AF.Gelu),
                              reads=[tb], writes=[tb])
                        fw.op(DVE, lambda e, t=t, u_ap=u_ap, j=j: e.tensor_tensor(
                            out=a_t[:, j, :], in0=t[:], in1=u_ap, op=ALU.mult),
                            reads=[tb] + uB, writes=[ab[j]])
                        j += 1
                    ws.done(gk)
                    ws.done(uk)
                    if mid_hook is not None:
                        mid_hook(jg)
                for oc in range(KC):
                    w, wb, wk = ws.next(dnv[:, :, oc * 128:(oc + 1) * 128], (FC, 128))
                    for tt in range(2):
                        bank = (oc * 2 + tt) % 8

                        def mm(e, w=w, tt=tt, bank=bank):
                            r = None
                            for k in range(FC):
                                r = e.matmul(ps(bank), w[:, k, :], a_t[:, k, tsl(tt)],
                                             start=(k == 0), stop=(k == FC - 1))
                            return r
                        fw.op(PE, mm, reads=[wb] + ab[0:FC], writes=[ps_b[bank]])
                        fw.op(DVE, lambda e, oc=oc, tt=tt, bank=bank: e.scalar_tensor_tensor(
                            out=x_t[:, oc, tsl(tt)], in0=ps(bank), scalar=mod_t[l][:, 40 + oc:41 + oc],
                            in1=x_t[:, oc, tsl(tt)], op0=ALU.mult, op1=ALU.add),
                            reads=[ps_b[bank], mod_b[l], xb[oc][tt]], writes=[xb[oc][tt]])
                    ws.done(wk)


            def evac(i, out_ap, in_ap, reads, writes):
                if i % 2 == 0:
                    fw.op(DVE, lambda e: e.tensor_copy(out=out_ap, in_=in_ap), reads=reads, writes=writes)
                else:
                    fw.op(ACT, lambda e: e.activation(out=out_ap, in_=in_ap, func=AF.Copy),
                          reads=reads, writes=writes)

            def out_linear(l, wdram, src_ap, src_bufs, gcol):
                wv = wdram.rearrange("(k p) n -> p k n", p=128)
                n = 0
                for nb in range(2):
                    w, wb, wk = ws.next(wv[:, :, nb * 512:(nb + 1) * 512], (KC, 512))
                    for o4 in range(4):
                        oc = nb * 4 + o4
                        for tt in range(2):
                            bank = n % 8
                            n += 1

                            def mm(e, w=w, o4=o4, tt=tt, bank=bank):
                                r = None
                                for kk in range(KC):
                                    r = e.matmul(ps(bank), w[:, kk, o4 * 128:(o4 + 1) * 128],
                                                 src_ap(kk, tt), start=(kk == 0), stop=(kk == KC - 1))
                                return r
                            fw.op(PE, mm, reads=[wb] + src_bufs(tt), writes=[ps_b[bank]])
                            fw.op(DVE, lambda e, oc=oc, tt=tt, bank=bank: e.scalar_tensor_tensor(
                                out=x_t[:, oc, tsl(tt)], in0=ps(bank), scalar=gcol(oc),
                                in1=x_t[:, oc, tsl(tt)], op0=ALU.mult, op1=ALU.add),
                                reads=[ps_b[bank], mod_b[l], mcol_b, xb[oc][tt]], writes=[xb[oc][tt]])
                    ws.done(wk)

            def mixer_pool(l):
                fw.barrier_bufs(ab)
                fw.op(DVE, lambda e: e.tensor_tensor(out=mcol_t[:, 0:8], in0=pv("pscale", 0, KC),
                                                     in1=mod_t[l][:, 16:24], op=ALU.mult),
                      reads=[pvec_b, mod_b[l]], writes=[mcol_b])
                for i in range(8):
                    bank = i % 8

                    def tr(e, i=i, bank=bank):
                        r = None
                        for c in range(KC):
                            r = e.transpose(psb(bank)[:, c * 128:(c + 1) * 128],
                                            h_t[:, c, i * 128:(i + 1) * 128], identb[:])
                        return r
                    fw.op(PE, tr, reads=[hb[c][i // 4] for c in range(KC)] + [identb_b],
                          writes=[ps_b[bank]])
                    evac(i, a_t[:, i, :], psb(bank), [ps_b[bank]], [ab[i]])
                aw = ab_k = None
                for cc in range(8):
                    g = cc // 2
                    if cc % 2 == 0:
                        aw, awb, ak = ws.next(poolA[:, g, :].rearrange("p (a b c) -> p a b c", a=8, b=3, c=128), (8, 3, 128))
                    dbl = cc % 4

                    def mm(e, cc=cc, aw=aw, dbl=dbl):
                        r = None
                        for i in range(8):
                            ds = [d for d in range(3) if 0 <= i + d - 1 < 8]
                            for n, d in enumerate(ds):
                                r = e.matmul(pd_t[dbl][:, i * 128:(i + 1) * 128],
                                             a_t[:, i + d - 1, cc * 128:(cc + 1) * 128],
                                             aw[:, i, d, :], start=(n == 0), stop=(n == len(ds) - 1))
                        return r
                    fw.op(PE, mm, reads=ab[0:8] + [awb], writes=[ps_b[2 * dbl], ps_b[2 * dbl + 1]])
                    evac(cc, a_t[:, 8 + cc, :], pd_t[dbl][:], [ps_b[2 * dbl], ps_b[2 * dbl + 1]],
                         [ab[8 + cc]])
                    if cc % 2 == 1:
                        ws.done(ak)
                pw, pwb, pk = ws.next(pool_w.rearrange("g (kk p) d -> p g kk d", p=128), (4, 2, 256))
                n = 0
                for dc in range(8):
                    g = dc // 2
                    for tt in range(2):
                        bank = n % 8
                        n += 1

                        def mm2(e, dc=dc, g=g, tt=tt, bank=bank):
                            r = None
                            for kk in range(2):
                                r = e.matmul(ps(bank), pw[:, g, kk, (dc % 2) * 128:(dc % 2) * 128 + 128],
                                             a_t[:, 8 + g * 2 + kk, tsl(tt)], start=(kk == 0), stop=(kk == 1))
                            return r
                        fw.op(PE, mm2, reads=[pwb, ab[8 + g * 2], ab[9 + g * 2]], writes=[ps_b[bank]])
                        fw.op(DVE, lambda e, dc=dc, tt=tt, bank=bank: e.scalar_tensor_tensor(
                            out=x_t[:, dc, tsl(tt)], in0=ps(bank), scalar=mcol_t[:, dc:dc + 1],
                            in1=x_t[:, dc, tsl(tt)], op0=ALU.mult, op1=ALU.add),
                            reads=[ps_b[bank], mcol_b, xb[dc][tt]], writes=[xb[dc][tt]])
                ws.done(pk)

            def mixer_fnet(l):
                fw.barrier_bufs(ab)

                def pq(i, g):
                    return a_t[:, 2 * i + g // 2, (g % 2) * 512:(g % 2) * 512 + 512]
                n = 0
                for i in range(8):
                    for g in range(4):
                        bank = n % 8

                        def mm(e, i=i, g=g, bank=bank):
                            r = None
                            for kk in range(2):
                                r = e.matmul(ps(bank), h_t[:, g * 2 + kk, i * 128:(i + 1) * 128],
                                             cs_t[:, kk, :], start=(kk == 0), stop=(kk == 1))
                            return r
                        fw.op(PE, mm, reads=[hb[g * 2][i // 4], hb[g * 2 + 1][i // 4], cs_b],
                              writes=[ps_b[bank]])
                        evac(n, pq(i, g), ps(bank), [ps_b[bank]], [ab[2 * i + g // 2]])
                        n += 1
                for tt in range(2):
                    cw, cwb, ck = ws.next(dftC[:, :, tsl(tt)], (8, 512))
                    sw, swb, sk = ws.next(dftS[:, :, tsl(tt)], (8, 512))
                    for mc in range(8):
                        g, m2 = mc // 2, mc % 2
                        bank = n % 8

                        def mm(e, g=g, m2=m2, bank=bank, cw=cw, sw=sw):
                            r = None
                            for i in range(8):
                                r = e.matmul(ps(bank), pq(i, g)[:, m2 * 128:(m2 + 1) * 128], cw[:, i, :],
                                             start=(i == 0), stop=False)
                                r = e.matmul(ps(bank), pq(i, g)[:, 256 + m2 * 128:256 + (m2 + 1) * 128],
                                             sw[:, i, :], start=False, stop=(i == 7))
                            return r
                        fw.op(PE, mm, reads=ab[0:16] + [cwb, swb], writes=[ps_b[bank]])
                        evac(n, a_t[:, 16 + mc, tsl(tt)], ps(bank), [ps_b[bank]], [ab[16 + mc]])
                        n += 1
                    ws.done(ck)
                    ws.done(sk)
                out_linear(l, fnet_w, lambda kk, tt: a_t[:, 16 + kk, tsl(tt)],
                           lambda tt: ab[16:24], lambda oc: mod_t[l][:, 16 + oc:17 + oc])

            def mixer_sconv(l):
                fw.barrier_bufs(ab)
                fw.op(DVE, lambda e: e.tensor_scalar(out=mcol_t[:, 8:16], in0=pv("sw0", 0, KC),
                                                     scalar1=pv("flagneg"), scalar2=None, op0=ALU.mult),
                      reads=[pvec_b], writes=[mcol_b])
                fw.op(DVE, lambda e: e.tensor_scalar(out=mcol_t[:, 16:24], in0=pv("sw2", 0, KC),
                                                     scalar1=pv("flagneg"), scalar2=None, op0=ALU.mult),
                      reads=[pvec_b], writes=[mcol_b])
                wv = sconv_win.rearrange("(k p) n -> p k n", p=128)
                nd = 0
                for jg in range(2):
                    wl = [ws.next(wv[:, :, part * D + jg * 512:part * D + jg * 512 + 512], (KC, 512))
                          for part in range(3)]
                    for jj in range(4):
                        j = jg * 4 + jj
                        dbls = []
                        for part in range(3):
                            dbl = nd % 4
                            nd += 1
                            dbls.append(dbl)
                            w, wb, _ = wl[part]
                            for tt in range(2):
                                bank = 2 * dbl + tt

                                def mm(e, w=w, jj=jj, tt=tt, bank=bank):
                                    r = None
                                    for kk in range(KC):
                                        r = e.matmul(ps(bank), w[:, kk, jj * 128:(jj + 1) * 128],
                                                     h_t[:, kk, tsl(tt)], start=(kk == 0), stop=(kk == KC - 1))
                                    return r
                                fw.op(PE, mm, reads=[wb] + [hb[kk][tt] for kk in range(KC)],
                                      writes=[ps_b[bank]])
                        gbB = [ps_b[2 * dbls[0]], ps_b[2 * dbls[0] + 1]]
                        gcB = [ps_b[2 * dbls[1]], ps_b[2 * dbls[1] + 1]]
                        uB = [ps_b[2 * dbls[2]], ps_b[2 * dbls[2] + 1]]
                        s = j % 2
                        v, vb = tA_t[s], tA_b[s]
                        t2, t2b = stg_t[s], stg_b[s]
                        fw.op(ACT, lambda e, v=v, d=dbls[2]: e.activation(out=v[:], in_=pd_t[d][:], func=AF.Copy),
                              reads=uB, writes=[vb])
                        fw.op(DVE, lambda e, v=v, d=dbls[1]: e.tensor_tensor(out=v[:], in0=pd_t[d][:], in1=v[:],
                                                                             op=ALU.mult),
                              reads=gcB + [vb], writes=[vb])
                        fw.op(ACT, lambda e, v=v, t2=t2, j=j: e.activation(out=t2[:], in_=v[:], func=AF.Copy,
                                                                             scale=pv("sw1", j)),
                              reads=[vb, pvec_b], writes=[t2b])
                        fw.op(DVE, lambda e, v=v, t2=t2, j=j: e.scalar_tensor_tensor(
                            out=t2[:, 1:T], in0=v[:, 0:T - 1], scalar=pv("sw0", j), in1=t2[:, 1:T],
                            op0=ALU.mult, op1=ALU.add), reads=[vb, pvec_b, t2b], writes=[t2b])
                        fw.op(DVE, lambda e, v=v, t2=t2, j=j: e.scalar_tensor_tensor(
                            out=t2[:, 0:T - 1], in0=v[:, 1:T], scalar=pv("sw2", j), in1=t2[:, 0:T - 1],
                            op0=ALU.mult, op1=ALU.add), reads=[vb, pvec_b, t2b], writes=[t2b])
                        conv_fix(DVE, t2, v, mcol_t[:, 8 + j:9 + j], mcol_t[:, 16 + j:17 + j],
                                 reads=[vb, mcol_b, t2b], writes=[t2b])
                        fw.op(DVE, lambda e, t2=t2, j=j, d=dbls[0]: e.tensor_tensor(
                            out=a_t[:, j, :], in0=pd_t[d][:], in1=t2[:], op=ALU.mult),
                            reads=gbB + [t2b], writes=[ab[j]])
                    for _, _, wk in wl:
                        ws.done(wk)
                out_linear(l, sconv_wout, lambda kk, tt: a_t[:, kk, tsl(tt)],
                           lambda tt: ab[0:8], lambda oc: mod_t[l][:, 16 + oc:17 + oc])


            def mixer_mla(l):
                SC = 1.0 / float(np.sqrt(96.0))
                arena = a_t[:].rearrange("p a b -> p (a b)")
                cqT = lambda c: a_t[:, c, :]
                ckvall = lambda c: arena[:, 3 * T + c * 1536:3 * T + (c + 1) * 1536]
                KR = arena[:, 6 * T:6 * T + 1536]
                QT = lambda r: a_t[:, 8 + r, :]
                KT = lambda r: arena[:, (11 + 2 * r) * T:(11 + 2 * r) * T + 1536]
                VP = lambda r: arena[:, (17 + 3 * r) * T:(17 + 3 * r) * T + 3072].rearrange(
                    "p (k x) -> p k x", k=12, x=256)
                PT = lambda r: arena[:, 23 * T + r * 512:23 * T + (r + 1) * 512]
                for i in range(2):
                    rr = fw.dma(SP, ropecs[:, i, :], ropeCS_d[:, i, :], dst=ropecs_b)
                    if MLA_SUB == 0.11 and not fw.dry:
                        fw.out_recs.append(rr)
                for c in range(2):
                    fw.dma(POOL, ckvall(c)[:, 0:512], cacheT_d[:, c, :], dst=ckvall_b[c])
                fw.dma(POOL, KR, krmask_d, dst=kr_b)
                for r in range(3):
                    fw.dma(POOL, QT(r), eq_d, dst=qt_b[r])
                for r in range(2):
                    fw.op(DVE, lambda e, r=r: e.memset(VP(r), 0.0), writes=[vp_b[r]])
                    fw.op(DVE, lambda e, r=r: e.memset(VP(r)[:, :, 64:65], 1.0), writes=[vp_b[r]])
                    fw.op(DVE, lambda e, r=r: e.memset(VP(r)[:, :, 128:129], 1.0), writes=[vp_b[r]])

                if MLA_SUB < 0.2:
                    return
                w, wb, wk = ws.next(wdq.rearrange("(k p) n -> p k n", p=128), (KC, 384))
                for tt in range(2):
                    banks = [(4 * tt + c) % 8 for c in range(3)]
                    for c in range(3):
                        def mm(e, c=c, tt=tt, bank=banks[c], w=w):
                            r = None
                            for kk in range(KC):
                                r = e.matmul(ps(bank), w[:, kk, c * 128:(c + 1) * 128], h_t[:, kk, tsl(tt)],
                                             start=(kk == 0), stop=(kk == KC - 1))
                            return r
                        fw.op(PE, mm, reads=[wb] + [hb[kk][tt] for kk in range(KC)], writes=[ps_b[banks[c]]])
                    rms_stats([(ps(banks[c]), [ps_b[banks[c]]]) for c in range(3)], 3, 1.0 / 384,
                              bank=(4 * tt + 3) % 8)
                    for c in range(3):
                        s = c % 2
                        fw.op(DVE, lambda e, s=s, bank=banks[c]: e.tensor_tensor(
                            out=xn_t[s][:], in0=ps(bank), in1=rstd_t[:], op=ALU.mult),
                            reads=[ps_b[banks[c]], rstd_b], writes=[xn_b[s]])
                        fw.op(ACT, lambda e, s=s, c=c, tt=tt: e.activation(
                            out=cqT(c)[:, tsl(tt)], in_=xn_t[s][:], func=AF.Copy, scale=pv("qnw", c)),
                            reads=[xn_b[s], pvec_b], writes=[cq_b[c]])
                ws.done(wk)

                if MLA_SUB < 0.5:
                    return
                w, wb, wk = ws.next(wdkv_aug.rearrange("(k p) n -> p k n", p=128), (KC, 448))
                for tt in range(2):
                    banks = [(5 * tt + i) % 8 for i in range(4)]
                    cols = [(0, 128), (128, 128), (256, 96), (352, 96)]
                    for i in range(4):
                        c0, m = cols[i]

                        def mm(e, c0=c0, m=m, tt=tt, bank=banks[i], w=w):
                            r = None
                            for kk in range(KC):
                                r = e.matmul(ps(bank)[0:m, :], w[:, kk, c0:c0 + m], h_t[:, kk, tsl(tt)],
                                             start=(kk == 0), stop=(kk == KC - 1))
                            return r
                        fw.op(PE, mm, reads=[wb] + [hb[kk][tt] for kk in range(KC)], writes=[ps_b[banks[i]]])
                    if MLA_SUB < 0.51:
                        continue
                    rms_stats([(ps(banks[c]), [ps_b[banks[c]]]) for c in range(2)], 2, 1.0 / 256,
                              bank=(5 * tt + 4) % 8)
                    if MLA_SUB < 0.52:
                        continue
                    for c in range(2):
                        s = c % 2
                        fw.op(DVE, lambda e, s=s, bank=banks[c]: e.tensor_tensor(
                            out=xn_t[s][:], in0=ps(bank), in1=rstd_t[:], op=ALU.mult),
                            reads=[ps_b[banks[c]], rstd_b], writes=[xn_b[s]])
                        fw.op(ACT, lambda e, s=s, c=c, tt=tt: e.activation(
                            out=ckvall(c)[:, 512 + tt * 512:1024 + tt * 512], in_=xn_t[s][:], func=AF.Copy,
                            scale=pv("kvnw", c)),
                            reads=[xn_b[s], pvec_b], writes=[ckvall_b[c]])
                        fw.op(DVE, lambda e, s=s, c=c, tt=tt: e.tensor_scalar(
                            out=tA_t[c][:, tsl(tt)], in0=xn_t[s][:], scalar1=pv("kvnw", c), scalar2=None,
                            op0=ALU.mult),
                            reads=[xn_b[s], pvec_b], writes=[tA_b[c]])
                    if MLA_SUB < 0.55:
                        continue
                    bA, bB = banks[2], banks[3]
                    fw.op(ACT, lambda e, tt=tt, bA=bA: e.activation(
                        out=stg_t[0][64:96, tsl(tt)], in_=ps(bA)[64:96, :], func=AF.Copy),
                        reads=[ps_b[bA]], writes=[stg_b[0]])
                    if MLA_SUB < 0.56:
                        continue
                    fw.op(DVE, lambda e, tt=tt, bA=bA: e.tensor_tensor(
                        out=xn_t[0][64:96, :], in0=ps(bA)[64:96, :], in1=ropecs[64:96, 0, tsl(tt)], op=ALU.mult),
                        reads=[ps_b[bA], ropecs_b], writes=[xn_b[0]])
                    if MLA_SUB < 0.57:
                        continue
                    fw.op(DVE, lambda e, tt=tt, bB=bB: e.tensor_tensor(
                        out=xn_t[1][64:96, :], in0=ps(bB)[64:96, :], in1=ropecs[64:96, 1, tsl(tt)], op=ALU.mult),
                        reads=[ps_b[bB], ropecs_b], writes=[xn_b[1]])
                    if MLA_SUB < 0.58:
                        continue
                    fw.op(DVE, lambda e, tt=tt: e.tensor_tensor(
                        out=KR[64:96, 512 + tt * 512:1024 + tt * 512], in0=xn_t[0][64:96, :],
                        in1=xn_t[1][64:96, :], op=ALU.add),
                        reads=[xn_b[0], xn_b[1]], writes=[kr_b])
                ws.done(wk)

                for nb in range(4, 12):
                    ada_block(0, nb, 2 + nb % 2, 4 + nb % 2)
                ada_finish2(0)
                for i in range(8):
                    bank = i % 8

                    def tr(e, i=i, bank=bank):
                        r = None
                        for c in range(2):
                            r = e.transpose(ps(bank)[:, c * 128:(c + 1) * 128],
                                            tA_t[c][:, i * 128:(i + 1) * 128], ident[:])
                        r = e.transpose(ps(bank)[:, 256:384], stg_t[0][:, i * 128:(i + 1) * 128], ident[:])
                        return r
                    fw.op(PE, tr, reads=[tA_b[0], tA_b[1], stg_b[0], ident_b], writes=[ps_b[bank]])
                    fw.op(DVE, lambda e, i=i, bank=bank: e.tensor_copy(out=ckvst[:, i, :], in_=ps(bank)[:, 0:256]),
                          reads=[ps_b[bank]], writes=[ckvst_b])
                    fw.op(ACT, lambda e, i=i, bank=bank: e.activation(out=krst[:, i, :], in_=ps(bank)[:, 320:352],
                                                                     func=AF.Copy),
                          reads=[ps_b[bank]], writes=[krst_b])
                fw.dma(SP, ckv_o.rearrange("(i p) c -> p i c", p=128), ckvst[:], dst=None, src=[ckvst_b])
                fw.dma(SP, kr_o.rearrange("(i p) c -> p i c", p=128), krst[:], dst=None, src=[krst_b])

                for r in range(3):
                    fw.op(DVE, lambda e, r=r: e.tensor_copy(out=KT(r)[64:128, :], in_=KR[64:128, :]),
                          reads=[kr_b], writes=[kt_b[r]])

                fw.op(DVE, lambda e: e.memset(stg_t[1][:], 0.0), writes=[stg_b[1]])
                if MLA_SUB < 2:
                    return
                ukv, ukvb, ukvk = ws.next(wukv.rearrange("(k p) n -> p k n", p=128), (2, 2048))
                wqv = wuq_aug.rearrange("(k p) n -> p k n", p=128)
                st = {'qw': None, 'n_o': 0, 'n_s': 0, 'n_p': 0}

                def prep_V(p):
                    vr = p % 2
                    for ktg in range(3):
                        bank = 6 + ktg % 2

                        def mmv(e, ktg=ktg, bank=bank, p=p):
                            r = None
                            for j in range(4):
                                kt = ktg * 4 + j
                                for kc in range(2):
                                    rhs = ukv[:, kc, p * 256:(p + 1) * 256].rearrange("p (h x) -> p h x", h=2)[:, :, 64:128]
                                    r = e.matmul(ps(bank)[:, j * 128:(j + 1) * 128].rearrange("p (h x) -> p h x", h=2),
                                                 ckvall(kc)[:, kt * 128:(kt + 1) * 128], rhs,
                                                 start=(kc == 0), stop=(kc == 1))
                            return r
                        fw.op(PE, mmv, reads=[ukvb, ckvall_b[0], ckvall_b[1]], writes=[ps_b[bank]])
                        src = ps(bank).rearrange("p (j h x) -> p j h x", j=4, h=2, x=64)
                        fw.op(DVE, lambda e, src=src, ktg=ktg, vr=vr: e.tensor_copy(
                            out=VP(vr)[:, ktg * 4:ktg * 4 + 4, 0:64], in_=src[:, :, 0, :]),
                            reads=[ps_b[bank]], writes=[vp_b[vr]])
                        fw.op(ACT, lambda e, src=src, ktg=ktg, vr=vr: e.activation(
                            out=VP(vr)[:, ktg * 4:ktg * 4 + 4, 192:256], in_=src[:, :, 1, :], func=AF.Copy),
                            reads=[ps_b[bank]], writes=[vp_b[vr]])

                def prep_KQ(h):
                    r3 = h % 3
                    if h % 4 == 0:
                        if st['qw'] is not None:
                            ws.done(st['qw'][2])
                        st['qw'] = ws.next(wqv[:, :, (h // 4) * 768:(h // 4 + 1) * 768], (3, 768))
                    qw = st['qw']
                    for kt5 in range(3):
                        bank = 6 + kt5 % 2

                        def mmk(e, h=h, kt5=kt5, bank=bank):
                            r = None
                            for kc in range(2):
                                r = e.matmul(ps(bank)[0:64, :], ukv[:, kc, h * 128:h * 128 + 64],
                                             ckvall(kc)[:, kt5 * 512:(kt5 + 1) * 512],
                                             start=(kc == 0), stop=(kc == 1))
                            return r
                        fw.op(PE, mmk, reads=[ukvb, ckvall_b[0], ckvall_b[1]], writes=[ps_b[bank]])
                        evac(kt5, KT(r3)[0:64, kt5 * 512:(kt5 + 1) * 512], ps(bank)[0:64, :],
                             [ps_b[bank]], [kt_b[r3]])
                    qcol = (h % 4) * 192
                    for tt in range(2):
                        for which in range(2):
                            bank = 6 + which

                            def mmq(e, which=which, tt=tt, bank=bank, qcol=qcol, qwv=qw[0]):
                                r = None
                                for kc in range(3):
                                    r = e.matmul(ps(bank)[0:96, :],
                                                 qwv[:, kc, qcol + which * 96:qcol + which * 96 + 96],
                                                 cqT(kc)[:, tsl(tt)], start=(kc == 0), stop=(kc == 2))
                                return r
                            fw.op(PE, mmq, reads=[qw[1]] + cq_b, writes=[ps_b[bank]])
                        fw.op(ACT, lambda e, r3=r3, tt=tt: e.activation(
                            out=QT(r3)[0:64, tsl(tt)], in_=ps(6)[0:64, :], func=AF.Copy),
                            reads=[ps_b[6]], writes=[qt_b[r3]])
                        fw.op(DVE, lambda e, tt=tt: e.tensor_tensor(
                            out=xn_t[0][64:96, :], in0=ps(6)[64:96, :], in1=ropecs[64:96, 0, tsl(tt)],
                            op=ALU.mult), reads=[ps_b[6], ropecs_b], writes=[xn_b[0]])
                        fw.op(DVE, lambda e, tt=tt: e.tensor_tensor(
                            out=xn_t[1][64:96, :], in0=ps(7)[64:96, :], in1=ropecs[64:96, 1, tsl(tt)],
                            op=ALU.mult), reads=[ps_b[7], ropecs_b], writes=[xn_b[1]])
                        fw.op(DVE, lambda e, r3=r3, tt=tt: e.tensor_tensor(
                            out=QT(r3)[64:96, tsl(tt)], in0=xn_t[0][64:96, :], in1=xn_t[1][64:96, :],
                            op=ALU.add), reads=[xn_b[0], xn_b[1]], writes=[qt_b[r3]])

                def attend(h, tts, pending=None):
                    p, hh = h // 2, h % 2
                    vr = p % 2
                    r3 = h % 3
                    if hh == 0:
                        vcols, orows, srow, om = (0, 65), (0, 64), 64, 65
                    else:
                        vcols, orows, srow, om = (128, 256), (64, 128), 0, 128
                    for tt in (tts if MLA_SUB >= 3 else ()):
                        ob = 2 + st['n_o'] % 3
                        st['n_o'] += 1
                        pend = []
                        for kt in range(14):
                            if kt < 12:
                                sb_ = (0, 1, 5)[st['n_s'] % 3]
                                st['n_s'] += 1
                                fw.op(PE, lambda e, sb_=sb_, kt=kt, r3=r3, tt=tt: e.matmul(
                                    ps(sb_), KT(r3)[:, kt * 128:(kt + 1) * 128], QT(r3)[:, tsl(tt)],
                                    start=True, stop=True),
                                    reads=[kt_b[r3], qt_b[r3]], writes=[ps_b[sb_]])
                                pi = st['n_p'] % 4
                                st['n_p'] += 1
                                fw.op(ACT, lambda e, sb_=sb_, pi=pi: e.activation(
                                    out=PT(pi), in_=ps(sb_), func=AF.Exp, scale=SC),
                                    reads=[ps_b[sb_]], writes=[pt_b[pi]])
                            if kt >= 2:
                                pkt, ppi = pend.pop(0)
                                fw.op(PE, lambda e, ob=ob, om=om, vr=vr, pkt=pkt, ppi=ppi, vcols=vcols: e.matmul(
                                    ps(ob)[0:om, :], VP(vr)[:, pkt, vcols[0]:vcols[1]], PT(ppi),
                                    start=(pkt == 0), stop=(pkt == 11)),
                                    reads=[vp_b[vr], pt_b[ppi]], writes=[ps_b[ob]])
                            if kt < 12:
                                pend.append((kt, pi))
                            if kt == 8 and pending is not None:
                                pending()
                                pending = None
                        if MLA_SUB < 4:
                            continue
                        rs = stg_t[1][srow:srow + 1, 0:512] if hh == 0 else stg_t[1][srow:srow + 1, 512:1024]
                        fw.op(ACT, lambda e, rs=rs, ob=ob, srow=srow: e.activation(
                            out=rs, in_=ps(ob)[srow:srow + 1, :], func=AF.Ln), reads=[ps_b[ob]], writes=[stg_b[1]])
                        fw.op(ACT, lambda e, rs=rs: e.activation(out=rs, in_=rs, func=AF.Exp, scale=-1.0),
                              reads=[stg_b[1]], writes=[stg_b[1]])
                        return lambda ob=ob, tt=tt: norm_o(h, tt, ob)
                    return None

                def norm_o(h, tt, ob):
                    p, hh = h // 2, h % 2
                    if hh == 0:
                        orows, srow = (0, 64), 64
                    else:
                        orows, srow = (64, 128), 0
                    if True:
                        bb = 6 + st['n_o'] % 2
                        fw.op(PE, lambda e, bb=bb, hh=hh: e.matmul(
                            ps(bb), sel_t[:, hh, :], stg_t[1][:, hh * 512:(hh + 1) * 512],
                            start=True, stop=True),
                            reads=[stg_b[1], const_b], writes=[ps_b[bb]])
                        fw.op(ACT, lambda e, bb=bb: e.activation(out=rstd_t[:], in_=ps(bb), func=AF.Copy),
                              reads=[ps_b[bb]], writes=[rstd_b])
                        fw.op(DVE, lambda e, ob=ob, orows=orows, p=p, tt=tt: e.tensor_tensor(
                            out=h_t[orows[0]:orows[1], p, tsl(tt)], in0=ps(ob)[orows[0]:orows[1], :],
                            in1=rstd_t[orows[0]:orows[1], :], op=ALU.mult),
                            reads=[ps_b[ob], rstd_b], writes=[hb[p][tt]])

                prep_V(0)
                prep_KQ(0)
                pnd = None
                for h in range(16):
                    pnd = attend(h, (0,), pnd)
                    if h + 1 < 16:
                        if (h + 1) % 2 == 0:
                            prep_V((h + 1) // 2)
                        prep_KQ(h + 1)
                    pnd = attend(h, (1,), pnd)
                if pnd is not None:
                    pnd()
                qw = st['qw']
                ws.done(qw[2])
                ws.done(ukvk)
                out_linear(l, wo, lambda kk, tt: h_t[:, kk, tsl(tt)],
                           lambda tt: [hb[kk][tt] for kk in range(KC)], lambda oc: mod_t[l][:, 16 + oc:17 + oc])

            def mixer(l):
                norm_mod(l, 1)
                if l == 0 and stage >= 4:
                    mixer_mla(l)
                elif l == 1 and stage >= 3:
                    mixer_pool(l)
                elif l == 2 and stage >= 3:
                    mixer_fnet(l)
                elif l == 3 and stage >= 3:
                    mixer_sconv(l)
                fw.barrier_bufs(ab)

            if stage >= 2:
                for nb in range(4 if stage >= 4 else 12):
                    ada_block(0, nb, 4 + nb % 2, 6 + nb % 2)
                ada_finish1(0)
                if stage < 4:
                    ada_finish2(0)
                for l in range(DEPTH):
                    mixer(l)
                    norm_mod(l, 2)
                    if l + 1 < DEPTH:
                        def hook(jg, l=l):
                            ada_block(l + 1, 2 * jg, 0, 1)
                            ada_block(l + 1, 2 * jg + 1, 2, 3)
                            if jg == 5:
                                ada_finish(l + 1)
                        ffn(l, hook)
                    else:
                        ffn(l)

            fo, _ = PV["fnw"]
            for tt in range(2):
                rms_stats([(x_t[:, c, tsl(tt)], [xb[c][tt]]) for c in range(KC)], KC, 1.0 / D,
                          bank=tt)
                for c in range(KC):
                    fw.op(DVE, lambda e, c=c, tt=tt: e.scalar_tensor_tensor(
                        out=x_t[:, c, tsl(tt)], in0=x_t[:, c, tsl(tt)],
                        scalar=pvec[:, fo + c:fo + c + 1], in1=rstd_t[:],
                        op0=ALU.mult, op1=ALU.mult),
                        reads=[xb[c][tt], rstd_b, pvec_b], writes=[xb[c][tt]])
            for i in range(8):
                s = i % 2
                tt = i // 4
                for half in range(2):
                    bank = 2 + (i * 2 + half) % 6

                    def mm(e, half=half, bank=bank, i=i):
                        r = None
                        for cc in range(4):
                            c = half * 4 + cc
                            r = e.transpose(ps(bank)[:, cc * 128:(cc + 1) * 128],
                                            x_t[:, c, i * 128:(i + 1) * 128], ident[:])
                        return r
                    fw.op(PE, mm, reads=[xb[c][tt] for c in range(half * 4, half * 4 + 4)] + [ident_b],
                          writes=[ps_b[bank]])
                    if half == 0:
                        fw.op(DVE, lambda e, s=s, bank=bank: e.tensor_copy(
                            out=stg_t[s][:, 0:512], in_=ps(bank)),
                            reads=[ps_b[bank]], writes=[stg_b[s]])
                    else:
                        fw.op(ACT, lambda e, s=s, bank=bank: e.activation(
                            out=stg_t[s][:, 512:1024], in_=ps(bank), func=AF.Copy),
                            reads=[ps_b[bank]], writes=[stg_b[s]])
                fw.dma(SP, yout[i * 128:(i + 1) * 128, :], stg_t[s][:], dst=None, src=[stg_b[s]])

        fw.dry = True
        emit()
        fw.dry = False
        ws.reset()
        emit()
        assert ws.consumed == len(ws.specs)

        final_waits = {}
        for sem, val in fw.out_recs:
            final_waits[sem] = max(final_waits.get(sem, 0), val)

        with nc.Block() as block:
            @block.sync
            def _(e):
                fw.replay(SP, e)
                for sem, val in final_waits.items():
                    e.wait_ge(sem, val)

            @block.tensor
            def _(e):
                fw.replay(PE, e)

            @block.scalar
            def _(e):
                fw.replay(ACT, e)

            @block.vector
            def _(e):
                fw.replay(DVE, e)

            @block.gpsimd
            def _(e):
                fw.replay(POOL, e)
    return nc


def _cols(v):
    v = np.asarray(v, np.float32)
    return np.ascontiguousarray(v.reshape(-1, 128).T)


def _make_pvec(inp, cond_vec, flagneg):
    pv = np.zeros((128, NPV), np.float32)

    def put(name, arr):
        o, n = PV[name]
        assert arr.shape == (128, n), (name, arr.shape, n)
        pv[:, o:o + n] = arr

    for l in range(DEPTH):
        put(f"n1w{l}", _cols(inp["norm1_w"][l]))
        put(f"n2w{l}", _cols(inp["norm2_w"][l]))
        put(f"fw0_{l}", _cols(inp["ffn_conv_w"][l, 0]))
        put(f"fw1_{l}", _cols(inp["ffn_conv_w"][l, 1]))
        put(f"fw2_{l}", _cols(inp["ffn_conv_w"][l, 2]))
        put(f"fb_{l}", _cols(inp["ffn_conv_b"][l]))
        put(f"adab{l}", _cols(inp["ada_b"][l]))
    put("fnw", _cols(inp["final_norm_w"]))
    put("cond", _cols(cond_vec))
    pv[:, PV["flagneg"][0]] = flagneg
    put("qnw", _cols(inp["mla_q_norm"][0]))
    put("kvnw", _cols(inp["mla_kv_norm"][0]))
    put("pscale", _cols(inp["pool_scale"][0]))
    put("sw0", _cols(inp["sconv_conv"][0, 0]))
    put("sw1", _cols(inp["sconv_conv"][0, 1]))
    put("sw2", _cols(inp["sconv_conv"][0, 2]))
    return pv


def _pool_tables(L):
    A = np.zeros((4, T, T), np.float64)
    wins = (2, 4, 8, 16)
    for g, w in enumerate(wins):
        for t in range(T):
            s0 = (t // L) * L
            tl = t - s0
            lo = max(tl - w // 2, 0)
            hi = min(tl + w - w // 2, L)
            A[g, t, s0 + lo:s0 + hi] = 1.0 / (hi - lo)
            A[g, t, t] -= 1.0
    out = np.zeros((128, 4, 8, 3, 128), np.float32)
    for g in range(4):
        for i in range(8):
            for d in range(3):
                ip = i + d - 1
                if 0 <= ip < 8:
                    out[:, g, i, d, :] = A[g, i * 128:(i + 1) * 128, ip * 128:(ip + 1) * 128].T
    return out.reshape(128, 4, 8 * 3 * 128).astype(ml_dtypes.bfloat16)


def _dft_tables(L):
    t = np.arange(T)
    same = (t[:, None] // L) == (t[None, :] // L)
    ang = 2.0 * np.pi * ((t[:, None] % L) * (t[None, :] % L) % L) / L
    nrm = 1.0 / np.sqrt(L * 256.0)
    C = np.where(same, np.cos(ang), 0.0) * nrm
    S = np.where(same, -np.sin(ang), 0.0) * nrm

    def lay(M):
        return np.ascontiguousarray(M.reshape(8, 128, T).transpose(1, 0, 2)).astype(ml_dtypes.bfloat16)
    c = np.arange(256)
    a2 = 2.0 * np.pi * ((c[:, None] * c[None, :]) % 256) / 256.0
    CS = np.concatenate([np.cos(a2), np.sin(a2)], axis=1)
    CS = np.ascontiguousarray(CS.reshape(2, 128, 512).transpose(1, 0, 2)).astype(ml_dtypes.bfloat16)
    return lay(C), lay(S), CS


_PERM = np.concatenate([np.arange(0, 32, 2), np.arange(1, 32, 2)])
_PERM_SW = np.concatenate([np.arange(1, 32, 2), np.arange(0, 32, 2)])


def _mla_weights(inp):
    wdkv = np.asarray(inp["mla_wdkv"][0], np.float32)
    aug = np.zeros((D, 448), np.float32)
    aug[:, 0:256] = wdkv[:, 0:256]
    aug[:, 320:352] = wdkv[:, 256 + _PERM]
    aug[:, 416:448] = wdkv[:, 256 + _PERM_SW]
    wuq = np.asarray(inp["mla_wuq"][0], np.float32).reshape(384, 16, 96)
    qa = np.zeros((384, 16, 192), np.float32)
    qa[:, :, 0:64] = wuq[:, :, 0:64]
    qa[:, :, 64:96] = wuq[:, :, 64 + _PERM]
    qa[:, :, 160:192] = wuq[:, :, 64 + _PERM_SW]
    return aug, np.ascontiguousarray(qa.reshape(384, 16 * 192))


def _rope_tables(kind):
    cs = np.zeros((128, 2, T), np.float32)
    if kind == "p":
        cs[64:96, 0, :] = 1.0
        return cs
    t = np.arange(T)
    r = (t // 64).astype(np.float32)
    col = (t % 64).astype(np.float32)
    inv = (np.float32(10000.0) ** (-np.arange(8, dtype=np.float32) / np.float32(8))).astype(np.float32)
    ang = np.concatenate([r[:, None] * inv, col[:, None] * inv], axis=-1).astype(np.float32)
    c, s = np.cos(ang).T, np.sin(ang).T
    cs[64:80, 0, :] = c
    cs[80:96, 0, :] = c
    cs[64:80, 1, :] = -s
    cs[80:96, 1, :] = s
    return cs


def _mask_tables(kind):
    NEG = -30000.0
    eq = np.zeros((128, T), np.float32)
    ek = np.zeros((128, 1536), np.float32)
    if kind == "p":
        seq = np.arange(T) // 256
        for r in range(4):
            eq[96 + r, :] = (seq == r)
            ek[96 + r, 0:512] = NEG
            ek[96 + r, 512:] = np.where(seq == r, 0.0, NEG)
    return eq, ek


def _core_roles():
    return [("s", 0), ("s", 1), ("p", 0), ("p", 1), ("p", 2), ("p", 3), ("p", 3), ("p", 3)]


_NC_CACHE = {}


def make_in_maps(inp):
    roles = _core_roles()
    ident = np.eye(128, dtype=np.float32)
    shared = {
        "ident": ident,
        "ada_w": np.ascontiguousarray(inp["ada_w"], dtype=np.float32),
        "ffn_up": np.ascontiguousarray(inp["ffn_up"], dtype=np.float32),
        "ffn_down": np.ascontiguousarray(inp["ffn_down"], dtype=np.float32),
        "pool_w": np.ascontiguousarray(inp["pool_w"][0], dtype=np.float32),
        "fnet_w": np.ascontiguousarray(inp["fnet_w"][0], dtype=np.float32),
        "sconv_win": np.ascontiguousarray(inp["sconv_win"][0], dtype=np.float32),
        "sconv_wout": np.ascontiguousarray(inp["sconv_wout"][0], dtype=np.float32),
        "identb": ident.astype(ml_dtypes.bfloat16),
        "wdq": np.ascontiguousarray(inp["mla_wdq"][0], dtype=np.float32),
        "wukv": np.ascontiguousarray(inp["mla_wukv"][0], dtype=np.float32),
        "wo": np.ascontiguousarray(inp["mla_wo"][0], dtype=np.float32),
    }
    shared["wdkv_aug"], shared["wuq_aug"] = _mla_weights(inp)
    tabs = {}
    for kind, L in (("s", 1024), ("p", 256)):
        C, S, CS = _dft_tables(L)
        eq, ek = _mask_tables(kind)
        tabs[kind] = {"poolA": _pool_tables(L), "dftC": C, "dftS": S, "dftCS": CS,
                      "ropeCS": _rope_tables(kind), "eq": eq, "_ek": ek}
    in_maps = []
    for kind, idx in roles:
        if kind == "s":
            xc = inp["x_sample"][idx]
            cond = inp["c"][idx]
            flag = 0.0
        else:
            xc = inp["x_prompt"][4 * idx:4 * idx + 4].reshape(T, D)
            cond = inp["c_ctx"]
            flag = -1.0
        m = dict(shared)
        m.update({a: b for a, b in tabs[kind].items() if not a.startswith("_")})
        krm = tabs[kind]["_ek"].copy()
        cT = np.zeros((128, 2, 512), np.float32)
        if kind == "s":
            cT[:] = np.asarray(inp["cache_ckv"][idx, 0], np.float32).T.reshape(2, 128, 512).transpose(1, 0, 2)
            krm[64:96, 0:512] = np.asarray(inp["cache_krope"][idx, 0], np.float32)[:, _PERM].T
        m["cacheT"] = cT
        m["krmask"] = krm
        m["xin"] = np.ascontiguousarray(xc, dtype=np.float32)
        m["pvec"] = _make_pvec(inp, cond, flag)
        in_maps.append(m)
    return in_maps


def kernel(**inputs):
    inp = {k: np.asarray(v) for k, v in inputs.items()}
    in_maps = make_in_maps(inp)
    if "nc" not in _NC_CACHE:
        _NC_CACHE["nc"] = build_program(STAGE)
    nc = _NC_CACHE["nc"]
    res = run_bass_kernel_spmd(nc, in_maps, core_ids=list(range(NCORES)))
    outs = res.results
    y_sample = np.stack([outs[0]["yout"], outs[1]["yout"]], axis=0).astype(np.float32)
    y_prompt = np.concatenate([outs[2 + g]["yout"].reshape(4, 256, D) for g in range(4)], axis=0)
    y_prompt = y_prompt.astype(np.float32)
    new_ckv = np.concatenate([outs[2 + g]["ckv_o"].reshape(4, 1, 256, 256) for g in range(4)], axis=0)
    krp = np.concatenate([outs[2 + g]["kr_o"].reshape(4, 1, 256, 32) for g in range(4)], axis=0)
    new_kr = np.empty_like(krp)
    new_kr[..., _PERM] = krp
    new_ckv = new_ckv.astype(np.float32)
    new_kr = new_kr.astype(np.float32)
    return (y_prompt, y_sample, new_ckv, new_kr)
```

```python
import numpy as np
from contextlib import ExitStack
import ml_dtypes

import concourse.bass as bass
import concourse.mybir as mybir
from concourse.bass_utils import run_bass_kernel_spmd

F32 = mybir.dt.float32
BF16 = mybir.dt.bfloat16
AF = mybir.ActivationFunctionType
ALU = mybir.AluOpType

D = 1024
T = 1024
KC = 8
DFF = 2816
FC = 22
DEPTH = 4
EPS = 1e-6
NCORES = 8


class Buf:
    __slots__ = ("name", "w", "r", "sem", "cum", "excl")

    def __init__(self, name):
        self.name = name
        self.excl = False
        self.w = None
        self.r = {}
        self.sem = None
        self.cum = 0


class Q:
    def __init__(self, fw, name, own_wait=True):
        self.fw = fw
        self.name = name
        self.thunks = []
        self.sem = fw.new_sem("q_" + name)
        self.cnt = 0
        self.known = {}
        self.own_wait = own_wait


class FW:
    def __init__(self, nc, es):
        self.nc = nc
        self.es = es
        self.nsem = 0
        self.pe = Q(self, "pe", own_wait=False)
        self.act = Q(self, "act")
        self.dve = Q(self, "dve")
        self.pool = Q(self, "pool")
        self.sp = Q(self, "sp")
        self.out_recs = []
        self.dry = False

    def new_sem(self, name):
        self.nsem += 1
        return self.es.enter_context(self.nc.semaphore(f"s{self.nsem}_{name}"))

    def buf(self, name, dma=False):
        b = Buf(name)
        if dma:
            b.sem = self.new_sem("d_" + name)
        return b

    def _collect(self, q, reads, writes):
        waits = {}

        def need(rec):
            if rec is None:
                return
            sem, val = rec
            if sem is q.sem and not q.own_wait:
                return
            if q.known.get(sem, 0) >= val:
                return
            if waits.get(sem, 0) < val:
                waits[sem] = val

        for b in reads:
            need(b.w)
            if b.excl:
                for sem, val in b.r.items():
                    if sem is not q.sem:
                        need((sem, val))
        for b in writes:
            need(b.w)
            for sem, val in b.r.items():
                need((sem, val))
        for sem, val in waits.items():
            q.known[sem] = val
        return list(waits.items())

    @staticmethod
    def _commit(rec, reads, writes):
        sem, val = rec
        for b in reads:
            if b.r.get(sem, 0) < val:
                b.r[sem] = val
        for b in writes:
            b.w = rec
            b.r = {}

    def op(self, q, fn, reads=(), writes=()):
        if self.dry:
            return None
        wl = self._collect(q, reads, writes)
        q.cnt += 1
        rec = (q.sem, q.cnt)
        q.thunks.append((wl, fn, rec, 1))
        self._commit(rec, reads, writes)
        return rec

    def dma(self, q, out_ap, in_ap, dst=None, src=(), reads=(), kw=None):
        if self.dry:
            return None
        kw = kw or {}
        writes = [dst] if dst is not None else []
        rds = list(src) + list(reads)
        wl = self._collect(q, rds, writes)
        owner = dst if dst is not None else src[0]
        owner.cum += 16
        rec = (owner.sem, owner.cum)

        def fn(e, out_ap=out_ap, in_ap=in_ap, kw=kw):
            return e.dma_start(out=out_ap, in_=in_ap, **kw)

        q.thunks.append((wl, fn, rec, 16))
        self._commit(rec, rds, writes)
        if dst is None:
            self.out_recs.append(rec)
        return rec

    def barrier_bufs(self, bufs):
        allq = [self.pe, self.act, self.dve, self.pool]
        for b in bufs:
            for q in allq:
                if q.cnt > 0:
                    if b.r.get(q.sem, 0) < q.cnt:
                        b.r[q.sem] = q.cnt

    def replay(self, q, eng):
        for wl, fn, rec, inc in q.thunks:
            for sem, val in wl:
                eng.wait_ge(sem, val)
            ins = fn(eng)
            if isinstance(ins, (list, tuple)):
                ins = ins[-1]
            ins.then_inc(rec[0], inc)


class WStream:
    def __init__(self, fw, q, slots, bufs, slot_elems):
        self.fw = fw
        self.q = q
        self.slots = slots
        self.bufs = bufs
        self.n = len(slots)
        self.slot_elems = slot_elems
        self.specs = []
        self.reset()

    def reset(self):
        self.issued = 0
        self.consumed = 0
        self.done_flags = []

    def _view(self, k, shape):
        t = self.slots[k % self.n]
        n = int(np.prod(shape))
        assert n <= self.slot_elems, shape
        if len(shape) == 1:
            return t[:, 0:n]
        if len(shape) == 2:
            return t[:, 0:n].rearrange("p (a b) -> p a b", a=shape[0], b=shape[1])
        return t[:, 0:n].rearrange("p (a b c) -> p a b c", a=shape[0], b=shape[1], c=shape[2])

    def _pump(self):
        while self.issued < len(self.specs):
            k = self.issued
            if k >= self.n and not (k - self.n < len(self.done_flags) and self.done_flags[k - self.n]):
                break
            dram_ap, shape = self.specs[k]
            self.fw.dma(self.q, self._view(k, shape), dram_ap, dst=self.bufs[k % self.n])
            self.issued += 1

    def next(self, dram_ap, shape):
        shape = tuple(shape)
        if self.fw.dry:
            self.specs.append((dram_ap, shape))
            return self._view(0, shape), self.bufs[0], None
        k = self.consumed
        self.consumed += 1
        assert self.specs[k][1] == shape, (k, self.specs[k][1], shape)
        self.done_flags.append(False)
        self._pump()
        assert self.issued > k, (k, self.issued)
        return self._view(k, shape), self.bufs[k % self.n], k

    def done(self, k):
        if self.fw.dry:
            return
        self.done_flags[k] = True
        self._pump()


def _pvec_map():
    m = {}
    o = 0

    def add(name, n):
        nonlocal o
        m[name] = (o, n)
        o += n

    for l in range(DEPTH):
        add(f"n1w{l}", KC)
        add(f"n2w{l}", KC)
        add(f"fw0_{l}", FC)
        add(f"fw1_{l}", FC)
        add(f"fw2_{l}", FC)
        add(f"fb_{l}", FC)
        add(f"adab{l}", 48)
    add("fnw", KC)
    add("cond", KC)
    add("flagneg", 1)
    add("qnw", 3)
    add("kvnw", 2)
    add("pscale", KC)
    add("sw0", KC)
    add("sw1", KC)
    add("sw2", KC)
    m["_n"] = o
    return m


PV = _pvec_map()
NPV = PV["_n"]

STAGE = 4
MLA_SUB = 4.0
NSLOT = 6
SLOT_ELEMS = 4096


def build_program(stage=STAGE):
    nc = bass.Bass("TRN2", target_bir_lowering=False)

    def din(name, shape, dt=F32):
        return nc.dram_tensor(name, list(shape), dt, kind="ExternalInput").ap()

    def dout(name, shape, dt=F32):
        return nc.dram_tensor(name, list(shape), dt, kind="ExternalOutput").ap()

    xin = din("xin", [T, D])
    pvec_d = din("pvec", [128, NPV])
    ident_d = din("ident", [128, 128])
    ada_w = din("ada_w", [DEPTH, D, 6 * D])
    ffn_up = din("ffn_up", [DEPTH, D, 2 * DFF])
    ffn_down = din("ffn_down", [DEPTH, DFF, D])
    pool_w = din("pool_w", [4, 256, 256])
    poolA = din("poolA", [128, 4, 8 * 3 * 128], BF16)
    fnet_w = din("fnet_w", [D, D])
    dftC = din("dftC", [128, 8, T], BF16)
    dftS = din("dftS", [128, 8, T], BF16)
    dftCS = din("dftCS", [128, 2, 512], BF16)
    identb_d = din("identb", [128, 128], BF16)
    sconv_win = din("sconv_win", [D, 3 * D])
    sconv_wout = din("sconv_wout", [D, D])
    wdq = din("wdq", [D, 384])
    wdkv_aug = din("wdkv_aug", [D, 448])
    wuq_aug = din("wuq_aug", [384, 16 * 192])
    wukv = din("wukv", [256, 2048])
    wo = din("wo", [D, D])
    ropeCS_d = din("ropeCS", [128, 2, T])
    cacheT_d = din("cacheT", [128, 2, 512])
    krmask_d = din("krmask", [128, 1536])
    eq_d = din("eq", [128, T])
    yout = dout("yout", [T, D])
    ckv_o = dout("ckv_o", [T, 256])
    kr_o = dout("kr_o", [T, 32])

    es = ExitStack()
    with es:
        fw = FW(nc, es)
        PE, ACT, DVE, POOL, SP = fw.pe, fw.act, fw.dve, fw.pool, fw.sp

        def sb(name, shape, dt):
            return es.enter_context(nc.sbuf_tensor(name, list(shape), dt))

        x_t = sb("x", [128, KC, T], F32)
        xb = [[fw.buf(f"x{c}_{tt}") for tt in range(2)] for c in range(KC)]
        h_t = sb("h", [128, KC, T], BF16)
        hb = [[fw.buf(f"h{c}_{tt}") for tt in range(2)] for c in range(KC)]
        a_t = sb("a", [128, 25, T], BF16)
        ab = [fw.buf(f"a{j}") for j in range(25)]
        pvec = sb("pvec_sb", [128, NPV], F32)
        pvec_b = fw.buf("pvec", dma=True)
        ident = sb("ident_sb", [128, 128], F32)
        ident_b = fw.buf("ident", dma=True)
        ones_bf = sb("ones_bf", [128, 128], BF16)
        one_f = sb("one_f", [128, 1], F32)
        eps_t = sb("eps", [128, 1], F32)
        const_b = fw.buf("consts")
        stg_t = [sb(f"stg{i}", [128, D], F32) for i in range(2)]
        stg_b = [fw.buf(f"stg{i}", dma=True) for i in range(2)]
        tA_t = [sb(f"tA{i}", [128, T], F32) for i in range(2)]
        tA_b = [fw.buf(f"tA{i}") for i in range(2)]
        sq_t = [sb(f"sq{i}", [128, 512], BF16) for i in range(4)]
        sq_b = [fw.buf(f"sq{i}") for i in range(4)]
        rstd_t = sb("rstd", [128, 512], F32)
        rstd_b = fw.buf("rstd")
        rstd1_t = sb("rstd1", [128, 512], F32)
        rstd1_b = fw.buf("rstd1")
        xn_t = [sb(f"xn{i}", [128, 512], F32) for i in range(4)]
        xn_b = [fw.buf(f"xn{i}") for i in range(4)]
        scond = sb("scond", [128, KC], BF16)
        scond_b = fw.buf("scond")
        row_t = [sb(f"row{i}", [1, 512], F32) for i in range(2)]
        row_b = [fw.buf(f"row{i}") for i in range(2)]
        mod_t = [sb(f"mod{l}", [128, 48], F32) for l in range(DEPTH)]
        mod_b = [fw.buf(f"mod{l}") for l in range(DEPTH)]
        col_t = [sb(f"cols{l}", [128, 16 + 2 * FC], F32) for l in range(DEPTH)]
        col_b = [fw.buf(f"cols{l}") for l in range(DEPTH)]
        identb = sb("identb_sb", [128, 128], BF16)
        identb_b = fw.buf("identb", dma=True)
        cs_t = sb("dftcs_sb", [128, 2, 512], BF16)
        cs_b = fw.buf("dftcs", dma=True)
        ropecs = sb("ropecs", [128, 2, T], F32)
        ropecs_b = fw.buf("ropecs", dma=True)
        sel_t = sb("sel", [128, 2, 128], F32)
        ckvst = sb("ckvst", [128, 8, 256], F32)
        ckvst_b = fw.buf("ckvst", dma=True)
        krst = sb("krst", [128, 8, 32], F32)
        krst_b = fw.buf("krst", dma=True)
        ckvall_b = [fw.buf(f"ckvall{c}", dma=True) for c in range(2)]
        kr_b = fw.buf("KR", dma=True)
        qt_b = [fw.buf(f"QT{i}", dma=True) for i in range(3)]
        kt_b = [fw.buf(f"KT{i}") for i in range(3)]
        vp_b = [fw.buf(f"VP{i}") for i in range(2)]
        pt_b = [fw.buf(f"PT{i}") for i in range(4)]
        cq_b = [fw.buf(f"cq{i}") for i in range(3)]
        mcol_t = sb("mcols", [128, 32], F32)
        mcol_b = fw.buf("mcols")
        slots = [sb(f"wslot{i}", [128, SLOT_ELEMS], BF16) for i in range(NSLOT)]
        slot_b = [fw.buf(f"wslot{i}", dma=True) for i in range(NSLOT)]
        ws = WStream(fw, POOL, slots, slot_b, SLOT_ELEMS)

        pd_t = [es.enter_context(nc.psum_tensor(f"pd{i}", [128, 1024], F32)) for i in range(4)]
        ps_b = [fw.buf(f"ps{i}") for i in range(8)]
        for b in ps_b:
            b.excl = True

        pdb_t = [t.bitcast(BF16) for t in pd_t]

        def ps(bank):
            return pd_t[bank // 2][:, (bank % 2) * 512:(bank % 2) * 512 + 512]

        def psb(bank):
            return pdb_t[bank // 2][:, (bank % 2) * 1024:(bank % 2) * 1024 + 1024]

        def pv(name, j=0, n=1):
            o, _ = PV[name]
            return pvec[:, o + j:o + j + n]

        def tsl(tt):
            return slice(tt * 512, (tt + 1) * 512)

        def emit():
            fw.dma(SP, pvec[:], pvec_d, dst=pvec_b)
            fw.dma(SP, ident[:], ident_d, dst=ident_b)
            fw.dma(SP, identb[:], identb_d, dst=identb_b)
            fw.dma(SP, cs_t[:], dftCS, dst=cs_b)
            fw.op(DVE, lambda e: e.memset(ones_bf[:], 1.0), writes=[const_b])
            fw.op(DVE, lambda e: e.memset(eps_t[:], EPS), writes=[const_b])
            fw.op(DVE, lambda e: e.memset(one_f[:], 1.0), writes=[const_b])
            fw.op(DVE, lambda e: e.memset(sel_t[:], 0.0), writes=[const_b])
            fw.op(DVE, lambda e: e.memset(sel_t[64:65, 0, :], 1.0), writes=[const_b])
            fw.op(DVE, lambda e: e.memset(sel_t[0:1, 1, :], 1.0), writes=[const_b])

            for i in range(8):
                s = i % 2
                tt = i // 4
                fw.dma(SP, stg_t[s][:], xin[i * 128:(i + 1) * 128, :], dst=stg_b[s])
                for half in range(2):
                    bank = (i * 2 + half) % 8

                    def mm(e, s=s, half=half, bank=bank):
                        r = None
                        for cc in range(4):
                            c = half * 4 + cc
                            r = e.transpose(ps(bank)[:, cc * 128:(cc + 1) * 128],
                                            stg_t[s][:, c * 128:(c + 1) * 128], ident[:])
                        return r
                    fw.op(PE, mm, reads=[stg_b[s], ident_b], writes=[ps_b[bank]])
                    wr = [xb[c][tt] for c in range(half * 4, half * 4 + 4)]
                    if half == 0:
                        fw.op(DVE, lambda e, half=half, bank=bank, i=i: e.tensor_copy(
                            out=x_t[:, half * 4:half * 4 + 4, i * 128:(i + 1) * 128],
                            in_=ps(bank).rearrange("p (c t) -> p c t", c=4)),
                            reads=[ps_b[bank]], writes=wr)
                    else:
                        fw.op(ACT, lambda e, half=half, bank=bank, i=i: e.activation(
                            out=x_t[:, half * 4:half * 4 + 4, i * 128:(i + 1) * 128],
                            in_=ps(bank).rearrange("p (c t) -> p c t", c=4), func=AF.Copy),
                            reads=[ps_b[bank]], writes=wr)

            fw.op(ACT, lambda e: e.activation(out=scond[:], in_=pv("cond", 0, KC), func=AF.Silu),
                  reads=[pvec_b], writes=[scond_b])

            sqn = [0]

            def rms_stats(srcs, nch, inv_n, bank, rt=None, rb=None):
                rt = rstd_t if rt is None else rt
                rb = rstd_b if rb is None else rb
                for c, (ap, bufs) in enumerate(srcs):
                    s = sqn[0] % 4
                    sqn[0] += 1
                    fw.op(ACT, lambda e, ap=ap, s=s: e.activation(out=sq_t[s][:], in_=ap,
                                                                   func=AF.Square),
                          reads=bufs, writes=[sq_b[s]])
                    fw.op(PE, lambda e, c=c, s=s: e.matmul(ps(bank), ones_bf[:], sq_t[s][:],
                                                           start=(c == 0), stop=(c == nch - 1)),
                          reads=[const_b, sq_b[s]], writes=[ps_b[bank]])
                rms_tail(inv_n, bank, rt, rb)

            def rms_tail(inv_n, bank, rt, rb):
                fw.op(ACT, lambda e: e.activation(out=rt[:], in_=ps(bank), func=AF.Ln,
                                                  bias=eps_t[:, 0:1], scale=inv_n),
                      reads=[ps_b[bank], const_b], writes=[rb])
                fw.op(ACT, lambda e: e.activation(out=rt[:], in_=rt[:], func=AF.Exp, scale=-0.5),
                      reads=[rb], writes=[rb])

            acc = {"pend": [], "cnt": [0, 0]}

            def acc_begin():
                acc["pend"] = []
                acc["cnt"] = [0, 0]

            def acc_x(oc, tt):
                s = sqn[0] % 4
                sqn[0] += 1
                fw.op(ACT, lambda e, oc=oc, tt=tt, s=s: e.activation(out=sq_t[s][:], in_=x_t[:, oc, tsl(tt)],
                                                                       func=AF.Square),
                      reads=[xb[oc][tt]], writes=[sq_b[s]])
                acc["pend"].append((tt, s))

            def acc_pe(lag):
                while len(acc["pend"]) > lag:
                    tt, s = acc["pend"].pop(0)
                    c = acc["cnt"][tt]
                    acc["cnt"][tt] += 1
                    fw.op(PE, lambda e, c=c, s=s, tt=tt: e.matmul(ps(6 + tt), ones_bf[:], sq_t[s][:],
                                                                 start=(c == 0), stop=(c == KC - 1)),
                          reads=[const_b, sq_b[s]], writes=[ps_b[6 + tt]])

            def norm_mod(l, which, pre=False):
                ao = 0 if which == 1 else 8
                bo = 0 if which == 1 else 24
                rts = [(rstd_t, rstd_b), (rstd1_t, rstd1_b)]
                for tt in range(2):
                    if pre:
                        rms_tail(1.0 / D, 6 + tt, rts[tt][0], rts[tt][1])
                    else:
                        rms_stats([(x_t[:, c, tsl(tt)], [xb[c][tt]]) for c in range(KC)], KC, 1.0 / D,
                                  bank=tt, rt=rts[tt][0], rb=rts[tt][1])
                n = 0
                for tt in range(2):
                    rt, rb = rts[tt]
                    for c in range(KC):
                        s = n % 4
                        n += 1
                        fw.op(DVE, lambda e, c=c, s=s, tt=tt, rt=rt: e.tensor_tensor(
                            out=xn_t[s][:], in0=x_t[:, c, tsl(tt)], in1=rt[:], op=ALU.mult),
                            reads=[xb[c][tt], rb], writes=[xn_b[s]])
                        if c % 2 == 0:
                            fw.op(ACT, lambda e, c=c, s=s, tt=tt: e.activation(
                                out=h_t[:, c, tsl(tt)], in_=xn_t[s][:], func=AF.Identity,
                                bias=mod_t[l][:, bo + c:bo + c + 1],
                                scale=col_t[l][:, ao + c:ao + c + 1]),
                                reads=[xn_b[s], mod_b[l], col_b[l]], writes=[hb[c][tt]])
                        else:
                            fw.op(DVE, lambda e, c=c, s=s, tt=tt: e.tensor_scalar(
                                out=h_t[:, c, tsl(tt)], in0=xn_t[s][:],
                                scalar1=col_t[l][:, ao + c:ao + c + 1],
                                scalar2=mod_t[l][:, bo + c:bo + c + 1], op0=ALU.mult, op1=ALU.add),
                                reads=[xn_b[s], mod_b[l], col_b[l]], writes=[hb[c][tt]])

            def ada_block(l, nb, b0=0, b1=1):
                wv = ada_w[l].rearrange("(k p) n -> p k n", p=128)
                w, wb, wk = ws.next(wv[:, :, nb * 512:(nb + 1) * 512], (KC, 512))

                def mm(e, w=w):
                    r = None
                    for k in range(KC):
                        r = e.matmul(ps(b0)[0:1, :], scond[:, k:k + 1], w[:, k, :],
                                     start=(k == 0), stop=(k == KC - 1))
                    return r
                fw.op(PE, mm, reads=[scond_b, wb], writes=[ps_b[b0]])
                ws.done(wk)
                s = nb % 2
                fw.op(ACT, lambda e, s=s: e.activation(out=row_t[s][:], in_=ps(b0)[0:1, :], func=AF.Copy),
                      reads=[ps_b[b0]], writes=[row_b[s]])

                def mt(e, s=s):
                    r = None
                    for j in range(4):
                        r = e.matmul(ps(b1)[:, j:j + 1], row_t[s][0:1, j * 128:(j + 1) * 128],
                                     one_f[0:1, 0:1], start=True, stop=True)
                    return r
                fw.op(PE, mt, reads=[row_b[s], const_b], writes=[ps_b[b1]])
                fw.op(DVE, lambda e: e.tensor_tensor(out=mod_t[l][:, nb * 4:nb * 4 + 4], in0=ps(b1)[:, 0:4],
                                                     in1=pv(f"adab{l}", nb * 4, 4), op=ALU.add),
                      reads=[ps_b[b1], pvec_b], writes=[mod_b[l]])

            def ada_finish1(l):
                fw.op(DVE, lambda e: e.scalar_tensor_tensor(
                    out=col_t[l][:, 0:8], in0=mod_t[l][:, 8:16], scalar=1.0,
                    in1=pv(f"n1w{l}", 0, KC), op0=ALU.add, op1=ALU.mult),
                    reads=[mod_b[l], pvec_b], writes=[col_b[l]])
                fw.op(DVE, lambda e: e.tensor_scalar(
                    out=col_t[l][:, 16:16 + FC], in0=pv(f"fw0_{l}", 0, FC),
                    scalar1=pv("flagneg"), scalar2=None, op0=ALU.mult),
                    reads=[pvec_b], writes=[col_b[l]])
                fw.op(DVE, lambda e: e.tensor_scalar(
                    out=col_t[l][:, 16 + FC:16 + 2 * FC], in0=pv(f"fw2_{l}", 0, FC),
                    scalar1=pv("flagneg"), scalar2=None, op0=ALU.mult),
                    reads=[pvec_b], writes=[col_b[l]])


            def ada_finish2(l):
                fw.op(DVE, lambda e: e.scalar_tensor_tensor(
                    out=col_t[l][:, 8:16], in0=mod_t[l][:, 32:40], scalar=1.0,
                    in1=pv(f"n2w{l}", 0, KC), op0=ALU.add, op1=ALU.mult),
                    reads=[mod_b[l], pvec_b], writes=[col_b[l]])

            def ada_finish(l):
                ada_finish1(l)
                ada_finish2(l)

            def conv_fix(eng, t_ap, src_ap, nf0, nf2, reads, writes):
                fw.op(eng, lambda e: e.scalar_tensor_tensor(
                    out=t_ap[:, 256:1024:256], in0=src_ap[:, 255:1023:256], scalar=nf0,
                    in1=t_ap[:, 256:1024:256], op0=ALU.mult, op1=ALU.add),
                    reads=reads, writes=writes)
                fw.op(eng, lambda e: e.scalar_tensor_tensor(
                    out=t_ap[:, 255:1023:256], in0=src_ap[:, 256:1024:256], scalar=nf2,
                    in1=t_ap[:, 255:1023:256], op0=ALU.mult, op1=ALU.add),
                    reads=reads, writes=writes)

            def ffn(l, mid_hook=None):
                upv = ffn_up[l].rearrange("(k p) n -> p k n", p=128)
                dnv = ffn_down[l].rearrange("(k p) n -> p k n", p=128)
                j = 0
                for jg in range(6):
                    ncol = 512 if jg < 5 else 256
                    gw, gwb, gk = ws.next(upv[:, :, jg * 512:jg * 512 + ncol], (KC, ncol))
                    uw, uwb, uk = ws.next(upv[:, :, DFF + jg * 512:DFF + jg * 512 + ncol],
                                          (KC, ncol))
                    for jj in range(ncol // 128):
                        dbl = 2 * (j % 2)
                        for which, (w, wb) in enumerate(((gw, gwb), (uw, uwb))):
                            for tt in range(2):
                                bank = (dbl + which) * 2 + tt

                                def mm(e, w=w, jj=jj, tt=tt, bank=bank):
                                    r = None
                                    for k in range(KC):
                                        r = e.matmul(ps(bank), w[:, k, jj * 128:(jj + 1) * 128],
                                                     h_t[:, k, tsl(tt)],
                                                     start=(k == 0), stop=(k == KC - 1))
                                    return r
                                fw.op(PE, mm, reads=[wb] + [hb[k][tt] for k in range(KC)],
                                      writes=[ps_b[bank]])
                        g_ap = pd_t[dbl][:]
                        u_ap = pd_t[dbl + 1][:]
                        gB = [ps_b[dbl * 2], ps_b[dbl * 2 + 1]]
                        uB = [ps_b[dbl * 2 + 2], ps_b[dbl * 2 + 3]]
                        s = j % 2
                        t = tA_t[s]
                        tb = tA_b[s]
                        fw.op(ACT, lambda e, t=t, g_ap=g_ap, j=j: e.activation(
                            out=t[:], in_=g_ap, func=AF.Identity, bias=pv(f"fb_{l}", j),
                            scale=pv(f"fw1_{l}", j)),
                            reads=gB + [pvec_b], writes=[tb])
                        fw.op(DVE, lambda e, t=t, g_ap=g_ap, j=j: e.scalar_tensor_tensor(
                            out=t[:, 1:T], in0=g_ap[:, 0:T - 1], scalar=pv(f"fw0_{l}", j),
                            in1=t[:, 1:T], op0=ALU.mult, op1=ALU.add),
                            reads=gB + [pvec_b, tb], writes=[tb])
                        fw.op(DVE, lambda e, t=t, g_ap=g_ap, j=j: e.scalar_tensor_tensor(
                            out=t[:, 0:T - 1], in0=g_ap[:, 1:T], scalar=pv(f"fw2_{l}", j),
                            in1=t[:, 0:T - 1], op0=ALU.mult, op1=ALU.add),
                            reads=gB + [pvec_b, tb], writes=[tb])
                        conv_fix(DVE, t, g_ap, col_t[l][:, 16 + j:17 + j],
                                 col_t[l][:, 16 + FC + j:17 + FC + j],
                                 reads=gB + [col_b[l], tb], writes=[tb])
                        fw.op(ACT, lambda e, t=t: e.activation(out=t[:], in_=t[:], func=AF.Gelu),
                              reads=[tb], writes=[tb])
                        fw.op(DVE, lambda e, t=t, u_ap=u_ap, j=j: e.tensor_tensor(
                            out=a_t[:, j, :], in0=t[:], in1=u_ap, op=ALU.mult),
                            reads=[tb] + uB, writes=[ab[j]])
                        j += 1
                    ws.done(gk)
                    ws.done(uk)
                    if mid_hook is not None:
                        mid_hook(jg)
                acc_begin()
                for oc in range(KC):
                    w, wb, wk = ws.next(dnv[:, :, oc * 128:(oc + 1) * 128], (FC, 128))
                    for tt in range(2):
                        bank = (oc * 2 + tt) % 6

                        def mm(e, w=w, tt=tt, bank=bank):
                            r = None
                            for k in range(FC):
                                r = e.matmul(ps(bank), w[:, k, :], a_t[:, k, tsl(tt)],
                                             start=(k == 0), stop=(k == FC - 1))
                            return r
                        fw.op(PE, mm, reads=[wb] + ab[0:FC], writes=[ps_b[bank]])
                        acc_pe(1)
                        fw.op(DVE, lambda e, oc=oc, tt=tt, bank=bank: e.scalar_tensor_tensor(
                            out=x_t[:, oc, tsl(tt)], in0=ps(bank), scalar=mod_t[l][:, 40 + oc:41 + oc],
                            in1=x_t[:, oc, tsl(tt)], op0=ALU.mult, op1=ALU.add),
                            reads=[ps_b[bank], mod_b[l], xb[oc][tt]], writes=[xb[oc][tt]])
                        acc_x(oc, tt)
                    ws.done(wk)
                acc_pe(0)


            def evac(i, out_ap, in_ap, reads, writes):
                if i % 2 == 0:
                    fw.op(DVE, lambda e: e.tensor_copy(out=out_ap, in_=in_ap), reads=reads, writes=writes)
                else:
                    fw.op(ACT, lambda e: e.activation(out=out_ap, in_=in_ap, func=AF.Copy),
                          reads=reads, writes=writes)

            def out_linear(l, wdram, src_ap, src_bufs, gcol):
                wv = wdram.rearrange("(k p) n -> p k n", p=128)
                n = 0
                acc_begin()
                for nb in range(2):
                    w, wb, wk = ws.next(wv[:, :, nb * 512:(nb + 1) * 512], (KC, 512))
                    for o4 in range(4):
                        oc = nb * 4 + o4
                        for tt in range(2):
                            bank = n % 6
                            n += 1

                            def mm(e, w=w, o4=o4, tt=tt, bank=bank):
                                r = None
                                for kk in range(KC):
                                    r = e.matmul(ps(bank), w[:, kk, o4 * 128:(o4 + 1) * 128],
                                                 src_ap(kk, tt), start=(kk == 0), stop=(kk == KC - 1))
                                return r
                            fw.op(PE, mm, reads=[wb] + src_bufs(tt), writes=[ps_b[bank]])
                            acc_pe(2)
                            fw.op(DVE, lambda e, oc=oc, tt=tt, bank=bank: e.scalar_tensor_tensor(
                                out=x_t[:, oc, tsl(tt)], in0=ps(bank), scalar=gcol(oc),
                                in1=x_t[:, oc, tsl(tt)], op0=ALU.mult, op1=ALU.add),
                                reads=[ps_b[bank], mod_b[l], mcol_b, xb[oc][tt]], writes=[xb[oc][tt]])
                            acc_x(oc, tt)
                    ws.done(wk)
                acc_pe(0)

            def mixer_pool(l):
                fw.barrier_bufs(ab)
                fw.op(DVE, lambda e: e.tensor_tensor(out=mcol_t[:, 0:8], in0=pv("pscale", 0, KC),
                                                     in1=mod_t[l][:, 16:24], op=ALU.mult),
                      reads=[pvec_b, mod_b[l]], writes=[mcol_b])
                for i in range(8):
                    bank = i % 8

                    def tr(e, i=i, bank=bank):
                        r = None
                        for c in range(KC):
                            r = e.transpose(psb(bank)[:, c * 128:(c + 1) * 128],
                                            h_t[:, c, i * 128:(i + 1) * 128], identb[:])
                        return r
                    fw.op(PE, tr, reads=[hb[c][i // 4] for c in range(KC)] + [identb_b],
                          writes=[ps_b[bank]])
                    evac(i, a_t[:, i, :], psb(bank), [ps_b[bank]], [ab[i]])
                aw = ab_k = None
                for cc in range(8):
                    g = cc // 2
                    if cc % 2 == 0:
                        aw, awb, ak = ws.next(poolA[:, g, :].rearrange("p (a b c) -> p a b c", a=8, b=3, c=128), (8, 3, 128))
                    dbl = cc % 4

                    def mm(e, cc=cc, aw=aw, dbl=dbl):
                        r = None
                        for i in range(8):
                            ds = [d for d in range(3) if 0 <= i + d - 1 < 8]
                            for n, d in enumerate(ds):
                                r = e.matmul(pd_t[dbl][:, i * 128:(i + 1) * 128],
                                             a_t[:, i + d - 1, cc * 128:(cc + 1) * 128],
                                             aw[:, i, d, :], start=(n == 0), stop=(n == len(ds) - 1))
                        return r
                    fw.op(PE, mm, reads=ab[0:8] + [awb], writes=[ps_b[2 * dbl], ps_b[2 * dbl + 1]])
                    evac(cc, a_t[:, 8 + cc, :], pd_t[dbl][:], [ps_b[2 * dbl], ps_b[2 * dbl + 1]],
                         [ab[8 + cc]])
                    if cc % 2 == 1:
                        ws.done(ak)
                pw, pwb, pk = ws.next(pool_w.rearrange("g (kk p) d -> p g kk d", p=128), (4, 2, 256))
                n = 0
                acc_begin()
                for dc in range(8):
                    g = dc // 2
                    for tt in range(2):
                        bank = n % 6
                        n += 1

                        def mm2(e, dc=dc, g=g, tt=tt, bank=bank):
                            r = None
                            for kk in range(2):
                                r = e.matmul(ps(bank), pw[:, g, kk, (dc % 2) * 128:(dc % 2) * 128 + 128],
                                             a_t[:, 8 + g * 2 + kk, tsl(tt)], start=(kk == 0), stop=(kk == 1))
                            return r
                        fw.op(PE, mm2, reads=[pwb, ab[8 + g * 2], ab[9 + g * 2]], writes=[ps_b[bank]])
                        acc_pe(4)
                        fw.op(DVE, lambda e, dc=dc, tt=tt, bank=bank: e.scalar_tensor_tensor(
                            out=x_t[:, dc, tsl(tt)], in0=ps(bank), scalar=mcol_t[:, dc:dc + 1],
                            in1=x_t[:, dc, tsl(tt)], op0=ALU.mult, op1=ALU.add),
                            reads=[ps_b[bank], mcol_b, xb[dc][tt]], writes=[xb[dc][tt]])
                        acc_x(dc, tt)
                acc_pe(0)
                ws.done(pk)

            def mixer_fnet(l):
                fw.barrier_bufs(ab)

                def pq(i, g):
                    return a_t[:, 2 * i + g // 2, (g % 2) * 512:(g % 2) * 512 + 512]
                n = 0
                for i in range(8):
                    for g in range(4):
                        bank = n % 8

                        def mm(e, i=i, g=g, bank=bank):
                            r = None
                            for kk in range(2):
                                r = e.matmul(ps(bank), h_t[:, g * 2 + kk, i * 128:(i + 1) * 128],
                                             cs_t[:, kk, :], start=(kk == 0), stop=(kk == 1))
                            return r
                        fw.op(PE, mm, reads=[hb[g * 2][i // 4], hb[g * 2 + 1][i // 4], cs_b],
                              writes=[ps_b[bank]])
                        evac(n, pq(i, g), ps(bank), [ps_b[bank]], [ab[2 * i + g // 2]])
                        n += 1
                for tt in range(2):
                    cw, cwb, ck = ws.next(dftC[:, :, tsl(tt)], (8, 512))
                    sw, swb, sk = ws.next(dftS[:, :, tsl(tt)], (8, 512))
                    for mc in range(8):
                        g, m2 = mc // 2, mc % 2
                        bank = n % 8

                        def mm(e, g=g, m2=m2, bank=bank, cw=cw, sw=sw):
                            r = None
                            for i in range(8):
                                r = e.matmul(ps(bank), pq(i, g)[:, m2 * 128:(m2 + 1) * 128], cw[:, i, :],
                                             start=(i == 0), stop=False)
                                r = e.matmul(ps(bank), pq(i, g)[:, 256 + m2 * 128:256 + (m2 + 1) * 128],
                                             sw[:, i, :], start=False, stop=(i == 7))
                            return r
                        fw.op(PE, mm, reads=ab[0:16] + [cwb, swb], writes=[ps_b[bank]])
                        evac(n, a_t[:, 16 + mc, tsl(tt)], ps(bank), [ps_b[bank]], [ab[16 + mc]])
                        n += 1
                    ws.done(ck)
                    ws.done(sk)
                out_linear(l, fnet_w, lambda kk, tt: a_t[:, 16 + kk, tsl(tt)],
                           lambda tt: ab[16:24], lambda oc: mod_t[l][:, 16 + oc:17 + oc])

            def mixer_sconv(l):
                fw.barrier_bufs(ab)
                fw.op(DVE, lambda e: e.tensor_scalar(out=mcol_t[:, 8:16], in0=pv("sw0", 0, KC),
                                                     scalar1=pv("flagneg"), scalar2=None, op0=ALU.mult),
                      reads=[pvec_b], writes=[mcol_b])
                fw.op(DVE, lambda e: e.tensor_scalar(out=mcol_t[:, 16:24], in0=pv("sw2", 0, KC),
                                                     scalar1=pv("flagneg"), scalar2=None, op0=ALU.mult),
                      reads=[pvec_b], writes=[mcol_b])
                wv = sconv_win.rearrange("(k p) n -> p k n", p=128)
                nd = 0
                for jg in range(2):
                    wl = [ws.next(wv[:, :, part * D + jg * 512:part * D + jg * 512 + 512], (KC, 512))
                          for part in range(3)]
                    for jj in range(4):
                        j = jg * 4 + jj
                        dbls = []
                        dbls = [None, None, None]
                        for part in (2, 1, 0):
                            dbl = nd % 4
                            nd += 1
                            dbls[part] = dbl
                            w, wb, _ = wl[part]
                            for tt in range(2):
                                bank = 2 * dbl + tt

                                def mm(e, w=w, jj=jj, tt=tt, bank=bank):
                                    r = None
                                    for kk in range(KC):
                                        r = e.matmul(ps(bank), w[:, kk, jj * 128:(jj + 1) * 128],
                                                     h_t[:, kk, tsl(tt)], start=(kk == 0), stop=(kk == KC - 1))
                                    return r
                                fw.op(PE, mm, reads=[wb] + [hb[kk][tt] for kk in range(KC)],
                                      writes=[ps_b[bank]])
                        gbB = [ps_b[2 * dbls[0]], ps_b[2 * dbls[0] + 1]]
                        gcB = [ps_b[2 * dbls[1]], ps_b[2 * dbls[1] + 1]]
                        uB = [ps_b[2 * dbls[2]], ps_b[2 * dbls[2] + 1]]
                        s = j % 2
                        v, vb = tA_t[s], tA_b[s]
                        t2, t2b = stg_t[s], stg_b[s]
                        fw.op(ACT, lambda e, v=v, d=dbls[2]: e.activation(out=v[:], in_=pd_t[d][:], func=AF.Copy),
                              reads=uB, writes=[vb])
                        fw.op(DVE, lambda e, v=v, d=dbls[1]: e.tensor_tensor(out=v[:], in0=pd_t[d][:], in1=v[:],
                                                                             op=ALU.mult),
                              reads=gcB + [vb], writes=[vb])
                        fw.op(ACT, lambda e, v=v, t2=t2, j=j: e.activation(out=t2[:], in_=v[:], func=AF.Copy,
                                                                             scale=pv("sw1", j)),
                              reads=[vb, pvec_b], writes=[t2b])
                        fw.op(DVE, lambda e, v=v, t2=t2, j=j: e.scalar_tensor_tensor(
                            out=t2[:, 1:T], in0=v[:, 0:T - 1], scalar=pv("sw0", j), in1=t2[:, 1:T],
                            op0=ALU.mult, op1=ALU.add), reads=[vb, pvec_b, t2b], writes=[t2b])
                        fw.op(DVE, lambda e, v=v, t2=t2, j=j: e.scalar_tensor_tensor(
                            out=t2[:, 0:T - 1], in0=v[:, 1:T], scalar=pv("sw2", j), in1=t2[:, 0:T - 1],
                            op0=ALU.mult, op1=ALU.add), reads=[vb, pvec_b, t2b], writes=[t2b])
                        conv_fix(DVE, t2, v, mcol_t[:, 8 + j:9 + j], mcol_t[:, 16 + j:17 + j],
                                 reads=[vb, mcol_b, t2b], writes=[t2b])
                        fw.op(DVE, lambda e, t2=t2, j=j, d=dbls[0]: e.tensor_tensor(
                            out=a_t[:, j, :], in0=pd_t[d][:], in1=t2[:], op=ALU.mult),
                            reads=gbB + [t2b], writes=[ab[j]])
                    for _, _, wk in wl:
                        ws.done(wk)
                out_linear(l, sconv_wout, lambda kk, tt: a_t[:, kk, tsl(tt)],
                           lambda tt: ab[0:8], lambda oc: mod_t[l][:, 16 + oc:17 + oc])


            def mixer_mla(l):
                SC = 1.0 / float(np.sqrt(96.0))
                arena = a_t[:].rearrange("p a b -> p (a b)")
                cqT = lambda c: a_t[:, c, :]
                ckvall = lambda c: arena[:, 3 * T + c * 1536:3 * T + (c + 1) * 1536]
                KR = arena[:, 6 * T:6 * T + 1536]
                QT = lambda r: a_t[:, 8 + r, :]
                KT = lambda r: arena[:, (11 + 2 * r) * T:(11 + 2 * r) * T + 1536]
                VP = lambda r: arena[:, (17 + 3 * r) * T:(17 + 3 * r) * T + 3072].rearrange(
                    "p (k x) -> p k x", k=12, x=256)
                PT = lambda r: arena[:, 23 * T + r * 512:23 * T + (r + 1) * 512]
                for i in range(2):
                    rr = fw.dma(SP, ropecs[:, i, :], ropeCS_d[:, i, :], dst=ropecs_b)
                    if MLA_SUB == 0.11 and not fw.dry:
                        fw.out_recs.append(rr)
                for c in range(2):
                    fw.dma(POOL, ckvall(c)[:, 0:512], cacheT_d[:, c, :], dst=ckvall_b[c])
                fw.dma(POOL, KR, krmask_d, dst=kr_b)
                for r in range(3):
                    fw.dma(POOL, QT(r), eq_d, dst=qt_b[r])
                for r in range(2):
                    fw.op(DVE, lambda e, r=r: e.memset(VP(r), 0.0), writes=[vp_b[r]])
                    fw.op(DVE, lambda e, r=r: e.memset(VP(r)[:, :, 64:65], 1.0), writes=[vp_b[r]])
                    fw.op(DVE, lambda e, r=r: e.memset(VP(r)[:, :, 128:129], 1.0), writes=[vp_b[r]])

                if MLA_SUB < 0.2:
                    return
                w, wb, wk = ws.next(wdq.rearrange("(k p) n -> p k n", p=128), (KC, 384))
                for tt in range(2):
                    banks = [(4 * tt + c) % 8 for c in range(3)]
                    for c in range(3):
                        def mm(e, c=c, tt=tt, bank=banks[c], w=w):
                            r = None
                            for kk in range(KC):
                                r = e.matmul(ps(bank), w[:, kk, c * 128:(c + 1) * 128], h_t[:, kk, tsl(tt)],
                                             start=(kk == 0), stop=(kk == KC - 1))
                            return r
                        fw.op(PE, mm, reads=[wb] + [hb[kk][tt] for kk in range(KC)], writes=[ps_b[banks[c]]])
                    rms_stats([(ps(banks[c]), [ps_b[banks[c]]]) for c in range(3)], 3, 1.0 / 384,
                              bank=(4 * tt + 3) % 8)
                    for c in range(3):
                        s = c % 2
                        fw.op(DVE, lambda e, s=s, bank=banks[c]: e.tensor_tensor(
                            out=xn_t[s][:], in0=ps(bank), in1=rstd_t[:], op=ALU.mult),
                            reads=[ps_b[banks[c]], rstd_b], writes=[xn_b[s]])
                        fw.op(ACT, lambda e, s=s, c=c, tt=tt: e.activation(
                            out=cqT(c)[:, tsl(tt)], in_=xn_t[s][:], func=AF.Copy, scale=pv("qnw", c)),
                            reads=[xn_b[s], pvec_b], writes=[cq_b[c]])
                ws.done(wk)

                if MLA_SUB < 0.5:
                    return
                w, wb, wk = ws.next(wdkv_aug.rearrange("(k p) n -> p k n", p=128), (KC, 448))
                for tt in range(2):
                    banks = [(5 * tt + i) % 8 for i in range(4)]
                    cols = [(0, 128), (128, 128), (256, 96), (352, 96)]
                    for i in range(4):
                        c0, m = cols[i]

                        def mm(e, c0=c0, m=m, tt=tt, bank=banks[i], w=w):
                            r = None
                            for kk in range(KC):
                                r = e.matmul(ps(bank)[0:m, :], w[:, kk, c0:c0 + m], h_t[:, kk, tsl(tt)],
                                             start=(kk == 0), stop=(kk == KC - 1))
                            return r
                        fw.op(PE, mm, reads=[wb] + [hb[kk][tt] for kk in range(KC)], writes=[ps_b[banks[i]]])
                    if MLA_SUB < 0.51:
                        continue
                    rms_stats([(ps(banks[c]), [ps_b[banks[c]]]) for c in range(2)], 2, 1.0 / 256,
                              bank=(5 * tt + 4) % 8)
                    if MLA_SUB < 0.52:
                        continue
                    for c in range(2):
                        s = c % 2
                        fw.op(DVE, lambda e, s=s, bank=banks[c]: e.tensor_tensor(
                            out=xn_t[s][:], in0=ps(bank), in1=rstd_t[:], op=ALU.mult),
                            reads=[ps_b[banks[c]], rstd_b], writes=[xn_b[s]])
                        fw.op(ACT, lambda e, s=s, c=c, tt=tt: e.activation(
                            out=ckvall(c)[:, 512 + tt * 512:1024 + tt * 512], in_=xn_t[s][:], func=AF.Copy,
                            scale=pv("kvnw", c)),
                            reads=[xn_b[s], pvec_b], writes=[ckvall_b[c]])
                        fw.op(DVE, lambda e, s=s, c=c, tt=tt: e.tensor_scalar(
                            out=tA_t[c][:, tsl(tt)], in0=xn_t[s][:], scalar1=pv("kvnw", c), scalar2=None,
                            op0=ALU.mult),
                            reads=[xn_b[s], pvec_b], writes=[tA_b[c]])
                    if MLA_SUB < 0.55:
                        continue
                    bA, bB = banks[2], banks[3]
                    fw.op(ACT, lambda e, tt=tt, bA=bA: e.activation(
                        out=stg_t[0][64:96, tsl(tt)], in_=ps(bA)[64:96, :], func=AF.Copy),
                        reads=[ps_b[bA]], writes=[stg_b[0]])
                    if MLA_SUB < 0.56:
                        continue
                    fw.op(DVE, lambda e, tt=tt, bA=bA: e.tensor_tensor(
                        out=xn_t[0][64:96, :], in0=ps(bA)[64:96, :], in1=ropecs[64:96, 0, tsl(tt)], op=ALU.mult),
                        reads=[ps_b[bA], ropecs_b], writes=[xn_b[0]])
                    if MLA_SUB < 0.57:
                        continue
                    fw.op(DVE, lambda e, tt=tt, bB=bB: e.tensor_tensor(
                        out=xn_t[1][64:96, :], in0=ps(bB)[64:96, :], in1=ropecs[64:96, 1, tsl(tt)], op=ALU.mult),
                        reads=[ps_b[bB], ropecs_b], writes=[xn_b[1]])
                    if MLA_SUB < 0.58:
                        continue
                    fw.op(DVE, lambda e, tt=tt: e.tensor_tensor(
                        out=KR[64:96, 512 + tt * 512:1024 + tt * 512], in0=xn_t[0][64:96, :],
                        in1=xn_t[1][64:96, :], op=ALU.add),
                        reads=[xn_b[0], xn_b[1]], writes=[kr_b])
                ws.done(wk)

                for nb in range(4, 12):
                    ada_block(0, nb, 2 + nb % 2, 4 + nb % 2)
                ada_finish2(0)
                for i in range(8):
                    bank = i % 8

                    def tr(e, i=i, bank=bank):
                        r = None
                        for c in range(2):
                            r = e.transpose(ps(bank)[:, c * 128:(c + 1) * 128],
                                            tA_t[c][:, i * 128:(i + 1) * 128], ident[:])
                        r = e.transpose(ps(bank)[:, 256:384], stg_t[0][:, i * 128:(i + 1) * 128], ident[:])
                        return r
                    fw.op(PE, tr, reads=[tA_b[0], tA_b[1], stg_b[0], ident_b], writes=[ps_b[bank]])
                    fw.op(DVE, lambda e, i=i, bank=bank: e.tensor_copy(out=ckvst[:, i, :], in_=ps(bank)[:, 0:256]),
                          reads=[ps_b[bank]], writes=[ckvst_b])
                    fw.op(ACT, lambda e, i=i, bank=bank: e.activation(out=krst[:, i, :], in_=ps(bank)[:, 320:352],
                                                                     func=AF.Copy),
                          reads=[ps_b[bank]], writes=[krst_b])
                fw.dma(SP, ckv_o.rearrange("(i p) c -> p i c", p=128), ckvst[:], dst=None, src=[ckvst_b])
                fw.dma(SP, kr_o.rearrange("(i p) c -> p i c", p=128), krst[:], dst=None, src=[krst_b])

                for r in range(3):
                    fw.op(DVE, lambda e, r=r: e.tensor_copy(out=KT(r)[64:128, :], in_=KR[64:128, :]),
                          reads=[kr_b], writes=[kt_b[r]])

                fw.op(DVE, lambda e: e.memset(stg_t[1][:], 0.0), writes=[stg_b[1]])
                if MLA_SUB < 2:
                    return
                ukv, ukvb, ukvk = ws.next(wukv.rearrange("(k p) n -> p k n", p=128), (2, 2048))
                wqv = wuq_aug.rearrange("(k p) n -> p k n", p=128)
                st = {'qw': None, 'n_o': 0, 'n_s': 0, 'n_p': 0}

                def prep_V(p):
                    vr = p % 2
                    for ktg in range(3):
                        bank = 6 + ktg % 2

                        def mmv(e, ktg=ktg, bank=bank, p=p):
                            r = None
                            for j in range(4):
                                kt = ktg * 4 + j
                                for kc in range(2):
                                    rhs = ukv[:, kc, p * 256:(p + 1) * 256].rearrange("p (h x) -> p h x", h=2)[:, :, 64:128]
                                    r = e.matmul(ps(bank)[:, j * 128:(j + 1) * 128].rearrange("p (h x) -> p h x", h=2),
                                                 ckvall(kc)[:, kt * 128:(kt + 1) * 128], rhs,
                                                 start=(kc == 0), stop=(kc == 1))
                            return r
                        fw.op(PE, mmv, reads=[ukvb, ckvall_b[0], ckvall_b[1]], writes=[ps_b[bank]])
                        src = ps(bank).rearrange("p (j h x) -> p j h x", j=4, h=2, x=64)
                        fw.op(DVE, lambda e, src=src, ktg=ktg, vr=vr: e.tensor_copy(
                            out=VP(vr)[:, ktg * 4:ktg * 4 + 4, 0:64], in_=src[:, :, 0, :]),
                            reads=[ps_b[bank]], writes=[vp_b[vr]])
                        fw.op(ACT, lambda e, src=src, ktg=ktg, vr=vr: e.activation(
                            out=VP(vr)[:, ktg * 4:ktg * 4 + 4, 192:256], in_=src[:, :, 1, :], func=AF.Copy),
                            reads=[ps_b[bank]], writes=[vp_b[vr]])

                def prep_KQ(h):
                    r3 = h % 3
                    if h % 4 == 0:
                        if st['qw'] is not None:
                            ws.done(st['qw'][2])
                        st['qw'] = ws.next(wqv[:, :, (h // 4) * 768:(h // 4 + 1) * 768], (3, 768))
                    qw = st['qw']
                    for kt5 in range(3):
                        bank = 6 + kt5 % 2

                        def mmk(e, h=h, kt5=kt5, bank=bank):
                            r = None
                            for kc in range(2):
                                r = e.matmul(ps(bank)[0:64, :], ukv[:, kc, h * 128:h * 128 + 64],
                                             ckvall(kc)[:, kt5 * 512:(kt5 + 1) * 512],
                                             start=(kc == 0), stop=(kc == 1))
                            return r
                        fw.op(PE, mmk, reads=[ukvb, ckvall_b[0], ckvall_b[1]], writes=[ps_b[bank]])
                        evac(kt5, KT(r3)[0:64, kt5 * 512:(kt5 + 1) * 512], ps(bank)[0:64, :],
                             [ps_b[bank]], [kt_b[r3]])
                    qcol = (h % 4) * 192
                    for tt in range(2):
                        for which in range(2):
                            bank = 6 + which

                            def mmq(e, which=which, tt=tt, bank=bank, qcol=qcol, qwv=qw[0]):
                                r = None
                                for kc in range(3):
                                    r = e.matmul(ps(bank)[0:96, :],
                                                 qwv[:, kc, qcol + which * 96:qcol + which * 96 + 96],
                                                 cqT(kc)[:, tsl(tt)], start=(kc == 0), stop=(kc == 2))
                                return r
                            fw.op(PE, mmq, reads=[qw[1]] + cq_b, writes=[ps_b[bank]])
                        fw.op(ACT, lambda e, r3=r3, tt=tt: e.activation(
                            out=QT(r3)[0:64, tsl(tt)], in_=ps(6)[0:64, :], func=AF.Copy),
                            reads=[ps_b[6]], writes=[qt_b[r3]])
                        fw.op(DVE, lambda e, tt=tt: e.tensor_tensor(
                            out=xn_t[0][64:96, :], in0=ps(6)[64:96, :], in1=ropecs[64:96, 0, tsl(tt)],
                            op=ALU.mult), reads=[ps_b[6], ropecs_b], writes=[xn_b[0]])
                        fw.op(DVE, lambda e, tt=tt: e.tensor_tensor(
                            out=xn_t[1][64:96, :], in0=ps(7)[64:96, :], in1=ropecs[64:96, 1, tsl(tt)],
                            op=ALU.mult), reads=[ps_b[7], ropecs_b], writes=[xn_b[1]])
                        fw.op(DVE, lambda e, r3=r3, tt=tt: e.tensor_tensor(
                            out=QT(r3)[64:96, tsl(tt)], in0=xn_t[0][64:96, :], in1=xn_t[1][64:96, :],
                            op=ALU.add), reads=[xn_b[0], xn_b[1]], writes=[qt_b[r3]])

                def attend(h, tts, pending=None):
                    p, hh = h // 2, h % 2
                    vr = p % 2
                    r3 = h % 3
                    if hh == 0:
                        vcols, orows, srow, om = (0, 65), (0, 64), 64, 65
                    else:
                        vcols, orows, srow, om = (128, 256), (64, 128), 0, 128
                    for tt in (tts if MLA_SUB >= 3 else ()):
                        ob = 2 + st['n_o'] % 3
                        st['n_o'] += 1
                        pend = []
                        for kt in range(14):
                            if kt < 12:
                                sb_ = (0, 1, 5)[st['n_s'] % 3]
                                st['n_s'] += 1
                                fw.op(PE, lambda e, sb_=sb_, kt=kt, r3=r3, tt=tt: e.matmul(
                                    ps(sb_), KT(r3)[:, kt * 128:(kt + 1) * 128], QT(r3)[:, tsl(tt)],
                                    start=True, stop=True),
                                    reads=[kt_b[r3], qt_b[r3]], writes=[ps_b[sb_]])
                                pi = st['n_p'] % 4
                                st['n_p'] += 1
                                fw.op(ACT, lambda e, sb_=sb_, pi=pi: e.activation(
                                    out=PT(pi), in_=ps(sb_), func=AF.Exp, scale=SC),
                                    reads=[ps_b[sb_]], writes=[pt_b[pi]])
                            if kt >= 2:
                                pkt, ppi = pend.pop(0)
                                fw.op(PE, lambda e, ob=ob, om=om, vr=vr, pkt=pkt, ppi=ppi, vcols=vcols: e.matmul(
                                    ps(ob)[0:om, :], VP(vr)[:, pkt, vcols[0]:vcols[1]], PT(ppi),
                                    start=(pkt == 0), stop=(pkt == 11)),
                                    reads=[vp_b[vr], pt_b[ppi]], writes=[ps_b[ob]])
                            if kt < 12:
                                pend.append((kt, pi))
                            if kt == 8 and pending is not None:
                                pending()
                                pending = None
                        if MLA_SUB < 4:
                            continue
                        rs = stg_t[1][srow:srow + 1, 0:512] if hh == 0 else stg_t[1][srow:srow + 1, 512:1024]
                        fw.op(ACT, lambda e, rs=rs, ob=ob, srow=srow: e.activation(
                            out=rs, in_=ps(ob)[srow:srow + 1, :], func=AF.Ln), reads=[ps_b[ob]], writes=[stg_b[1]])
                        fw.op(ACT, lambda e, rs=rs: e.activation(out=rs, in_=rs, func=AF.Exp, scale=-1.0),
                              reads=[stg_b[1]], writes=[stg_b[1]])
                        return lambda ob=ob, tt=tt: norm_o(h, tt, ob)
                    return None

                def norm_o(h, tt, ob):
                    p, hh = h // 2, h % 2
                    if hh == 0:
                        orows, srow = (0, 64), 64
                    else:
                        orows, srow = (64, 128), 0
                    if True:
                        bb = 6 + st['n_o'] % 2
                        fw.op(PE, lambda e, bb=bb, hh=hh: e.matmul(
                            ps(bb), sel_t[:, hh, :], stg_t[1][:, hh * 512:(hh + 1) * 512],
                            start=True, stop=True),
                            reads=[stg_b[1], const_b], writes=[ps_b[bb]])
                        fw.op(ACT, lambda e, bb=bb: e.activation(out=rstd_t[:], in_=ps(bb), func=AF.Copy),
                              reads=[ps_b[bb]], writes=[rstd_b])
                        fw.op(DVE, lambda e, ob=ob, orows=orows, p=p, tt=tt: e.tensor_tensor(
                            out=h_t[orows[0]:orows[1], p, tsl(tt)], in0=ps(ob)[orows[0]:orows[1], :],
                            in1=rstd_t[orows[0]:orows[1], :], op=ALU.mult),
                            reads=[ps_b[ob], rstd_b], writes=[hb[p][tt]])

                prep_V(0)
                prep_KQ(0)
                pnd = None
                for h in range(16):
                    pnd = attend(h, (0,), pnd)
                    if h + 1 < 16:
                        if (h + 1) % 2 == 0:
                            prep_V((h + 1) // 2)
                        prep_KQ(h + 1)
                    pnd = attend(h, (1,), pnd)
                if pnd is not None:
                    pnd()
                qw = st['qw']
                ws.done(qw[2])
                ws.done(ukvk)
                out_linear(l, wo, lambda kk, tt: h_t[:, kk, tsl(tt)],
                           lambda tt: [hb[kk][tt] for kk in range(KC)], lambda oc: mod_t[l][:, 16 + oc:17 + oc])

            def mixer(l):
                norm_mod(l, 1, pre=(l > 0))
                if l == 0 and stage >= 4:
                    mixer_mla(l)
                elif l == 1 and stage >= 3:
                    mixer_pool(l)
                elif l == 2 and stage >= 3:
                    mixer_fnet(l)
                elif l == 3 and stage >= 3:
                    mixer_sconv(l)
                fw.barrier_bufs(ab)

            if stage >= 2:
                for nb in range(4 if stage >= 4 else 12):
                    ada_block(0, nb, 4 + nb % 2, 6 + nb % 2)
                ada_finish1(0)
                if stage < 4:
                    ada_finish2(0)
                for l in range(DEPTH):
                    mixer(l)
                    norm_mod(l, 2, pre=(stage >= 4))
                    if l + 1 < DEPTH:
                        def hook(jg, l=l):
                            ada_block(l + 1, 2 * jg, 0, 1)
                            ada_block(l + 1, 2 * jg + 1, 2, 3)
                            if jg == 5:
                                ada_finish(l + 1)
                        ffn(l, hook)
                    else:
                        ffn(l)

            fo, _ = PV["fnw"]
            for tt in range(2):
                if stage >= 4:
                    rms_tail(1.0 / D, 6 + tt, rstd_t, rstd_b)
                else:
                    rms_stats([(x_t[:, c, tsl(tt)], [xb[c][tt]]) for c in range(KC)], KC, 1.0 / D,
                              bank=tt)
                for c in range(KC):
                    fw.op(DVE, lambda e, c=c, tt=tt: e.scalar_tensor_tensor(
                        out=x_t[:, c, tsl(tt)], in0=x_t[:, c, tsl(tt)],
                        scalar=pvec[:, fo + c:fo + c + 1], in1=rstd_t[:],
                        op0=ALU.mult, op1=ALU.mult),
                        reads=[xb[c][tt], rstd_b, pvec_b], writes=[xb[c][tt]])
            for i in range(8):
                s = i % 2
                tt = i // 4
                for half in range(2):
                    bank = 2 + (i * 2 + half) % 6

                    def mm(e, half=half, bank=bank, i=i):
                        r = None
                        for cc in range(4):
                            c = half * 4 + cc
                            r = e.transpose(ps(bank)[:, cc * 128:(cc + 1) * 128],
                                            x_t[:, c, i * 128:(i + 1) * 128], ident[:])
                        return r
                    fw.op(PE, mm, reads=[xb[c][tt] for c in range(half * 4, half * 4 + 4)] + [ident_b],
                          writes=[ps_b[bank]])
                    if half == 0:
                        fw.op(DVE, lambda e, s=s, bank=bank: e.tensor_copy(
                            out=stg_t[s][:, 0:512], in_=ps(bank)),
                            reads=[ps_b[bank]], writes=[stg_b[s]])
                    else:
                        fw.op(ACT, lambda e, s=s, bank=bank: e.activation(
                            out=stg_t[s][:, 512:1024], in_=ps(bank), func=AF.Copy),
                            reads=[ps_b[bank]], writes=[stg_b[s]])
                fw.dma(SP, yout[i * 128:(i + 1) * 128, :], stg_t[s][:], dst=None, src=[stg_b[s]])

        fw.dry = True
        emit()
        fw.dry = False
        ws.reset()
        emit()
        assert ws.consumed == len(ws.specs)

        final_waits = {}
        for sem, val in fw.out_recs:
            final_waits[sem] = max(final_waits.get(sem, 0), val)

        with nc.Block() as block:
            @block.sync
            def _(e):
                fw.replay(SP, e)
                for sem, val in final_waits.items():
                    e.wait_ge(sem, val)

            @block.tensor
            def _(e):
                fw.replay(PE, e)

            @block.scalar
            def _(e):
                fw.replay(ACT, e)

            @block.vector
            def _(e):
                fw.replay(DVE, e)

            @block.gpsimd
            def _(e):
                fw.replay(POOL, e)
    return nc


def _cols(v):
    v = np.asarray(v, np.float32)
    return np.ascontiguousarray(v.reshape(-1, 128).T)


def _make_pvec(inp, cond_vec, flagneg):
    pv = np.zeros((128, NPV), np.float32)

    def put(name, arr):
        o, n = PV[name]
        assert arr.shape == (128, n), (name, arr.shape, n)
        pv[:, o:o + n] = arr

    for l in range(DEPTH):
        put(f"n1w{l}", _cols(inp["norm1_w"][l]))
        put(f"n2w{l}", _cols(inp["norm2_w"][l]))
        put(f"fw0_{l}", _cols(inp["ffn_conv_w"][l, 0]))
        put(f"fw1_{l}", _cols(inp["ffn_conv_w"][l, 1]))
        put(f"fw2_{l}", _cols(inp["ffn_conv_w"][l, 2]))
        put(f"fb_{l}", _cols(inp["ffn_conv_b"][l]))
        put(f"adab{l}", _cols(inp["ada_b"][l]))
    put("fnw", _cols(inp["final_norm_w"]))
    put("cond", _cols(cond_vec))
    pv[:, PV["flagneg"][0]] = flagneg
    put("qnw", _cols(inp["mla_q_norm"][0]))
    put("kvnw", _cols(inp["mla_kv_norm"][0]))
    put("pscale", _cols(inp["pool_scale"][0]))
    put("sw0", _cols(inp["sconv_conv"][0, 0]))
    put("sw1", _cols(inp["sconv_conv"][0, 1]))
    put("sw2", _cols(inp["sconv_conv"][0, 2]))
    return pv


def _pool_tables(L):
    A = np.zeros((4, T, T), np.float64)
    wins = (2, 4, 8, 16)
    for g, w in enumerate(wins):
        for t in range(T):
            s0 = (t // L) * L
            tl = t - s0
            lo = max(tl - w // 2, 0)
            hi = min(tl + w - w // 2, L)
            A[g, t, s0 + lo:s0 + hi] = 1.0 / (hi - lo)
            A[g, t, t] -= 1.0
    out = np.zeros((128, 4, 8, 3, 128), np.float32)
    for g in range(4):
        for i in range(8):
            for d in range(3):
                ip = i + d - 1
                if 0 <= ip < 8:
                    out[:, g, i, d, :] = A[g, i * 128:(i + 1) * 128, ip * 128:(ip + 1) * 128].T
    return out.reshape(128, 4, 8 * 3 * 128).astype(ml_dtypes.bfloat16)


def _dft_tables(L):
    t = np.arange(T)
    same = (t[:, None] // L) == (t[None, :] // L)
    ang = 2.0 * np.pi * ((t[:, None] % L) * (t[None, :] % L) % L) / L
    nrm = 1.0 / np.sqrt(L * 256.0)
    C = np.where(same, np.cos(ang), 0.0) * nrm
    S = np.where(same, -np.sin(ang), 0.0) * nrm

    def lay(M):
        return np.ascontiguousarray(M.reshape(8, 128, T).transpose(1, 0, 2)).astype(ml_dtypes.bfloat16)
    c = np.arange(256)
    a2 = 2.0 * np.pi * ((c[:, None] * c[None, :]) % 256) / 256.0
    CS = np.concatenate([np.cos(a2), np.sin(a2)], axis=1)
    CS = np.ascontiguousarray(CS.reshape(2, 128, 512).transpose(1, 0, 2)).astype(ml_dtypes.bfloat16)
    return lay(C), lay(S), CS


_PERM = np.concatenate([np.arange(0, 32, 2), np.arange(1, 32, 2)])
_PERM_SW = np.concatenate([np.arange(1, 32, 2), np.arange(0, 32, 2)])


def _mla_weights(inp):
    wdkv = np.asarray(inp["mla_wdkv"][0], np.float32)
    aug = np.zeros((D, 448), np.float32)
    aug[:, 0:256] = wdkv[:, 0:256]
    aug[:, 320:352] = wdkv[:, 256 + _PERM]
    aug[:, 416:448] = wdkv[:, 256 + _PERM_SW]
    wuq = np.asarray(inp["mla_wuq"][0], np.float32).reshape(384, 16, 96)
    qa = np.zeros((384, 16, 192), np.float32)
    qa[:, :, 0:64] = wuq[:, :, 0:64]
    qa[:, :, 64:96] = wuq[:, :, 64 + _PERM]
    qa[:, :, 160:192] = wuq[:, :, 64 + _PERM_SW]
    return aug, np.ascontiguousarray(qa.reshape(384, 16 * 192))


def _rope_tables(kind):
    cs = np.zeros((128, 2, T), np.float32)
    if kind == "p":
        cs[64:96, 0, :] = 1.0
        return cs
    t = np.arange(T)
    r = (t // 64).astype(np.float32)
    col = (t % 64).astype(np.float32)
    inv = (np.float32(10000.0) ** (-np.arange(8, dtype=np.float32) / np.float32(8))).astype(np.float32)
    ang = np.concatenate([r[:, None] * inv, col[:, None] * inv], axis=-1).astype(np.float32)
    c, s = np.cos(ang).T, np.sin(ang).T
    cs[64:80, 0, :] = c
    cs[80:96, 0, :] = c
    cs[64:80, 1, :] = -s
    cs[80:96, 1, :] = s
    return cs


def _mask_tables(kind):
    NEG = -30000.0
    eq = np.zeros((128, T), np.float32)
    ek = np.zeros((128, 1536), np.float32)
    if kind == "p":
        seq = np.arange(T) // 256
        for r in range(4):
            eq[96 + r, :] = (seq == r)
            ek[96 + r, 0:512] = NEG
            ek[96 + r, 512:] = np.where(seq == r, 0.0, NEG)
    return eq, ek


def _core_roles():
    return [("s", 0), ("s", 1), ("p", 0), ("p", 1), ("p", 2), ("p", 3), ("p", 3), ("p", 3)]


_NC_CACHE = {}


def make_in_maps(inp):
    roles = _core_roles()
    ident = np.eye(128, dtype=np.float32)
    shared = {
        "ident": ident,
        "ada_w": np.ascontiguousarray(inp["ada_w"], dtype=np.float32),
        "ffn_up": np.ascontiguousarray(inp["ffn_up"], dtype=np.float32),
        "ffn_down": np.ascontiguousarray(inp["ffn_down"], dtype=np.float32),
        "pool_w": np.ascontiguousarray(inp["pool_w"][0], dtype=np.float32),
        "fnet_w": np.ascontiguousarray(inp["fnet_w"][0], dtype=np.float32),
        "sconv_win": np.ascontiguousarray(inp["sconv_win"][0], dtype=np.float32),
        "sconv_wout": np.ascontiguousarray(inp["sconv_wout"][0], dtype=np.float32),
        "identb": ident.astype(ml_dtypes.bfloat16),
        "wdq": np.ascontiguousarray(inp["mla_wdq"][0], dtype=np.float32),
        "wukv": np.ascontiguousarray(inp["mla_wukv"][0], dtype=np.float32),
        "wo": np.ascontiguousarray(inp["mla_wo"][0], dtype=np.float32),
    }
    shared["wdkv_aug"], shared["wuq_aug"] = _mla_weights(inp)
    tabs = {}
    for kind, L in (("s", 1024), ("p", 256)):
        C, S, CS = _dft_tables(L)
        eq, ek = _mask_tables(kind)
        tabs[kind] = {"poolA": _pool_tables(L), "dftC": C, "dftS": S, "dftCS": CS,
                      "ropeCS": _rope_tables(kind), "eq": eq, "_ek": ek}
    in_maps = []
    for kind, idx in roles:
        if kind == "s":
            xc = inp["x_sample"][idx]
            cond = inp["c"][idx]
            flag = 0.0
        else:
            xc = inp["x_prompt"][4 * idx:4 * idx + 4].reshape(T, D)
            cond = inp["c_ctx"]
            flag = -1.0
        m = dict(shared)
        m.update({a: b for a, b in tabs[kind].items() if not a.startswith("_")})
        krm = tabs[kind]["_ek"].copy()
        cT = np.zeros((128, 2, 512), np.float32)
        if kind == "s":
            cT[:] = np.asarray(inp["cache_ckv"][idx, 0], np.float32).T.reshape(2, 128, 512).transpose(1, 0, 2)
            krm[64:96, 0:512] = np.asarray(inp["cache_krope"][idx, 0], np.float32)[:, _PERM].T
        m["cacheT"] = cT
        m["krmask"] = krm
        m["xin"] = np.ascontiguousarray(xc, dtype=np.float32)
        m["pvec"] = _make_pvec(inp, cond, flag)
        in_maps.append(m)
    return in_maps


def kernel(**inputs):
    inp = {k: np.asarray(v) for k, v in inputs.items()}
    in_maps = make_in_maps(inp)
    if "nc" not in _NC_CACHE:
        _NC_CACHE["nc"] = build_program(STAGE)
    nc = _NC_CACHE["nc"]
    res = run_bass_kernel_spmd(nc, in_maps, core_ids=list(range(NCORES)))
    outs = res.results
    y_sample = np.stack([outs[0]["yout"], outs[1]["yout"]], axis=0).astype(np.float32)
    y_prompt = np.concatenate([outs[2 + g]["yout"].reshape(4, 256, D) for g in range(4)], axis=0)
    y_prompt = y_prompt.astype(np.float32)
    new_ckv = np.concatenate([outs[2 + g]["ckv_o"].reshape(4, 1, 256, 256) for g in range(4)], axis=0)
    krp = np.concatenate([outs[2 + g]["kr_o"].reshape(4, 1, 256, 32) for g in range(4)], axis=0)
    new_kr = np.empty_like(krp)
    new_kr[..., _PERM] = krp
    new_ckv = new_ckv.astype(np.float32)
    new_kr = new_kr.astype(np.float32)
    return (y_prompt, y_sample, new_ckv, new_kr)
```

```python
import numpy as np
from contextlib import ExitStack
import ml_dtypes

import concourse.bass as bass
import concourse.mybir as mybir
from concourse.bass_utils import run_bass_kernel_spmd

F32 = mybir.dt.float32
BF16 = mybir.dt.bfloat16
AF = mybir.ActivationFunctionType
ALU = mybir.AluOpType

D = 1024
T = 1024
KC = 8
DFF = 2816
FC = 22
DEPTH = 4
EPS = 1e-6
NCORES = 8


class Buf:
    __slots__ = ("name", "w", "r", "sem", "cum", "excl")

    def __init__(self, name):
        self.name = name
        self.excl = False
        self.w = None
        self.r = {}
        self.sem = None
        self.cum = 0


class Q:
    def __init__(self, fw, name, own_wait=True):
        self.fw = fw
        self.name = name
        self.thunks = []
        self.sem = fw.new_sem("q_" + name)
        self.cnt = 0
        self.known = {}
        self.own_wait = own_wait


class FW:
    def __init__(self, nc, es):
        self.nc = nc
        self.es = es
        self.nsem = 0
        self.pe = Q(self, "pe", own_wait=False)
        self.act = Q(self, "act")
        self.dve = Q(self, "dve")
        self.pool = Q(self, "pool")
        self.sp = Q(self, "sp")
        self.out_recs = []
        self.dry = False

    def new_sem(self, name):
        self.nsem += 1
        return self.es.enter_context(self.nc.semaphore(f"s{self.nsem}_{name}"))

    def buf(self, name, dma=False):
        b = Buf(name)
        if dma:
            b.sem = self.new_sem("d_" + name)
        return b

    def _collect(self, q, reads, writes):
        waits = {}

        def need(rec):
            if rec is None:
                return
            sem, val = rec
            if sem is q.sem and not q.own_wait:
                return
            if q.known.get(sem, 0) >= val:
                return
            if waits.get(sem, 0) < val:
                waits[sem] = val

        for b in reads:
            need(b.w)
            if b.excl:
                for sem, val in b.r.items():
                    if sem is not q.sem:
                        need((sem, val))
        for b in writes:
            need(b.w)
            for sem, val in b.r.items():
                need((sem, val))
        for sem, val in waits.items():
            q.known[sem] = val
        return list(waits.items())

    @staticmethod
    def _commit(rec, reads, writes):
        sem, val = rec
        for b in reads:
            if b.r.get(sem, 0) < val:
                b.r[sem] = val
        for b in writes:
            b.w = rec
            b.r = {}

    def op(self, q, fn, reads=(), writes=()):
        if self.dry:
            return None
        wl = self._collect(q, reads, writes)
        q.cnt += 1
        rec = (q.sem, q.cnt)
        q.thunks.append((wl, fn, rec, 1))
        self._commit(rec, reads, writes)
        return rec

    def dma(self, q, out_ap, in_ap, dst=None, src=(), reads=(), kw=None):
        if self.dry:
            return None
        kw = kw or {}
        writes = [dst] if dst is not None else []
        rds = list(src) + list(reads)
        wl = self._collect(q, rds, writes)
        owner = dst if dst is not None else src[0]
        owner.cum += 16
        rec = (owner.sem, owner.cum)

        def fn(e, out_ap=out_ap, in_ap=in_ap, kw=kw):
            return e.dma_start(out=out_ap, in_=in_ap, **kw)

        q.thunks.append((wl, fn, rec, 16))
        self._commit(rec, rds, writes)
        if dst is None:
            self.out_recs.append(rec)
        return rec

    def barrier_bufs(self, bufs):
        allq = [self.pe, self.act, self.dve, self.pool]
        for b in bufs:
            for q in allq:
                if q.cnt > 0:
                    if b.r.get(q.sem, 0) < q.cnt:
                        b.r[q.sem] = q.cnt

    def replay(self, q, eng):
        for wl, fn, rec, inc in q.thunks:
            for sem, val in wl:
                eng.wait_ge(sem, val)
            ins = fn(eng)
            if isinstance(ins, (list, tuple)):
                ins = ins[-1]
            ins.then_inc(rec[0], inc)


class WStream:
    def __init__(self, fw, q, slots, bufs, slot_elems):
        self.fw = fw
        self.q = q
        self.slots = slots
        self.bufs = bufs
        self.n = len(slots)
        self.slot_elems = slot_elems
        self.specs = []
        self.reset()

    def reset(self):
        self.issued = 0
        self.consumed = 0
        self.done_flags = []

    def _view(self, k, shape):
        t = self.slots[k % self.n]
        n = int(np.prod(shape))
        assert n <= self.slot_elems, shape
        if len(shape) == 1:
            return t[:, 0:n]
        if len(shape) == 2:
            return t[:, 0:n].rearrange("p (a b) -> p a b", a=shape[0], b=shape[1])
        return t[:, 0:n].rearrange("p (a b c) -> p a b c", a=shape[0], b=shape[1], c=shape[2])

    def _pump(self):
        while self.issued < len(self.specs):
            k = self.issued
            if k >= self.n and not (k - self.n < len(self.done_flags) and self.done_flags[k - self.n]):
                break
            dram_ap, shape = self.specs[k]
            self.fw.dma(self.q, self._view(k, shape), dram_ap, dst=self.bufs[k % self.n])
            self.issued += 1

    def next(self, dram_ap, shape):
        shape = tuple(shape)
        if self.fw.dry:
            self.specs.append((dram_ap, shape))
            return self._view(0, shape), self.bufs[0], None
        k = self.consumed
        self.consumed += 1
        assert self.specs[k][1] == shape, (k, self.specs[k][1], shape)
        self.done_flags.append(False)
        self._pump()
        assert self.issued > k, (k, self.issued)
        return self._view(k, shape), self.bufs[k % self.n], k

    def done(self, k):
        if self.fw.dry:
            return
        self.done_flags[k] = True
        self._pump()


def _pvec_map():
    m = {}
    o = 0

    def add(name, n):
        nonlocal o
        m[name] = (o, n)
        o += n

    for l in range(DEPTH):
        add(f"n1w{l}", KC)
        add(f"n2w{l}", KC)
        add(f"fw0_{l}", FC)
        add(f"fw1_{l}", FC)
        add(f"fw2_{l}", FC)
        add(f"fb_{l}", FC)
        add(f"adab{l}", 48)
    add("fnw", KC)
    add("cond", KC)
    add("flagneg", 1)
    add("qnw", 3)
    add("kvnw", 2)
    add("pscale", KC)
    add("sw0", KC)
    add("sw1", KC)
    add("sw2", KC)
    m["_n"] = o
    return m


PV = _pvec_map()
NPV = PV["_n"]

STAGE = 4
MLA_SUB = 4.0
NSLOT = 6
SLOT_ELEMS = 4096


def build_program(stage=STAGE):
    nc = bass.Bass("TRN2", target_bir_lowering=False)

    def din(name, shape, dt=F32):
        return nc.dram_tensor(name, list(shape), dt, kind="ExternalInput").ap()

    def dout(name, shape, dt=F32):
        return nc.dram_tensor(name, list(shape), dt, kind="ExternalOutput").ap()

    xin = din("xin", [T, D])
    pvec_d = din("pvec", [128, NPV])
    ident_d = din("ident", [128, 128])
    ada_w = din("ada_w", [DEPTH, D, 6 * D])
    ffn_up = din("ffn_up", [DEPTH, D, 2 * DFF])
    ffn_down = din("ffn_down", [DEPTH, DFF, D])
    pool_w = din("pool_w", [4, 256, 256])
    poolA = din("poolA", [128, 4, 8 * 3 * 128], BF16)
    fnet_w = din("fnet_w", [D, D])
    dftC = din("dftC", [128, 8, T], BF16)
    dftS = din("dftS", [128, 8, T], BF16)
    dftCS = din("dftCS", [128, 2, 512], BF16)
    identb_d = din("identb", [128, 128], BF16)
    sconv_win = din("sconv_win", [D, 3 * D])
    sconv_wout = din("sconv_wout", [D, D])
    wdq = din("wdq", [D, 384])
    wdkv_aug = din("wdkv_aug", [D, 448])
    wuq_aug = din("wuq_aug", [384, 16 * 192])
    wukv = din("wukv", [256, 2048])
    wo = din("wo", [D, D])
    ropeCS_d = din("ropeCS", [128, 2, T])
    cacheT_d = din("cacheT", [128, 2, 512])
    krmask_d = din("krmask", [128, 1536])
    eq_d = din("eq", [128, T])
    yout = dout("yout", [T, D])
    ckv_o = dout("ckv_o", [T, 256])
    kr_o = dout("kr_o", [T, 32])

    es = ExitStack()
    with es:
        fw = FW(nc, es)
        PE, ACT, DVE, POOL, SP = fw.pe, fw.act, fw.dve, fw.pool, fw.sp

        def sb(name, shape, dt):
            return es.enter_context(nc.sbuf_tensor(name, list(shape), dt))

        x_t = sb("x", [128, KC, T], F32)
        xb = [[fw.buf(f"x{c}_{tt}") for tt in range(2)] for c in range(KC)]
        h_t = sb("h", [128, KC, T], BF16)
        hb = [[fw.buf(f"h{c}_{tt}") for tt in range(2)] for c in range(KC)]
        a_t = sb("a", [128, 25, T], BF16)
        ab = [fw.buf(f"a{j}") for j in range(25)]
        pvec = sb("pvec_sb", [128, NPV], F32)
        pvec_b = fw.buf("pvec", dma=True)
        ident = sb("ident_sb", [128, 128], F32)
        ident_b = fw.buf("ident", dma=True)
        ones_bf = sb("ones_bf", [128, 128], BF16)
        one_f = sb("one_f", [128, 1], F32)
        eps_t = sb("eps", [128, 1], F32)
        const_b = fw.buf("consts")
        stg_t = [sb(f"stg{i}", [128, D], F32) for i in range(2)]
        stg_b = [fw.buf(f"stg{i}", dma=True) for i in range(2)]
        tA_t = [sb(f"tA{i}", [128, T], F32) for i in range(2)]
        tA_b = [fw.buf(f"tA{i}") for i in range(2)]
        sq_t = [sb(f"sq{i}", [128, 512], BF16) for i in range(4)]
        sq_b = [fw.buf(f"sq{i}") for i in range(4)]
        rstd_t = sb("rstd", [128, 512], F32)
        rstd_b = fw.buf("rstd")
        rstd1_t = sb("rstd1", [128, 512], F32)
        rstd1_b = fw.buf("rstd1")
        xn_t = [sb(f"xn{i}", [128, 512], F32) for i in range(4)]
        xn_b = [fw.buf(f"xn{i}") for i in range(4)]
        scond = sb("scond", [128, KC], BF16)
        scond_b = fw.buf("scond")
        row_t = [sb(f"row{i}", [1, 512], F32) for i in range(2)]
        row_b = [fw.buf(f"row{i}") for i in range(2)]
        mod_t = [sb(f"mod{l}", [128, 48], F32) for l in range(DEPTH)]
        mod_b = [fw.buf(f"mod{l}") for l in range(DEPTH)]
        col_t = [sb(f"cols{l}", [128, 16 + 2 * FC], F32) for l in range(DEPTH)]
        col_b = [fw.buf(f"cols{l}") for l in range(DEPTH)]
        identb = sb("identb_sb", [128, 128], BF16)
        identb_b = fw.buf("identb", dma=True)
        cs_t = sb("dftcs_sb", [128, 2, 512], BF16)
        cs_b = fw.buf("dftcs", dma=True)
        ropecs = sb("ropecs", [128, 2, T], F32)
        ropecs_b = fw.buf("ropecs", dma=True)
        sel_t = sb("sel", [128, 2, 128], F32)
        ckvst = sb("ckvst", [128, 8, 256], F32)
        ckvst_b = fw.buf("ckvst", dma=True)
        krst = sb("krst", [128, 8, 32], F32)
        krst_b = fw.buf("krst", dma=True)
        ckvall_b = [fw.buf(f"ckvall{c}", dma=True) for c in range(2)]
        kr_b = fw.buf("KR", dma=True)
        qt_b = [fw.buf(f"QT{i}", dma=True) for i in range(3)]
        kt_b = [fw.buf(f"KT{i}") for i in range(3)]
        vp_b = [fw.buf(f"VP{i}") for i in range(2)]
        pt_b = [fw.buf(f"PT{i}") for i in range(4)]
        cq_b = [fw.buf(f"cq{i}") for i in range(3)]
        mcol_t = sb("mcols", [128, 32], F32)
        mcol_b = fw.buf("mcols")
        slots = [sb(f"wslot{i}", [128, SLOT_ELEMS], BF16) for i in range(NSLOT)]
        slot_b = [fw.buf(f"wslot{i}", dma=True) for i in range(NSLOT)]
        ws = WStream(fw, POOL, slots, slot_b, SLOT_ELEMS)

        pd_t = [es.enter_context(nc.psum_tensor(f"pd{i}", [128, 1024], F32)) for i in range(4)]
        ps_b = [fw.buf(f"ps{i}") for i in range(8)]
        for b in ps_b:
            b.excl = True

        pdb_t = [t.bitcast(BF16) for t in pd_t]

        def ps(bank):
            return pd_t[bank // 2][:, (bank % 2) * 512:(bank % 2) * 512 + 512]

        def psb(bank):
            return pdb_t[bank // 2][:, (bank % 2) * 1024:(bank % 2) * 1024 + 1024]

        def pv(name, j=0, n=1):
            o, _ = PV[name]
            return pvec[:, o + j:o + j + n]

        def tsl(tt):
            return slice(tt * 512, (tt + 1) * 512)

        def emit():
            fw.dma(SP, pvec[:], pvec_d, dst=pvec_b)
            fw.dma(SP, ident[:], ident_d, dst=ident_b)
            fw.dma(SP, identb[:], identb_d, dst=identb_b)
            fw.dma(SP, cs_t[:], dftCS, dst=cs_b)
            fw.op(DVE, lambda e: e.memset(ones_bf[:], 1.0), writes=[const_b])
            fw.op(DVE, lambda e: e.memset(eps_t[:], EPS), writes=[const_b])
            fw.op(DVE, lambda e: e.memset(one_f[:], 1.0), writes=[const_b])
            fw.op(DVE, lambda e: e.memset(sel_t[:], 0.0), writes=[const_b])
            fw.op(DVE, lambda e: e.memset(sel_t[64:65, 0, :], 1.0), writes=[const_b])
            fw.op(DVE, lambda e: e.memset(sel_t[0:1, 1, :], 1.0), writes=[const_b])

            for i in range(8):
                s = i % 2
                tt = i // 4
                fw.dma(SP, stg_t[s][:], xin[i * 128:(i + 1) * 128, :], dst=stg_b[s])
                for half in range(2):
                    bank = (i * 2 + half) % 8

                    def mm(e, s=s, half=half, bank=bank):
                        r = None
                        for cc in range(4):
                            c = half * 4 + cc
                            r = e.transpose(ps(bank)[:, cc * 128:(cc + 1) * 128],
                                            stg_t[s][:, c * 128:(c + 1) * 128], ident[:])
                        return r
                    fw.op(PE, mm, reads=[stg_b[s], ident_b], writes=[ps_b[bank]])
                    wr = [xb[c][tt] for c in range(half * 4, half * 4 + 4)]
                    if half == 0:
                        fw.op(DVE, lambda e, half=half, bank=bank, i=i: e.tensor_copy(
                            out=x_t[:, half * 4:half * 4 + 4, i * 128:(i + 1) * 128],
                            in_=ps(bank).rearrange("p (c t) -> p c t", c=4)),
                            reads=[ps_b[bank]], writes=wr)
                    else:
                        fw.op(ACT, lambda e, half=half, bank=bank, i=i: e.activation(
                            out=x_t[:, half * 4:half * 4 + 4, i * 128:(i + 1) * 128],
                            in_=ps(bank).rearrange("p (c t) -> p c t", c=4), func=AF.Copy),
                            reads=[ps_b[bank]], writes=wr)

            fw.op(ACT, lambda e: e.activation(out=scond[:], in_=pv("cond", 0, KC), func=AF.Silu),
                  reads=[pvec_b], writes=[scond_b])

            sqn = [0]

            def rms_stats(srcs, nch, inv_n, bank, rt=None, rb=None):
                rt = rstd_t if rt is None else rt
                rb = rstd_b if rb is None else rb
                for c, (ap, bufs) in enumerate(srcs):
                    s = sqn[0] % 4
                    sqn[0] += 1
                    fw.op(ACT, lambda e, ap=ap, s=s: e.activation(out=sq_t[s][:], in_=ap,
                                                                   func=AF.Square),
                          reads=bufs, writes=[sq_b[s]])
                    fw.op(PE, lambda e, c=c, s=s: e.matmul(ps(bank), ones_bf[:], sq_t[s][:],
                                                           start=(c == 0), stop=(c == nch - 1)),
                          reads=[const_b, sq_b[s]], writes=[ps_b[bank]])
                rms_tail(inv_n, bank, rt, rb)

            def rms_tail(inv_n, bank, rt, rb):
                fw.op(ACT, lambda e: e.activation(out=rt[:], in_=ps(bank), func=AF.Ln,
                                                  bias=eps_t[:, 0:1], scale=inv_n),
                      reads=[ps_b[bank], const_b], writes=[rb])
                fw.op(ACT, lambda e: e.activation(out=rt[:], in_=rt[:], func=AF.Exp, scale=-0.5),
                      reads=[rb], writes=[rb])

            acc = {"pend": [], "cnt": [0, 0]}

            def acc_begin():
                acc["pend"] = []
                acc["cnt"] = [0, 0]

            def acc_x(oc, tt):
                assert fw.dry or len(acc["pend"]) < 4, len(acc["pend"])
                s = sqn[0] % 4
                sqn[0] += 1
                fw.op(ACT, lambda e, oc=oc, tt=tt, s=s: e.activation(out=sq_t[s][:], in_=x_t[:, oc, tsl(tt)],
                                                                       func=AF.Square),
                      reads=[xb[oc][tt]], writes=[sq_b[s]])
                acc["pend"].append((tt, s))

            def acc_pe(lag):
                while len(acc["pend"]) > lag:
                    tt, s = acc["pend"].pop(0)
                    c = acc["cnt"][tt]
                    acc["cnt"][tt] += 1
                    fw.op(PE, lambda e, c=c, s=s, tt=tt: e.matmul(ps(6 + tt), ones_bf[:], sq_t[s][:],
                                                                 start=(c == 0), stop=(c == KC - 1)),
                          reads=[const_b, sq_b[s]], writes=[ps_b[6 + tt]])

            def norm_mod(l, which, pre=False):
                ao = 0 if which == 1 else 8
                bo = 0 if which == 1 else 24
                rts = [(rstd_t, rstd_b), (rstd1_t, rstd1_b)]
                for tt in range(2):
                    if pre:
                        rms_tail(1.0 / D, 6 + tt, rts[tt][0], rts[tt][1])
                    else:
                        rms_stats([(x_t[:, c, tsl(tt)], [xb[c][tt]]) for c in range(KC)], KC, 1.0 / D,
                                  bank=tt, rt=rts[tt][0], rb=rts[tt][1])
                n = 0
                for tt in range(2):
                    rt, rb = rts[tt]
                    for c in range(KC):
                        s = n % 4
                        n += 1
                        fw.op(DVE, lambda e, c=c, s=s, tt=tt, rt=rt: e.tensor_tensor(
                            out=xn_t[s][:], in0=x_t[:, c, tsl(tt)], in1=rt[:], op=ALU.mult),
                            reads=[xb[c][tt], rb], writes=[xn_b[s]])
                        if c % 2 == 0:
                            fw.op(ACT, lambda e, c=c, s=s, tt=tt: e.activation(
                                out=h_t[:, c, tsl(tt)], in_=xn_t[s][:], func=AF.Identity,
                                bias=mod_t[l][:, bo + c:bo + c + 1],
                                scale=col_t[l][:, ao + c:ao + c + 1]),
                                reads=[xn_b[s], mod_b[l], col_b[l]], writes=[hb[c][tt]])
                        else:
                            fw.op(DVE, lambda e, c=c, s=s, tt=tt: e.tensor_scalar(
                                out=h_t[:, c, tsl(tt)], in0=xn_t[s][:],
                                scalar1=col_t[l][:, ao + c:ao + c + 1],
                                scalar2=mod_t[l][:, bo + c:bo + c + 1], op0=ALU.mult, op1=ALU.add),
                                reads=[xn_b[s], mod_b[l], col_b[l]], writes=[hb[c][tt]])

            def ada_block(l, nb, b0=0, b1=1):
                wv = ada_w[l].rearrange("(k p) n -> p k n", p=128)
                w, wb, wk = ws.next(wv[:, :, nb * 512:(nb + 1) * 512], (KC, 512))

                def mm(e, w=w):
                    r = None
                    for k in range(KC):
                        r = e.matmul(ps(b0)[0:1, :], scond[:, k:k + 1], w[:, k, :],
                                     start=(k == 0), stop=(k == KC - 1))
                    return r
                fw.op(PE, mm, reads=[scond_b, wb], writes=[ps_b[b0]])
                ws.done(wk)
                s = nb % 2
                fw.op(ACT, lambda e, s=s: e.activation(out=row_t[s][:], in_=ps(b0)[0:1, :], func=AF.Copy),
                      reads=[ps_b[b0]], writes=[row_b[s]])

                def mt(e, s=s):
                    r = None
                    for j in range(4):
                        r = e.matmul(ps(b1)[:, j:j + 1], row_t[s][0:1, j * 128:(j + 1) * 128],
                                     one_f[0:1, 0:1], start=True, stop=True)
                    return r
                fw.op(PE, mt, reads=[row_b[s], const_b], writes=[ps_b[b1]])
                fw.op(DVE, lambda e: e.tensor_tensor(out=mod_t[l][:, nb * 4:nb * 4 + 4], in0=ps(b1)[:, 0:4],
                                                     in1=pv(f"adab{l}", nb * 4, 4), op=ALU.add),
                      reads=[ps_b[b1], pvec_b], writes=[mod_b[l]])

            def ada_finish1(l):
                fw.op(DVE, lambda e: e.scalar_tensor_tensor(
                    out=col_t[l][:, 0:8], in0=mod_t[l][:, 8:16], scalar=1.0,
                    in1=pv(f"n1w{l}", 0, KC), op0=ALU.add, op1=ALU.mult),
                    reads=[mod_b[l], pvec_b], writes=[col_b[l]])
                fw.op(DVE, lambda e: e.tensor_scalar(
                    out=col_t[l][:, 16:16 + FC], in0=pv(f"fw0_{l}", 0, FC),
                    scalar1=pv("flagneg"), scalar2=None, op0=ALU.mult),
                    reads=[pvec_b], writes=[col_b[l]])
                fw.op(DVE, lambda e: e.tensor_scalar(
                    out=col_t[l][:, 16 + FC:16 + 2 * FC], in0=pv(f"fw2_{l}", 0, FC),
                    scalar1=pv("flagneg"), scalar2=None, op0=ALU.mult),
                    reads=[pvec_b], writes=[col_b[l]])


            def ada_finish2(l):
                fw.op(DVE, lambda e: e.scalar_tensor_tensor(
                    out=col_t[l][:, 8:16], in0=mod_t[l][:, 32:40], scalar=1.0,
                    in1=pv(f"n2w{l}", 0, KC), op0=ALU.add, op1=ALU.mult),
                    reads=[mod_b[l], pvec_b], writes=[col_b[l]])

            def ada_finish(l):
                ada_finish1(l)
                ada_finish2(l)

            def conv_fix(eng, t_ap, src_ap, nf0, nf2, reads, writes):
                fw.op(eng, lambda e: e.scalar_tensor_tensor(
                    out=t_ap[:, 256:1024:256], in0=src_ap[:, 255:1023:256], scalar=nf0,
                    in1=t_ap[:, 256:1024:256], op0=ALU.mult, op1=ALU.add),
                    reads=reads, writes=writes)
                fw.op(eng, lambda e: e.scalar_tensor_tensor(
                    out=t_ap[:, 255:1023:256], in0=src_ap[:, 256:1024:256], scalar=nf2,
                    in1=t_ap[:, 255:1023:256], op0=ALU.mult, op1=ALU.add),
                    reads=reads, writes=writes)

            def ffn(l, mid_hook=None):
                upv = ffn_up[l].rearrange("(k p) n -> p k n", p=128)
                dnv = ffn_down[l].rearrange("(k p) n -> p k n", p=128)
                j = 0
                for jg in range(6):
                    ncol = 512 if jg < 5 else 256
                    gw, gwb, gk = ws.next(upv[:, :, jg * 512:jg * 512 + ncol], (KC, ncol))
                    uw, uwb, uk = ws.next(upv[:, :, DFF + jg * 512:DFF + jg * 512 + ncol],
                                          (KC, ncol))
                    for jj in range(ncol // 128):
                        dbl = 2 * (j % 2)
                        for which, (w, wb) in enumerate(((gw, gwb), (uw, uwb))):
                            for tt in range(2):
                                bank = (dbl + which) * 2 + tt

                                def mm(e, w=w, jj=jj, tt=tt, bank=bank):
                                    r = None
                                    for k in range(KC):
                                        r = e.matmul(ps(bank), w[:, k, jj * 128:(jj + 1) * 128],
                                                     h_t[:, k, tsl(tt)],
                                                     start=(k == 0), stop=(k == KC - 1))
                                    return r
                                fw.op(PE, mm, reads=[wb] + [hb[k][tt] for k in range(KC)],
                                      writes=[ps_b[bank]])
                        g_ap = pd_t[dbl][:]
                        u_ap = pd_t[dbl + 1][:]
                        gB = [ps_b[dbl * 2], ps_b[dbl * 2 + 1]]
                        uB = [ps_b[dbl * 2 + 2], ps_b[dbl * 2 + 3]]
                        s = j % 2
                        t = tA_t[s]
                        tb = tA_b[s]
                        fw.op(ACT, lambda e, t=t, g_ap=g_ap, j=j: e.activation(
                            out=t[:], in_=g_ap, func=AF.Identity, bias=pv(f"fb_{l}", j),
                            scale=pv(f"fw1_{l}", j)),
                            reads=gB + [pvec_b], writes=[tb])
                        fw.op(DVE, lambda e, t=t, g_ap=g_ap, j=j: e.scalar_tensor_tensor(
                            out=t[:, 1:T], in0=g_ap[:, 0:T - 1], scalar=pv(f"fw0_{l}", j),
                            in1=t[:, 1:T], op0=ALU.mult, op1=ALU.add),
                            reads=gB + [pvec_b, tb], writes=[tb])
                        fw.op(DVE, lambda e, t=t, g_ap=g_ap, j=j: e.scalar_tensor_tensor(
                            out=t[:, 0:T - 1], in0=g_ap[:, 1:T], scalar=pv(f"fw2_{l}", j),
                            in1=t[:, 0:T - 1], op0=ALU.mult, op1=ALU.add),
                            reads=gB + [pvec_b, tb], writes=[tb])
                        conv_fix(DVE, t, g_ap, col_t[l][:, 16 + j:17 + j],
                                 col_t[l][:, 16 + FC + j:17 + FC + j],
                                 reads=gB + [col_b[l], tb], writes=[tb])
                        fw.op(ACT, lambda e, t=t: e.activation(out=t[:], in_=t[:], func=AF.Gelu),
                              reads=[tb], writes=[tb])
                        fw.op(DVE, lambda e, t=t, u_ap=u_ap, j=j: e.tensor_tensor(
                            out=a_t[:, j, :], in0=t[:], in1=u_ap, op=ALU.mult),
                            reads=[tb] + uB, writes=[ab[j]])
                        j += 1
                    ws.done(gk)
                    ws.done(uk)
                    if mid_hook is not None:
                        mid_hook(jg)
                acc_begin()
                for oc in range(KC):
                    w, wb, wk = ws.next(dnv[:, :, oc * 128:(oc + 1) * 128], (FC, 128))
                    for tt in range(2):
                        bank = (oc * 2 + tt) % 6

                        def mm(e, w=w, tt=tt, bank=bank):
                            r = None
                            for k in range(FC):
                                r = e.matmul(ps(bank), w[:, k, :], a_t[:, k, tsl(tt)],
                                             start=(k == 0), stop=(k == FC - 1))
                            return r
                        fw.op(PE, mm, reads=[wb] + ab[0:FC], writes=[ps_b[bank]])
                        acc_pe(1)
                        fw.op(DVE, lambda e, oc=oc, tt=tt, bank=bank: e.scalar_tensor_tensor(
                            out=x_t[:, oc, tsl(tt)], in0=ps(bank), scalar=mod_t[l][:, 40 + oc:41 + oc],
                            in1=x_t[:, oc, tsl(tt)], op0=ALU.mult, op1=ALU.add),
                            reads=[ps_b[bank], mod_b[l], xb[oc][tt]], writes=[xb[oc][tt]])
                        acc_x(oc, tt)
                    ws.done(wk)
                acc_pe(0)


            def evac(i, out_ap, in_ap, reads, writes):
                if i % 2 == 0:
                    fw.op(DVE, lambda e: e.tensor_copy(out=out_ap, in_=in_ap), reads=reads, writes=writes)
                else:
                    fw.op(ACT, lambda e: e.activation(out=out_ap, in_=in_ap, func=AF.Copy),
                          reads=reads, writes=writes)

            def out_linear(l, wdram, src_ap, src_bufs, gcol):
                wv = wdram.rearrange("(k p) n -> p k n", p=128)
                n = 0
                acc_begin()
                for nb in range(2):
                    w, wb, wk = ws.next(wv[:, :, nb * 512:(nb + 1) * 512], (KC, 512))
                    for o4 in range(4):
                        oc = nb * 4 + o4
                        for tt in range(2):
                            bank = n % 6
                            n += 1

                            def mm(e, w=w, o4=o4, tt=tt, bank=bank):
                                r = None
                                for kk in range(KC):
                                    r = e.matmul(ps(bank), w[:, kk, o4 * 128:(o4 + 1) * 128],
                                                 src_ap(kk, tt), start=(kk == 0), stop=(kk == KC - 1))
                                return r
                            fw.op(PE, mm, reads=[wb] + src_bufs(tt), writes=[ps_b[bank]])
                            acc_pe(2)
                            fw.op(DVE, lambda e, oc=oc, tt=tt, bank=bank: e.scalar_tensor_tensor(
                                out=x_t[:, oc, tsl(tt)], in0=ps(bank), scalar=gcol(oc),
                                in1=x_t[:, oc, tsl(tt)], op0=ALU.mult, op1=ALU.add),
                                reads=[ps_b[bank], mod_b[l], mcol_b, xb[oc][tt]], writes=[xb[oc][tt]])
                            acc_x(oc, tt)
                    ws.done(wk)
                acc_pe(0)

            def mixer_pool(l):
                fw.barrier_bufs(ab)
                fw.op(DVE, lambda e: e.tensor_tensor(out=mcol_t[:, 0:8], in0=pv("pscale", 0, KC),
                                                     in1=mod_t[l][:, 16:24], op=ALU.mult),
                      reads=[pvec_b, mod_b[l]], writes=[mcol_b])
                for i in range(8):
                    bank = i % 8

                    def tr(e, i=i, bank=bank):
                        r = None
                        for c in range(KC):
                            r = e.transpose(psb(bank)[:, c * 128:(c + 1) * 128],
                                            h_t[:, c, i * 128:(i + 1) * 128], identb[:])
                        return r
                    fw.op(PE, tr, reads=[hb[c][i // 4] for c in range(KC)] + [identb_b],
                          writes=[ps_b[bank]])
                    evac(i, a_t[:, i, :], psb(bank), [ps_b[bank]], [ab[i]])
                aw = ab_k = None
                for cc in range(8):
                    g = cc // 2
                    if cc % 2 == 0:
                        aw, awb, ak = ws.next(poolA[:, g, :].rearrange("p (a b c) -> p a b c", a=8, b=3, c=128), (8, 3, 128))
                    dbl = cc % 4

                    def mm(e, cc=cc, aw=aw, dbl=dbl):
                        r = None
                        for i in range(8):
                            ds = [d for d in range(3) if 0 <= i + d - 1 < 8]
                            for n, d in enumerate(ds):
                                r = e.matmul(pd_t[dbl][:, i * 128:(i + 1) * 128],
                                             a_t[:, i + d - 1, cc * 128:(cc + 1) * 128],
                                             aw[:, i, d, :], start=(n == 0), stop=(n == len(ds) - 1))
                        return r
                    fw.op(PE, mm, reads=ab[0:8] + [awb], writes=[ps_b[2 * dbl], ps_b[2 * dbl + 1]])
                    evac(cc, a_t[:, 8 + cc, :], pd_t[dbl][:], [ps_b[2 * dbl], ps_b[2 * dbl + 1]],
                         [ab[8 + cc]])
                    if cc % 2 == 1:
                        ws.done(ak)
                pw, pwb, pk = ws.next(pool_w.rearrange("g (kk p) d -> p g kk d", p=128), (4, 2, 256))
                n = 0
                acc_begin()
                for dc in range(8):
                    g = dc // 2
                    for tt in range(2):
                        bank = n % 6
                        n += 1

                        def mm2(e, dc=dc, g=g, tt=tt, bank=bank):
                            r = None
                            for kk in range(2):
                                r = e.matmul(ps(bank), pw[:, g, kk, (dc % 2) * 128:(dc % 2) * 128 + 128],
                                             a_t[:, 8 + g * 2 + kk, tsl(tt)], start=(kk == 0), stop=(kk == 1))
                            return r
                        fw.op(PE, mm2, reads=[pwb, ab[8 + g * 2], ab[9 + g * 2]], writes=[ps_b[bank]])
                        acc_pe(3)
                        fw.op(DVE, lambda e, dc=dc, tt=tt, bank=bank: e.scalar_tensor_tensor(
                            out=x_t[:, dc, tsl(tt)], in0=ps(bank), scalar=mcol_t[:, dc:dc + 1],
                            in1=x_t[:, dc, tsl(tt)], op0=ALU.mult, op1=ALU.add),
                            reads=[ps_b[bank], mcol_b, xb[dc][tt]], writes=[xb[dc][tt]])
                        acc_x(dc, tt)
                acc_pe(0)
                ws.done(pk)

            def mixer_fnet(l):
                fw.barrier_bufs(ab)

                def pq(i, g):
                    return a_t[:, 2 * i + g // 2, (g % 2) * 512:(g % 2) * 512 + 512]
                n = 0
                for i in range(8):
                    for g in range(4):
                        bank = n % 8

                        def mm(e, i=i, g=g, bank=bank):
                            r = None
                            for kk in range(2):
                                r = e.matmul(ps(bank), h_t[:, g * 2 + kk, i * 128:(i + 1) * 128],
                                             cs_t[:, kk, :], start=(kk == 0), stop=(kk == 1))
                            return r
                        fw.op(PE, mm, reads=[hb[g * 2][i // 4], hb[g * 2 + 1][i // 4], cs_b],
                              writes=[ps_b[bank]])
                        evac(n, pq(i, g), ps(bank), [ps_b[bank]], [ab[2 * i + g // 2]])
                        n += 1
                for tt in range(2):
                    cw, cwb, ck = ws.next(dftC[:, :, tsl(tt)], (8, 512))
                    sw, swb, sk = ws.next(dftS[:, :, tsl(tt)], (8, 512))
                    for mc in range(8):
                        g, m2 = mc // 2, mc % 2
                        bank = n % 8

                        def mm(e, g=g, m2=m2, bank=bank, cw=cw, sw=sw):
                            r = None
                            for i in range(8):
                                r = e.matmul(ps(bank), pq(i, g)[:, m2 * 128:(m2 + 1) * 128], cw[:, i, :],
                                             start=(i == 0), stop=False)
                                r = e.matmul(ps(bank), pq(i, g)[:, 256 + m2 * 128:256 + (m2 + 1) * 128],
                                             sw[:, i, :], start=False, stop=(i == 7))
                            return r
                        fw.op(PE, mm, reads=ab[0:16] + [cwb, swb], writes=[ps_b[bank]])
                        evac(n, a_t[:, 16 + mc, tsl(tt)], ps(bank), [ps_b[bank]], [ab[16 + mc]])
                        n += 1
                    ws.done(ck)
                    ws.done(sk)
                out_linear(l, fnet_w, lambda kk, tt: a_t[:, 16 + kk, tsl(tt)],
                           lambda tt: ab[16:24], lambda oc: mod_t[l][:, 16 + oc:17 + oc])

            def mixer_sconv(l):
                fw.barrier_bufs(ab)
                fw.op(DVE, lambda e: e.tensor_scalar(out=mcol_t[:, 8:16], in0=pv("sw0", 0, KC),
                                                     scalar1=pv("flagneg"), scalar2=None, op0=ALU.mult),
                      reads=[pvec_b], writes=[mcol_b])
                fw.op(DVE, lambda e: e.tensor_scalar(out=mcol_t[:, 16:24], in0=pv("sw2", 0, KC),
                                                     scalar1=pv("flagneg"), scalar2=None, op0=ALU.mult),
                      reads=[pvec_b], writes=[mcol_b])
                wv = sconv_win.rearrange("(k p) n -> p k n", p=128)
                nd = 0
                for jg in range(2):
                    wl = [ws.next(wv[:, :, part * D + jg * 512:part * D + jg * 512 + 512], (KC, 512))
                          for part in range(3)]
                    for jj in range(4):
                        j = jg * 4 + jj
                        dbls = []
                        dbls = [None, None, None]
                        for part in (2, 1, 0):
                            dbl = nd % 4
                            nd += 1
                            dbls[part] = dbl
                            w, wb, _ = wl[part]
                            for tt in range(2):
                                bank = 2 * dbl + tt

                                def mm(e, w=w, jj=jj, tt=tt, bank=bank):
                                    r = None
                                    for kk in range(KC):
                                        r = e.matmul(ps(bank), w[:, kk, jj * 128:(jj + 1) * 128],
                                                     h_t[:, kk, tsl(tt)], start=(kk == 0), stop=(kk == KC - 1))
                                    return r
                                fw.op(PE, mm, reads=[wb] + [hb[kk][tt] for kk in range(KC)],
                                      writes=[ps_b[bank]])
                        gbB = [ps_b[2 * dbls[0]], ps_b[2 * dbls[0] + 1]]
                        gcB = [ps_b[2 * dbls[1]], ps_b[2 * dbls[1] + 1]]
                        uB = [ps_b[2 * dbls[2]], ps_b[2 * dbls[2] + 1]]
                        s = j % 2
                        v, vb = tA_t[s], tA_b[s]
                        t2, t2b = stg_t[s], stg_b[s]
                        fw.op(ACT, lambda e, v=v, d=dbls[2]: e.activation(out=v[:], in_=pd_t[d][:], func=AF.Copy),
                              reads=uB, writes=[vb])
                        fw.op(DVE, lambda e, v=v, d=dbls[1]: e.tensor_tensor(out=v[:], in0=pd_t[d][:], in1=v[:],
                                                                             op=ALU.mult),
                              reads=gcB + [vb], writes=[vb])
                        fw.op(ACT, lambda e, v=v, t2=t2, j=j: e.activation(out=t2[:], in_=v[:], func=AF.Copy,
                                                                             scale=pv("sw1", j)),
                              reads=[vb, pvec_b], writes=[t2b])
                        fw.op(DVE, lambda e, v=v, t2=t2, j=j: e.scalar_tensor_tensor(
                            out=t2[:, 1:T], in0=v[:, 0:T - 1], scalar=pv("sw0", j), in1=t2[:, 1:T],
                            op0=ALU.mult, op1=ALU.add), reads=[vb, pvec_b, t2b], writes=[t2b])
                        fw.op(DVE, lambda e, v=v, t2=t2, j=j: e.scalar_tensor_tensor(
                            out=t2[:, 0:T - 1], in0=v[:, 1:T], scalar=pv("sw2", j), in1=t2[:, 0:T - 1],
                            op0=ALU.mult, op1=ALU.add), reads=[vb, pvec_b, t2b], writes=[t2b])
                        conv_fix(DVE, t2, v, mcol_t[:, 8 + j:9 + j], mcol_t[:, 16 + j:17 + j],
                                 reads=[vb, mcol_b, t2b], writes=[t2b])
                        fw.op(DVE, lambda e, t2=t2, j=j, d=dbls[0]: e.tensor_tensor(
                            out=a_t[:, j, :], in0=pd_t[d][:], in1=t2[:], op=ALU.mult),
                            reads=gbB + [t2b], writes=[ab[j]])
                    for _, _, wk in wl:
                        ws.done(wk)
                out_linear(l, sconv_wout, lambda kk, tt: a_t[:, kk, tsl(tt)],
                           lambda tt: ab[0:8], lambda oc: mod_t[l][:, 16 + oc:17 + oc])


            def mixer_mla(l):
                SC = 1.0 / float(np.sqrt(96.0))
                arena = a_t[:].rearrange("p a b -> p (a b)")
                cqT = lambda c: a_t[:, c, :]
                ckvall = lambda c: arena[:, 3 * T + c * 1536:3 * T + (c + 1) * 1536]
                KR = arena[:, 6 * T:6 * T + 1536]
                QT = lambda r: a_t[:, 8 + r, :]
                KT = lambda r: arena[:, (11 + 2 * r) * T:(11 + 2 * r) * T + 1536]
                VP = lambda r: arena[:, (17 + 3 * r) * T:(17 + 3 * r) * T + 3072].rearrange(
                    "p (k x) -> p k x", k=12, x=256)
                PT = lambda r: arena[:, 23 * T + r * 512:23 * T + (r + 1) * 512]
                for i in range(2):
                    rr = fw.dma(SP, ropecs[:, i, :], ropeCS_d[:, i, :], dst=ropecs_b)
                    if MLA_SUB == 0.11 and not fw.dry:
                        fw.out_recs.append(rr)
                for c in range(2):
                    fw.dma(POOL, ckvall(c)[:, 0:512], cacheT_d[:, c, :], dst=ckvall_b[c])
                fw.dma(POOL, KR, krmask_d, dst=kr_b)
                for r in range(3):
                    fw.dma(POOL, QT(r), eq_d, dst=qt_b[r])
                for r in range(2):
                    fw.op(DVE, lambda e, r=r: e.memset(VP(r), 0.0), writes=[vp_b[r]])
                    fw.op(DVE, lambda e, r=r: e.memset(VP(r)[:, :, 64:65], 1.0), writes=[vp_b[r]])
                    fw.op(DVE, lambda e, r=r: e.memset(VP(r)[:, :, 128:129], 1.0), writes=[vp_b[r]])

                if MLA_SUB < 0.2:
                    return
                w, wb, wk = ws.next(wdq.rearrange("(k p) n -> p k n", p=128), (KC, 384))
                for tt in range(2):
                    banks = [(4 * tt + c) % 8 for c in range(3)]
                    for c in range(3):
                        def mm(e, c=c, tt=tt, bank=banks[c], w=w):
                            r = None
                            for kk in range(KC):
                                r = e.matmul(ps(bank), w[:, kk, c * 128:(c + 1) * 128], h_t[:, kk, tsl(tt)],
                                             start=(kk == 0), stop=(kk == KC - 1))
                            return r
                        fw.op(PE, mm, reads=[wb] + [hb[kk][tt] for kk in range(KC)], writes=[ps_b[banks[c]]])
                    rms_stats([(ps(banks[c]), [ps_b[banks[c]]]) for c in range(3)], 3, 1.0 / 384,
                              bank=(4 * tt + 3) % 8)
                    for c in range(3):
                        s = c % 2
                        fw.op(DVE, lambda e, s=s, bank=banks[c]: e.tensor_tensor(
                            out=xn_t[s][:], in0=ps(bank), in1=rstd_t[:], op=ALU.mult),
                            reads=[ps_b[banks[c]], rstd_b], writes=[xn_b[s]])
                        fw.op(ACT, lambda e, s=s, c=c, tt=tt: e.activation(
                            out=cqT(c)[:, tsl(tt)], in_=xn_t[s][:], func=AF.Copy, scale=pv("qnw", c)),
                            reads=[xn_b[s], pvec_b], writes=[cq_b[c]])
                ws.done(wk)

                if MLA_SUB < 0.5:
                    return
                w, wb, wk = ws.next(wdkv_aug.rearrange("(k p) n -> p k n", p=128), (KC, 448))
                for tt in range(2):
                    banks = [(5 * tt + i) % 8 for i in range(4)]
                    cols = [(0, 128), (128, 128), (256, 96), (352, 96)]
                    for i in range(4):
                        c0, m = cols[i]

                        def mm(e, c0=c0, m=m, tt=tt, bank=banks[i], w=w):
                            r = None
                            for kk in range(KC):
                                r = e.matmul(ps(bank)[0:m, :], w[:, kk, c0:c0 + m], h_t[:, kk, tsl(tt)],
                                             start=(kk == 0), stop=(kk == KC - 1))
                            return r
                        fw.op(PE, mm, reads=[wb] + [hb[kk][tt] for kk in range(KC)], writes=[ps_b[banks[i]]])
                    if MLA_SUB < 0.51:
                        continue
                    rms_stats([(ps(banks[c]), [ps_b[banks[c]]]) for c in range(2)], 2, 1.0 / 256,
                              bank=(5 * tt + 4) % 8)
                    if MLA_SUB < 0.52:
                        continue
                    for c in range(2):
                        s = c % 2
                        fw.op(DVE, lambda e, s=s, bank=banks[c]: e.tensor_tensor(
                            out=xn_t[s][:], in0=ps(bank), in1=rstd_t[:], op=ALU.mult),
                            reads=[ps_b[banks[c]], rstd_b], writes=[xn_b[s]])
                        fw.op(ACT, lambda e, s=s, c=c, tt=tt: e.activation(
                            out=ckvall(c)[:, 512 + tt * 512:1024 + tt * 512], in_=xn_t[s][:], func=AF.Copy,
                            scale=pv("kvnw", c)),
                            reads=[xn_b[s], pvec_b], writes=[ckvall_b[c]])
                        fw.op(DVE, lambda e, s=s, c=c, tt=tt: e.tensor_scalar(
                            out=tA_t[c][:, tsl(tt)], in0=xn_t[s][:], scalar1=pv("kvnw", c), scalar2=None,
                            op0=ALU.mult),
                            reads=[xn_b[s], pvec_b], writes=[tA_b[c]])
                    if MLA_SUB < 0.55:
                        continue
                    bA, bB = banks[2], banks[3]
                    fw.op(ACT, lambda e, tt=tt, bA=bA: e.activation(
                        out=stg_t[0][64:96, tsl(tt)], in_=ps(bA)[64:96, :], func=AF.Copy),
                        reads=[ps_b[bA]], writes=[stg_b[0]])
                    if MLA_SUB < 0.56:
                        continue
                    fw.op(DVE, lambda e, tt=tt, bA=bA: e.tensor_tensor(
                        out=xn_t[0][64:96, :], in0=ps(bA)[64:96, :], in1=ropecs[64:96, 0, tsl(tt)], op=ALU.mult),
                        reads=[ps_b[bA], ropecs_b], writes=[xn_b[0]])
                    if MLA_SUB < 0.57:
                        continue
                    fw.op(DVE, lambda e, tt=tt, bB=bB: e.tensor_tensor(
                        out=xn_t[1][64:96, :], in0=ps(bB)[64:96, :], in1=ropecs[64:96, 1, tsl(tt)], op=ALU.mult),
                        reads=[ps_b[bB], ropecs_b], writes=[xn_b[1]])
                    if MLA_SUB < 0.58:
                        continue
                    fw.op(DVE, lambda e, tt=tt: e.tensor_tensor(
                        out=KR[64:96, 512 + tt * 512:1024 + tt * 512], in0=xn_t[0][64:96, :],
                        in1=xn_t[1][64:96, :], op=ALU.add),
                        reads=[xn_b[0], xn_b[1]], writes=[kr_b])
                ws.done(wk)

                for nb in range(4, 12):
                    ada_block(0, nb, 2 + nb % 2, 4 + nb % 2)
                ada_finish2(0)
                for i in range(8):
                    bank = i % 8

                    def tr(e, i=i, bank=bank):
                        r = None
                        for c in range(2):
                            r = e.transpose(ps(bank)[:, c * 128:(c + 1) * 128],
                                            tA_t[c][:, i * 128:(i + 1) * 128], ident[:])
                        r = e.transpose(ps(bank)[:, 256:384], stg_t[0][:, i * 128:(i + 1) * 128], ident[:])
                        return r
                    fw.op(PE, tr, reads=[tA_b[0], tA_b[1], stg_b[0], ident_b], writes=[ps_b[bank]])
                    fw.op(DVE, lambda e, i=i, bank=bank: e.tensor_copy(out=ckvst[:, i, :], in_=ps(bank)[:, 0:256]),
                          reads=[ps_b[bank]], writes=[ckvst_b])
                    fw.op(ACT, lambda e, i=i, bank=bank: e.activation(out=krst[:, i, :], in_=ps(bank)[:, 320:352],
                                                                     func=AF.Copy),
                          reads=[ps_b[bank]], writes=[krst_b])
                fw.dma(SP, ckv_o.rearrange("(i p) c -> p i c", p=128), ckvst[:], dst=None, src=[ckvst_b])
                fw.dma(SP, kr_o.rearrange("(i p) c -> p i c", p=128), krst[:], dst=None, src=[krst_b])

                for r in range(3):
                    fw.op(DVE, lambda e, r=r: e.tensor_copy(out=KT(r)[64:128, :], in_=KR[64:128, :]),
                          reads=[kr_b], writes=[kt_b[r]])

                fw.op(DVE, lambda e: e.memset(stg_t[1][:], 0.0), writes=[stg_b[1]])
                if MLA_SUB < 2:
                    return
                ukv, ukvb, ukvk = ws.next(wukv.rearrange("(k p) n -> p k n", p=128), (2, 2048))
                wqv = wuq_aug.rearrange("(k p) n -> p k n", p=128)
                st = {'qw': None, 'n_o': 0, 'n_s': 0, 'n_p': 0}

                def prep_V(p):
                    vr = p % 2
                    for ktg in range(3):
                        bank = 6 + ktg % 2

                        def mmv(e, ktg=ktg, bank=bank, p=p):
                            r = None
                            for j in range(4):
                                kt = ktg * 4 + j
                                for kc in range(2):
                                    rhs = ukv[:, kc, p * 256:(p + 1) * 256].rearrange("p (h x) -> p h x", h=2)[:, :, 64:128]
                                    r = e.matmul(ps(bank)[:, j * 128:(j + 1) * 128].rearrange("p (h x) -> p h x", h=2),
                                                 ckvall(kc)[:, kt * 128:(kt + 1) * 128], rhs,
                                                 start=(kc == 0), stop=(kc == 1))
                            return r
                        fw.op(PE, mmv, reads=[ukvb, ckvall_b[0], ckvall_b[1]], writes=[ps_b[bank]])
                        src = ps(bank).rearrange("p (j h x) -> p j h x", j=4, h=2, x=64)
                        fw.op(DVE, lambda e, src=src, ktg=ktg, vr=vr: e.tensor_copy(
                            out=VP(vr)[:, ktg * 4:ktg * 4 + 4, 0:64], in_=src[:, :, 0, :]),
                            reads=[ps_b[bank]], writes=[vp_b[vr]])
                        fw.op(ACT, lambda e, src=src, ktg=ktg, vr=vr: e.activation(
                            out=VP(vr)[:, ktg * 4:ktg * 4 + 4, 192:256], in_=src[:, :, 1, :], func=AF.Copy),
                            reads=[ps_b[bank]], writes=[vp_b[vr]])

                def prep_KQ(h):
                    r3 = h % 3
                    if h % 4 == 0:
                        if st['qw'] is not None:
                            ws.done(st['qw'][2])
                        st['qw'] = ws.next(wqv[:, :, (h // 4) * 768:(h // 4 + 1) * 768], (3, 768))
                    qw = st['qw']
                    for kt5 in range(3):
                        bank = 6 + kt5 % 2

                        def mmk(e, h=h, kt5=kt5, bank=bank):
                            r = None
                            for kc in range(2):
                                r = e.matmul(ps(bank)[0:64, :], ukv[:, kc, h * 128:h * 128 + 64],
                                             ckvall(kc)[:, kt5 * 512:(kt5 + 1) * 512],
                                             start=(kc == 0), stop=(kc == 1))
                            return r
                        fw.op(PE, mmk, reads=[ukvb, ckvall_b[0], ckvall_b[1]], writes=[ps_b[bank]])
                        evac(kt5, KT(r3)[0:64, kt5 * 512:(kt5 + 1) * 512], ps(bank)[0:64, :],
                             [ps_b[bank]], [kt_b[r3]])
                    qcol = (h % 4) * 192
                    for tt in range(2):
                        for which in range(2):
                            bank = 6 + which

                            def mmq(e, which=which, tt=tt, bank=bank, qcol=qcol, qwv=qw[0]):
                                r = None
                                for kc in range(3):
                                    r = e.matmul(ps(bank)[0:96, :],
                                                 qwv[:, kc, qcol + which * 96:qcol + which * 96 + 96],
                                                 cqT(kc)[:, tsl(tt)], start=(kc == 0), stop=(kc == 2))
                                return r
                            fw.op(PE, mmq, reads=[qw[1]] + cq_b, writes=[ps_b[bank]])
                        fw.op(ACT, lambda e, r3=r3, tt=tt: e.activation(
                            out=QT(r3)[0:64, tsl(tt)], in_=ps(6)[0:64, :], func=AF.Copy),
                            reads=[ps_b[6]], writes=[qt_b[r3]])
                        fw.op(DVE, lambda e, tt=tt: e.tensor_tensor(
                            out=xn_t[0][64:96, :], in0=ps(6)[64:96, :], in1=ropecs[64:96, 0, tsl(tt)],
                            op=ALU.mult), reads=[ps_b[6], ropecs_b], writes=[xn_b[0]])
                        fw.op(DVE, lambda e, tt=tt: e.tensor_tensor(
                            out=xn_t[1][64:96, :], in0=ps(7)[64:96, :], in1=ropecs[64:96, 1, tsl(tt)],
                            op=ALU.mult), reads=[ps_b[7], ropecs_b], writes=[xn_b[1]])
                        fw.op(DVE, lambda e, r3=r3, tt=tt: e.tensor_tensor(
                            out=QT(r3)[64:96, tsl(tt)], in0=xn_t[0][64:96, :], in1=xn_t[1][64:96, :],
                            op=ALU.add), reads=[xn_b[0], xn_b[1]], writes=[qt_b[r3]])

                def attend(h, tts, pending=None):
                    p, hh = h // 2, h % 2
                    vr = p % 2
                    r3 = h % 3
                    if hh == 0:
                        vcols, orows, srow, om = (0, 65), (0, 64), 64, 65
                    else:
                        vcols, orows, srow, om = (128, 256), (64, 128), 0, 128
                    for tt in (tts if MLA_SUB >= 3 else ()):
                        ob = 2 + st['n_o'] % 3
                        st['n_o'] += 1
                        pend = []
                        for kt in range(14):
                            if kt < 12:
                                sb_ = (0, 1, 5)[st['n_s'] % 3]
                                st['n_s'] += 1
                                fw.op(PE, lambda e, sb_=sb_, kt=kt, r3=r3, tt=tt: e.matmul(
                                    ps(sb_), KT(r3)[:, kt * 128:(kt + 1) * 128], QT(r3)[:, tsl(tt)],
                                    start=True, stop=True),
                                    reads=[kt_b[r3], qt_b[r3]], writes=[ps_b[sb_]])
                                pi = st['n_p'] % 4
                                st['n_p'] += 1
                                fw.op(ACT, lambda e, sb_=sb_, pi=pi: e.activation(
                                    out=PT(pi), in_=ps(sb_), func=AF.Exp, scale=SC),
                                    reads=[ps_b[sb_]], writes=[pt_b[pi]])
                            if kt >= 2:
                                pkt, ppi = pend.pop(0)
                                fw.op(PE, lambda e, ob=ob, om=om, vr=vr, pkt=pkt, ppi=ppi, vcols=vcols: e.matmul(
                                    ps(ob)[0:om, :], VP(vr)[:, pkt, vcols[0]:vcols[1]], PT(ppi),
                                    start=(pkt == 0), stop=(pkt == 11)),
                                    reads=[vp_b[vr], pt_b[ppi]], writes=[ps_b[ob]])
                            if kt < 12:
                                pend.append((kt, pi))
                            if kt == 8 and pending is not None:
                                pending()
                                pending = None
                        if MLA_SUB < 4:
                            continue
                        rs = stg_t[1][srow:srow + 1, 0:512] if hh == 0 else stg_t[1][srow:srow + 1, 512:1024]
                        fw.op(ACT, lambda e, rs=rs, ob=ob, srow=srow: e.activation(
                            out=rs, in_=ps(ob)[srow:srow + 1, :], func=AF.Ln), reads=[ps_b[ob]], writes=[stg_b[1]])
                        fw.op(ACT, lambda e, rs=rs: e.activation(out=rs, in_=rs, func=AF.Exp, scale=-1.0),
                              reads=[stg_b[1]], writes=[stg_b[1]])
                        return lambda ob=ob, tt=tt: norm_o(h, tt, ob)
                    return None

                def norm_o(h, tt, ob):
                    p, hh = h // 2, h % 2
                    if hh == 0:
                        orows, srow = (0, 64), 64
                    else:
                        orows, srow = (64, 128), 0
                    if True:
                        bb = 6 + st['n_o'] % 2
                        fw.op(PE, lambda e, bb=bb, hh=hh: e.matmul(
                            ps(bb), sel_t[:, hh, :], stg_t[1][:, hh * 512:(hh + 1) * 512],
                            start=True, stop=True),
                            reads=[stg_b[1], const_b], writes=[ps_b[bb]])
                        fw.op(ACT, lambda e, bb=bb: e.activation(out=rstd_t[:], in_=ps(bb), func=AF.Copy),
                              reads=[ps_b[bb]], writes=[rstd_b])
                        fw.op(DVE, lambda e, ob=ob, orows=orows, p=p, tt=tt: e.tensor_tensor(
                            out=h_t[orows[0]:orows[1], p, tsl(tt)], in0=ps(ob)[orows[0]:orows[1], :],
                            in1=rstd_t[orows[0]:orows[1], :], op=ALU.mult),
                            reads=[ps_b[ob], rstd_b], writes=[hb[p][tt]])

                prep_V(0)
                prep_KQ(0)
                pnd = None
                for h in range(16):
                    pnd = attend(h, (0,), pnd)
                    if h + 1 < 16:
                        if (h + 1) % 2 == 0:
                            prep_V((h + 1) // 2)
                        prep_KQ(h + 1)
                    pnd = attend(h, (1,), pnd)
                if pnd is not None:
                    pnd()
                qw = st['qw']
                ws.done(qw[2])
                ws.done(ukvk)
                out_linear(l, wo, lambda kk, tt: h_t[:, kk, tsl(tt)],
                           lambda tt: [hb[kk][tt] for kk in range(KC)], lambda oc: mod_t[l][:, 16 + oc:17 + oc])

            def mixer(l):
                norm_mod(l, 1, pre=(l > 0))
                if l == 0 and stage >= 4:
                    mixer_mla(l)
                elif l == 1 and stage >= 3:
                    mixer_pool(l)
                elif l == 2 and stage >= 3:
                    mixer_fnet(l)
                elif l == 3 and stage >= 3:
                    mixer_sconv(l)
                fw.barrier_bufs(ab)

            if stage >= 2:
                for nb in range(4 if stage >= 4 else 12):
                    ada_block(0, nb, 4 + nb % 2, 6 + nb % 2)
                ada_finish1(0)
                if stage < 4:
                    ada_finish2(0)
                for l in range(DEPTH):
                    mixer(l)
                    norm_mod(l, 2, pre=(stage >= 4))
                    if l + 1 < DEPTH:
                        def hook(jg, l=l):
                            ada_block(l + 1, 2 * jg, 0, 1)
                            ada_block(l + 1, 2 * jg + 1, 2, 3)
                            if jg == 5:
                                ada_finish(l + 1)
                        ffn(l, hook)
                    else:
                        ffn(l)

            fo, _ = PV["fnw"]
            for tt in range(2):
                if stage >= 4:
                    rms_tail(1.0 / D, 6 + tt, rstd_t, rstd_b)
                else:
                    rms_stats([(x_t[:, c, tsl(tt)], [xb[c][tt]]) for c in range(KC)], KC, 1.0 / D,
                              bank=tt)
                for c in range(KC):
                    fw.op(DVE, lambda e, c=c, tt=tt: e.scalar_tensor_tensor(
                        out=x_t[:, c, tsl(tt)], in0=x_t[:, c, tsl(tt)],
                        scalar=pvec[:, fo + c:fo + c + 1], in1=rstd_t[:],
                        op0=ALU.mult, op1=ALU.mult),
                        reads=[xb[c][tt], rstd_b, pvec_b], writes=[xb[c][tt]])
            for i in range(8):
                s = i % 2
                tt = i // 4
                for half in range(2):
                    bank = 2 + (i * 2 + half) % 6

                    def mm(e, half=half, bank=bank, i=i):
                        r = None
                        for cc in range(4):
                            c = half * 4 + cc
                            r = e.transpose(ps(bank)[:, cc * 128:(cc + 1) * 128],
                                            x_t[:, c, i * 128:(i + 1) * 128], ident[:])
                        return r
                    fw.op(PE, mm, reads=[xb[c][tt] for c in range(half * 4, half * 4 + 4)] + [ident_b],
                          writes=[ps_b[bank]])
                    if half == 0:
                        fw.op(DVE, lambda e, s=s, bank=bank: e.tensor_copy(
                            out=stg_t[s][:, 0:512], in_=ps(bank)),
                            reads=[ps_b[bank]], writes=[stg_b[s]])
                    else:
                        fw.op(ACT, lambda e, s=s, bank=bank: e.activation(
                            out=stg_t[s][:, 512:1024], in_=ps(bank), func=AF.Copy),
                            reads=[ps_b[bank]], writes=[stg_b[s]])
                fw.dma(SP, yout[i * 128:(i + 1) * 128, :], stg_t[s][:], dst=None, src=[stg_b[s]])

        fw.dry = True
        emit()
        fw.dry = False
        ws.reset()
        emit()
        assert ws.consumed == len(ws.specs)

        final_waits = {}
        for sem, val in fw.out_recs:
            final_waits[sem] = max(final_waits.get(sem, 0), val)

        with nc.Block() as block:
            @block.sync
            def _(e):
                fw.replay(SP, e)
                for sem, val in final_waits.items():
                    e.wait_ge(sem, val)

            @block.tensor
            def _(e):
                fw.replay(PE, e)

            @block.scalar
            def _(e):
                fw.replay(ACT, e)

            @block.vector
            def _(e):
                fw.replay(DVE, e)

            @block.gpsimd
            def _(e):
                fw.replay(POOL, e)
    return nc


def _cols(v):
    v = np.asarray(v, np.float32)
    return np.ascontiguousarray(v.reshape(-1, 128).T)


def _make_pvec(inp, cond_vec, flagneg):
    pv = np.zeros((128, NPV), np.float32)

    def put(name, arr):
        o, n = PV[name]
        assert arr.shape == (128, n), (name, arr.shape, n)
        pv[:, o:o + n] = arr

    for l in range(DEPTH):
        put(f"n1w{l}", _cols(inp["norm1_w"][l]))
        put(f"n2w{l}", _cols(inp["norm2_w"][l]))
        put(f"fw0_{l}", _cols(inp["ffn_conv_w"][l, 0]))
        put(f"fw1_{l}", _cols(inp["ffn_conv_w"][l, 1]))
        put(f"fw2_{l}", _cols(inp["ffn_conv_w"][l, 2]))
        put(f"fb_{l}", _cols(inp["ffn_conv_b"][l]))
        put(f"adab{l}", _cols(inp["ada_b"][l]))
    put("fnw", _cols(inp["final_norm_w"]))
    put("cond", _cols(cond_vec))
    pv[:, PV["flagneg"][0]] = flagneg
    put("qnw", _cols(inp["mla_q_norm"][0]))
    put("kvnw", _cols(inp["mla_kv_norm"][0]))
    put("pscale", _cols(inp["pool_scale"][0]))
    put("sw0", _cols(inp["sconv_conv"][0, 0]))
    put("sw1", _cols(inp["sconv_conv"][0, 1]))
    put("sw2", _cols(inp["sconv_conv"][0, 2]))
    return pv


def _pool_tables(L):
    A = np.zeros((4, T, T), np.float64)
    wins = (2, 4, 8, 16)
    for g, w in enumerate(wins):
        for t in range(T):
            s0 = (t // L) * L
            tl = t - s0
            lo = max(tl - w // 2, 0)
            hi = min(tl + w - w // 2, L)
            A[g, t, s0 + lo:s0 + hi] = 1.0 / (hi - lo)
            A[g, t, t] -= 1.0
    out = np.zeros((128, 4, 8, 3, 128), np.float32)
    for g in range(4):
        for i in range(8):
            for d in range(3):
                ip = i + d - 1
                if 0 <= ip < 8:
                    out[:, g, i, d, :] = A[g, i * 128:(i + 1) * 128, ip * 128:(ip + 1) * 128].T
    return out.reshape(128, 4, 8 * 3 * 128).astype(ml_dtypes.bfloat16)


def _dft_tables(L):
    t = np.arange(T)
    same = (t[:, None] // L) == (t[None, :] // L)
    ang = 2.0 * np.pi * ((t[:, None] % L) * (t[None, :] % L) % L) / L
    nrm = 1.0 / np.sqrt(L * 256.0)
    C = np.where(same, np.cos(ang), 0.0) * nrm
    S = np.where(same, -np.sin(ang), 0.0) * nrm

    def lay(M):
        return np.ascontiguousarray(M.reshape(8, 128, T).transpose(1, 0, 2)).astype(ml_dtypes.bfloat16)
    c = np.arange(256)
    a2 = 2.0 * np.pi * ((c[:, None] * c[None, :]) % 256) / 256.0
    CS = np.concatenate([np.cos(a2), np.sin(a2)], axis=1)
    CS = np.ascontiguousarray(CS.reshape(2, 128, 512).transpose(1, 0, 2)).astype(ml_dtypes.bfloat16)
    return lay(C), lay(S), CS


_PERM = np.concatenate([np.arange(0, 32, 2), np.arange(1, 32, 2)])
_PERM_SW = np.concatenate([np.arange(1, 32, 2), np.arange(0, 32, 2)])


def _mla_weights(inp):
    wdkv = np.asarray(inp["mla_wdkv"][0], np.float32)
    aug = np.zeros((D, 448), np.float32)
    aug[:, 0:256] = wdkv[:, 0:256]
    aug[:, 320:352] = wdkv[:, 256 + _PERM]
    aug[:, 416:448] = wdkv[:, 256 + _PERM_SW]
    wuq = np.asarray(inp["mla_wuq"][0], np.float32).reshape(384, 16, 96)
    qa = np.zeros((384, 16, 192), np.float32)
    qa[:, :, 0:64] = wuq[:, :, 0:64]
    qa[:, :, 64:96] = wuq[:, :, 64 + _PERM]
    qa[:, :, 160:192] = wuq[:, :, 64 + _PERM_SW]
    return aug, np.ascontiguousarray(qa.reshape(384, 16 * 192))


def _rope_tables(kind):
    cs = np.zeros((128, 2, T), np.float32)
    if kind == "p":
        cs[64:96, 0, :] = 1.0
        return cs
    t = np.arange(T)
    r = (t // 64).astype(np.float32)
    col = (t % 64).astype(np.float32)
    inv = (np.float32(10000.0) ** (-np.arange(8, dtype=np.float32) / np.float32(8))).astype(np.float32)
    ang = np.concatenate([r[:, None] * inv, col[:, None] * inv], axis=-1).astype(np.float32)
    c, s = np.cos(ang).T, np.sin(ang).T
    cs[64:80, 0, :] = c
    cs[80:96, 0, :] = c
    cs[64:80, 1, :] = -s
    cs[80:96, 1, :] = s
    return cs


def _mask_tables(kind):
    NEG = -30000.0
    eq = np.zeros((128, T), np.float32)
    ek = np.zeros((128, 1536), np.float32)
    if kind == "p":
        seq = np.arange(T) // 256
        for r in range(4):
            eq[96 + r, :] = (seq == r)
            ek[96 + r, 0:512] = NEG
            ek[96 + r, 512:] = np.where(seq == r, 0.0, NEG)
    return eq, ek


def _core_roles():
    return [("s", 0), ("s", 1), ("p", 0), ("p", 1), ("p", 2), ("p", 3), ("p", 3), ("p", 3)]


_NC_CACHE = {}


def make_in_maps(inp):
    roles = _core_roles()
    ident = np.eye(128, dtype=np.float32)
    shared = {
        "ident": ident,
        "ada_w": np.ascontiguousarray(inp["ada_w"], dtype=np.float32),
        "ffn_up": np.ascontiguousarray(inp["ffn_up"], dtype=np.float32),
        "ffn_down": np.ascontiguousarray(inp["ffn_down"], dtype=np.float32),
        "pool_w": np.ascontiguousarray(inp["pool_w"][0], dtype=np.float32),
        "fnet_w": np.ascontiguousarray(inp["fnet_w"][0], dtype=np.float32),
        "sconv_win": np.ascontiguousarray(inp["sconv_win"][0], dtype=np.float32),
        "sconv_wout": np.ascontiguousarray(inp["sconv_wout"][0], dtype=np.float32),
        "identb": ident.astype(ml_dtypes.bfloat16),
        "wdq": np.ascontiguousarray(inp["mla_wdq"][0], dtype=np.float32),
        "wukv": np.ascontiguousarray(inp["mla_wukv"][0], dtype=np.float32),
        "wo": np.ascontiguousarray(inp["mla_wo"][0], dtype=np.float32),
    }
    shared["wdkv_aug"], shared["wuq_aug"] = _mla_weights(inp)
    tabs = {}
    for kind, L in (("s", 1024), ("p", 256)):
        C, S, CS = _dft_tables(L)
        eq, ek = _mask_tables(kind)
        tabs[kind] = {"poolA": _pool_tables(L), "dftC": C, "dftS": S, "dftCS": CS,
                      "ropeCS": _rope_tables(kind), "eq": eq, "_ek": ek}
    in_maps = []
    for kind, idx in roles:
        if kind == "s":
            xc = inp["x_sample"][idx]
            cond = inp["c"][idx]
            flag = 0.0
        else:
            xc = inp["x_prompt"][4 * idx:4 * idx + 4].reshape(T, D)
            cond = inp["c_ctx"]
            flag = -1.0
        m = dict(shared)
        m.update({a: b for a, b in tabs[kind].items() if not a.startswith("_")})
        krm = tabs[kind]["_ek"].copy()
        cT = np.zeros((128, 2, 512), np.float32)
        if kind == "s":
            cT[:] = np.asarray(inp["cache_ckv"][idx, 0], np.float32).T.reshape(2, 128, 512).transpose(1, 0, 2)
            krm[64:96, 0:512] = np.asarray(inp["cache_krope"][idx, 0], np.float32)[:, _PERM].T
        m["cacheT"] = cT
        m["krmask"] = krm
        m["xin"] = np.ascontiguousarray(xc, dtype=np.float32)
        m["pvec"] = _make_pvec(inp, cond, flag)
        in_maps.append(m)
    return in_maps


def kernel(**inputs):
    inp = {k: np.asarray(v) for k, v in inputs.items()}
    in_maps = make_in_maps(inp)
    if "nc" not in _NC_CACHE:
        _NC_CACHE["nc"] = build_program(STAGE)
    nc = _NC_CACHE["nc"]
    res = run_bass_kernel_spmd(nc, in_maps, core_ids=list(range(NCORES)))
    outs = res.results
    y_sample = np.stack([outs[0]["yout"], outs[1]["yout"]], axis=0).astype(np.float32)
    y_prompt = np.concatenate([outs[2 + g]["yout"].reshape(4, 256, D) for g in range(4)], axis=0)
    y_prompt = y_prompt.astype(np.float32)
    new_ckv = np.concatenate([outs[2 + g]["ckv_o"].reshape(4, 1, 256, 256) for g in range(4)], axis=0)
    krp = np.concatenate([outs[2 + g]["kr_o"].reshape(4, 1, 256, 32) for g in range(4)], axis=0)
    new_kr = np.empty_like(krp)
    new_kr[..., _PERM] = krp
    new_ckv = new_ckv.astype(np.float32)
    new_kr = new_kr.astype(np.float32)
    return (y_prompt, y_sample, new_ckv, new_kr)
```

```python
import numpy as np
from contextlib import ExitStack
import ml_dtypes

import concourse.bass as bass
import concourse.mybir as mybir
from concourse.bass_utils import run_bass_kernel_spmd

F32 = mybir.dt.float32
BF16 = mybir.dt.bfloat16
AF = mybir.ActivationFunctionType
ALU = mybir.AluOpType

D = 1024
T = 1024
KC = 8
DFF = 2816
FC = 22
DEPTH = 4
EPS = 1e-6
NCORES = 8


class Buf:
    __slots__ = ("name", "w", "r", "sem", "cum", "excl")

    def __init__(self, name):
        self.name = name
        self.excl = False
        self.w = None
        self.r = {}
        self.sem = None
        self.cum = 0


class Q:
    def __init__(self, fw, name, own_wait=True):
        self.fw = fw
        self.name = name
        self.thunks = []
        self.sem = fw.new_sem("q_" + name)
        self.cnt = 0
        self.known = {}
        self.own_wait = own_wait


class FW:
    def __init__(self, nc, es):
        self.nc = nc
        self.es = es
        self.nsem = 0
        self.pe = Q(self, "pe", own_wait=False)
        self.act = Q(self, "act")
        self.dve = Q(self, "dve")
        self.pool = Q(self, "pool")
        self.sp = Q(self, "sp")
        self.out_recs = []
        self.dry = False

    def new_sem(self, name):
        self.nsem += 1
        return self.es.enter_context(self.nc.semaphore(f"s{self.nsem}_{name}"))

    def buf(self, name, dma=False):
        b = Buf(name)
        if dma:
            b.sem = self.new_sem("d_" + name)
        return b

    def _collect(self, q, reads, writes):
        waits = {}

        def need(rec):
            if rec is None:
                return
            sem, val = rec
            if sem is q.sem and not q.own_wait:
                return
            if q.known.get(sem, 0) >= val:
                return
            if waits.get(sem, 0) < val:
                waits[sem] = val

        for b in reads:
            need(b.w)
            if b.excl:
                for sem, val in b.r.items():
                    if sem is not q.sem:
                        need((sem, val))
        for b in writes:
            need(b.w)
            for sem, val in b.r.items():
                need((sem, val))
        for sem, val in waits.items():
            q.known[sem] = val
        return list(waits.items())

    @staticmethod
    def _commit(rec, reads, writes):
        sem, val = rec
        for b in reads:
            if b.r.get(sem, 0) < val:
                b.r[sem] = val
        for b in writes:
            b.w = rec
            b.r = {}

    def op(self, q, fn, reads=(), writes=()):
        if self.dry:
            return None
        wl = self._collect(q, reads, writes)
        q.cnt += 1
        rec = (q.sem, q.cnt)
        q.thunks.append((wl, fn, rec, 1))
        self._commit(rec, reads, writes)
        return rec

    def dma(self, q, out_ap, in_ap, dst=None, src=(), reads=(), kw=None):
        if self.dry:
            return None
        kw = kw or {}
        writes = [dst] if dst is not None else []
        rds = list(src) + list(reads)
        wl = self._collect(q, rds, writes)
        owner = dst if dst is not None else src[0]
        owner.cum += 16
        rec = (owner.sem, owner.cum)

        def fn(e, out_ap=out_ap, in_ap=in_ap, kw=kw):
            return e.dma_start(out=out_ap, in_=in_ap, **kw)

        q.thunks.append((wl, fn, rec, 16))
        self._commit(rec, rds, writes)
        if dst is None:
            self.out_recs.append(rec)
        return rec

    def barrier_bufs(self, bufs):
        allq = [self.pe, self.act, self.dve, self.pool]
        for b in bufs:
            for q in allq:
                if q.cnt > 0:
                    if b.r.get(q.sem, 0) < q.cnt:
                        b.r[q.sem] = q.cnt

    def replay(self, q, eng):
        for wl, fn, rec, inc in q.thunks:
            for sem, val in wl:
                eng.wait_ge(sem, val)
            ins = fn(eng)
            if isinstance(ins, (list, tuple)):
                ins = ins[-1]
            ins.then_inc(rec[0], inc)


class WStream:
    def __init__(self, fw, q, slots, bufs, slot_elems):
        self.fw = fw
        self.q = q
        self.slots = slots
        self.bufs = bufs
        self.n = len(slots)
        self.slot_elems = slot_elems
        self.specs = []
        self.reset()

    def reset(self):
        self.issued = 0
        self.consumed = 0
        self.done_flags = []

    def _view(self, k, shape):
        t = self.slots[k % self.n]
        n = int(np.prod(shape))
        assert n <= self.slot_elems, shape
        if len(shape) == 1:
            return t[:, 0:n]
        if len(shape) == 2:
            return t[:, 0:n].rearrange("p (a b) -> p a b", a=shape[0], b=shape[1])
        return t[:, 0:n].rearrange("p (a b c) -> p a b c", a=shape[0], b=shape[1], c=shape[2])

    def _pump(self):
        while self.issued < len(self.specs):
            k = self.issued
            if k >= self.n and not (k - self.n < len(self.done_flags) and self.done_flags[k - self.n]):
                break
            dram_ap, shape = self.specs[k]
            self.fw.dma(self.q, self._view(k, shape), dram_ap, dst=self.bufs[k % self.n])
            self.issued += 1

    def next(self, dram_ap, shape):
        shape = tuple(shape)
        if self.fw.dry:
            self.specs.append((dram_ap, shape))
            return self._view(0, shape), self.bufs[0], None
        k = self.consumed
        self.consumed += 1
        assert self.specs[k][1] == shape, (k, self.specs[k][1], shape)
        self.done_flags.append(False)
        self._pump()
        assert self.issued > k, (k, self.issued)
        return self._view(k, shape), self.bufs[k % self.n], k

    def done(self, k):
        if self.fw.dry:
            return
        self.done_flags[k] = True
        self._pump()


def _pvec_map():
    m = {}
    o = 0

    def add(name, n):
        nonlocal o
        m[name] = (o, n)
        o += n

    for l in range(DEPTH):
        add(f"n1w{l}", KC)
        add(f"n2w{l}", KC)
        add(f"fw0_{l}", FC)
        add(f"fw1_{l}", FC)
        add(f"fw2_{l}", FC)
        add(f"fb_{l}", FC)
        add(f"adab{l}", 48)
    add("fnw", KC)
    add("cond", KC)
    add("flagneg", 1)
    add("qnw", 3)
    add("kvnw", 2)
    add("pscale", KC)
    add("sw0", KC)
    add("sw1", KC)
    add("sw2", KC)
    m["_n"] = o
    return m


PV = _pvec_map()
NPV = PV["_n"]

STAGE = 4
MLA_SUB = 4.0
NSLOT = 6
SLOT_ELEMS = 4096


def build_program(stage=STAGE):
    nc = bass.Bass("TRN2", target_bir_lowering=False)

    def din(name, shape, dt=F32):
        return nc.dram_tensor(name, list(shape), dt, kind="ExternalInput").ap()

    def dout(name, shape, dt=F32):
        return nc.dram_tensor(name, list(shape), dt, kind="ExternalOutput").ap()

    xin = din("xin", [T, D])
    pvec_d = din("pvec", [128, NPV])
    ident_d = din("ident", [128, 128])
    ada_w = din("ada_w", [DEPTH, D, 6 * D])
    ffn_up = din("ffn_up", [DEPTH, D, 2 * DFF])
    ffn_down = din("ffn_down", [DEPTH, DFF, D])
    pool_w = din("pool_w", [4, 256, 256])
    poolA = din("poolA", [128, 4, 8 * 3 * 128], BF16)
    fnet_w = din("fnet_w", [D, D])
    dftC = din("dftC", [128, 8, T], BF16)
    dftS = din("dftS", [128, 8, T], BF16)
    dftCS = din("dftCS", [128, 2, 512], BF16)
    identb_d = din("identb", [128, 128], BF16)
    sconv_win = din("sconv_win", [D, 3 * D])
    sconv_wout = din("sconv_wout", [D, D])
    wdq = din("wdq", [D, 384])
    wdkv_aug = din("wdkv_aug", [D, 448])
    wuq_aug = din("wuq_aug", [384, 16 * 192])
    wukv = din("wukv", [256, 2048])
    wo = din("wo", [D, D])
    ropeCS_d = din("ropeCS", [128, 2, T])
    cacheT_d = din("cacheT", [128, 2, 512])
    krmask_d = din("krmask", [128, 1536])
    eq_d = din("eq", [128, T])
    yout = dout("yout", [T, D])
    ckv_o = dout("ckv_o", [T, 256])
    kr_o = dout("kr_o", [T, 32])

    es = ExitStack()
    with es:
        fw = FW(nc, es)
        PE, ACT, DVE, POOL, SP = fw.pe, fw.act, fw.dve, fw.pool, fw.sp

        def sb(name, shape, dt):
            return es.enter_context(nc.sbuf_tensor(name, list(shape), dt))

        x_t = sb("x", [128, KC, T], F32)
        xb = [[fw.buf(f"x{c}_{tt}") for tt in range(2)] for c in range(KC)]
        h_t = sb("h", [128, KC, T], BF16)
        hb = [[fw.buf(f"h{c}_{tt}") for tt in range(2)] for c in range(KC)]
        a_t = sb("a", [128, 25, T], BF16)
        ab = [fw.buf(f"a{j}") for j in range(25)]
        pvec = sb("pvec_sb", [128, NPV], F32)
        pvec_b = fw.buf("pvec", dma=True)
        ident = sb("ident_sb", [128, 128], F32)
        ident_b = fw.buf("ident", dma=True)
        ones_bf = sb("ones_bf", [128, 128], BF16)
        one_f = sb("one_f", [128, 1], F32)
        eps_t = sb("eps", [128, 1], F32)
        const_b = fw.buf("consts")
        stg_t = [sb(f"stg{i}", [128, D], F32) for i in range(2)]
        stg_b = [fw.buf(f"stg{i}", dma=True) for i in range(2)]
        tA_t = [sb(f"tA{i}", [128, T], F32) for i in range(2)]
        tA_b = [fw.buf(f"tA{i}") for i in range(2)]
        sq_t = [sb(f"sq{i}", [128, 512], BF16) for i in range(4)]
        sq_b = [fw.buf(f"sq{i}") for i in range(4)]
        rstd_t = sb("rstd", [128, 512], F32)
        rstd_b = fw.buf("rstd")
        rstd1_t = sb("rstd1", [128, 512], F32)
        rstd1_b = fw.buf("rstd1")
        xn_t = [sb(f"xn{i}", [128, 512], F32) for i in range(4)]
        xn_b = [fw.buf(f"xn{i}") for i in range(4)]
        scond = sb("scond", [128, KC], BF16)
        scond_b = fw.buf("scond")
        row_t = [sb(f"row{i}", [1, 512], F32) for i in range(2)]
        row_b = [fw.buf(f"row{i}") for i in range(2)]
        mod_t = [sb(f"mod{l}", [128, 48], F32) for l in range(DEPTH)]
        mod_b = [fw.buf(f"mod{l}") for l in range(DEPTH)]
        col_t = [sb(f"cols{l}", [128, 16 + 2 * FC], F32) for l in range(DEPTH)]
        col_b = [fw.buf(f"cols{l}") for l in range(DEPTH)]
        identb = sb("identb_sb", [128, 128], BF16)
        identb_b = fw.buf("identb", dma=True)
        cs_t = sb("dftcs_sb", [128, 2, 512], BF16)
        cs_b = fw.buf("dftcs", dma=True)
        ropecs = sb("ropecs", [128, 2, T], F32)
        ropecs_b = fw.buf("ropecs", dma=True)
        sel_t = sb("sel", [128, 2, 128], F32)
        ckvst = sb("ckvst", [128, 8, 256], F32)
        ckvst_b = fw.buf("ckvst", dma=True)
        krst = sb("krst", [128, 8, 32], F32)
        krst_b = fw.buf("krst", dma=True)
        ckvall_b = [fw.buf(f"ckvall{c}", dma=True) for c in range(2)]
        kr_b = fw.buf("KR", dma=True)
        qt_b = [fw.buf(f"QT{i}", dma=True) for i in range(3)]
        kt_b = [fw.buf(f"KT{i}") for i in range(3)]
        vp_b = [fw.buf(f"VP{i}") for i in range(2)]
        pt_b = [fw.buf(f"PT{i}") for i in range(4)]
        cq_b = [fw.buf(f"cq{i}") for i in range(3)]
        mcol_t = sb("mcols", [128, 32], F32)
        mcol_b = fw.buf("mcols")
        slots = [sb(f"wslot{i}", [128, SLOT_ELEMS], BF16) for i in range(NSLOT)]
        slot_b = [fw.buf(f"wslot{i}", dma=True) for i in range(NSLOT)]
        ws = WStream(fw, POOL, slots, slot_b, SLOT_ELEMS)

        pd_t = [es.enter_context(nc.psum_tensor(f"pd{i}", [128, 1024], F32)) for i in range(4)]
        ps_b = [fw.buf(f"ps{i}") for i in range(8)]
        for b in ps_b:
            b.excl = True

        pdb_t = [t.bitcast(BF16) for t in pd_t]

        def ps(bank):
            return pd_t[bank // 2][:, (bank % 2) * 512:(bank % 2) * 512 + 512]

        def psb(bank):
            return pdb_t[bank // 2][:, (bank % 2) * 1024:(bank % 2) * 1024 + 1024]

        def pv(name, j=0, n=1):
            o, _ = PV[name]
            return pvec[:, o + j:o + j + n]

        def tsl(tt):
            return slice(tt * 512, (tt + 1) * 512)

        def emit():
            fw.dma(SP, pvec[:], pvec_d, dst=pvec_b)
            fw.dma(SP, ident[:], ident_d, dst=ident_b)
            fw.dma(SP, identb[:], identb_d, dst=identb_b)
            fw.dma(SP, cs_t[:], dftCS, dst=cs_b)
            fw.op(DVE, lambda e: e.memset(ones_bf[:], 1.0), writes=[const_b])
            fw.op(DVE, lambda e: e.memset(eps_t[:], EPS), writes=[const_b])
            fw.op(DVE, lambda e: e.memset(one_f[:], 1.0), writes=[const_b])
            fw.op(DVE, lambda e: e.memset(sel_t[:], 0.0), writes=[const_b])
            fw.op(DVE, lambda e: e.memset(sel_t[64:65, 0, :], 1.0), writes=[const_b])
            fw.op(DVE, lambda e: e.memset(sel_t[0:1, 1, :], 1.0), writes=[const_b])

            for i in range(8):
                s = i % 2
                tt = i // 4
                fw.dma(SP, stg_t[s][:], xin[i * 128:(i + 1) * 128, :], dst=stg_b[s])
                for half in range(2):
                    bank = (i * 2 + half) % 8

                    def mm(e, s=s, half=half, bank=bank):
                        r = None
                        for cc in range(4):
                            c = half * 4 + cc
                            r = e.transpose(ps(bank)[:, cc * 128:(cc + 1) * 128],
                                            stg_t[s][:, c * 128:(c + 1) * 128], ident[:])
                        return r
                    fw.op(PE, mm, reads=[stg_b[s], ident_b], writes=[ps_b[bank]])
                    wr = [xb[c][tt] for c in range(half * 4, half * 4 + 4)]
                    if half == 0:
                        fw.op(DVE, lambda e, half=half, bank=bank, i=i: e.tensor_copy(
                            out=x_t[:, half * 4:half * 4 + 4, i * 128:(i + 1) * 128],
                            in_=ps(bank).rearrange("p (c t) -> p c t", c=4)),
                            reads=[ps_b[bank]], writes=wr)
                    else:
                        fw.op(ACT, lambda e, half=half, bank=bank, i=i: e.activation(
                            out=x_t[:, half * 4:half * 4 + 4, i * 128:(i + 1) * 128],
                            in_=ps(bank).rearrange("p (c t) -> p c t", c=4), func=AF.Copy),
                            reads=[ps_b[bank]], writes=wr)

            fw.op(ACT, lambda e: e.activation(out=scond[:], in_=pv("cond", 0, KC), func=AF.Silu),
                  reads=[pvec_b], writes=[scond_b])

            sqn = [0]

            def rms_stats(srcs, nch, inv_n, bank, rt=None, rb=None):
                rt = rstd_t if rt is None else rt
                rb = rstd_b if rb is None else rb
                for c, (ap, bufs) in enumerate(srcs):
                    s = sqn[0] % 4
                    sqn[0] += 1
                    fw.op(ACT, lambda e, ap=ap, s=s: e.activation(out=sq_t[s][:], in_=ap,
                                                                   func=AF.Square),
                          reads=bufs, writes=[sq_b[s]])
                    fw.op(PE, lambda e, c=c, s=s: e.matmul(ps(bank), ones_bf[:], sq_t[s][:],
                                                           start=(c == 0), stop=(c == nch - 1)),
                          reads=[const_b, sq_b[s]], writes=[ps_b[bank]])
                rms_tail(inv_n, bank, rt, rb)

            def rms_tail(inv_n, bank, rt, rb):
                fw.op(ACT, lambda e: e.activation(out=rt[:], in_=ps(bank), func=AF.Ln,
                                                  bias=eps_t[:, 0:1], scale=inv_n),
                      reads=[ps_b[bank], const_b], writes=[rb])
                fw.op(ACT, lambda e: e.activation(out=rt[:], in_=rt[:], func=AF.Exp, scale=-0.5),
                      reads=[rb], writes=[rb])

            acc = {"pend": [], "cnt": [0, 0]}

            def acc_begin():
                acc["pend"] = []
                acc["cnt"] = [0, 0]

            def acc_x(oc, tt):
                assert fw.dry or len(acc["pend"]) < 4, len(acc["pend"])
                s = sqn[0] % 4
                sqn[0] += 1
                fw.op(ACT, lambda e, oc=oc, tt=tt, s=s: e.activation(out=sq_t[s][:], in_=x_t[:, oc, tsl(tt)],
                                                                       func=AF.Square),
                      reads=[xb[oc][tt]], writes=[sq_b[s]])
                acc["pend"].append((tt, s))

            def acc_pe(lag):
                while len(acc["pend"]) > lag:
                    tt, s = acc["pend"].pop(0)
                    c = acc["cnt"][tt]
                    acc["cnt"][tt] += 1
                    fw.op(PE, lambda e, c=c, s=s, tt=tt: e.matmul(ps(6 + tt), ones_bf[:], sq_t[s][:],
                                                                 start=(c == 0), stop=(c == KC - 1)),
                          reads=[const_b, sq_b[s]], writes=[ps_b[6 + tt]])

            def norm_mod(l, which, pre=False):
                ao = 0 if which == 1 else 8
                bo = 0 if which == 1 else 24
                rts = [(rstd_t, rstd_b), (rstd1_t, rstd1_b)]
                for tt in range(2):
                    if pre:
                        rms_tail(1.0 / D, 6 + tt, rts[tt][0], rts[tt][1])
                    else:
                        rms_stats([(x_t[:, c, tsl(tt)], [xb[c][tt]]) for c in range(KC)], KC, 1.0 / D,
                                  bank=tt, rt=rts[tt][0], rb=rts[tt][1])
                n = 0
                for tt in range(2):
                    rt, rb = rts[tt]
                    for c in range(KC):
                        s = n % 4
                        n += 1
                        fw.op(DVE, lambda e, c=c, s=s, tt=tt, rt=rt: e.tensor_tensor(
                            out=xn_t[s][:], in0=x_t[:, c, tsl(tt)], in1=rt[:], op=ALU.mult),
                            reads=[xb[c][tt], rb], writes=[xn_b[s]])
                        if c % 4 != 3:
                            fw.op(ACT, lambda e, c=c, s=s, tt=tt: e.activation(
                                out=h_t[:, c, tsl(tt)], in_=xn_t[s][:], func=AF.Identity,
                                bias=mod_t[l][:, bo + c:bo + c + 1],
                                scale=col_t[l][:, ao + c:ao + c + 1]),
                                reads=[xn_b[s], mod_b[l], col_b[l]], writes=[hb[c][tt]])
                        else:
                            fw.op(DVE, lambda e, c=c, s=s, tt=tt: e.tensor_scalar(
                                out=h_t[:, c, tsl(tt)], in0=xn_t[s][:],
                                scalar1=col_t[l][:, ao + c:ao + c + 1],
                                scalar2=mod_t[l][:, bo + c:bo + c + 1], op0=ALU.mult, op1=ALU.add),
                                reads=[xn_b[s], mod_b[l], col_b[l]], writes=[hb[c][tt]])

            def ada_block(l, nb, b0=0, b1=1):
                wv = ada_w[l].rearrange("(k p) n -> p k n", p=128)
                w, wb, wk = ws.next(wv[:, :, nb * 512:(nb + 1) * 512], (KC, 512))

                def mm(e, w=w):
                    r = None
                    for k in range(KC):
                        r = e.matmul(ps(b0)[0:1, :], scond[:, k:k + 1], w[:, k, :],
                                     start=(k == 0), stop=(k == KC - 1))
                    return r
                fw.op(PE, mm, reads=[scond_b, wb], writes=[ps_b[b0]])
                ws.done(wk)
                s = nb % 2
                fw.op(ACT, lambda e, s=s: e.activation(out=row_t[s][:], in_=ps(b0)[0:1, :], func=AF.Copy),
                      reads=[ps_b[b0]], writes=[row_b[s]])

                def mt(e, s=s):
                    r = None
                    for j in range(4):
                        r = e.matmul(ps(b1)[:, j:j + 1], row_t[s][0:1, j * 128:(j + 1) * 128],
                                     one_f[0:1, 0:1], start=True, stop=True)
                    return r
                fw.op(PE, mt, reads=[row_b[s], const_b], writes=[ps_b[b1]])
                fw.op(DVE, lambda e: e.tensor_tensor(out=mod_t[l][:, nb * 4:nb * 4 + 4], in0=ps(b1)[:, 0:4],
                                                     in1=pv(f"adab{l}", nb * 4, 4), op=ALU.add),
                      reads=[ps_b[b1], pvec_b], writes=[mod_b[l]])

            def ada_finish1(l):
                fw.op(DVE, lambda e: e.scalar_tensor_tensor(
                    out=col_t[l][:, 0:8], in0=mod_t[l][:, 8:16], scalar=1.0,
                    in1=pv(f"n1w{l}", 0, KC), op0=ALU.add, op1=ALU.mult),
                    reads=[mod_b[l], pvec_b], writes=[col_b[l]])
                fw.op(DVE, lambda e: e.tensor_scalar(
                    out=col_t[l][:, 16:16 + FC], in0=pv(f"fw0_{l}", 0, FC),
                    scalar1=pv("flagneg"), scalar2=None, op0=ALU.mult),
                    reads=[pvec_b], writes=[col_b[l]])
                fw.op(DVE, lambda e: e.tensor_scalar(
                    out=col_t[l][:, 16 + FC:16 + 2 * FC], in0=pv(f"fw2_{l}", 0, FC),
                    scalar1=pv("flagneg"), scalar2=None, op0=ALU.mult),
                    reads=[pvec_b], writes=[col_b[l]])


            def ada_finish2(l):
                fw.op(DVE, lambda e: e.scalar_tensor_tensor(
                    out=col_t[l][:, 8:16], in0=mod_t[l][:, 32:40], scalar=1.0,
                    in1=pv(f"n2w{l}", 0, KC), op0=ALU.add, op1=ALU.mult),
                    reads=[mod_b[l], pvec_b], writes=[col_b[l]])

            def ada_finish(l):
                ada_finish1(l)
                ada_finish2(l)

            def conv_fix(eng, t_ap, src_ap, nf0, nf2, reads, writes):
                fw.op(eng, lambda e: e.scalar_tensor_tensor(
                    out=t_ap[:, 256:1024:256], in0=src_ap[:, 255:1023:256], scalar=nf0,
                    in1=t_ap[:, 256:1024:256], op0=ALU.mult, op1=ALU.add),
                    reads=reads, writes=writes)
                fw.op(eng, lambda e: e.scalar_tensor_tensor(
                    out=t_ap[:, 255:1023:256], in0=src_ap[:, 256:1024:256], scalar=nf2,
                    in1=t_ap[:, 255:1023:256], op0=ALU.mult, op1=ALU.add),
                    reads=reads, writes=writes)

            def ffn(l, mid_hook=None):
                upv = ffn_up[l].rearrange("(k p) n -> p k n", p=128)
                dnv = ffn_down[l].rearrange("(k p) n -> p k n", p=128)
                j = 0
                for jg in range(6):
                    ncol = 512 if jg < 5 else 256
                    gw, gwb, gk = ws.next(upv[:, :, jg * 512:jg * 512 + ncol], (KC, ncol))
                    uw, uwb, uk = ws.next(upv[:, :, DFF + jg * 512:DFF + jg * 512 + ncol],
                                          (KC, ncol))
                    for jj in range(ncol // 128):
                        dbl = 2 * (j % 2)
                        for which, (w, wb) in enumerate(((gw, gwb), (uw, uwb))):
                            for tt in range(2):
                                bank = (dbl + which) * 2 + tt

                                def mm(e, w=w, jj=jj, tt=tt, bank=bank):
                                    r = None
                                    for k in range(KC):
                                        r = e.matmul(ps(bank), w[:, k, jj * 128:(jj + 1) * 128],
                                                     h_t[:, k, tsl(tt)],
                                                     start=(k == 0), stop=(k == KC - 1))
                                    return r
                                fw.op(PE, mm, reads=[wb] + [hb[k][tt] for k in range(KC)],
                                      writes=[ps_b[bank]])
                        g_ap = pd_t[dbl][:]
                        u_ap = pd_t[dbl + 1][:]
                        gB = [ps_b[dbl * 2], ps_b[dbl * 2 + 1]]
                        uB = [ps_b[dbl * 2 + 2], ps_b[dbl * 2 + 3]]
                        s = j % 2
                        t = tA_t[s]
                        tb = tA_b[s]
                        fw.op(ACT, lambda e, t=t, g_ap=g_ap, j=j: e.activation(
                            out=t[:], in_=g_ap, func=AF.Identity, bias=pv(f"fb_{l}", j),
                            scale=pv(f"fw1_{l}", j)),
                            reads=gB + [pvec_b], writes=[tb])
                        fw.op(DVE, lambda e, t=t, g_ap=g_ap, j=j: e.scalar_tensor_tensor(
                            out=t[:, 1:T], in0=g_ap[:, 0:T - 1], scalar=pv(f"fw0_{l}", j),
                            in1=t[:, 1:T], op0=ALU.mult, op1=ALU.add),
                            reads=gB + [pvec_b, tb], writes=[tb])
                        fw.op(DVE, lambda e, t=t, g_ap=g_ap, j=j: e.scalar_tensor_tensor(
                            out=t[:, 0:T - 1], in0=g_ap[:, 1:T], scalar=pv(f"fw2_{l}", j),
                            in1=t[:, 0:T - 1], op0=ALU.mult, op1=ALU.add),
                            reads=gB + [pvec_b, tb], writes=[tb])
                        conv_fix(DVE, t, g_ap, col_t[l][:, 16 + j:17 + j],
                                 col_t[l][:, 16 + FC + j:17 + FC + j],
                                 reads=gB + [col_b[l], tb], writes=[tb])
                        fw.op(ACT, lambda e, t=t: e.activation(out=t[:], in_=t[:], func=AF.Gelu),
                              reads=[tb], writes=[tb])
                        fw.op(DVE, lambda e, t=t, u_ap=u_ap, j=j: e.tensor_tensor(
                            out=a_t[:, j, :], in0=t[:], in1=u_ap, op=ALU.mult),
                            reads=[tb] + uB, writes=[ab[j]])
                        j += 1
                    ws.done(gk)
                    ws.done(uk)
                    if mid_hook is not None:
                        mid_hook(jg)
                acc_begin()
                fw.op(ACT, lambda e: e.activation(out=row_t[0][0:1, 0:1], in_=eps_t[0:1, 0:1], func=AF.Ln),
                      reads=[const_b], writes=[row_b[0]])
                for oc in range(KC):
                    w, wb, wk = ws.next(dnv[:, :, oc * 128:(oc + 1) * 128], (FC, 128))
                    for tt in range(2):
                        bank = (oc * 2 + tt) % 6

                        def mm(e, w=w, tt=tt, bank=bank):
                            r = None
                            for k in range(FC):
                                r = e.matmul(ps(bank), w[:, k, :], a_t[:, k, tsl(tt)],
                                             start=(k == 0), stop=(k == FC - 1))
                            return r
                        fw.op(PE, mm, reads=[wb] + ab[0:FC], writes=[ps_b[bank]])
                        acc_pe(1)
                        fw.op(DVE, lambda e, oc=oc, tt=tt, bank=bank: e.scalar_tensor_tensor(
                            out=x_t[:, oc, tsl(tt)], in0=ps(bank), scalar=mod_t[l][:, 40 + oc:41 + oc],
                            in1=x_t[:, oc, tsl(tt)], op0=ALU.mult, op1=ALU.add),
                            reads=[ps_b[bank], mod_b[l], xb[oc][tt]], writes=[xb[oc][tt]])
                        acc_x(oc, tt)
                    ws.done(wk)
                acc_pe(0)


            def evac(i, out_ap, in_ap, reads, writes):
                if i % 2 == 0:
                    fw.op(DVE, lambda e: e.tensor_copy(out=out_ap, in_=in_ap), reads=reads, writes=writes)
                else:
                    fw.op(ACT, lambda e: e.activation(out=out_ap, in_=in_ap, func=AF.Copy),
                          reads=reads, writes=writes)

            def out_linear(l, wdram, src_ap, src_bufs, gcol):
                wv = wdram.rearrange("(k p) n -> p k n", p=128)
                n = 0
                acc_begin()
                for nb in range(2):
                    w, wb, wk = ws.next(wv[:, :, nb * 512:(nb + 1) * 512], (KC, 512))
                    for o4 in range(4):
                        oc = nb * 4 + o4
                        for tt in range(2):
                            bank = n % 6
                            n += 1

                            def mm(e, w=w, o4=o4, tt=tt, bank=bank):
                                r = None
                                for kk in range(KC):
                                    r = e.matmul(ps(bank), w[:, kk, o4 * 128:(o4 + 1) * 128],
                                                 src_ap(kk, tt), start=(kk == 0), stop=(kk == KC - 1))
                                return r
                            fw.op(PE, mm, reads=[wb] + src_bufs(tt), writes=[ps_b[bank]])
                            acc_pe(2)
                            fw.op(DVE, lambda e, oc=oc, tt=tt, bank=bank: e.scalar_tensor_tensor(
                                out=x_t[:, oc, tsl(tt)], in0=ps(bank), scalar=gcol(oc),
                                in1=x_t[:, oc, tsl(tt)], op0=ALU.mult, op1=ALU.add),
                                reads=[ps_b[bank], mod_b[l], mcol_b, xb[oc][tt]], writes=[xb[oc][tt]])
                            acc_x(oc, tt)
                    ws.done(wk)
                acc_pe(0)

            def mixer_pool(l):
                fw.barrier_bufs(ab)
                fw.op(DVE, lambda e: e.tensor_tensor(out=mcol_t[:, 0:8], in0=pv("pscale", 0, KC),
                                                     in1=mod_t[l][:, 16:24], op=ALU.mult),
                      reads=[pvec_b, mod_b[l]], writes=[mcol_b])
                for i in range(8):
                    bank = i % 8

                    def tr(e, i=i, bank=bank):
                        r = None
                        for c in range(KC):
                            r = e.transpose(psb(bank)[:, c * 128:(c + 1) * 128],
                                            h_t[:, c, i * 128:(i + 1) * 128], identb[:])
                        return r
                    fw.op(PE, tr, reads=[hb[c][i // 4] for c in range(KC)] + [identb_b],
                          writes=[ps_b[bank]])
                    evac(i, a_t[:, i, :], psb(bank), [ps_b[bank]], [ab[i]])
                aw = ab_k = None
                for cc in range(8):
                    g = cc // 2
                    if cc % 2 == 0:
                        aw, awb, ak = ws.next(poolA[:, g, :].rearrange("p (a b c) -> p a b c", a=8, b=3, c=128), (8, 3, 128))
                    dbl = cc % 4

                    def mm(e, cc=cc, aw=aw, dbl=dbl):
                        r = None
                        for i in range(8):
                            ds = [d for d in range(3) if 0 <= i + d - 1 < 8]
                            for n, d in enumerate(ds):
                                r = e.matmul(pd_t[dbl][:, i * 128:(i + 1) * 128],
                                             a_t[:, i + d - 1, cc * 128:(cc + 1) * 128],
                                             aw[:, i, d, :], start=(n == 0), stop=(n == len(ds) - 1))
                        return r
                    fw.op(PE, mm, reads=ab[0:8] + [awb], writes=[ps_b[2 * dbl], ps_b[2 * dbl + 1]])
                    evac(cc, a_t[:, 8 + cc, :], pd_t[dbl][:], [ps_b[2 * dbl], ps_b[2 * dbl + 1]],
                         [ab[8 + cc]])
                    if cc % 2 == 1:
                        ws.done(ak)
                pw, pwb, pk = ws.next(pool_w.rearrange("g (kk p) d -> p g kk d", p=128), (4, 2, 256))
                n = 0
                acc_begin()
                for dc in range(8):
                    g = dc // 2
                    for tt in range(2):
                        bank = n % 6
                        n += 1

                        def mm2(e, dc=dc, g=g, tt=tt, bank=bank):
                            r = None
                            for kk in range(2):
                                r = e.matmul(ps(bank), pw[:, g, kk, (dc % 2) * 128:(dc % 2) * 128 + 128],
                                             a_t[:, 8 + g * 2 + kk, tsl(tt)], start=(kk == 0), stop=(kk == 1))
                            return r
                        fw.op(PE, mm2, reads=[pwb, ab[8 + g * 2], ab[9 + g * 2]], writes=[ps_b[bank]])
                        acc_pe(3)
                        fw.op(DVE, lambda e, dc=dc, tt=tt, bank=bank: e.scalar_tensor_tensor(
                            out=x_t[:, dc, tsl(tt)], in0=ps(bank), scalar=mcol_t[:, dc:dc + 1],
                            in1=x_t[:, dc, tsl(tt)], op0=ALU.mult, op1=ALU.add),
                            reads=[ps_b[bank], mcol_b, xb[dc][tt]], writes=[xb[dc][tt]])
                        acc_x(dc, tt)
                acc_pe(0)
                ws.done(pk)

            def mixer_fnet(l):
                fw.barrier_bufs(ab)

                def pq(i, g):
                    return a_t[:, 2 * i + g // 2, (g % 2) * 512:(g % 2) * 512 + 512]
                n = 0
                for i in range(8):
                    for g in range(4):
                        bank = n % 8

                        def mm(e, i=i, g=g, bank=bank):
                            r = None
                            for kk in range(2):
                                r = e.matmul(ps(bank), h_t[:, g * 2 + kk, i * 128:(i + 1) * 128],
                                             cs_t[:, kk, :], start=(kk == 0), stop=(kk == 1))
                            return r
                        fw.op(PE, mm, reads=[hb[g * 2][i // 4], hb[g * 2 + 1][i // 4], cs_b],
                              writes=[ps_b[bank]])
                        evac(n, pq(i, g), ps(bank), [ps_b[bank]], [ab[2 * i + g // 2]])
                        n += 1
                for tt in range(2):
                    cw, cwb, ck = ws.next(dftC[:, :, tsl(tt)], (8, 512))
                    sw, swb, sk = ws.next(dftS[:, :, tsl(tt)], (8, 512))
                    for mc in range(8):
                        g, m2 = mc // 2, mc % 2
                        bank = n % 8

                        def mm(e, g=g, m2=m2, bank=bank, cw=cw, sw=sw):
                            r = None
                            for i in range(8):
                                r = e.matmul(ps(bank), pq(i, g)[:, m2 * 128:(m2 + 1) * 128], cw[:, i, :],
                                             start=(i == 0), stop=False)
                                r = e.matmul(ps(bank), pq(i, g)[:, 256 + m2 * 128:256 + (m2 + 1) * 128],
                                             sw[:, i, :], start=False, stop=(i == 7))
                            return r
                        fw.op(PE, mm, reads=ab[0:16] + [cwb, swb], writes=[ps_b[bank]])
                        evac(n, a_t[:, 16 + mc, tsl(tt)], ps(bank), [ps_b[bank]], [ab[16 + mc]])
                        n += 1
                    ws.done(ck)
                    ws.done(sk)
                out_linear(l, fnet_w, lambda kk, tt: a_t[:, 16 + kk, tsl(tt)],
                           lambda tt: ab[16:24], lambda oc: mod_t[l][:, 16 + oc:17 + oc])

            def mixer_sconv(l):
                fw.barrier_bufs(ab)
                fw.op(DVE, lambda e: e.tensor_scalar(out=mcol_t[:, 8:16], in0=pv("sw0", 0, KC),
                                                     scalar1=pv("flagneg"), scalar2=None, op0=ALU.mult),
                      reads=[pvec_b], writes=[mcol_b])
                fw.op(DVE, lambda e: e.tensor_scalar(out=mcol_t[:, 16:24], in0=pv("sw2", 0, KC),
                                                     scalar1=pv("flagneg"), scalar2=None, op0=ALU.mult),
                      reads=[pvec_b], writes=[mcol_b])
                wv = sconv_win.rearrange("(k p) n -> p k n", p=128)
                nd = 0
                for jg in range(2):
                    wl = [ws.next(wv[:, :, part * D + jg * 512:part * D + jg * 512 + 512], (KC, 512))
                          for part in range(3)]
                    for jj in range(4):
                        j = jg * 4 + jj
                        dbls = []
                        dbls = [None, None, None]
                        for part in (2, 1, 0):
                            dbl = nd % 4
                            nd += 1
                            dbls[part] = dbl
                            w, wb, _ = wl[part]
                            for tt in range(2):
                                bank = 2 * dbl + tt

                                def mm(e, w=w, jj=jj, tt=tt, bank=bank):
                                    r = None
                                    for kk in range(KC):
                                        r = e.matmul(ps(bank), w[:, kk, jj * 128:(jj + 1) * 128],
                                                     h_t[:, kk, tsl(tt)], start=(kk == 0), stop=(kk == KC - 1))
                                    return r
                                fw.op(PE, mm, reads=[wb] + [hb[kk][tt] for kk in range(KC)],
                                      writes=[ps_b[bank]])
                        gbB = [ps_b[2 * dbls[0]], ps_b[2 * dbls[0] + 1]]
                        gcB = [ps_b[2 * dbls[1]], ps_b[2 * dbls[1] + 1]]
                        uB = [ps_b[2 * dbls[2]], ps_b[2 * dbls[2] + 1]]
                        s = j % 2
                        v, vb = tA_t[s], tA_b[s]
                        t2, t2b = stg_t[s], stg_b[s]
                        fw.op(ACT, lambda e, v=v, d=dbls[2]: e.activation(out=v[:], in_=pd_t[d][:], func=AF.Copy),
                              reads=uB, writes=[vb])
                        fw.op(DVE, lambda e, v=v, d=dbls[1]: e.tensor_tensor(out=v[:], in0=pd_t[d][:], in1=v[:],
                                                                             op=ALU.mult),
                              reads=gcB + [vb], writes=[vb])
                        fw.op(ACT, lambda e, v=v, t2=t2, j=j: e.activation(out=t2[:], in_=v[:], func=AF.Copy,
                                                                             scale=pv("sw1", j)),
                              reads=[vb, pvec_b], writes=[t2b])
                        fw.op(DVE, lambda e, v=v, t2=t2, j=j: e.scalar_tensor_tensor(
                            out=t2[:, 1:T], in0=v[:, 0:T - 1], scalar=pv("sw0", j), in1=t2[:, 1:T],
                            op0=ALU.mult, op1=ALU.add), reads=[vb, pvec_b, t2b], writes=[t2b])
                        fw.op(DVE, lambda e, v=v, t2=t2, j=j: e.scalar_tensor_tensor(
                            out=t2[:, 0:T - 1], in0=v[:, 1:T], scalar=pv("sw2", j), in1=t2[:, 0:T - 1],
                            op0=ALU.mult, op1=ALU.add), reads=[vb, pvec_b, t2b], writes=[t2b])
                        conv_fix(DVE, t2, v, mcol_t[:, 8 + j:9 + j], mcol_t[:, 16 + j:17 + j],
                                 reads=[vb, mcol_b, t2b], writes=[t2b])
                        fw.op(DVE, lambda e, t2=t2, j=j, d=dbls[0]: e.tensor_tensor(
                            out=a_t[:, j, :], in0=pd_t[d][:], in1=t2[:], op=ALU.mult),
                            reads=gbB + [t2b], writes=[ab[j]])
                    for _, _, wk in wl:
                        ws.done(wk)
                out_linear(l, sconv_wout, lambda kk, tt: a_t[:, kk, tsl(tt)],
                           lambda tt: ab[0:8], lambda oc: mod_t[l][:, 16 + oc:17 + oc])


            def mixer_mla(l):
                SC = 1.0 / float(np.sqrt(96.0))
                arena = a_t[:].rearrange("p a b -> p (a b)")
                cqT = lambda c: a_t[:, c, :]
                ckvall = lambda c: arena[:, 3 * T + c * 1536:3 * T + (c + 1) * 1536]
                KR = arena[:, 6 * T:6 * T + 1536]
                QT = lambda r: a_t[:, 8 + r, :]
                KT = lambda r: arena[:, (11 + 2 * r) * T:(11 + 2 * r) * T + 1536]
                VP = lambda r: arena[:, (17 + 3 * r) * T:(17 + 3 * r) * T + 3072].rearrange(
                    "p (k x) -> p k x", k=12, x=256)
                PT = lambda r: arena[:, 23 * T + r * 512:23 * T + (r + 1) * 512]
                for i in range(2):
                    rr = fw.dma(SP, ropecs[:, i, :], ropeCS_d[:, i, :], dst=ropecs_b)
                    if MLA_SUB == 0.11 and not fw.dry:
                        fw.out_recs.append(rr)
                for c in range(2):
                    fw.dma(POOL, ckvall(c)[:, 0:512], cacheT_d[:, c, :], dst=ckvall_b[c])
                fw.dma(POOL, KR, krmask_d, dst=kr_b)
                for r in range(3):
                    fw.dma(POOL, QT(r), eq_d, dst=qt_b[r])
                for r in range(2):
                    fw.op(DVE, lambda e, r=r: e.memset(VP(r), 0.0), writes=[vp_b[r]])
                    fw.op(DVE, lambda e, r=r: e.memset(VP(r)[:, :, 64:65], 1.0), writes=[vp_b[r]])
                    fw.op(DVE, lambda e, r=r: e.memset(VP(r)[:, :, 128:129], 1.0), writes=[vp_b[r]])

                if MLA_SUB < 0.2:
                    return
                w, wb, wk = ws.next(wdq.rearrange("(k p) n -> p k n", p=128), (KC, 384))
                for tt in range(2):
                    banks = [(4 * tt + c) % 8 for c in range(3)]
                    for c in range(3):
                        def mm(e, c=c, tt=tt, bank=banks[c], w=w):
                            r = None
                            for kk in range(KC):
                                r = e.matmul(ps(bank), w[:, kk, c * 128:(c + 1) * 128], h_t[:, kk, tsl(tt)],
                                             start=(kk == 0), stop=(kk == KC - 1))
                            return r
                        fw.op(PE, mm, reads=[wb] + [hb[kk][tt] for kk in range(KC)], writes=[ps_b[banks[c]]])
                    rms_stats([(ps(banks[c]), [ps_b[banks[c]]]) for c in range(3)], 3, 1.0 / 384,
                              bank=(4 * tt + 3) % 8)
                    for c in range(3):
                        s = c % 2
                        fw.op(DVE, lambda e, s=s, bank=banks[c]: e.tensor_tensor(
                            out=xn_t[s][:], in0=ps(bank), in1=rstd_t[:], op=ALU.mult),
                            reads=[ps_b[banks[c]], rstd_b], writes=[xn_b[s]])
                        fw.op(ACT, lambda e, s=s, c=c, tt=tt: e.activation(
                            out=cqT(c)[:, tsl(tt)], in_=xn_t[s][:], func=AF.Copy, scale=pv("qnw", c)),
                            reads=[xn_b[s], pvec_b], writes=[cq_b[c]])
                ws.done(wk)

                if MLA_SUB < 0.5:
                    return
                w, wb, wk = ws.next(wdkv_aug.rearrange("(k p) n -> p k n", p=128), (KC, 448))
                for tt in range(2):
                    banks = [(5 * tt + i) % 8 for i in range(4)]
                    cols = [(0, 128), (128, 128), (256, 96), (352, 96)]
                    for i in range(4):
                        c0, m = cols[i]

                        def mm(e, c0=c0, m=m, tt=tt, bank=banks[i], w=w):
                            r = None
                            for kk in range(KC):
                                r = e.matmul(ps(bank)[0:m, :], w[:, kk, c0:c0 + m], h_t[:, kk, tsl(tt)],
                                             start=(kk == 0), stop=(kk == KC - 1))
                            return r
                        fw.op(PE, mm, reads=[wb] + [hb[kk][tt] for kk in range(KC)], writes=[ps_b[banks[i]]])
                    if MLA_SUB < 0.51:
                        continue
                    rms_stats([(ps(banks[c]), [ps_b[banks[c]]]) for c in range(2)], 2, 1.0 / 256,
                              bank=(5 * tt + 4) % 8)
                    if MLA_SUB < 0.52:
                        continue
                    for c in range(2):
                        s = c % 2
                        fw.op(DVE, lambda e, s=s, bank=banks[c]: e.tensor_tensor(
                            out=xn_t[s][:], in0=ps(bank), in1=rstd_t[:], op=ALU.mult),
                            reads=[ps_b[banks[c]], rstd_b], writes=[xn_b[s]])
                        fw.op(ACT, lambda e, s=s, c=c, tt=tt: e.activation(
                            out=ckvall(c)[:, 512 + tt * 512:1024 + tt * 512], in_=xn_t[s][:], func=AF.Copy,
                            scale=pv("kvnw", c)),
                            reads=[xn_b[s], pvec_b], writes=[ckvall_b[c]])
                        fw.op(DVE, lambda e, s=s, c=c, tt=tt: e.tensor_scalar(
                            out=tA_t[c][:, tsl(tt)], in0=xn_t[s][:], scalar1=pv("kvnw", c), scalar2=None,
                            op0=ALU.mult),
                            reads=[xn_b[s], pvec_b], writes=[tA_b[c]])
                    if MLA_SUB < 0.55:
                        continue
                    bA, bB = banks[2], banks[3]
                    fw.op(ACT, lambda e, tt=tt, bA=bA: e.activation(
                        out=stg_t[0][64:96, tsl(tt)], in_=ps(bA)[64:96, :], func=AF.Copy),
                        reads=[ps_b[bA]], writes=[stg_b[0]])
                    if MLA_SUB < 0.56:
                        continue
                    fw.op(DVE, lambda e, tt=tt, bA=bA: e.tensor_tensor(
                        out=xn_t[0][64:96, :], in0=ps(bA)[64:96, :], in1=ropecs[64:96, 0, tsl(tt)], op=ALU.mult),
                        reads=[ps_b[bA], ropecs_b], writes=[xn_b[0]])
                    if MLA_SUB < 0.57:
                        continue
                    fw.op(DVE, lambda e, tt=tt, bB=bB: e.tensor_tensor(
                        out=xn_t[1][64:96, :], in0=ps(bB)[64:96, :], in1=ropecs[64:96, 1, tsl(tt)], op=ALU.mult),
                        reads=[ps_b[bB], ropecs_b], writes=[xn_b[1]])
                    if MLA_SUB < 0.58:
                        continue
                    fw.op(DVE, lambda e, tt=tt: e.tensor_tensor(
                        out=KR[64:96, 512 + tt * 512:1024 + tt * 512], in0=xn_t[0][64:96, :],
                        in1=xn_t[1][64:96, :], op=ALU.add),
                        reads=[xn_b[0], xn_b[1]], writes=[kr_b])
                ws.done(wk)

                for nb in range(4, 12):
                    ada_block(0, nb, 2 + nb % 2, 4 + nb % 2)
                ada_finish2(0)
                for i in range(8):
                    bank = i % 8

                    def tr(e, i=i, bank=bank):
                        r = None
                        for c in range(2):
                            r = e.transpose(ps(bank)[:, c * 128:(c + 1) * 128],
                                            tA_t[c][:, i * 128:(i + 1) * 128], ident[:])
                        r = e.transpose(ps(bank)[:, 256:384], stg_t[0][:, i * 128:(i + 1) * 128], ident[:])
                        return r
                    fw.op(PE, tr, reads=[tA_b[0], tA_b[1], stg_b[0], ident_b], writes=[ps_b[bank]])
                    fw.op(DVE, lambda e, i=i, bank=bank: e.tensor_copy(out=ckvst[:, i, :], in_=ps(bank)[:, 0:256]),
                          reads=[ps_b[bank]], writes=[ckvst_b])
                    fw.op(ACT, lambda e, i=i, bank=bank: e.activation(out=krst[:, i, :], in_=ps(bank)[:, 320:352],
                                                                     func=AF.Copy),
                          reads=[ps_b[bank]], writes=[krst_b])
                fw.dma(SP, ckv_o.rearrange("(i p) c -> p i c", p=128), ckvst[:], dst=None, src=[ckvst_b])
                fw.dma(SP, kr_o.rearrange("(i p) c -> p i c", p=128), krst[:], dst=None, src=[krst_b])

                for r in range(3):
                    fw.op(DVE, lambda e, r=r: e.tensor_copy(out=KT(r)[64:128, :], in_=KR[64:128, :]),
                          reads=[kr_b], writes=[kt_b[r]])

                fw.op(DVE, lambda e: e.memset(stg_t[1][:], 0.0), writes=[stg_b[1]])
                if MLA_SUB < 2:
                    return
                ukv, ukvb, ukvk = ws.next(wukv.rearrange("(k p) n -> p k n", p=128), (2, 2048))
                wqv = wuq_aug.rearrange("(k p) n -> p k n", p=128)
                st = {'qw': None, 'n_o': 0, 'n_s': 0, 'n_p': 0}

                def prep_V(p):
                    vr = p % 2
                    for ktg in range(3):
                        bank = 6 + ktg % 2

                        def mmv(e, ktg=ktg, bank=bank, p=p):
                            r = None
                            for j in range(4):
                                kt = ktg * 4 + j
                                for kc in range(2):
                                    rhs = ukv[:, kc, p * 256:(p + 1) * 256].rearrange("p (h x) -> p h x", h=2)[:, :, 64:128]
                                    r = e.matmul(ps(bank)[:, j * 128:(j + 1) * 128].rearrange("p (h x) -> p h x", h=2),
                                                 ckvall(kc)[:, kt * 128:(kt + 1) * 128], rhs,
                                                 start=(kc == 0), stop=(kc == 1))
                            return r
                        fw.op(PE, mmv, reads=[ukvb, ckvall_b[0], ckvall_b[1]], writes=[ps_b[bank]])
                        src = ps(bank).rearrange("p (j h x) -> p j h x", j=4, h=2, x=64)
                        fw.op(DVE, lambda e, src=src, ktg=ktg, vr=vr: e.tensor_copy(
                            out=VP(vr)[:, ktg * 4:ktg * 4 + 4, 0:64], in_=src[:, :, 0, :]),
                            reads=[ps_b[bank]], writes=[vp_b[vr]])
                        fw.op(ACT, lambda e, src=src, ktg=ktg, vr=vr: e.activation(
                            out=VP(vr)[:, ktg * 4:ktg * 4 + 4, 192:256], in_=src[:, :, 1, :], func=AF.Copy),
                            reads=[ps_b[bank]], writes=[vp_b[vr]])

                def prep_KQ(h):
                    r3 = h % 3
                    if h % 4 == 0:
                        if st['qw'] is not None:
                            ws.done(st['qw'][2])
                        st['qw'] = ws.next(wqv[:, :, (h // 4) * 768:(h // 4 + 1) * 768], (3, 768))
                    qw = st['qw']
                    for kt5 in range(3):
                        bank = 6 + kt5 % 2

                        def mmk(e, h=h, kt5=kt5, bank=bank):
                            r = None
                            for kc in range(2):
                                r = e.matmul(ps(bank)[0:64, :], ukv[:, kc, h * 128:h * 128 + 64],
                                             ckvall(kc)[:, kt5 * 512:(kt5 + 1) * 512],
                                             start=(kc == 0), stop=(kc == 1))
                            return r
                        fw.op(PE, mmk, reads=[ukvb, ckvall_b[0], ckvall_b[1]], writes=[ps_b[bank]])
                        evac(kt5, KT(r3)[0:64, kt5 * 512:(kt5 + 1) * 512], ps(bank)[0:64, :],
                             [ps_b[bank]], [kt_b[r3]])
                    qcol = (h % 4) * 192
                    for tt in range(2):
                        for which in range(2):
                            bank = 6 + which

                            def mmq(e, which=which, tt=tt, bank=bank, qcol=qcol, qwv=qw[0]):
                                r = None
                                for kc in range(3):
                                    r = e.matmul(ps(bank)[0:96, :],
                                                 qwv[:, kc, qcol + which * 96:qcol + which * 96 + 96],
                                                 cqT(kc)[:, tsl(tt)], start=(kc == 0), stop=(kc == 2))
                                return r
                            fw.op(PE, mmq, reads=[qw[1]] + cq_b, writes=[ps_b[bank]])
                        fw.op(ACT, lambda e, r3=r3, tt=tt: e.activation(
                            out=QT(r3)[0:64, tsl(tt)], in_=ps(6)[0:64, :], func=AF.Copy),
                            reads=[ps_b[6]], writes=[qt_b[r3]])
                        fw.op(DVE, lambda e, tt=tt: e.tensor_tensor(
                            out=xn_t[0][64:96, :], in0=ps(6)[64:96, :], in1=ropecs[64:96, 0, tsl(tt)],
                            op=ALU.mult), reads=[ps_b[6], ropecs_b], writes=[xn_b[0]])
                        fw.op(DVE, lambda e, tt=tt: e.tensor_tensor(
                            out=xn_t[1][64:96, :], in0=ps(7)[64:96, :], in1=ropecs[64:96, 1, tsl(tt)],
                            op=ALU.mult), reads=[ps_b[7], ropecs_b], writes=[xn_b[1]])
                        fw.op(DVE, lambda e, r3=r3, tt=tt: e.tensor_tensor(
                            out=QT(r3)[64:96, tsl(tt)], in0=xn_t[0][64:96, :], in1=xn_t[1][64:96, :],
                            op=ALU.add), reads=[xn_b[0], xn_b[1]], writes=[qt_b[r3]])

                def attend(h, tts, pending=None):
                    p, hh = h // 2, h % 2
                    vr = p % 2
                    r3 = h % 3
                    if hh == 0:
                        vcols, orows, srow, om = (0, 65), (0, 64), 64, 65
                    else:
                        vcols, orows, srow, om = (128, 256), (64, 128), 0, 128
                    for tt in (tts if MLA_SUB >= 3 else ()):
                        ob = 2 + st['n_o'] % 3
                        st['n_o'] += 1
                        pend = []
                        for kt in range(14):
                            if kt < 12:
                                sb_ = (0, 1, 5)[st['n_s'] % 3]
                                st['n_s'] += 1
                                fw.op(PE, lambda e, sb_=sb_, kt=kt, r3=r3, tt=tt: e.matmul(
                                    ps(sb_), KT(r3)[:, kt * 128:(kt + 1) * 128], QT(r3)[:, tsl(tt)],
                                    start=True, stop=True),
                                    reads=[kt_b[r3], qt_b[r3]], writes=[ps_b[sb_]])
                                pi = st['n_p'] % 4
                                st['n_p'] += 1
                                fw.op(ACT, lambda e, sb_=sb_, pi=pi: e.activation(
                                    out=PT(pi), in_=ps(sb_), func=AF.Exp, scale=SC),
                                    reads=[ps_b[sb_]], writes=[pt_b[pi]])
                            if kt >= 2:
                                pkt, ppi = pend.pop(0)
                                fw.op(PE, lambda e, ob=ob, om=om, vr=vr, pkt=pkt, ppi=ppi, vcols=vcols: e.matmul(
                                    ps(ob)[0:om, :], VP(vr)[:, pkt, vcols[0]:vcols[1]], PT(ppi),
                                    start=(pkt == 0), stop=(pkt == 11)),
                                    reads=[vp_b[vr], pt_b[ppi]], writes=[ps_b[ob]])
                            if kt < 12:
                                pend.append((kt, pi))
                            if kt == 8 and pending is not None:
                                pending()
                                pending = None
                        if MLA_SUB < 4:
                            continue
                        rs = stg_t[1][srow:srow + 1, 0:512] if hh == 0 else stg_t[1][srow:srow + 1, 512:1024]
                        fw.op(ACT, lambda e, rs=rs, ob=ob, srow=srow: e.activation(
                            out=rs, in_=ps(ob)[srow:srow + 1, :], func=AF.Ln), reads=[ps_b[ob]], writes=[stg_b[1]])
                        fw.op(ACT, lambda e, rs=rs: e.activation(out=rs, in_=rs, func=AF.Exp, scale=-1.0),
                              reads=[stg_b[1]], writes=[stg_b[1]])
                        return lambda ob=ob, tt=tt: norm_o(h, tt, ob)
                    return None

                def norm_o(h, tt, ob):
                    p, hh = h // 2, h % 2
                    if hh == 0:
                        orows, srow = (0, 64), 64
                    else:
                        orows, srow = (64, 128), 0
                    if True:
                        bb = 6 + st['n_o'] % 2
                        fw.op(PE, lambda e, bb=bb, hh=hh: e.matmul(
                            ps(bb), sel_t[:, hh, :], stg_t[1][:, hh * 512:(hh + 1) * 512],
                            start=True, stop=True),
                            reads=[stg_b[1], const_b], writes=[ps_b[bb]])
                        fw.op(ACT, lambda e, bb=bb: e.activation(out=rstd_t[:], in_=ps(bb), func=AF.Copy),
                              reads=[ps_b[bb]], writes=[rstd_b])
                        fw.op(DVE, lambda e, ob=ob, orows=orows, p=p, tt=tt: e.tensor_tensor(
                            out=h_t[orows[0]:orows[1], p, tsl(tt)], in0=ps(ob)[orows[0]:orows[1], :],
                            in1=rstd_t[orows[0]:orows[1], :], op=ALU.mult),
                            reads=[ps_b[ob], rstd_b], writes=[hb[p][tt]])

                prep_V(0)
                prep_KQ(0)
                pnd = None
                for h in range(16):
                    pnd = attend(h, (0,), pnd)
                    if h + 1 < 16:
                        if (h + 1) % 2 == 0:
                            prep_V((h + 1) // 2)
                        prep_KQ(h + 1)
                    pnd = attend(h, (1,), pnd)
                if pnd is not None:
                    pnd()
                qw = st['qw']
                ws.done(qw[2])
                ws.done(ukvk)
                out_linear(l, wo, lambda kk, tt: h_t[:, kk, tsl(tt)],
                           lambda tt: [hb[kk][tt] for kk in range(KC)], lambda oc: mod_t[l][:, 16 + oc:17 + oc])

            def mixer(l):
                norm_mod(l, 1, pre=(l > 0))
                if l == 0 and stage >= 4:
                    mixer_mla(l)
                elif l == 1 and stage >= 3:
                    mixer_pool(l)
                elif l == 2 and stage >= 3:
                    mixer_fnet(l)
                elif l == 3 and stage >= 3:
                    mixer_sconv(l)
                fw.barrier_bufs(ab)

            if stage >= 2:
                for nb in range(4 if stage >= 4 else 12):
                    ada_block(0, nb, 4 + nb % 2, 6 + nb % 2)
                ada_finish1(0)
                if stage < 4:
                    ada_finish2(0)
                for l in range(DEPTH):
                    mixer(l)
                    norm_mod(l, 2, pre=(stage >= 4))
                    if l + 1 < DEPTH:
                        def hook(jg, l=l):
                            ada_block(l + 1, 2 * jg, 0, 1)
                            ada_block(l + 1, 2 * jg + 1, 2, 3)
                            if jg == 5:
                                ada_finish(l + 1)
                        ffn(l, hook)
                    else:
                        ffn(l)

            fo, _ = PV["fnw"]
            for tt in range(2):
                if stage >= 4:
                    rms_tail(1.0 / D, 6 + tt, rstd_t, rstd_b)
                else:
                    rms_stats([(x_t[:, c, tsl(tt)], [xb[c][tt]]) for c in range(KC)], KC, 1.0 / D,
                              bank=tt)
                for c in range(KC):
                    fw.op(DVE, lambda e, c=c, tt=tt: e.scalar_tensor_tensor(
                        out=x_t[:, c, tsl(tt)], in0=x_t[:, c, tsl(tt)],
                        scalar=pvec[:, fo + c:fo + c + 1], in1=rstd_t[:],
                        op0=ALU.mult, op1=ALU.mult),
                        reads=[xb[c][tt], rstd_b, pvec_b], writes=[xb[c][tt]])
            for i in range(8):
                s = i % 2
                tt = i // 4
                for half in range(2):
                    bank = 2 + (i * 2 + half) % 6

                    def mm(e, half=half, bank=bank, i=i):
                        r = None
                        for cc in range(4):
                            c = half * 4 + cc
                            r = e.transpose(ps(bank)[:, cc * 128:(cc + 1) * 128],
                                            x_t[:, c, i * 128:(i + 1) * 128], ident[:])
                        return r
                    fw.op(PE, mm, reads=[xb[c][tt] for c in range(half * 4, half * 4 + 4)] + [ident_b],
                          writes=[ps_b[bank]])
                    if half == 0:
                        fw.op(DVE, lambda e, s=s, bank=bank: e.tensor_copy(
                            out=stg_t[s][:, 0:512], in_=ps(bank)),
                            reads=[ps_b[bank]], writes=[stg_b[s]])
                    else:
                        fw.op(ACT, lambda e, s=s, bank=bank: e.activation(
                            out=stg_t[s][:, 512:1024], in_=ps(bank), func=AF.Copy),
                            reads=[ps_b[bank]], writes=[stg_b[s]])
                fw.dma(SP, yout[i * 128:(i + 1) * 128, :], stg_t[s][:], dst=None, src=[stg_b[s]])

        fw.dry = True
        emit()
        fw.dry = False
        ws.reset()
        emit()
        assert ws.consumed == len(ws.specs)

        final_waits = {}
        for sem, val in fw.out_recs:
            final_waits[sem] = max(final_waits.get(sem, 0), val)

        with nc.Block() as block:
            @block.sync
            def _(e):
                fw.replay(SP, e)
                for sem, val in final_waits.items():
                    e.wait_ge(sem, val)

            @block.tensor
            def _(e):
                fw.replay(PE, e)

            @block.scalar
            def _(e):
                fw.replay(ACT, e)

            @block.vector
            def _(e):
                fw.replay(DVE, e)

            @block.gpsimd
            def _(e):
                fw.replay(POOL, e)
    return nc


def _cols(v):
    v = np.asarray(v, np.float32)
    return np.ascontiguousarray(v.reshape(-1, 128).T)


def _make_pvec(inp, cond_vec, flagneg):
    pv = np.zeros((128, NPV), np.float32)

    def put(name, arr):
        o, n = PV[name]
        assert arr.shape == (128, n), (name, arr.shape, n)
        pv[:, o:o + n] = arr

    for l in range(DEPTH):
        put(f"n1w{l}", _cols(inp["norm1_w"][l]))
        put(f"n2w{l}", _cols(inp["norm2_w"][l]))
        put(f"fw0_{l}", _cols(inp["ffn_conv_w"][l, 0]))
        put(f"fw1_{l}", _cols(inp["ffn_conv_w"][l, 1]))
        put(f"fw2_{l}", _cols(inp["ffn_conv_w"][l, 2]))
        put(f"fb_{l}", _cols(inp["ffn_conv_b"][l]))
        put(f"adab{l}", _cols(inp["ada_b"][l]))
    put("fnw", _cols(inp["final_norm_w"]))
    put("cond", _cols(cond_vec))
    pv[:, PV["flagneg"][0]] = flagneg
    put("qnw", _cols(inp["mla_q_norm"][0]))
    put("kvnw", _cols(inp["mla_kv_norm"][0]))
    put("pscale", _cols(inp["pool_scale"][0]))
    put("sw0", _cols(inp["sconv_conv"][0, 0]))
    put("sw1", _cols(inp["sconv_conv"][0, 1]))
    put("sw2", _cols(inp["sconv_conv"][0, 2]))
    return pv


def _pool_tables(L):
    A = np.zeros((4, T, T), np.float64)
    wins = (2, 4, 8, 16)
    for g, w in enumerate(wins):
        for t in range(T):
            s0 = (t // L) * L
            tl = t - s0
            lo = max(tl - w // 2, 0)
            hi = min(tl + w - w // 2, L)
            A[g, t, s0 + lo:s0 + hi] = 1.0 / (hi - lo)
            A[g, t, t] -= 1.0
    out = np.zeros((128, 4, 8, 3, 128), np.float32)
    for g in range(4):
        for i in range(8):
            for d in range(3):
                ip = i + d - 1
                if 0 <= ip < 8:
                    out[:, g, i, d, :] = A[g, i * 128:(i + 1) * 128, ip * 128:(ip + 1) * 128].T
    return out.reshape(128, 4, 8 * 3 * 128).astype(ml_dtypes.bfloat16)


def _dft_tables(L):
    t = np.arange(T)
    same = (t[:, None] // L) == (t[None, :] // L)
    ang = 2.0 * np.pi * ((t[:, None] % L) * (t[None, :] % L) % L) / L
    nrm = 1.0 / np.sqrt(L * 256.0)
    C = np.where(same, np.cos(ang), 0.0) * nrm
    S = np.where(same, -np.sin(ang), 0.0) * nrm

    def lay(M):
        return np.ascontiguousarray(M.reshape(8, 128, T).transpose(1, 0, 2)).astype(ml_dtypes.bfloat16)
    c = np.arange(256)
    a2 = 2.0 * np.pi * ((c[:, None] * c[None, :]) % 256) / 256.0
    CS = np.concatenate([np.cos(a2), np.sin(a2)], axis=1)
    CS = np.ascontiguousarray(CS.reshape(2, 128, 512).transpose(1, 0, 2)).astype(ml_dtypes.bfloat16)
    return lay(C), lay(S), CS


_PERM = np.concatenate([np.arange(0, 32, 2), np.arange(1, 32, 2)])
_PERM_SW = np.concatenate([np.arange(1, 32, 2), np.arange(0, 32, 2)])


def _mla_weights(inp):
    wdkv = np.asarray(inp["mla_wdkv"][0], np.float32)
    aug = np.zeros((D, 448), np.float32)
    aug[:, 0:256] = wdkv[:, 0:256]
    aug[:, 320:352] = wdkv[:, 256 + _PERM]
    aug[:, 416:448] = wdkv[:, 256 + _PERM_SW]
    wuq = np.asarray(inp["mla_wuq"][0], np.float32).reshape(384, 16, 96)
    qa = np.zeros((384, 16, 192), np.float32)
    qa[:, :, 0:64] = wuq[:, :, 0:64]
    qa[:, :, 64:96] = wuq[:, :, 64 + _PERM]
    qa[:, :, 160:192] = wuq[:, :, 64 + _PERM_SW]
    return aug, np.ascontiguousarray(qa.reshape(384, 16 * 192))


def _rope_tables(kind):
    cs = np.zeros((128, 2, T), np.float32)
    if kind == "p":
        cs[64:96, 0, :] = 1.0
        return cs
    t = np.arange(T)
    r = (t // 64).astype(np.float32)
    col = (t % 64).astype(np.float32)
    inv = (np.float32(10000.0) ** (-np.arange(8, dtype=np.float32) / np.float32(8))).astype(np.float32)
    ang = np.concatenate([r[:, None] * inv, col[:, None] * inv], axis=-1).astype(np.float32)
    c, s = np.cos(ang).T, np.sin(ang).T
    cs[64:80, 0, :] = c
    cs[80:96, 0, :] = c
    cs[64:80, 1, :] = -s
    cs[80:96, 1, :] = s
    return cs


def _mask_tables(kind):
    NEG = -30000.0
    eq = np.zeros((128, T), np.float32)
    ek = np.zeros((128, 1536), np.float32)
    if kind == "p":
        seq = np.arange(T) // 256
        for r in range(4):
            eq[96 + r, :] = (seq == r)
            ek[96 + r, 0:512] = NEG
            ek[96 + r, 512:] = np.where(seq == r, 0.0, NEG)
    return eq, ek


def _core_roles():
    return [("s", 0), ("s", 1), ("p", 0), ("p", 1), ("p", 2), ("p", 3), ("p", 3), ("p", 3)]


_NC_CACHE = {}


def make_in_maps(inp):
    roles = _core_roles()
    ident = np.eye(128, dtype=np.float32)
    shared = {
        "ident": ident,
        "ada_w": np.ascontiguousarray(inp["ada_w"], dtype=np.float32),
        "ffn_up": np.ascontiguousarray(inp["ffn_up"], dtype=np.float32),
        "ffn_down": np.ascontiguousarray(inp["ffn_down"], dtype=np.float32),
        "pool_w": np.ascontiguousarray(inp["pool_w"][0], dtype=np.float32),
        "fnet_w": np.ascontiguousarray(inp["fnet_w"][0], dtype=np.float32),
        "sconv_win": np.ascontiguousarray(inp["sconv_win"][0], dtype=np.float32),
        "sconv_wout": np.ascontiguousarray(inp["sconv_wout"][0], dtype=np.float32),
        "identb": ident.astype(ml_dtypes.bfloat16),
        "wdq": np.ascontiguousarray(inp["mla_wdq"][0], dtype=np.float32),
        "wukv": np.ascontiguousarray(inp["mla_wukv"][0], dtype=np.float32),
        "wo": np.ascontiguousarray(inp["mla_wo"][0], dtype=np.float32),
    }
    shared["wdkv_aug"], shared["wuq_aug"] = _mla_weights(inp)
    tabs = {}
    for kind, L in (("s", 1024), ("p", 256)):
        C, S, CS = _dft_tables(L)
        eq, ek = _mask_tables(kind)
        tabs[kind] = {"poolA": _pool_tables(L), "dftC": C, "dftS": S, "dftCS": CS,
                      "ropeCS": _rope_tables(kind), "eq": eq, "_ek": ek}
    in_maps = []
    for kind, idx in roles:
        if kind == "s":
            xc = inp["x_sample"][idx]
            cond = inp["c"][idx]
            flag = 0.0
        else:
            xc = inp["x_prompt"][4 * idx:4 * idx + 4].reshape(T, D)
            cond = inp["c_ctx"]
            flag = -1.0
        m = dict(shared)
        m.update({a: b for a, b in tabs[kind].items() if not a.startswith("_")})
        krm = tabs[kind]["_ek"].copy()
        cT = np.zeros((128, 2, 512), np.float32)
        if kind == "s":
            cT[:] = np.asarray(inp["cache_ckv"][idx, 0], np.float32).T.reshape(2, 128, 512).transpose(1, 0, 2)
            krm[64:96, 0:512] = np.asarray(inp["cache_krope"][idx, 0], np.float32)[:, _PERM].T
        m["cacheT"] = cT
        m["krmask"] = krm
        m["xin"] = np.ascontiguousarray(xc, dtype=np.float32)
        m["pvec"] = _make_pvec(inp, cond, flag)
        in_maps.append(m)
    return in_maps


def kernel(**inputs):
    inp = {k: np.asarray(v) for k, v in inputs.items()}
    in_maps = make_in_maps(inp)
    if "nc" not in _NC_CACHE:
        _NC_CACHE["nc"] = build_program(STAGE)
    nc = _NC_CACHE["nc"]
    res = run_bass_kernel_spmd(nc, in_maps, core_ids=list(range(NCORES)))
    outs = res.results
    y_sample = np.stack([outs[0]["yout"], outs[1]["yout"]], axis=0).astype(np.float32)
    y_prompt = np.concatenate([outs[2 + g]["yout"].reshape(4, 256, D) for g in range(4)], axis=0)
    y_prompt = y_prompt.astype(np.float32)
    new_ckv = np.concatenate([outs[2 + g]["ckv_o"].reshape(4, 1, 256, 256) for g in range(4)], axis=0)
    krp = np.concatenate([outs[2 + g]["kr_o"].reshape(4, 1, 256, 32) for g in range(4)], axis=0)
    new_kr = np.empty_like(krp)
    new_kr[..., _PERM] = krp
    new_ckv = new_ckv.astype(np.float32)
    new_kr = new_kr.astype(np.float32)
    return (y_prompt, y_sample, new_ckv, new_kr)
```

```python
import numpy as np
from contextlib import ExitStack
import ml_dtypes

import concourse.bass as bass
import concourse.mybir as mybir
from concourse.bass_utils import run_bass_kernel_spmd

F32 = mybir.dt.float32
BF16 = mybir.dt.bfloat16
AF = mybir.ActivationFunctionType
ALU = mybir.AluOpType

D = 1024
T = 1024
KC = 8
DFF = 2816
FC = 22
DEPTH = 4
EPS = 1e-6
NCORES = 8


class Buf:
    __slots__ = ("name", "w", "r", "sem", "cum", "excl")

    def __init__(self, name):
        self.name = name
        self.excl = False
        self.w = None
        self.r = {}
        self.sem = None
        self.cum = 0


class Q:
    def __init__(self, fw, name, own_wait=True):
        self.fw = fw
        self.name = name
        self.thunks = []
        self.sem = fw.new_sem("q_" + name)
        self.cnt = 0
        self.known = {}
        self.own_wait = own_wait


class FW:
    def __init__(self, nc, es):
        self.nc = nc
        self.es = es
        self.nsem = 0
        self.pe = Q(self, "pe", own_wait=False)
        self.act = Q(self, "act")
        self.dve = Q(self, "dve")
        self.pool = Q(self, "pool")
        self.sp = Q(self, "sp")
        self.out_recs = []
        self.dry = False

    def new_sem(self, name):
        self.nsem += 1
        return self.es.enter_context(self.nc.semaphore(f"s{self.nsem}_{name}"))

    def buf(self, name, dma=False):
        b = Buf(name)
        if dma:
            b.sem = self.new_sem("d_" + name)
        return b

    def _collect(self, q, reads, writes):
        waits = {}

        def need(rec):
            if rec is None:
                return
            sem, val = rec
            if sem is q.sem and not q.own_wait:
                return
            if q.known.get(sem, 0) >= val:
                return
            if waits.get(sem, 0) < val:
                waits[sem] = val

        for b in reads:
            need(b.w)
            if b.excl:
                for sem, val in b.r.items():
                    if sem is not q.sem:
                        need((sem, val))
        for b in writes:
            need(b.w)
            for sem, val in b.r.items():
                need((sem, val))
        for sem, val in waits.items():
            q.known[sem] = val
        return list(waits.items())

    @staticmethod
    def _commit(rec, reads, writes):
        sem, val = rec
        for b in reads:
            if b.r.get(sem, 0) < val:
                b.r[sem] = val
        for b in writes:
            b.w = rec
            b.r = {}

    def op(self, q, fn, reads=(), writes=()):
        if self.dry:
            return None
        wl = self._collect(q, reads, writes)
        q.cnt += 1
        rec = (q.sem, q.cnt)
        q.thunks.append((wl, fn, rec, 1))
        self._commit(rec, reads, writes)
        return rec

    def dma(self, q, out_ap, in_ap, dst=None, src=(), reads=(), kw=None):
        if self.dry:
            return None
        kw = kw or {}
        writes = [dst] if dst is not None else []
        rds = list(src) + list(reads)
        wl = self._collect(q, rds, writes)
        owner = dst if dst is not None else src[0]
        owner.cum += 16
        rec = (owner.sem, owner.cum)

        def fn(e, out_ap=out_ap, in_ap=in_ap, kw=kw):
            return e.dma_start(out=out_ap, in_=in_ap, **kw)

        q.thunks.append((wl, fn, rec, 16))
        self._commit(rec, rds, writes)
        if dst is None:
            self.out_recs.append(rec)
        return rec

    def barrier_bufs(self, bufs):
        allq = [self.pe, self.act, self.dve, self.pool]
        for b in bufs:
            for q in allq:
                if q.cnt > 0:
                    if b.r.get(q.sem, 0) < q.cnt:
                        b.r[q.sem] = q.cnt

    def replay(self, q, eng):
        for wl, fn, rec, inc in q.thunks:
            for sem, val in wl:
                eng.wait_ge(sem, val)
            ins = fn(eng)
            if isinstance(ins, (list, tuple)):
                ins = ins[-1]
            ins.then_inc(rec[0], inc)


class WStream:
    def __init__(self, fw, q, slots, bufs, slot_elems):
        self.fw = fw
        self.q = q
        self.slots = slots
        self.bufs = bufs
        self.n = len(slots)
        self.slot_elems = slot_elems
        self.specs = []
        self.reset()

    def reset(self):
        self.issued = 0
        self.consumed = 0
        self.done_flags = []

    def _view(self, k, shape):
        t = self.slots[k % self.n]
        n = int(np.prod(shape))
        assert n <= self.slot_elems, shape
        if len(shape) == 1:
            return t[:, 0:n]
        if len(shape) == 2:
            return t[:, 0:n].rearrange("p (a b) -> p a b", a=shape[0], b=shape[1])
        return t[:, 0:n].rearrange("p (a b c) -> p a b c", a=shape[0], b=shape[1], c=shape[2])

    def _pump(self):
        while self.issued < len(self.specs):
            k = self.issued
            if k >= self.n and not (k - self.n < len(self.done_flags) and self.done_flags[k - self.n]):
                break
            dram_ap, shape = self.specs[k]
            self.fw.dma(self.q, self._view(k, shape), dram_ap, dst=self.bufs[k % self.n])
            self.issued += 1

    def next(self, dram_ap, shape):
        shape = tuple(shape)
        if self.fw.dry:
            self.specs.append((dram_ap, shape))
            return self._view(0, shape), self.bufs[0], None
        k = self.consumed
        self.consumed += 1
        assert self.specs[k][1] == shape, (k, self.specs[k][1], shape)
        self.done_flags.append(False)
        self._pump()
        assert self.issued > k, (k, self.issued)
        return self._view(k, shape), self.bufs[k % self.n], k

    def done(self, k):
        if self.fw.dry:
            return
        self.done_flags[k] = True
        self._pump()


def _pvec_map():
    m = {}
    o = 0

    def add(name, n):
        nonlocal o
        m[name] = (o, n)
        o += n

    for l in range(DEPTH):
        add(f"n1w{l}", KC)
        add(f"n2w{l}", KC)
        add(f"fw0_{l}", FC)
        add(f"fw1_{l}", FC)
        add(f"fw2_{l}", FC)
        add(f"fb_{l}", FC)
        add(f"adab{l}", 48)
    add("fnw", KC)
    add("cond", KC)
    add("flagneg", 1)
    add("qnw", 3)
    add("kvnw", 2)
    add("pscale", KC)
    add("sw0", KC)
    add("sw1", KC)
    add("sw2", KC)
    m["_n"] = o
    return m


PV = _pvec_map()
NPV = PV["_n"]

STAGE = 4
MLA_SUB = 4.0
NSLOT = 6
SLOT_ELEMS = 4096


def build_program(stage=STAGE):
    nc = bass.Bass("TRN2", target_bir_lowering=False)

    def din(name, shape, dt=F32):
        return nc.dram_tensor(name, list(shape), dt, kind="ExternalInput").ap()

    def dout(name, shape, dt=F32):
        return nc.dram_tensor(name, list(shape), dt, kind="ExternalOutput").ap()

    xin = din("xin", [T, D])
    pvec_d = din("pvec", [128, NPV])
    ident_d = din("ident", [128, 128])
    ada_w = din("ada_w", [DEPTH, D, 6 * D])
    ffn_up = din("ffn_up", [DEPTH, D, 2 * DFF])
    ffn_down = din("ffn_down", [DEPTH, DFF, D])
    pool_w = din("pool_w", [4, 256, 256])
    poolA = din("poolA", [128, 4, 8 * 3 * 128], BF16)
    fnet_w = din("fnet_w", [D, D])
    dftC = din("dftC", [128, 8, T], BF16)
    dftS = din("dftS", [128, 8, T], BF16)
    dftCS = din("dftCS", [128, 2, 512], BF16)
    identb_d = din("identb", [128, 128], BF16)
    sconv_win = din("sconv_win", [D, 3 * D])
    sconv_wout = din("sconv_wout", [D, D])
    wdq = din("wdq", [D, 384])
    wdkv_aug = din("wdkv_aug", [D, 448])
    wuq_aug = din("wuq_aug", [384, 16 * 192])
    wukv = din("wukv", [256, 2048])
    wo = din("wo", [D, D])
    ropeCS_d = din("ropeCS", [128, 2, T])
    cacheT_d = din("cacheT", [128, 2, 512])
    krmask_d = din("krmask", [128, 1536])
    eq_d = din("eq", [128, T])
    yout = dout("yout", [T, D])
    ckv_o = dout("ckv_o", [T, 256])
    kr_o = dout("kr_o", [T, 32])

    es = ExitStack()
    with es:
        fw = FW(nc, es)
        PE, ACT, DVE, POOL, SP = fw.pe, fw.act, fw.dve, fw.pool, fw.sp

        def sb(name, shape, dt):
            return es.enter_context(nc.sbuf_tensor(name, list(shape), dt))

        x_t = sb("x", [128, KC, T], F32)
        xb = [[fw.buf(f"x{c}_{tt}") for tt in range(2)] for c in range(KC)]
        h_t = sb("h", [128, KC, T], BF16)
        hb = [[fw.buf(f"h{c}_{tt}") for tt in range(2)] for c in range(KC)]
        a_t = sb("a", [128, 25, T], BF16)
        ab = [fw.buf(f"a{j}") for j in range(25)]
        pvec = sb("pvec_sb", [128, NPV], F32)
        pvec_b = fw.buf("pvec", dma=True)
        ident = sb("ident_sb", [128, 128], F32)
        ident_b = fw.buf("ident", dma=True)
        ones_bf = sb("ones_bf", [128, 128], BF16)
        one_f = sb("one_f", [128, 1], F32)
        eps_t = sb("eps", [128, 1], F32)
        const_b = fw.buf("consts")
        stg_t = [sb(f"stg{i}", [128, D], F32) for i in range(2)]
        stg_b = [fw.buf(f"stg{i}", dma=True) for i in range(2)]
        tA_t = [sb(f"tA{i}", [128, T], F32) for i in range(2)]
        tA_b = [fw.buf(f"tA{i}", dma=True) for i in range(2)]
        stg4_t = stg_t + tA_t
        stg4_b = stg_b + tA_b
        sq_t = [sb(f"sq{i}", [128, 512], BF16) for i in range(4)]
        sq_b = [fw.buf(f"sq{i}") for i in range(4)]
        rstd_t = sb("rstd", [128, 512], F32)
        rstd_b = fw.buf("rstd")
        rstd1_t = sb("rstd1", [128, 512], F32)
        rstd1_b = fw.buf("rstd1")
        xn_t = [sb(f"xn{i}", [128, 512], F32) for i in range(4)]
        xn_b = [fw.buf(f"xn{i}") for i in range(4)]
        scond = sb("scond", [128, KC], BF16)
        scond_b = fw.buf("scond")
        row_t = [sb(f"row{i}", [1, 512], F32) for i in range(2)]
        row_b = [fw.buf(f"row{i}") for i in range(2)]
        mod_t = [sb(f"mod{l}", [128, 48], F32) for l in range(DEPTH)]
        mod_b = [fw.buf(f"mod{l}") for l in range(DEPTH)]
        col_t = [sb(f"cols{l}", [128, 16 + 2 * FC], F32) for l in range(DEPTH)]
        col_b = [fw.buf(f"cols{l}") for l in range(DEPTH)]
        identb = sb("identb_sb", [128, 128], BF16)
        identb_b = fw.buf("identb", dma=True)
        cs_t = sb("dftcs_sb", [128, 2, 512], BF16)
        cs_b = fw.buf("dftcs", dma=True)
        ropecs = sb("ropecs", [128, 2, T], F32)
        ropecs_b = fw.buf("ropecs", dma=True)
        sel_t = sb("sel", [128, 2, 128], F32)
        ckvst = sb("ckvst", [128, 8, 256], F32)
        ckvst_b = fw.buf("ckvst", dma=True)
        krst = sb("krst", [128, 8, 32], F32)
        krst_b = fw.buf("krst", dma=True)
        ckvall_b = [fw.buf(f"ckvall{c}", dma=True) for c in range(2)]
        kr_b = fw.buf("KR", dma=True)
        qt_b = [fw.buf(f"QT{i}", dma=True) for i in range(3)]
        kt_b = [fw.buf(f"KT{i}") for i in range(3)]
        vp_b = [fw.buf(f"VP{i}") for i in range(2)]
        pt_b = [fw.buf(f"PT{i}") for i in range(4)]
        cq_b = [fw.buf(f"cq{i}") for i in range(3)]
        mcol_t = sb("mcols", [128, 32], F32)
        mcol_b = fw.buf("mcols")
        slots = [sb(f"wslot{i}", [128, SLOT_ELEMS], BF16) for i in range(NSLOT)]
        slot_b = [fw.buf(f"wslot{i}", dma=True) for i in range(NSLOT)]
        ws = WStream(fw, POOL, slots, slot_b, SLOT_ELEMS)

        pd_t = [es.enter_context(nc.psum_tensor(f"pd{i}", [128, 1024], F32)) for i in range(4)]
        ps_b = [fw.buf(f"ps{i}") for i in range(8)]
        for b in ps_b:
            b.excl = True

        pdb_t = [t.bitcast(BF16) for t in pd_t]

        def ps(bank):
            return pd_t[bank // 2][:, (bank % 2) * 512:(bank % 2) * 512 + 512]

        def psb(bank):
            return pdb_t[bank // 2][:, (bank % 2) * 1024:(bank % 2) * 1024 + 1024]

        def pv(name, j=0, n=1):
            o, _ = PV[name]
            return pvec[:, o + j:o + j + n]

        def tsl(tt):
            return slice(tt * 512, (tt + 1) * 512)

        def emit():
            fw.dma(SP, pvec[:], pvec_d, dst=pvec_b)
            fw.dma(SP, ident[:], ident_d, dst=ident_b)
            fw.dma(SP, identb[:], identb_d, dst=identb_b)
            fw.dma(SP, cs_t[:], dftCS, dst=cs_b)
            fw.op(DVE, lambda e: e.memset(ones_bf[:], 1.0), writes=[const_b])
            fw.op(DVE, lambda e: e.memset(eps_t[:], EPS), writes=[const_b])
            fw.op(DVE, lambda e: e.memset(one_f[:], 1.0), writes=[const_b])
            fw.op(DVE, lambda e: e.memset(sel_t[:], 0.0), writes=[const_b])
            fw.op(DVE, lambda e: e.memset(sel_t[64:65, 0, :], 1.0), writes=[const_b])
            fw.op(DVE, lambda e: e.memset(sel_t[0:1, 1, :], 1.0), writes=[const_b])

            for i in range(8):
                s = i % 4
                tt = i // 4
                fw.dma(SP, stg4_t[s][:], xin[i * 128:(i + 1) * 128, :], dst=stg4_b[s])
                for half in range(2):
                    bank = (i * 2 + half) % 8

                    def mm(e, s=s, half=half, bank=bank):
                        r = None
                        for cc in range(4):
                            c = half * 4 + cc
                            r = e.transpose(ps(bank)[:, cc * 128:(cc + 1) * 128],
                                            stg4_t[s][:, c * 128:(c + 1) * 128], ident[:])
                        return r
                    fw.op(PE, mm, reads=[stg4_b[s], ident_b], writes=[ps_b[bank]])
                    wr = [xb[c][tt] for c in range(half * 4, half * 4 + 4)]
                    if half == 0:
                        fw.op(DVE, lambda e, half=half, bank=bank, i=i: e.tensor_copy(
                            out=x_t[:, half * 4:half * 4 + 4, i * 128:(i + 1) * 128],
                            in_=ps(bank).rearrange("p (c t) -> p c t", c=4)),
                            reads=[ps_b[bank]], writes=wr)
                    else:
                        fw.op(ACT, lambda e, half=half, bank=bank, i=i: e.activation(
                            out=x_t[:, half * 4:half * 4 + 4, i * 128:(i + 1) * 128],
                            in_=ps(bank).rearrange("p (c t) -> p c t", c=4), func=AF.Copy),
                            reads=[ps_b[bank]], writes=wr)

            fw.op(ACT, lambda e: e.activation(out=scond[:], in_=pv("cond", 0, KC), func=AF.Silu),
                  reads=[pvec_b], writes=[scond_b])

            sqn = [0]

            def rms_stats(srcs, nch, inv_n, bank, rt=None, rb=None):
                rt = rstd_t if rt is None else rt
                rb = rstd_b if rb is None else rb
                for c, (ap, bufs) in enumerate(srcs):
                    s = sqn[0] % 4
                    sqn[0] += 1
                    fw.op(ACT, lambda e, ap=ap, s=s: e.activation(out=sq_t[s][:], in_=ap,
                                                                   func=AF.Square),
                          reads=bufs, writes=[sq_b[s]])
                    fw.op(PE, lambda e, c=c, s=s: e.matmul(ps(bank), ones_bf[:], sq_t[s][:],
                                                           start=(c == 0), stop=(c == nch - 1)),
                          reads=[const_b, sq_b[s]], writes=[ps_b[bank]])
                rms_tail(inv_n, bank, rt, rb)

            def rms_tail(inv_n, bank, rt, rb):
                fw.op(ACT, lambda e: e.activation(out=rt[:], in_=ps(bank), func=AF.Ln,
                                                  bias=eps_t[:, 0:1], scale=inv_n),
                      reads=[ps_b[bank], const_b], writes=[rb])
                fw.op(ACT, lambda e: e.activation(out=rt[:], in_=rt[:], func=AF.Exp, scale=-0.5),
                      reads=[rb], writes=[rb])

            acc = {"pend": [], "cnt": [0, 0]}

            def acc_begin():
                acc["pend"] = []
                acc["cnt"] = [0, 0]

            def acc_x(oc, tt):
                assert fw.dry or len(acc["pend"]) < 4, len(acc["pend"])
                s = sqn[0] % 4
                sqn[0] += 1
                fw.op(ACT, lambda e, oc=oc, tt=tt, s=s: e.activation(out=sq_t[s][:], in_=x_t[:, oc, tsl(tt)],
                                                                       func=AF.Square),
                      reads=[xb[oc][tt]], writes=[sq_b[s]])
                acc["pend"].append((tt, s))

            def acc_pe(lag):
                while len(acc["pend"]) > lag:
                    tt, s = acc["pend"].pop(0)
                    c = acc["cnt"][tt]
                    acc["cnt"][tt] += 1
                    fw.op(PE, lambda e, c=c, s=s, tt=tt: e.matmul(ps(6 + tt), ones_bf[:], sq_t[s][:],
                                                                 start=(c == 0), stop=(c == KC - 1)),
                          reads=[const_b, sq_b[s]], writes=[ps_b[6 + tt]])

            def norm_mod(l, which, pre=False):
                ao = 0 if which == 1 else 8
                bo = 0 if which == 1 else 24
                rts = [(rstd_t, rstd_b), (rstd1_t, rstd1_b)]
                for tt in range(2):
                    if pre:
                        rms_tail(1.0 / D, 6 + tt, rts[tt][0], rts[tt][1])
                    else:
                        rms_stats([(x_t[:, c, tsl(tt)], [xb[c][tt]]) for c in range(KC)], KC, 1.0 / D,
                                  bank=tt, rt=rts[tt][0], rb=rts[tt][1])
                n = 0
                for tt in range(2):
                    rt, rb = rts[tt]
                    for c in range(KC):
                        s = n % 4
                        n += 1
                        fw.op(DVE, lambda e, c=c, s=s, tt=tt, rt=rt: e.tensor_tensor(
                            out=xn_t[s][:], in0=x_t[:, c, tsl(tt)], in1=rt[:], op=ALU.mult),
                            reads=[xb[c][tt], rb], writes=[xn_b[s]])
                        if c % 4 != 3:
                            fw.op(ACT, lambda e, c=c, s=s, tt=tt: e.activation(
                                out=h_t[:, c, tsl(tt)], in_=xn_t[s][:], func=AF.Identity,
                                bias=mod_t[l][:, bo + c:bo + c + 1],
                                scale=col_t[l][:, ao + c:ao + c + 1]),
                                reads=[xn_b[s], mod_b[l], col_b[l]], writes=[hb[c][tt]])
                        else:
                            fw.op(DVE, lambda e, c=c, s=s, tt=tt: e.tensor_scalar(
                                out=h_t[:, c, tsl(tt)], in0=xn_t[s][:],
                                scalar1=col_t[l][:, ao + c:ao + c + 1],
                                scalar2=mod_t[l][:, bo + c:bo + c + 1], op0=ALU.mult, op1=ALU.add),
                                reads=[xn_b[s], mod_b[l], col_b[l]], writes=[hb[c][tt]])

            def ada_block(l, nb, b0=0, b1=1):
                wv = ada_w[l].rearrange("(k p) n -> p k n", p=128)
                w, wb, wk = ws.next(wv[:, :, nb * 512:(nb + 1) * 512], (KC, 512))

                def mm(e, w=w):
                    r = None
                    for k in range(KC):
                        r = e.matmul(ps(b0)[0:1, :], scond[:, k:k + 1], w[:, k, :],
                                     start=(k == 0), stop=(k == KC - 1))
                    return r
                fw.op(PE, mm, reads=[scond_b, wb], writes=[ps_b[b0]])
                ws.done(wk)
                s = nb % 2
                fw.op(ACT, lambda e, s=s: e.activation(out=row_t[s][:], in_=ps(b0)[0:1, :], func=AF.Copy),
                      reads=[ps_b[b0]], writes=[row_b[s]])

                def mt(e, s=s):
                    r = None
                    for j in range(4):
                        r = e.matmul(ps(b1)[:, j:j + 1], row_t[s][0:1, j * 128:(j + 1) * 128],
                                     one_f[0:1, 0:1], start=True, stop=True)
                    return r
                fw.op(PE, mt, reads=[row_b[s], const_b], writes=[ps_b[b1]])
                fw.op(DVE, lambda e: e.tensor_tensor(out=mod_t[l][:, nb * 4:nb * 4 + 4], in0=ps(b1)[:, 0:4],
                                                     in1=pv(f"adab{l}", nb * 4, 4), op=ALU.add),
                      reads=[ps_b[b1], pvec_b], writes=[mod_b[l]])

            def ada_finish1(l):
                fw.op(DVE, lambda e: e.scalar_tensor_tensor(
                    out=col_t[l][:, 0:8], in0=mod_t[l][:, 8:16], scalar=1.0,
                    in1=pv(f"n1w{l}", 0, KC), op0=ALU.add, op1=ALU.mult),
                    reads=[mod_b[l], pvec_b], writes=[col_b[l]])
                fw.op(DVE, lambda e: e.tensor_scalar(
                    out=col_t[l][:, 16:16 + FC], in0=pv(f"fw0_{l}", 0, FC),
                    scalar1=pv("flagneg"), scalar2=None, op0=ALU.mult),
                    reads=[pvec_b], writes=[col_b[l]])
                fw.op(DVE, lambda e: e.tensor_scalar(
                    out=col_t[l][:, 16 + FC:16 + 2 * FC], in0=pv(f"fw2_{l}", 0, FC),
                    scalar1=pv("flagneg"), scalar2=None, op0=ALU.mult),
                    reads=[pvec_b], writes=[col_b[l]])


            def ada_finish2(l):
                fw.op(DVE, lambda e: e.scalar_tensor_tensor(
                    out=col_t[l][:, 8:16], in0=mod_t[l][:, 32:40], scalar=1.0,
                    in1=pv(f"n2w{l}", 0, KC), op0=ALU.add, op1=ALU.mult),
                    reads=[mod_b[l], pvec_b], writes=[col_b[l]])

            def ada_finish(l):
                ada_finish1(l)
                ada_finish2(l)

            def conv_fix(eng, t_ap, src_ap, nf0, nf2, reads, writes):
                fw.op(eng, lambda e: e.scalar_tensor_tensor(
                    out=t_ap[:, 256:1024:256], in0=src_ap[:, 255:1023:256], scalar=nf0,
                    in1=t_ap[:, 256:1024:256], op0=ALU.mult, op1=ALU.add),
                    reads=reads, writes=writes)
                fw.op(eng, lambda e: e.scalar_tensor_tensor(
                    out=t_ap[:, 255:1023:256], in0=src_ap[:, 256:1024:256], scalar=nf2,
                    in1=t_ap[:, 255:1023:256], op0=ALU.mult, op1=ALU.add),
                    reads=reads, writes=writes)

            def ffn(l, mid_hook=None):
                upv = ffn_up[l].rearrange("(k p) n -> p k n", p=128)
                dnv = ffn_down[l].rearrange("(k p) n -> p k n", p=128)
                j = 0
                for jg in range(6):
                    ncol = 512 if jg < 5 else 256
                    gw, gwb, gk = ws.next(upv[:, :, jg * 512:jg * 512 + ncol], (KC, ncol))
                    uw, uwb, uk = ws.next(upv[:, :, DFF + jg * 512:DFF + jg * 512 + ncol],
                                          (KC, ncol))
                    for jj in range(ncol // 128):
                        dbl = 2 * (j % 2)
                        for which, (w, wb) in enumerate(((gw, gwb), (uw, uwb))):
                            for tt in range(2):
                                bank = (dbl + which) * 2 + tt

                                def mm(e, w=w, jj=jj, tt=tt, bank=bank):
                                    r = None
                                    for k in range(KC):
                                        r = e.matmul(ps(bank), w[:, k, jj * 128:(jj + 1) * 128],
                                                     h_t[:, k, tsl(tt)],
                                                     start=(k == 0), stop=(k == KC - 1))
                                    return r
                                fw.op(PE, mm, reads=[wb] + [hb[k][tt] for k in range(KC)],
                                      writes=[ps_b[bank]])
                        g_ap = pd_t[dbl][:]
                        u_ap = pd_t[dbl + 1][:]
                        gB = [ps_b[dbl * 2], ps_b[dbl * 2 + 1]]
                        uB = [ps_b[dbl * 2 + 2], ps_b[dbl * 2 + 3]]
                        s = j % 2
                        t = tA_t[s]
                        tb = tA_b[s]
                        fw.op(ACT, lambda e, t=t, g_ap=g_ap, j=j: e.activation(
                            out=t[:], in_=g_ap, func=AF.Identity, bias=pv(f"fb_{l}", j),
                            scale=pv(f"fw1_{l}", j)),
                            reads=gB + [pvec_b], writes=[tb])
                        fw.op(DVE, lambda e, t=t, g_ap=g_ap, j=j: e.scalar_tensor_tensor(
                            out=t[:, 1:T], in0=g_ap[:, 0:T - 1], scalar=pv(f"fw0_{l}", j),
                            in1=t[:, 1:T], op0=ALU.mult, op1=ALU.add),
                            reads=gB + [pvec_b, tb], writes=[tb])
                        fw.op(DVE, lambda e, t=t, g_ap=g_ap, j=j: e.scalar_tensor_tensor(
                            out=t[:, 0:T - 1], in0=g_ap[:, 1:T], scalar=pv(f"fw2_{l}", j),
                            in1=t[:, 0:T - 1], op0=ALU.mult, op1=ALU.add),
                            reads=gB + [pvec_b, tb], writes=[tb])
                        conv_fix(DVE, t, g_ap, col_t[l][:, 16 + j:17 + j],
                                 col_t[l][:, 16 + FC + j:17 + FC + j],
                                 reads=gB + [col_b[l], tb], writes=[tb])
                        fw.op(ACT, lambda e, t=t: e.activation(out=t[:], in_=t[:], func=AF.Gelu),
                              reads=[tb], writes=[tb])
                        fw.op(DVE, lambda e, t=t, u_ap=u_ap, j=j: e.tensor_tensor(
                            out=a_t[:, j, :], in0=t[:], in1=u_ap, op=ALU.mult),
                            reads=[tb] + uB, writes=[ab[j]])
                        j += 1
                    ws.done(gk)
                    ws.done(uk)
                    if mid_hook is not None:
                        mid_hook(jg)
                acc_begin()
                fw.op(ACT, lambda e: e.activation(out=row_t[0][0:1, 0:1], in_=eps_t[0:1, 0:1], func=AF.Ln),
                      reads=[const_b], writes=[row_b[0]])
                for oc in range(KC):
                    w, wb, wk = ws.next(dnv[:, :, oc * 128:(oc + 1) * 128], (FC, 128))
                    for tt in range(2):
                        bank = (oc * 2 + tt) % 6

                        def mm(e, w=w, tt=tt, bank=bank):
                            r = None
                            for k in range(FC):
                                r = e.matmul(ps(bank), w[:, k, :], a_t[:, k, tsl(tt)],
                                             start=(k == 0), stop=(k == FC - 1))
                            return r
                        fw.op(PE, mm, reads=[wb] + ab[0:FC], writes=[ps_b[bank]])
                        acc_pe(1)
                        fw.op(DVE, lambda e, oc=oc, tt=tt, bank=bank: e.scalar_tensor_tensor(
                            out=x_t[:, oc, tsl(tt)], in0=ps(bank), scalar=mod_t[l][:, 40 + oc:41 + oc],
                            in1=x_t[:, oc, tsl(tt)], op0=ALU.mult, op1=ALU.add),
                            reads=[ps_b[bank], mod_b[l], xb[oc][tt]], writes=[xb[oc][tt]])
                        acc_x(oc, tt)
                    ws.done(wk)
                acc_pe(0)


            def evac(i, out_ap, in_ap, reads, writes):
                if i % 2 == 0:
                    fw.op(DVE, lambda e: e.tensor_copy(out=out_ap, in_=in_ap), reads=reads, writes=writes)
                else:
                    fw.op(ACT, lambda e: e.activation(out=out_ap, in_=in_ap, func=AF.Copy),
                          reads=reads, writes=writes)

            def out_linear(l, wdram, src_ap, src_bufs, gcol):
                wv = wdram.rearrange("(k p) n -> p k n", p=128)
                n = 0
                acc_begin()
                for nb in range(2):
                    w, wb, wk = ws.next(wv[:, :, nb * 512:(nb + 1) * 512], (KC, 512))
                    for o4 in range(4):
                        oc = nb * 4 + o4
                        for tt in range(2):
                            bank = n % 6
                            n += 1

                            def mm(e, w=w, o4=o4, tt=tt, bank=bank):
                                r = None
                                for kk in range(KC):
                                    r = e.matmul(ps(bank), w[:, kk, o4 * 128:(o4 + 1) * 128],
                                                 src_ap(kk, tt), start=(kk == 0), stop=(kk == KC - 1))
                                return r
                            fw.op(PE, mm, reads=[wb] + src_bufs(tt), writes=[ps_b[bank]])
                            acc_pe(2)
                            fw.op(DVE, lambda e, oc=oc, tt=tt, bank=bank: e.scalar_tensor_tensor(
                                out=x_t[:, oc, tsl(tt)], in0=ps(bank), scalar=gcol(oc),
                                in1=x_t[:, oc, tsl(tt)], op0=ALU.mult, op1=ALU.add),
                                reads=[ps_b[bank], mod_b[l], mcol_b, xb[oc][tt]], writes=[xb[oc][tt]])
                            acc_x(oc, tt)
                    ws.done(wk)
                acc_pe(0)

            def mixer_pool(l):
                fw.barrier_bufs(ab)
                fw.op(DVE, lambda e: e.tensor_tensor(out=mcol_t[:, 0:8], in0=pv("pscale", 0, KC),
                                                     in1=mod_t[l][:, 16:24], op=ALU.mult),
                      reads=[pvec_b, mod_b[l]], writes=[mcol_b])
                for i in range(8):
                    bank = i % 8

                    def tr(e, i=i, bank=bank):
                        r = None
                        for c in range(KC):
                            r = e.transpose(psb(bank)[:, c * 128:(c + 1) * 128],
                                            h_t[:, c, i * 128:(i + 1) * 128], identb[:])
                        return r
                    fw.op(PE, tr, reads=[hb[c][i // 4] for c in range(KC)] + [identb_b],
                          writes=[ps_b[bank]])
                    evac(i, a_t[:, i, :], psb(bank), [ps_b[bank]], [ab[i]])
                aw = ab_k = None
                for cc in range(8):
                    g = cc // 2
                    if cc % 2 == 0:
                        aw, awb, ak = ws.next(poolA[:, g, :].rearrange("p (a b c) -> p a b c", a=8, b=3, c=128), (8, 3, 128))
                    dbl = cc % 4

                    def mm(e, cc=cc, aw=aw, dbl=dbl):
                        r = None
                        for i in range(8):
                            ds = [d for d in range(3) if 0 <= i + d - 1 < 8]
                            for n, d in enumerate(ds):
                                r = e.matmul(pd_t[dbl][:, i * 128:(i + 1) * 128],
                                             a_t[:, i + d - 1, cc * 128:(cc + 1) * 128],
                                             aw[:, i, d, :], start=(n == 0), stop=(n == len(ds) - 1))
                        return r
                    fw.op(PE, mm, reads=ab[0:8] + [awb], writes=[ps_b[2 * dbl], ps_b[2 * dbl + 1]])
                    evac(cc, a_t[:, 8 + cc, :], pd_t[dbl][:], [ps_b[2 * dbl], ps_b[2 * dbl + 1]],
                         [ab[8 + cc]])
                    if cc % 2 == 1:
                        ws.done(ak)
                pw, pwb, pk = ws.next(pool_w.rearrange("g (kk p) d -> p g kk d", p=128), (4, 2, 256))
                n = 0
                acc_begin()
                for dc in range(8):
                    g = dc // 2
                    for tt in range(2):
                        bank = n % 6
                        n += 1

                        def mm2(e, dc=dc, g=g, tt=tt, bank=bank):
                            r = None
                            for kk in range(2):
                                r = e.matmul(ps(bank), pw[:, g, kk, (dc % 2) * 128:(dc % 2) * 128 + 128],
                                             a_t[:, 8 + g * 2 + kk, tsl(tt)], start=(kk == 0), stop=(kk == 1))
                            return r
                        fw.op(PE, mm2, reads=[pwb, ab[8 + g * 2], ab[9 + g * 2]], writes=[ps_b[bank]])
                        acc_pe(3)
                        fw.op(DVE, lambda e, dc=dc, tt=tt, bank=bank: e.scalar_tensor_tensor(
                            out=x_t[:, dc, tsl(tt)], in0=ps(bank), scalar=mcol_t[:, dc:dc + 1],
                            in1=x_t[:, dc, tsl(tt)], op0=ALU.mult, op1=ALU.add),
                            reads=[ps_b[bank], mcol_b, xb[dc][tt]], writes=[xb[dc][tt]])
                        acc_x(dc, tt)
                acc_pe(0)
                ws.done(pk)

            def mixer_fnet(l):
                fw.barrier_bufs(ab)

                def pq(i, g):
                    return a_t[:, 2 * i + g // 2, (g % 2) * 512:(g % 2) * 512 + 512]
                n = 0
                for i in range(8):
                    for g in range(4):
                        bank = n % 8

                        def mm(e, i=i, g=g, bank=bank):
                            r = None
                            for kk in range(2):
                                r = e.matmul(ps(bank), h_t[:, g * 2 + kk, i * 128:(i + 1) * 128],
                                             cs_t[:, kk, :], start=(kk == 0), stop=(kk == 1))
                            return r
                        fw.op(PE, mm, reads=[hb[g * 2][i // 4], hb[g * 2 + 1][i // 4], cs_b],
                              writes=[ps_b[bank]])
                        evac(n, pq(i, g), ps(bank), [ps_b[bank]], [ab[2 * i + g // 2]])
                        n += 1
                for tt in range(2):
                    cw, cwb, ck = ws.next(dftC[:, :, tsl(tt)], (8, 512))
                    sw, swb, sk = ws.next(dftS[:, :, tsl(tt)], (8, 512))
                    for mc in range(8):
                        g, m2 = mc // 2, mc % 2
                        bank = n % 8

                        def mm(e, g=g, m2=m2, bank=bank, cw=cw, sw=sw):
                            r = None
                            for i in range(8):
                                r = e.matmul(ps(bank), pq(i, g)[:, m2 * 128:(m2 + 1) * 128], cw[:, i, :],
                                             start=(i == 0), stop=False)
                                r = e.matmul(ps(bank), pq(i, g)[:, 256 + m2 * 128:256 + (m2 + 1) * 128],
                                             sw[:, i, :], start=False, stop=(i == 7))
                            return r
                        fw.op(PE, mm, reads=ab[0:16] + [cwb, swb], writes=[ps_b[bank]])
                        evac(n, a_t[:, 16 + mc, tsl(tt)], ps(bank), [ps_b[bank]], [ab[16 + mc]])
                        n += 1
                    ws.done(ck)
                    ws.done(sk)
                out_linear(l, fnet_w, lambda kk, tt: a_t[:, 16 + kk, tsl(tt)],
                           lambda tt: ab[16:24], lambda oc: mod_t[l][:, 16 + oc:17 + oc])

            def mixer_sconv(l):
                fw.barrier_bufs(ab)
                fw.op(DVE, lambda e: e.tensor_scalar(out=mcol_t[:, 8:16], in0=pv("sw0", 0, KC),
                                                     scalar1=pv("flagneg"), scalar2=None, op0=ALU.mult),
                      reads=[pvec_b], writes=[mcol_b])
                fw.op(DVE, lambda e: e.tensor_scalar(out=mcol_t[:, 16:24], in0=pv("sw2", 0, KC),
                                                     scalar1=pv("flagneg"), scalar2=None, op0=ALU.mult),
                      reads=[pvec_b], writes=[mcol_b])
                wv = sconv_win.rearrange("(k p) n -> p k n", p=128)
                nd = 0
                for jg in range(2):
                    wl = [ws.next(wv[:, :, part * D + jg * 512:part * D + jg * 512 + 512], (KC, 512))
                          for part in range(3)]
                    for jj in range(4):
                        j = jg * 4 + jj
                        dbls = []
                        dbls = [None, None, None]
                        for part in (2, 1, 0):
                            dbl = nd % 4
                            nd += 1
                            dbls[part] = dbl
                            w, wb, _ = wl[part]
                            for tt in range(2):
                                bank = 2 * dbl + tt

                                def mm(e, w=w, jj=jj, tt=tt, bank=bank):
                                    r = None
                                    for kk in range(KC):
                                        r = e.matmul(ps(bank), w[:, kk, jj * 128:(jj + 1) * 128],
                                                     h_t[:, kk, tsl(tt)], start=(kk == 0), stop=(kk == KC - 1))
                                    return r
                                fw.op(PE, mm, reads=[wb] + [hb[kk][tt] for kk in range(KC)],
                                      writes=[ps_b[bank]])
                        gbB = [ps_b[2 * dbls[0]], ps_b[2 * dbls[0] + 1]]
                        gcB = [ps_b[2 * dbls[1]], ps_b[2 * dbls[1] + 1]]
                        uB = [ps_b[2 * dbls[2]], ps_b[2 * dbls[2] + 1]]
                        s = j % 2
                        v, vb = tA_t[s], tA_b[s]
                        t2, t2b = stg_t[s], stg_b[s]
                        fw.op(ACT, lambda e, v=v, d=dbls[2]: e.activation(out=v[:], in_=pd_t[d][:], func=AF.Copy),
                              reads=uB, writes=[vb])
                        fw.op(DVE, lambda e, v=v, d=dbls[1]: e.tensor_tensor(out=v[:], in0=pd_t[d][:], in1=v[:],
                                                                             op=ALU.mult),
                              reads=gcB + [vb], writes=[vb])
                        fw.op(ACT, lambda e, v=v, t2=t2, j=j: e.activation(out=t2[:], in_=v[:], func=AF.Copy,
                                                                             scale=pv("sw1", j)),
                              reads=[vb, pvec_b], writes=[t2b])
                        fw.op(DVE, lambda e, v=v, t2=t2, j=j: e.scalar_tensor_tensor(
                            out=t2[:, 1:T], in0=v[:, 0:T - 1], scalar=pv("sw0", j), in1=t2[:, 1:T],
                            op0=ALU.mult, op1=ALU.add), reads=[vb, pvec_b, t2b], writes=[t2b])
                        fw.op(DVE, lambda e, v=v, t2=t2, j=j: e.scalar_tensor_tensor(
                            out=t2[:, 0:T - 1], in0=v[:, 1:T], scalar=pv("sw2", j), in1=t2[:, 0:T - 1],
                            op0=ALU.mult, op1=ALU.add), reads=[vb, pvec_b, t2b], writes=[t2b])
                        conv_fix(DVE, t2, v, mcol_t[:, 8 + j:9 + j], mcol_t[:, 16 + j:17 + j],
                                 reads=[vb, mcol_b, t2b], writes=[t2b])
                        fw.op(DVE, lambda e, t2=t2, j=j, d=dbls[0]: e.tensor_tensor(
                            out=a_t[:, j, :], in0=pd_t[d][:], in1=t2[:], op=ALU.mult),
                            reads=gbB + [t2b], writes=[ab[j]])
                    for _, _, wk in wl:
                        ws.done(wk)
                out_linear(l, sconv_wout, lambda kk, tt: a_t[:, kk, tsl(tt)],
                           lambda tt: ab[0:8], lambda oc: mod_t[l][:, 16 + oc:17 + oc])


            def mixer_mla(l):
                SC = 1.0 / float(np.sqrt(96.0))
                arena = a_t[:].rearrange("p a b -> p (a b)")
                cqT = lambda c: a_t[:, c, :]
                ckvall = lambda c: arena[:, 3 * T + c * 1536:3 * T + (c + 1) * 1536]
                KR = arena[:, 6 * T:6 * T + 1536]
                QT = lambda r: a_t[:, 8 + r, :]
                KT = lambda r: arena[:, (11 + 2 * r) * T:(11 + 2 * r) * T + 1536]
                VP = lambda r: arena[:, (17 + 3 * r) * T:(17 + 3 * r) * T + 3072].rearrange(
                    "p (k x) -> p k x", k=12, x=256)
                PT = lambda r: arena[:, 23 * T + r * 512:23 * T + (r + 1) * 512]
                for i in range(2):
                    rr = fw.dma(SP, ropecs[:, i, :], ropeCS_d[:, i, :], dst=ropecs_b)
                    if MLA_SUB == 0.11 and not fw.dry:
                        fw.out_recs.append(rr)
                for c in range(2):
                    fw.dma(POOL, ckvall(c)[:, 0:512], cacheT_d[:, c, :], dst=ckvall_b[c])
                fw.dma(POOL, KR, krmask_d, dst=kr_b)
                for r in range(3):
                    fw.dma(POOL, QT(r), eq_d, dst=qt_b[r])
                for r in range(2):
                    fw.op(DVE, lambda e, r=r: e.memset(VP(r), 0.0), writes=[vp_b[r]])
                    fw.op(DVE, lambda e, r=r: e.memset(VP(r)[:, :, 64:65], 1.0), writes=[vp_b[r]])
                    fw.op(DVE, lambda e, r=r: e.memset(VP(r)[:, :, 128:129], 1.0), writes=[vp_b[r]])

                if MLA_SUB < 0.2:
                    return
                w, wb, wk = ws.next(wdq.rearrange("(k p) n -> p k n", p=128), (KC, 384))
                for tt in range(2):
                    banks = [(4 * tt + c) % 8 for c in range(3)]
                    for c in range(3):
                        def mm(e, c=c, tt=tt, bank=banks[c], w=w):
                            r = None
                            for kk in range(KC):
                                r = e.matmul(ps(bank), w[:, kk, c * 128:(c + 1) * 128], h_t[:, kk, tsl(tt)],
                                             start=(kk == 0), stop=(kk == KC - 1))
                            return r
                        fw.op(PE, mm, reads=[wb] + [hb[kk][tt] for kk in range(KC)], writes=[ps_b[banks[c]]])
                    rms_stats([(ps(banks[c]), [ps_b[banks[c]]]) for c in range(3)], 3, 1.0 / 384,
                              bank=(4 * tt + 3) % 8)
                    for c in range(3):
                        s = c % 2
                        fw.op(DVE, lambda e, s=s, bank=banks[c]: e.tensor_tensor(
                            out=xn_t[s][:], in0=ps(bank), in1=rstd_t[:], op=ALU.mult),
                            reads=[ps_b[banks[c]], rstd_b], writes=[xn_b[s]])
                        fw.op(ACT, lambda e, s=s, c=c, tt=tt: e.activation(
                            out=cqT(c)[:, tsl(tt)], in_=xn_t[s][:], func=AF.Copy, scale=pv("qnw", c)),
                            reads=[xn_b[s], pvec_b], writes=[cq_b[c]])
                ws.done(wk)

                if MLA_SUB < 0.5:
                    return
                w, wb, wk = ws.next(wdkv_aug.rearrange("(k p) n -> p k n", p=128), (KC, 448))
                for tt in range(2):
                    banks = [(5 * tt + i) % 8 for i in range(4)]
                    cols = [(0, 128), (128, 128), (256, 96), (352, 96)]
                    for i in range(4):
                        c0, m = cols[i]

                        def mm(e, c0=c0, m=m, tt=tt, bank=banks[i], w=w):
                            r = None
                            for kk in range(KC):
                                r = e.matmul(ps(bank)[0:m, :], w[:, kk, c0:c0 + m], h_t[:, kk, tsl(tt)],
                                             start=(kk == 0), stop=(kk == KC - 1))
                            return r
                        fw.op(PE, mm, reads=[wb] + [hb[kk][tt] for kk in range(KC)], writes=[ps_b[banks[i]]])
                    if MLA_SUB < 0.51:
                        continue
                    rms_stats([(ps(banks[c]), [ps_b[banks[c]]]) for c in range(2)], 2, 1.0 / 256,
                              bank=(5 * tt + 4) % 8)
                    if MLA_SUB < 0.52:
                        continue
                    for c in range(2):
                        s = c % 2
                        fw.op(DVE, lambda e, s=s, bank=banks[c]: e.tensor_tensor(
                            out=xn_t[s][:], in0=ps(bank), in1=rstd_t[:], op=ALU.mult),
                            reads=[ps_b[banks[c]], rstd_b], writes=[xn_b[s]])
                        fw.op(ACT, lambda e, s=s, c=c, tt=tt: e.activation(
                            out=ckvall(c)[:, 512 + tt * 512:1024 + tt * 512], in_=xn_t[s][:], func=AF.Copy,
                            scale=pv("kvnw", c)),
                            reads=[xn_b[s], pvec_b], writes=[ckvall_b[c]])
                        fw.op(DVE, lambda e, s=s, c=c, tt=tt: e.tensor_scalar(
                            out=tA_t[c][:, tsl(tt)], in0=xn_t[s][:], scalar1=pv("kvnw", c), scalar2=None,
                            op0=ALU.mult),
                            reads=[xn_b[s], pvec_b], writes=[tA_b[c]])
                    if MLA_SUB < 0.55:
                        continue
                    bA, bB = banks[2], banks[3]
                    fw.op(ACT, lambda e, tt=tt, bA=bA: e.activation(
                        out=stg_t[0][64:96, tsl(tt)], in_=ps(bA)[64:96, :], func=AF.Copy),
                        reads=[ps_b[bA]], writes=[stg_b[0]])
                    if MLA_SUB < 0.56:
                        continue
                    fw.op(DVE, lambda e, tt=tt, bA=bA: e.tensor_tensor(
                        out=xn_t[0][64:96, :], in0=ps(bA)[64:96, :], in1=ropecs[64:96, 0, tsl(tt)], op=ALU.mult),
                        reads=[ps_b[bA], ropecs_b], writes=[xn_b[0]])
                    if MLA_SUB < 0.57:
                        continue
                    fw.op(DVE, lambda e, tt=tt, bB=bB: e.tensor_tensor(
                        out=xn_t[1][64:96, :], in0=ps(bB)[64:96, :], in1=ropecs[64:96, 1, tsl(tt)], op=ALU.mult),
                        reads=[ps_b[bB], ropecs_b], writes=[xn_b[1]])
                    if MLA_SUB < 0.58:
                        continue
                    fw.op(DVE, lambda e, tt=tt: e.tensor_tensor(
                        out=KR[64:96, 512 + tt * 512:1024 + tt * 512], in0=xn_t[0][64:96, :],
                        in1=xn_t[1][64:96, :], op=ALU.add),
                        reads=[xn_b[0], xn_b[1]], writes=[kr_b])
                ws.done(wk)

                for nb in range(4, 12):
                    ada_block(0, nb, 2 + nb % 2, 4 + nb % 2)
                ada_finish2(0)
                for i in range(8):
                    bank = i % 8

                    def tr(e, i=i, bank=bank):
                        r = None
                        for c in range(2):
                            r = e.transpose(ps(bank)[:, c * 128:(c + 1) * 128],
                                            tA_t[c][:, i * 128:(i + 1) * 128], ident[:])
                        r = e.transpose(ps(bank)[:, 256:384], stg_t[0][:, i * 128:(i + 1) * 128], ident[:])
                        return r
                    fw.op(PE, tr, reads=[tA_b[0], tA_b[1], stg_b[0], ident_b], writes=[ps_b[bank]])
                    fw.op(DVE, lambda e, i=i, bank=bank: e.tensor_copy(out=ckvst[:, i, :], in_=ps(bank)[:, 0:256]),
                          reads=[ps_b[bank]], writes=[ckvst_b])
                    fw.op(ACT, lambda e, i=i, bank=bank: e.activation(out=krst[:, i, :], in_=ps(bank)[:, 320:352],
                                                                     func=AF.Copy),
                          reads=[ps_b[bank]], writes=[krst_b])
                fw.dma(SP, ckv_o.rearrange("(i p) c -> p i c", p=128), ckvst[:], dst=None, src=[ckvst_b])
                fw.dma(SP, kr_o.rearrange("(i p) c -> p i c", p=128), krst[:], dst=None, src=[krst_b])

                for r in range(3):
                    fw.op(DVE, lambda e, r=r: e.tensor_copy(out=KT(r)[64:128, :], in_=KR[64:128, :]),
                          reads=[kr_b], writes=[kt_b[r]])

                fw.op(DVE, lambda e: e.memset(stg_t[1][:], 0.0), writes=[stg_b[1]])
                if MLA_SUB < 2:
                    return
                ukv, ukvb, ukvk = ws.next(wukv.rearrange("(k p) n -> p k n", p=128), (2, 2048))
                wqv = wuq_aug.rearrange("(k p) n -> p k n", p=128)
                st = {'qw': None, 'n_o': 0, 'n_s': 0, 'n_p': 0}

                def prep_V(p):
                    vr = p % 2
                    for ktg in range(3):
                        bank = 6 + ktg % 2

                        def mmv(e, ktg=ktg, bank=bank, p=p):
                            r = None
                            for j in range(4):
                                kt = ktg * 4 + j
                                for kc in range(2):
                                    rhs = ukv[:, kc, p * 256:(p + 1) * 256].rearrange("p (h x) -> p h x", h=2)[:, :, 64:128]
                                    r = e.matmul(ps(bank)[:, j * 128:(j + 1) * 128].rearrange("p (h x) -> p h x", h=2),
                                                 ckvall(kc)[:, kt * 128:(kt + 1) * 128], rhs,
                                                 start=(kc == 0), stop=(kc == 1))
                            return r
                        fw.op(PE, mmv, reads=[ukvb, ckvall_b[0], ckvall_b[1]], writes=[ps_b[bank]])
                        src = ps(bank).rearrange("p (j h x) -> p j h x", j=4, h=2, x=64)
                        fw.op(DVE, lambda e, src=src, ktg=ktg, vr=vr: e.tensor_copy(
                            out=VP(vr)[:, ktg * 4:ktg * 4 + 4, 0:64], in_=src[:, :, 0, :]),
                            reads=[ps_b[bank]], writes=[vp_b[vr]])
                        fw.op(ACT, lambda e, src=src, ktg=ktg, vr=vr: e.activation(
                            out=VP(vr)[:, ktg * 4:ktg * 4 + 4, 192:256], in_=src[:, :, 1, :], func=AF.Copy),
                            reads=[ps_b[bank]], writes=[vp_b[vr]])

                def prep_KQ(h):
                    r3 = h % 3
                    if h % 4 == 0:
                        if st['qw'] is not None:
                            ws.done(st['qw'][2])
                        st['qw'] = ws.next(wqv[:, :, (h // 4) * 768:(h // 4 + 1) * 768], (3, 768))
                    qw = st['qw']
                    for kt5 in range(3):
                        bank = 6 + kt5 % 2

                        def mmk(e, h=h, kt5=kt5, bank=bank):
                            r = None
                            for kc in range(2):
                                r = e.matmul(ps(bank)[0:64, :], ukv[:, kc, h * 128:h * 128 + 64],
                                             ckvall(kc)[:, kt5 * 512:(kt5 + 1) * 512],
                                             start=(kc == 0), stop=(kc == 1))
                            return r
                        fw.op(PE, mmk, reads=[ukvb, ckvall_b[0], ckvall_b[1]], writes=[ps_b[bank]])
                        evac(kt5, KT(r3)[0:64, kt5 * 512:(kt5 + 1) * 512], ps(bank)[0:64, :],
                             [ps_b[bank]], [kt_b[r3]])
                    qcol = (h % 4) * 192
                    for tt in range(2):
                        for which in range(2):
                            bank = 6 + which

                            def mmq(e, which=which, tt=tt, bank=bank, qcol=qcol, qwv=qw[0]):
                                r = None
                                for kc in range(3):
                                    r = e.matmul(ps(bank)[0:96, :],
                                                 qwv[:, kc, qcol + which * 96:qcol + which * 96 + 96],
                                                 cqT(kc)[:, tsl(tt)], start=(kc == 0), stop=(kc == 2))
                                return r
                            fw.op(PE, mmq, reads=[qw[1]] + cq_b, writes=[ps_b[bank]])
                        fw.op(ACT, lambda e, r3=r3, tt=tt: e.activation(
                            out=QT(r3)[0:64, tsl(tt)], in_=ps(6)[0:64, :], func=AF.Copy),
                            reads=[ps_b[6]], writes=[qt_b[r3]])
                        fw.op(DVE, lambda e, tt=tt: e.tensor_tensor(
                            out=xn_t[0][64:96, :], in0=ps(6)[64:96, :], in1=ropecs[64:96, 0, tsl(tt)],
                            op=ALU.mult), reads=[ps_b[6], ropecs_b], writes=[xn_b[0]])
                        fw.op(DVE, lambda e, tt=tt: e.tensor_tensor(
                            out=xn_t[1][64:96, :], in0=ps(7)[64:96, :], in1=ropecs[64:96, 1, tsl(tt)],
                            op=ALU.mult), reads=[ps_b[7], ropecs_b], writes=[xn_b[1]])
                        fw.op(DVE, lambda e, r3=r3, tt=tt: e.tensor_tensor(
                            out=QT(r3)[64:96, tsl(tt)], in0=xn_t[0][64:96, :], in1=xn_t[1][64:96, :],
                            op=ALU.add), reads=[xn_b[0], xn_b[1]], writes=[qt_b[r3]])

                def attend(h, tts, pending=None):
                    p, hh = h // 2, h % 2
                    vr = p % 2
                    r3 = h % 3
                    if hh == 0:
                        vcols, orows, srow, om = (0, 65), (0, 64), 64, 65
                    else:
                        vcols, orows, srow, om = (128, 256), (64, 128), 0, 128
                    for tt in (tts if MLA_SUB >= 3 else ()):
                        ob = 2 + st['n_o'] % 3
                        st['n_o'] += 1
                        pend = []
                        for kt in range(14):
                            if kt < 12:
                                sb_ = (0, 1, 5)[st['n_s'] % 3]
                                st['n_s'] += 1
                                fw.op(PE, lambda e, sb_=sb_, kt=kt, r3=r3, tt=tt: e.matmul(
                                    ps(sb_), KT(r3)[:, kt * 128:(kt + 1) * 128], QT(r3)[:, tsl(tt)],
                                    start=True, stop=True),
                                    reads=[kt_b[r3], qt_b[r3]], writes=[ps_b[sb_]])
                                pi = st['n_p'] % 4
                                st['n_p'] += 1
                                fw.op(ACT, lambda e, sb_=sb_, pi=pi: e.activation(
                                    out=PT(pi), in_=ps(sb_), func=AF.Exp, scale=SC),
                                    reads=[ps_b[sb_]], writes=[pt_b[pi]])
                            if kt >= 2:
                                pkt, ppi = pend.pop(0)
                                fw.op(PE, lambda e, ob=ob, om=om, vr=vr, pkt=pkt, ppi=ppi, vcols=vcols: e.matmul(
                                    ps(ob)[0:om, :], VP(vr)[:, pkt, vcols[0]:vcols[1]], PT(ppi),
                                    start=(pkt == 0), stop=(pkt == 11)),
                                    reads=[vp_b[vr], pt_b[ppi]], writes=[ps_b[ob]])
                            if kt < 12:
                                pend.append((kt, pi))
                            if kt == 8 and pending is not None:
                                pending()
                                pending = None
                        if MLA_SUB < 4:
                            continue
                        rs = stg_t[1][srow:srow + 1, 0:512] if hh == 0 else stg_t[1][srow:srow + 1, 512:1024]
                        fw.op(ACT, lambda e, rs=rs, ob=ob, srow=srow: e.activation(
                            out=rs, in_=ps(ob)[srow:srow + 1, :], func=AF.Ln), reads=[ps_b[ob]], writes=[stg_b[1]])
                        fw.op(ACT, lambda e, rs=rs: e.activation(out=rs, in_=rs, func=AF.Exp, scale=-1.0),
                              reads=[stg_b[1]], writes=[stg_b[1]])
                        return lambda ob=ob, tt=tt: norm_o(h, tt, ob)
                    return None

                def norm_o(h, tt, ob):
                    p, hh = h // 2, h % 2
                    if hh == 0:
                        orows, srow = (0, 64), 64
                    else:
                        orows, srow = (64, 128), 0
                    if True:
                        bb = 6 + st['n_o'] % 2
                        fw.op(PE, lambda e, bb=bb, hh=hh: e.matmul(
                            ps(bb), sel_t[:, hh, :], stg_t[1][:, hh * 512:(hh + 1) * 512],
                            start=True, stop=True),
                            reads=[stg_b[1], const_b], writes=[ps_b[bb]])
                        fw.op(ACT, lambda e, bb=bb: e.activation(out=rstd_t[:], in_=ps(bb), func=AF.Copy),
                              reads=[ps_b[bb]], writes=[rstd_b])
                        fw.op(DVE, lambda e, ob=ob, orows=orows, p=p, tt=tt: e.tensor_tensor(
                            out=h_t[orows[0]:orows[1], p, tsl(tt)], in0=ps(ob)[orows[0]:orows[1], :],
                            in1=rstd_t[orows[0]:orows[1], :], op=ALU.mult),
                            reads=[ps_b[ob], rstd_b], writes=[hb[p][tt]])

                prep_V(0)
                prep_KQ(0)
                pnd = None
                for h in range(16):
                    pnd = attend(h, (0,), pnd)
                    if h + 1 < 16:
                        if (h + 1) % 2 == 0:
                            prep_V((h + 1) // 2)
                        prep_KQ(h + 1)
                    pnd = attend(h, (1,), pnd)
                if pnd is not None:
                    pnd()
                qw = st['qw']
                ws.done(qw[2])
                ws.done(ukvk)
                out_linear(l, wo, lambda kk, tt: h_t[:, kk, tsl(tt)],
                           lambda tt: [hb[kk][tt] for kk in range(KC)], lambda oc: mod_t[l][:, 16 + oc:17 + oc])

            def mixer(l):
                norm_mod(l, 1, pre=(l > 0))
                if l == 0 and stage >= 4:
                    mixer_mla(l)
                elif l == 1 and stage >= 3:
                    mixer_pool(l)
                elif l == 2 and stage >= 3:
                    mixer_fnet(l)
                elif l == 3 and stage >= 3:
                    mixer_sconv(l)
                fw.barrier_bufs(ab)

            if stage >= 2:
                for nb in range(4 if stage >= 4 else 12):
                    ada_block(0, nb, 4 + nb % 2, 6 + nb % 2)
                ada_finish1(0)
                if stage < 4:
                    ada_finish2(0)
                for l in range(DEPTH):
                    mixer(l)
                    norm_mod(l, 2, pre=(stage >= 4))
                    if l + 1 < DEPTH:
                        def hook(jg, l=l):
                            ada_block(l + 1, 2 * jg, 0, 1)
                            ada_block(l + 1, 2 * jg + 1, 2, 3)
                            if jg == 5:
                                ada_finish(l + 1)
                        ffn(l, hook)
                    else:
                        ffn(l)

            fo, _ = PV["fnw"]
            for tt in range(2):
                if stage >= 4:
                    rms_tail(1.0 / D, 6 + tt, rstd_t, rstd_b)
                else:
                    rms_stats([(x_t[:, c, tsl(tt)], [xb[c][tt]]) for c in range(KC)], KC, 1.0 / D,
                              bank=tt)
                for c in range(KC):
                    fw.op(DVE, lambda e, c=c, tt=tt: e.scalar_tensor_tensor(
                        out=x_t[:, c, tsl(tt)], in0=x_t[:, c, tsl(tt)],
                        scalar=pvec[:, fo + c:fo + c + 1], in1=rstd_t[:],
                        op0=ALU.mult, op1=ALU.mult),
                        reads=[xb[c][tt], rstd_b, pvec_b], writes=[xb[c][tt]])
            for i in range(8):
                s = i % 4
                tt = i // 4
                for half in range(2):
                    bank = 2 + (i * 2 + half) % 6

                    def mm(e, half=half, bank=bank, i=i):
                        r = None
                        for cc in range(4):
                            c = half * 4 + cc
                            r = e.transpose(ps(bank)[:, cc * 128:(cc + 1) * 128],
                                            x_t[:, c, i * 128:(i + 1) * 128], ident[:])
                        return r
                    fw.op(PE, mm, reads=[xb[c][tt] for c in range(half * 4, half * 4 + 4)] + [ident_b],
                          writes=[ps_b[bank]])
                    if half == 0:
                        fw.op(DVE, lambda e, s=s, bank=bank: e.tensor_copy(
                            out=stg4_t[s][:, 0:512], in_=ps(bank)),
                            reads=[ps_b[bank]], writes=[stg4_b[s]])
                    else:
                        fw.op(ACT, lambda e, s=s, bank=bank: e.activation(
                            out=stg4_t[s][:, 512:1024], in_=ps(bank), func=AF.Copy),
                            reads=[ps_b[bank]], writes=[stg4_b[s]])
                fw.dma(SP, yout[i * 128:(i + 1) * 128, :], stg4_t[s][:], dst=None, src=[stg4_b[s]])

        fw.dry = True
        emit()
        fw.dry = False
        ws.reset()
        emit()
        assert ws.consumed == len(ws.specs)

        final_waits = {}
        for sem, val in fw.out_recs:
            final_waits[sem] = max(final_waits.get(sem, 0), val)

        with nc.Block() as block:
            @block.sync
            def _(e):
                fw.replay(SP, e)
                for sem, val in final_waits.items():
                    e.wait_ge(sem, val)

            @block.tensor
            def _(e):
                fw.replay(PE, e)

            @block.scalar
            def _(e):
                fw.replay(ACT, e)

            @block.vector
            def _(e):
                fw.replay(DVE, e)

            @block.gpsimd
            def _(e):
                fw.replay(POOL, e)
    return nc


def _cols(v):
    v = np.asarray(v, np.float32)
    return np.ascontiguousarray(v.reshape(-1, 128).T)


def _make_pvec(inp, cond_vec, flagneg):
    pv = np.zeros((128, NPV), np.float32)

    def put(name, arr):
        o, n = PV[name]
        assert arr.shape == (128, n), (name, arr.shape, n)
        pv[:, o:o + n] = arr

    for l in range(DEPTH):
        put(f"n1w{l}", _cols(inp["norm1_w"][l]))
        put(f"n2w{l}", _cols(inp["norm2_w"][l]))
        put(f"fw0_{l}", _cols(inp["ffn_conv_w"][l, 0]))
        put(f"fw1_{l}", _cols(inp["ffn_conv_w"][l, 1]))
        put(f"fw2_{l}", _cols(inp["ffn_conv_w"][l, 2]))
        put(f"fb_{l}", _cols(inp["ffn_conv_b"][l]))
        put(f"adab{l}", _cols(inp["ada_b"][l]))
    put("fnw", _cols(inp["final_norm_w"]))
    put("cond", _cols(cond_vec))
    pv[:, PV["flagneg"][0]] = flagneg
    put("qnw", _cols(inp["mla_q_norm"][0]))
    put("kvnw", _cols(inp["mla_kv_norm"][0]))
    put("pscale", _cols(inp["pool_scale"][0]))
    put("sw0", _cols(inp["sconv_conv"][0, 0]))
    put("sw1", _cols(inp["sconv_conv"][0, 1]))
    put("sw2", _cols(inp["sconv_conv"][0, 2]))
    return pv


def _pool_tables(L):
    A = np.zeros((4, T, T), np.float64)
    wins = (2, 4, 8, 16)
    for g, w in enumerate(wins):
        for t in range(T):
            s0 = (t // L) * L
            tl = t - s0
            lo = max(tl - w // 2, 0)
            hi = min(tl + w - w // 2, L)
            A[g, t, s0 + lo:s0 + hi] = 1.0 / (hi - lo)
            A[g, t, t] -= 1.0
    out = np.zeros((128, 4, 8, 3, 128), np.float32)
    for g in range(4):
        for i in range(8):
            for d in range(3):
                ip = i + d - 1
                if 0 <= ip < 8:
                    out[:, g, i, d, :] = A[g, i * 128:(i + 1) * 128, ip * 128:(ip + 1) * 128].T
    return out.reshape(128, 4, 8 * 3 * 128).astype(ml_dtypes.bfloat16)


def _dft_tables(L):
    t = np.arange(T)
    same = (t[:, None] // L) == (t[None, :] // L)
    ang = 2.0 * np.pi * ((t[:, None] % L) * (t[None, :] % L) % L) / L
    nrm = 1.0 / np.sqrt(L * 256.0)
    C = np.where(same, np.cos(ang), 0.0) * nrm
    S = np.where(same, -np.sin(ang), 0.0) * nrm

    def lay(M):
        return np.ascontiguousarray(M.reshape(8, 128, T).transpose(1, 0, 2)).astype(ml_dtypes.bfloat16)
    c = np.arange(256)
    a2 = 2.0 * np.pi * ((c[:, None] * c[None, :]) % 256) / 256.0
    CS = np.concatenate([np.cos(a2), np.sin(a2)], axis=1)
    CS = np.ascontiguousarray(CS.reshape(2, 128, 512).transpose(1, 0, 2)).astype(ml_dtypes.bfloat16)
    return lay(C), lay(S), CS


_PERM = np.concatenate([np.arange(0, 32, 2), np.arange(1, 32, 2)])
_PERM_SW = np.concatenate([np.arange(1, 32, 2), np.arange(0, 32, 2)])


def _mla_weights(inp):
    wdkv = np.asarray(inp["mla_wdkv"][0], np.float32)
    aug = np.zeros((D, 448), np.float32)
    aug[:, 0:256] = wdkv[:, 0:256]
    aug[:, 320:352] = wdkv[:, 256 + _PERM]
    aug[:, 416:448] = wdkv[:, 256 + _PERM_SW]
    wuq = np.asarray(inp["mla_wuq"][0], np.float32).reshape(384, 16, 96)
    qa = np.zeros((384, 16, 192), np.float32)
    qa[:, :, 0:64] = wuq[:, :, 0:64]
    qa[:, :, 64:96] = wuq[:, :, 64 + _PERM]
    qa[:, :, 160:192] = wuq[:, :, 64 + _PERM_SW]
    return aug, np.ascontiguousarray(qa.reshape(384, 16 * 192))


def _rope_tables(kind):
    cs = np.zeros((128, 2, T), np.float32)
    if kind == "p":
        cs[64:96, 0, :] = 1.0
        return cs
    t = np.arange(T)
    r = (t // 64).astype(np.float32)
    col = (t % 64).astype(np.float32)
    inv = (np.float32(10000.0) ** (-np.arange(8, dtype=np.float32) / np.float32(8))).astype(np.float32)
    ang = np.concatenate([r[:, None] * inv, col[:, None] * inv], axis=-1).astype(np.float32)
    c, s = np.cos(ang).T, np.sin(ang).T
    cs[64:80, 0, :] = c
    cs[80:96, 0, :] = c
    cs[64:80, 1, :] = -s
    cs[80:96, 1, :] = s
    return cs


def _mask_tables(kind):
    NEG = -30000.0
    eq = np.zeros((128, T), np.float32)
    ek = np.zeros((128, 1536), np.float32)
    if kind == "p":
        seq = np.arange(T) // 256
        for r in range(4):
            eq[96 + r, :] = (seq == r)
            ek[96 + r, 0:512] = NEG
            ek[96 + r, 512:] = np.where(seq == r, 0.0, NEG)
    return eq, ek


def _core_roles():
    return [("s", 0), ("s", 1), ("p", 0), ("p", 1), ("p", 2), ("p", 3), ("p", 3), ("p", 3)]


_NC_CACHE = {}


def make_in_maps(inp):
    roles = _core_roles()
    ident = np.eye(128, dtype=np.float32)
    shared = {
        "ident": ident,
        "ada_w": np.ascontiguousarray(inp["ada_w"], dtype=np.float32),
        "ffn_up": np.ascontiguousarray(inp["ffn_up"], dtype=np.float32),
        "ffn_down": np.ascontiguousarray(inp["ffn_down"], dtype=np.float32),
        "pool_w": np.ascontiguousarray(inp["pool_w"][0], dtype=np.float32),
        "fnet_w": np.ascontiguousarray(inp["fnet_w"][0], dtype=np.float32),
        "sconv_win": np.ascontiguousarray(inp["sconv_win"][0], dtype=np.float32),
        "sconv_wout": np.ascontiguousarray(inp["sconv_wout"][0], dtype=np.float32),
        "identb": ident.astype(ml_dtypes.bfloat16),
        "wdq": np.ascontiguousarray(inp["mla_wdq"][0], dtype=np.float32),
        "wukv": np.ascontiguousarray(inp["mla_wukv"][0], dtype=np.float32),
        "wo": np.ascontiguousarray(inp["mla_wo"][0], dtype=np.float32),
    }
    shared["wdkv_aug"], shared["wuq_aug"] = _mla_weights(inp)
    tabs = {}
    for kind, L in (("s", 1024), ("p", 256)):
        C, S, CS = _dft_tables(L)
        eq, ek = _mask_tables(kind)
        tabs[kind] = {"poolA": _pool_tables(L), "dftC": C, "dftS": S, "dftCS": CS,
                      "ropeCS": _rope_tables(kind), "eq": eq, "_ek": ek}
    in_maps = []
    for kind, idx in roles:
        if kind == "s":
            xc = inp["x_sample"][idx]
            cond = inp["c"][idx]
            flag = 0.0
        else:
            xc = inp["x_prompt"][4 * idx:4 * idx + 4].reshape(T, D)
            cond = inp["c_ctx"]
            flag = -1.0
        m = dict(shared)
        m.update({a: b for a, b in tabs[kind].items() if not a.startswith("_")})
        krm = tabs[kind]["_ek"].copy()
        cT = np.zeros((128, 2, 512), np.float32)
        if kind == "s":
            cT[:] = np.asarray(inp["cache_ckv"][idx, 0], np.float32).T.reshape(2, 128, 512).transpose(1, 0, 2)
            krm[64:96, 0:512] = np.asarray(inp["cache_krope"][idx, 0], np.float32)[:, _PERM].T
        m["cacheT"] = cT
        m["krmask"] = krm
        m["xin"] = np.ascontiguousarray(xc, dtype=np.float32)
        m["pvec"] = _make_pvec(inp, cond, flag)
        in_maps.append(m)
    return in_maps


def kernel(**inputs):
    inp = {k: np.asarray(v) for k, v in inputs.items()}
    in_maps = make_in_maps(inp)
    if "nc" not in _NC_CACHE:
        _NC_CACHE["nc"] = build_program(STAGE)
    nc = _NC_CACHE["nc"]
    res = run_bass_kernel_spmd(nc, in_maps, core_ids=list(range(NCORES)))
    outs = res.results
    y_sample = np.stack([outs[0]["yout"], outs[1]["yout"]], axis=0).astype(np.float32)
    y_prompt = np.concatenate([outs[2 + g]["yout"].reshape(4, 256, D) for g in range(4)], axis=0)
    y_prompt = y_prompt.astype(np.float32)
    new_ckv = np.concatenate([outs[2 + g]["ckv_o"].reshape(4, 1, 256, 256) for g in range(4)], axis=0)
    krp = np.concatenate([outs[2 + g]["kr_o"].reshape(4, 1, 256, 32) for g in range(4)], axis=0)
    new_kr = np.empty_like(krp)
    new_kr[..., _PERM] = krp
    new_ckv = new_ckv.astype(np.float32)
    new_kr = new_kr.astype(np.float32)
    return (y_prompt, y_sample, new_ckv, new_kr)
```

```python
import numpy as np
from contextlib import ExitStack
import ml_dtypes

import concourse.bass as bass
import concourse.mybir as mybir
from concourse.bass_utils import run_bass_kernel_spmd

F32 = mybir.dt.float32
BF16 = mybir.dt.bfloat16
AF = mybir.ActivationFunctionType
ALU = mybir.AluOpType

D = 1024
T = 1024
KC = 8
DFF = 2816
FC = 22
DEPTH = 4
EPS = 1e-6
NCORES = 8


class Buf:
    __slots__ = ("name", "w", "r", "sem", "cum", "excl")

    def __init__(self, name):
        self.name = name
        self.excl = False
        self.w = None
        self.r = {}
        self.sem = None
        self.cum = 0


class Q:
    def __init__(self, fw, name, own_wait=True):
        self.fw = fw
        self.name = name
        self.thunks = []
        self.sem = fw.new_sem("q_" + name)
        self.cnt = 0
        self.known = {}
        self.own_wait = own_wait


class FW:
    def __init__(self, nc, es):
        self.nc = nc
        self.es = es
        self.nsem = 0
        self.pe = Q(self, "pe", own_wait=False)
        self.act = Q(self, "act")
        self.dve = Q(self, "dve")
        self.pool = Q(self, "pool")
        self.sp = Q(self, "sp")
        self.out_recs = []
        self.dry = False

    def new_sem(self, name):
        self.nsem += 1
        return self.es.enter_context(self.nc.semaphore(f"s{self.nsem}_{name}"))

    def buf(self, name, dma=False):
        b = Buf(name)
        if dma:
            b.sem = self.new_sem("d_" + name)
        return b

    def _collect(self, q, reads, writes):
        waits = {}

        def need(rec):
            if rec is None:
                return
            sem, val = rec
            if sem is q.sem and not q.own_wait:
                return
            if q.known.get(sem, 0) >= val:
                return
            if waits.get(sem, 0) < val:
                waits[sem] = val

        for b in reads:
            need(b.w)
            if b.excl:
                for sem, val in b.r.items():
                    if sem is not q.sem:
                        need((sem, val))
        for b in writes:
            need(b.w)
            for sem, val in b.r.items():
                need((sem, val))
        for sem, val in waits.items():
            q.known[sem] = val
        return list(waits.items())

    @staticmethod
    def _commit(rec, reads, writes):
        sem, val = rec
        for b in reads:
            if b.r.get(sem, 0) < val:
                b.r[sem] = val
        for b in writes:
            b.w = rec
            b.r = {}

    def op(self, q, fn, reads=(), writes=()):
        if self.dry:
            return None
        wl = self._collect(q, reads, writes)
        q.cnt += 1
        rec = (q.sem, q.cnt)
        q.thunks.append((wl, fn, rec, 1))
        self._commit(rec, reads, writes)
        return rec

    def dma(self, q, out_ap, in_ap, dst=None, src=(), reads=(), kw=None):
        if self.dry:
            return None
        kw = kw or {}
        writes = [dst] if dst is not None else []
        rds = list(src) + list(reads)
        wl = self._collect(q, rds, writes)
        owner = dst if dst is not None else src[0]
        owner.cum += 16
        rec = (owner.sem, owner.cum)

        def fn(e, out_ap=out_ap, in_ap=in_ap, kw=kw):
            return e.dma_start(out=out_ap, in_=in_ap, **kw)

        q.thunks.append((wl, fn, rec, 16))
        self._commit(rec, rds, writes)
        if dst is None:
            self.out_recs.append(rec)
        return rec

    def barrier_bufs(self, bufs):
        allq = [self.pe, self.act, self.dve, self.pool]
        for b in bufs:
            for q in allq:
                if q.cnt > 0:
                    if b.r.get(q.sem, 0) < q.cnt:
                        b.r[q.sem] = q.cnt

    def replay(self, q, eng):
        for wl, fn, rec, inc in q.thunks:
            for sem, val in wl:
                eng.wait_ge(sem, val)
            ins = fn(eng)
            if isinstance(ins, (list, tuple)):
                ins = ins[-1]
            ins.then_inc(rec[0], inc)


class WStream:
    def __init__(self, fw, q, slots, bufs, slot_elems):
        self.fw = fw
        self.q = q
        self.slots = slots
        self.bufs = bufs
        self.n = len(slots)
        self.slot_elems = slot_elems
        self.specs = []
        self.reset()

    def reset(self):
        self.issued = 0
        self.consumed = 0
        self.done_flags = []

    def _view(self, k, shape):
        t = self.slots[k % self.n]
        n = int(np.prod(shape))
        assert n <= self.slot_elems, shape
        if len(shape) == 1:
            return t[:, 0:n]
        if len(shape) == 2:
            return t[:, 0:n].rearrange("p (a b) -> p a b", a=shape[0], b=shape[1])
        return t[:, 0:n].rearrange("p (a b c) -> p a b c", a=shape[0], b=shape[1], c=shape[2])

    def _pump(self):
        while self.issued < len(self.specs):
            k = self.issued
            if k >= self.n and not (k - self.n < len(self.done_flags) and self.done_flags[k - self.n]):
                break
            dram_ap, shape = self.specs[k]
            self.fw.dma(self.q, self._view(k, shape), dram_ap, dst=self.bufs[k % self.n])
            self.issued += 1

    def next(self, dram_ap, shape):
        shape = tuple(shape)
        if self.fw.dry:
            self.specs.append((dram_ap, shape))
            return self._view(0, shape), self.bufs[0], None
        k = self.consumed
        self.consumed += 1
        assert self.specs[k][1] == shape, (k, self.specs[k][1], shape)
        self.done_flags.append(False)
        self._pump()
        assert self.issued > k, (k, self.issued)
        return self._view(k, shape), self.bufs[k % self.n], k

    def done(self, k):
        if self.fw.dry:
            return
        self.done_flags[k] = True
        self._pump()


def _pvec_map():
    m = {}
    o = 0

    def add(name, n):
        nonlocal o
        m[name] = (o, n)
        o += n

    for l in range(DEPTH):
        add(f"n1w{l}", KC)
        add(f"n2w{l}", KC)
        add(f"fw0_{l}", FC)
        add(f"fw1_{l}", FC)
        add(f"fw2_{l}", FC)
        add(f"fb_{l}", FC)
        add(f"adab{l}", 48)
    add("fnw", KC)
    add("cond", KC)
    add("flagneg", 1)
    add("qnw", 3)
    add("kvnw", 2)
    add("pscale", KC)
    add("sw0", KC)
    add("sw1", KC)
    add("sw2", KC)
    m["_n"] = o
    return m


PV = _pvec_map()
NPV = PV["_n"]

STAGE = 4
MLA_SUB = 4.0
NSLOT = 6
SLOT_ELEMS = 4096


def build_program(stage=STAGE):
    nc = bass.Bass("TRN2", target_bir_lowering=False)

    def din(name, shape, dt=F32):
        return nc.dram_tensor(name, list(shape), dt, kind="ExternalInput").ap()

    def dout(name, shape, dt=F32):
        return nc.dram_tensor(name, list(shape), dt, kind="ExternalOutput").ap()

    xin = din("xin", [T, D])
    pvec_d = din("pvec", [128, NPV])
    ident_d = din("ident", [128, 128])
    ada_w = din("ada_w", [DEPTH, D, 6 * D])
    ffn_up = din("ffn_up", [DEPTH, D, 2 * DFF])
    ffn_down = din("ffn_down", [DEPTH, DFF, D])
    pool_w = din("pool_w", [4, 256, 256])
    poolA = din("poolA", [128, 4, 8 * 3 * 128], BF16)
    fnet_w = din("fnet_w", [D, D])
    dftC = din("dftC", [128, 8, T], BF16)
    dftS = din("dftS", [128, 8, T], BF16)
    dftCS = din("dftCS", [128, 2, 512], BF16)
    identb_d = din("identb", [128, 128], BF16)
    sconv_win = din("sconv_win", [D, 3 * D])
    sconv_wout = din("sconv_wout", [D, D])
    wdq = din("wdq", [D, 384])
    wdkv_aug = din("wdkv_aug", [D, 448])
    wuq_aug = din("wuq_aug", [384, 16 * 192])
    wukv = din("wukv", [256, 2048])
    wo = din("wo", [D, D])
    ropeCS_d = din("ropeCS", [128, 2, T])
    cacheT_d = din("cacheT", [128, 2, 512])
    krmask_d = din("krmask", [128, 1536])
    eq_d = din("eq", [128, T])
    yout = dout("yout", [T, D])
    ckv_o = dout("ckv_o", [T, 256])
    kr_o = dout("kr_o", [T, 32])

    es = ExitStack()
    with es:
        fw = FW(nc, es)
        PE, ACT, DVE, POOL, SP = fw.pe, fw.act, fw.dve, fw.pool, fw.sp

        def sb(name, shape, dt):
            return es.enter_context(nc.sbuf_tensor(name, list(shape), dt))

        x_t = sb("x", [128, KC, T], F32)
        xb = [[fw.buf(f"x{c}_{tt}") for tt in range(2)] for c in range(KC)]
        h_t = sb("h", [128, KC, T], BF16)
        hb = [[fw.buf(f"h{c}_{tt}") for tt in range(2)] for c in range(KC)]
        a_t = sb("a", [128, 25, T], BF16)
        ab = [fw.buf(f"a{j}") for j in range(25)]
        pvec = sb("pvec_sb", [128, NPV], F32)
        pvec_b = fw.buf("pvec", dma=True)
        ident = sb("ident_sb", [128, 128], F32)
        ident_b = fw.buf("ident", dma=True)
        ones_bf = sb("ones_bf", [128, 128], BF16)
        one_f = sb("one_f", [128, 1], F32)
        eps_t = sb("eps", [128, 1], F32)
        const_b = fw.buf("consts")
        stg_t = [sb(f"stg{i}", [128, D], F32) for i in range(2)]
        stg_b = [fw.buf(f"stg{i}", dma=True) for i in range(2)]
        tA_t = [sb(f"tA{i}", [128, T], F32) for i in range(2)]
        tA_b = [fw.buf(f"tA{i}", dma=True) for i in range(2)]
        stg4_t = stg_t + tA_t
        stg4_b = stg_b + tA_b
        sq_t = [sb(f"sq{i}", [128, 512], BF16) for i in range(4)]
        sq_b = [fw.buf(f"sq{i}") for i in range(4)]
        rstd_t = sb("rstd", [128, 512], F32)
        rstd_b = fw.buf("rstd")
        rstd1_t = sb("rstd1", [128, 512], F32)
        rstd1_b = fw.buf("rstd1")
        xn_t = [sb(f"xn{i}", [128, 512], F32) for i in range(4)]
        xn_b = [fw.buf(f"xn{i}") for i in range(4)]
        scond = sb("scond", [128, KC], BF16)
        scond_b = fw.buf("scond")
        row_t = [sb(f"row{i}", [1, 512], F32) for i in range(2)]
        row_b = [fw.buf(f"row{i}") for i in range(2)]
        mod_t = [sb(f"mod{l}", [128, 48], F32) for l in range(DEPTH)]
        mod_b = [fw.buf(f"mod{l}") for l in range(DEPTH)]
        col_t = [sb(f"cols{l}", [128, 16 + 2 * FC], F32) for l in range(DEPTH)]
        col_b = [fw.buf(f"cols{l}") for l in range(DEPTH)]
        identb = sb("identb_sb", [128, 128], BF16)
        identb_b = fw.buf("identb", dma=True)
        cs_t = sb("dftcs_sb", [128, 2, 512], BF16)
        cs_b = fw.buf("dftcs", dma=True)
        ropecs = sb("ropecs", [128, 2, T], F32)
        ropecs_b = fw.buf("ropecs", dma=True)
        sel_t = sb("sel", [128, 2, 128], F32)
        ckvst = sb("ckvst", [128, 8, 256], F32)
        ckvst_b = fw.buf("ckvst", dma=True)
        krst = sb("krst", [128, 8, 32], F32)
        krst_b = fw.buf("krst", dma=True)
        ckvall_b = [fw.buf(f"ckvall{c}", dma=True) for c in range(2)]
        kr_b = fw.buf("KR", dma=True)
        qt_b = [fw.buf(f"QT{i}", dma=True) for i in range(3)]
        kt_b = [fw.buf(f"KT{i}") for i in range(3)]
        vp_b = [fw.buf(f"VP{i}") for i in range(2)]
        pt_b = [fw.buf(f"PT{i}") for i in range(4)]
        cq_b = [fw.buf(f"cq{i}") for i in range(3)]
        mcol_t = sb("mcols", [128, 32], F32)
        mcol_b = fw.buf("mcols")
        slots = [sb(f"wslot{i}", [128, SLOT_ELEMS], BF16) for i in range(NSLOT)]
        slot_b = [fw.buf(f"wslot{i}", dma=True) for i in range(NSLOT)]
        ws = WStream(fw, POOL, slots, slot_b, SLOT_ELEMS)

        pd_t = [es.enter_context(nc.psum_tensor(f"pd{i}", [128, 1024], F32)) for i in range(4)]
        ps_b = [fw.buf(f"ps{i}") for i in range(8)]
        for b in ps_b:
            b.excl = True

        pdb_t = [t.bitcast(BF16) for t in pd_t]

        def ps(bank):
            return pd_t[bank // 2][:, (bank % 2) * 512:(bank % 2) * 512 + 512]

        def psb(bank):
            return pdb_t[bank // 2][:, (bank % 2) * 1024:(bank % 2) * 1024 + 1024]

        def pv(name, j=0, n=1):
            o, _ = PV[name]
            return pvec[:, o + j:o + j + n]

        def tsl(tt):
            return slice(tt * 512, (tt + 1) * 512)

        def emit():
            fw.dma(SP, pvec[:], pvec_d, dst=pvec_b)
            fw.dma(SP, ident[:], ident_d, dst=ident_b)
            fw.dma(SP, identb[:], identb_d, dst=identb_b)
            fw.dma(SP, cs_t[:], dftCS, dst=cs_b)
            fw.op(DVE, lambda e: e.memset(ones_bf[:], 1.0), writes=[const_b])
            fw.op(DVE, lambda e: e.memset(eps_t[:], EPS), writes=[const_b])
            fw.op(DVE, lambda e: e.memset(one_f[:], 1.0), writes=[const_b])
            fw.op(DVE, lambda e: e.memset(sel_t[:], 0.0), writes=[const_b])
            fw.op(DVE, lambda e: e.memset(sel_t[64:65, 0, :], 1.0), writes=[const_b])
            fw.op(DVE, lambda e: e.memset(sel_t[0:1, 1, :], 1.0), writes=[const_b])

            for i in range(8):
                s = i % 4
                tt = i // 4
                fw.dma(SP, stg4_t[s][:], xin[i * 128:(i + 1) * 128, :], dst=stg4_b[s])
                for half in range(2):
                    bank = (i * 2 + half) % 8

                    def mm(e, s=s, half=half, bank=bank):
                        r = None
                        for cc in range(4):
                            c = half * 4 + cc
                            r = e.transpose(ps(bank)[:, cc * 128:(cc + 1) * 128],
                                            stg4_t[s][:, c * 128:(c + 1) * 128], ident[:])
                        return r
                    fw.op(PE, mm, reads=[stg4_b[s], ident_b], writes=[ps_b[bank]])
                    wr = [xb[c][tt] for c in range(half * 4, half * 4 + 4)]
                    if half == 0:
                        fw.op(DVE, lambda e, half=half, bank=bank, i=i: e.tensor_copy(
                            out=x_t[:, half * 4:half * 4 + 4, i * 128:(i + 1) * 128],
                            in_=ps(bank).rearrange("p (c t) -> p c t", c=4)),
                            reads=[ps_b[bank]], writes=wr)
                    else:
                        fw.op(ACT, lambda e, half=half, bank=bank, i=i: e.activation(
                            out=x_t[:, half * 4:half * 4 + 4, i * 128:(i + 1) * 128],
                            in_=ps(bank).rearrange("p (c t) -> p c t", c=4), func=AF.Copy),
                            reads=[ps_b[bank]], writes=wr)

            fw.op(ACT, lambda e: e.activation(out=scond[:], in_=pv("cond", 0, KC), func=AF.Silu),
                  reads=[pvec_b], writes=[scond_b])

            sqn = [0]

            def rms_stats(srcs, nch, inv_n, bank, rt=None, rb=None):
                rt = rstd_t if rt is None else rt
                rb = rstd_b if rb is None else rb
                for c, (ap, bufs) in enumerate(srcs):
                    s = sqn[0] % 4
                    sqn[0] += 1
                    fw.op(ACT, lambda e, ap=ap, s=s: e.activation(out=sq_t[s][:], in_=ap,
                                                                   func=AF.Square),
                          reads=bufs, writes=[sq_b[s]])
                    fw.op(PE, lambda e, c=c, s=s: e.matmul(ps(bank), ones_bf[:], sq_t[s][:],
                                                           start=(c == 0), stop=(c == nch - 1)),
                          reads=[const_b, sq_b[s]], writes=[ps_b[bank]])
                rms_tail(inv_n, bank, rt, rb)

            def rms_tail(inv_n, bank, rt, rb):
                fw.op(ACT, lambda e: e.activation(out=rt[:], in_=ps(bank), func=AF.Ln,
                                                  bias=eps_t[:, 0:1], scale=inv_n),
                      reads=[ps_b[bank], const_b], writes=[rb])
                fw.op(ACT, lambda e: e.activation(out=rt[:], in_=rt[:], func=AF.Exp, scale=-0.5),
                      reads=[rb], writes=[rb])

            acc = {"pend": [], "cnt": [0, 0]}

            def acc_begin():
                acc["pend"] = []
                acc["cnt"] = [0, 0]

            def acc_x(oc, tt):
                assert fw.dry or len(acc["pend"]) < 4, len(acc["pend"])
                s = sqn[0] % 4
                sqn[0] += 1
                fw.op(ACT, lambda e, oc=oc, tt=tt, s=s: e.activation(out=sq_t[s][:], in_=x_t[:, oc, tsl(tt)],
                                                                       func=AF.Square),
                      reads=[xb[oc][tt]], writes=[sq_b[s]])
                acc["pend"].append((tt, s))

            def acc_pe(lag):
                while len(acc["pend"]) > lag:
                    tt, s = acc["pend"].pop(0)
                    c = acc["cnt"][tt]
                    acc["cnt"][tt] += 1
                    fw.op(PE, lambda e, c=c, s=s, tt=tt: e.matmul(ps(6 + tt), ones_bf[:], sq_t[s][:],
                                                                 start=(c == 0), stop=(c == KC - 1)),
                          reads=[const_b, sq_b[s]], writes=[ps_b[6 + tt]])

            def norm_mod(l, which, pre=False):
                ao = 0 if which == 1 else 8
                bo = 0 if which == 1 else 24
                rts = [(rstd_t, rstd_b), (rstd1_t, rstd1_b)]
                for tt in range(2):
                    if pre:
                        rms_tail(1.0 / D, 6 + tt, rts[tt][0], rts[tt][1])
                    else:
                        rms_stats([(x_t[:, c, tsl(tt)], [xb[c][tt]]) for c in range(KC)], KC, 1.0 / D,
                                  bank=tt, rt=rts[tt][0], rb=rts[tt][1])
                n = 0
                for tt in range(2):
                    rt, rb = rts[tt]
                    for c in range(KC):
                        s = n % 4
                        n += 1
                        fw.op(DVE, lambda e, c=c, s=s, tt=tt, rt=rt: e.tensor_tensor(
                            out=xn_t[s][:], in0=x_t[:, c, tsl(tt)], in1=rt[:], op=ALU.mult),
                            reads=[xb[c][tt], rb], writes=[xn_b[s]])
                        if c % 4 != 3:
                            fw.op(ACT, lambda e, c=c, s=s, tt=tt: e.activation(
                                out=h_t[:, c, tsl(tt)], in_=xn_t[s][:], func=AF.Identity,
                                bias=mod_t[l][:, bo + c:bo + c + 1],
                                scale=col_t[l][:, ao + c:ao + c + 1]),
                                reads=[xn_b[s], mod_b[l], col_b[l]], writes=[hb[c][tt]])
                        else:
                            fw.op(DVE, lambda e, c=c, s=s, tt=tt: e.tensor_scalar(
                                out=h_t[:, c, tsl(tt)], in0=xn_t[s][:],
                                scalar1=col_t[l][:, ao + c:ao + c + 1],
                                scalar2=mod_t[l][:, bo + c:bo + c + 1], op0=ALU.mult, op1=ALU.add),
                                reads=[xn_b[s], mod_b[l], col_b[l]], writes=[hb[c][tt]])

            def ada_block(l, nb, b0=0, b1=1):
                wv = ada_w[l].rearrange("(k p) n -> p k n", p=128)
                w, wb, wk = ws.next(wv[:, :, nb * 512:(nb + 1) * 512], (KC, 512))

                def mm(e, w=w):
                    r = None
                    for k in range(KC):
                        r = e.matmul(ps(b0)[0:1, :], scond[:, k:k + 1], w[:, k, :],
                                     start=(k == 0), stop=(k == KC - 1))
                    return r
                fw.op(PE, mm, reads=[scond_b, wb], writes=[ps_b[b0]])
                ws.done(wk)
                s = nb % 2
                fw.op(ACT, lambda e, s=s: e.activation(out=row_t[s][:], in_=ps(b0)[0:1, :], func=AF.Copy),
                      reads=[ps_b[b0]], writes=[row_b[s]])

                def mt(e, s=s):
                    r = None
                    for j in range(4):
                        r = e.matmul(ps(b1)[:, j:j + 1], row_t[s][0:1, j * 128:(j + 1) * 128],
                                     one_f[0:1, 0:1], start=True, stop=True)
                    return r
                fw.op(PE, mt, reads=[row_b[s], const_b], writes=[ps_b[b1]])
                fw.op(DVE, lambda e: e.tensor_tensor(out=mod_t[l][:, nb * 4:nb * 4 + 4], in0=ps(b1)[:, 0:4],
                                                     in1=pv(f"adab{l}", nb * 4, 4), op=ALU.add),
                      reads=[ps_b[b1], pvec_b], writes=[mod_b[l]])

            def ada_finish1(l):
                fw.op(DVE, lambda e: e.scalar_tensor_tensor(
                    out=col_t[l][:, 0:8], in0=mod_t[l][:, 8:16], scalar=1.0,
                    in1=pv(f"n1w{l}", 0, KC), op0=ALU.add, op1=ALU.mult),
                    reads=[mod_b[l], pvec_b], writes=[col_b[l]])
                fw.op(DVE, lambda e: e.tensor_scalar(
                    out=col_t[l][:, 16:16 + FC], in0=pv(f"fw0_{l}", 0, FC),
                    scalar1=pv("flagneg"), scalar2=None, op0=ALU.mult),
                    reads=[pvec_b], writes=[col_b[l]])
                fw.op(DVE, lambda e: e.tensor_scalar(
                    out=col_t[l][:, 16 + FC:16 + 2 * FC], in0=pv(f"fw2_{l}", 0, FC),
                    scalar1=pv("flagneg"), scalar2=None, op0=ALU.mult),
                    reads=[pvec_b], writes=[col_b[l]])


            def ada_finish2(l):
                fw.op(DVE, lambda e: e.scalar_tensor_tensor(
                    out=col_t[l][:, 8:16], in0=mod_t[l][:, 32:40], scalar=1.0,
                    in1=pv(f"n2w{l}", 0, KC), op0=ALU.add, op1=ALU.mult),
                    reads=[mod_b[l], pvec_b], writes=[col_b[l]])

            def ada_finish(l):
                ada_finish1(l)
                ada_finish2(l)

            def conv_fix(eng, t_ap, src_ap, nf0, nf2, reads, writes):
                fw.op(eng, lambda e: e.scalar_tensor_tensor(
                    out=t_ap[:, 256:1024:256], in0=src_ap[:, 255:1023:256], scalar=nf0,
                    in1=t_ap[:, 256:1024:256], op0=ALU.mult, op1=ALU.add),
                    reads=reads, writes=writes)
                fw.op(eng, lambda e: e.scalar_tensor_tensor(
                    out=t_ap[:, 255:1023:256], in0=src_ap[:, 256:1024:256], scalar=nf2,
                    in1=t_ap[:, 255:1023:256], op0=ALU.mult, op1=ALU.add),
                    reads=reads, writes=writes)

            def ffn(l, mid_hook=None):
                upv = ffn_up[l].rearrange("(k p) n -> p k n", p=128)
                dnv = ffn_down[l].rearrange("(k p) n -> p k n", p=128)
                j = 0
                for jg in range(6):
                    ncol = 512 if jg < 5 else 256
                    gw, gwb, gk = ws.next(upv[:, :, jg * 512:jg * 512 + ncol], (KC, ncol))
                    uw, uwb, uk = ws.next(upv[:, :, DFF + jg * 512:DFF + jg * 512 + ncol],
                                          (KC, ncol))
                    for jj in range(ncol // 128):
                        dbl = 2 * (j % 2)
                        for which, (w, wb) in enumerate(((gw, gwb), (uw, uwb))):
                            for tt in range(2):
                                bank = (dbl + which) * 2 + tt

                                def mm(e, w=w, jj=jj, tt=tt, bank=bank):
                                    r = None
                                    for k in range(KC):
                                        r = e.matmul(ps(bank), w[:, k, jj * 128:(jj + 1) * 128],
                                                     h_t[:, k, tsl(tt)],
                                                     start=(k == 0), stop=(k == KC - 1))
                                    return r
                                fw.op(PE, mm, reads=[wb] + [hb[k][tt] for k in range(KC)],
                                      writes=[ps_b[bank]])
                        g_ap = pd_t[dbl][:]
                        u_ap = pd_t[dbl + 1][:]
                        gB = [ps_b[dbl * 2], ps_b[dbl * 2 + 1]]
                        uB = [ps_b[dbl * 2 + 2], ps_b[dbl * 2 + 3]]
                        s = j % 2
                        t = tA_t[s]
                        tb = tA_b[s]
                        fw.op(ACT, lambda e, t=t, g_ap=g_ap, j=j: e.activation(
                            out=t[:], in_=g_ap, func=AF.Identity, bias=pv(f"fb_{l}", j),
                            scale=pv(f"fw1_{l}", j)),
                            reads=gB + [pvec_b], writes=[tb])
                        fw.op(DVE, lambda e, t=t, g_ap=g_ap, j=j: e.scalar_tensor_tensor(
                            out=t[:, 1:T], in0=g_ap[:, 0:T - 1], scalar=pv(f"fw0_{l}", j),
                            in1=t[:, 1:T], op0=ALU.mult, op1=ALU.add),
                            reads=gB + [pvec_b, tb], writes=[tb])
                        fw.op(DVE, lambda e, t=t, g_ap=g_ap, j=j: e.scalar_tensor_tensor(
                            out=t[:, 0:T - 1], in0=g_ap[:, 1:T], scalar=pv(f"fw2_{l}", j),
                            in1=t[:, 0:T - 1], op0=ALU.mult, op1=ALU.add),
                            reads=gB + [pvec_b, tb], writes=[tb])
                        conv_fix(DVE, t, g_ap, col_t[l][:, 16 + j:17 + j],
                                 col_t[l][:, 16 + FC + j:17 + FC + j],
                                 reads=gB + [col_b[l], tb], writes=[tb])
                        fw.op(ACT, lambda e, t=t: e.activation(out=t[:], in_=t[:], func=AF.Gelu),
                              reads=[tb], writes=[tb])
                        fw.op(DVE, lambda e, t=t, u_ap=u_ap, j=j: e.tensor_tensor(
                            out=a_t[:, j, :], in0=t[:], in1=u_ap, op=ALU.mult),
                            reads=[tb] + uB, writes=[ab[j]])
                        j += 1
                    ws.done(gk)
                    ws.done(uk)
                    if mid_hook is not None:
                        mid_hook(jg)
                acc_begin()
                fw.op(ACT, lambda e: e.activation(out=row_t[0][0:1, 0:1], in_=eps_t[0:1, 0:1], func=AF.Ln),
                      reads=[const_b], writes=[row_b[0]])
                for oc in range(KC):
                    w, wb, wk = ws.next(dnv[:, :, oc * 128:(oc + 1) * 128], (FC, 128))
                    for tt in range(2):
                        bank = (oc * 2 + tt) % 6

                        def mm(e, w=w, tt=tt, bank=bank):
                            r = None
                            for k in range(FC):
                                r = e.matmul(ps(bank), w[:, k, :], a_t[:, k, tsl(tt)],
                                             start=(k == 0), stop=(k == FC - 1))
                            return r
                        fw.op(PE, mm, reads=[wb] + ab[0:FC], writes=[ps_b[bank]])
                        acc_pe(1)
                        fw.op(DVE, lambda e, oc=oc, tt=tt, bank=bank: e.scalar_tensor_tensor(
                            out=x_t[:, oc, tsl(tt)], in0=ps(bank), scalar=mod_t[l][:, 40 + oc:41 + oc],
                            in1=x_t[:, oc, tsl(tt)], op0=ALU.mult, op1=ALU.add),
                            reads=[ps_b[bank], mod_b[l], xb[oc][tt]], writes=[xb[oc][tt]])
                        acc_x(oc, tt)
                    ws.done(wk)
                acc_pe(0)


            def evac(i, out_ap, in_ap, reads, writes):
                if i % 2 == 0:
                    fw.op(DVE, lambda e: e.tensor_copy(out=out_ap, in_=in_ap), reads=reads, writes=writes)
                else:
                    fw.op(ACT, lambda e: e.activation(out=out_ap, in_=in_ap, func=AF.Copy),
                          reads=reads, writes=writes)

            def out_linear(l, wdram, src_ap, src_bufs, gcol):
                wv = wdram.rearrange("(k p) n -> p k n", p=128)
                n = 0
                acc_begin()
                for nb in range(2):
                    w, wb, wk = ws.next(wv[:, :, nb * 512:(nb + 1) * 512], (KC, 512))
                    for o4 in range(4):
                        oc = nb * 4 + o4
                        for tt in range(2):
                            bank = n % 6
                            n += 1

                            def mm(e, w=w, o4=o4, tt=tt, bank=bank):
                                r = None
                                for kk in range(KC):
                                    r = e.matmul(ps(bank), w[:, kk, o4 * 128:(o4 + 1) * 128],
                                                 src_ap(kk, tt), start=(kk == 0), stop=(kk == KC - 1))
                                return r
                            fw.op(PE, mm, reads=[wb] + src_bufs(tt), writes=[ps_b[bank]])
                            acc_pe(2)
                            fw.op(DVE, lambda e, oc=oc, tt=tt, bank=bank: e.scalar_tensor_tensor(
                                out=x_t[:, oc, tsl(tt)], in0=ps(bank), scalar=gcol(oc),
                                in1=x_t[:, oc, tsl(tt)], op0=ALU.mult, op1=ALU.add),
                                reads=[ps_b[bank], mod_b[l], mcol_b, xb[oc][tt]], writes=[xb[oc][tt]])
                            acc_x(oc, tt)
                    ws.done(wk)
                acc_pe(0)

            def mixer_pool(l):
                fw.barrier_bufs(ab)
                fw.op(DVE, lambda e: e.tensor_tensor(out=mcol_t[:, 0:8], in0=pv("pscale", 0, KC),
                                                     in1=mod_t[l][:, 16:24], op=ALU.mult),
                      reads=[pvec_b, mod_b[l]], writes=[mcol_b])
                for i in range(8):
                    bank = i % 8

                    def tr(e, i=i, bank=bank):
                        r = None
                        for c in range(KC):
                            r = e.transpose(psb(bank)[:, c * 128:(c + 1) * 128],
                                            h_t[:, c, i * 128:(i + 1) * 128], identb[:])
                        return r
                    fw.op(PE, tr, reads=[hb[c][i // 4] for c in range(KC)] + [identb_b],
                          writes=[ps_b[bank]])
                    evac(i, a_t[:, i, :], psb(bank), [ps_b[bank]], [ab[i]])
                aw = ab_k = None
                for cc in range(8):
                    g = cc // 2
                    if cc % 2 == 0:
                        aw, awb, ak = ws.next(poolA[:, g, :].rearrange("p (a b c) -> p a b c", a=8, b=3, c=128), (8, 3, 128))
                    dbl = cc % 4

                    def mm(e, cc=cc, aw=aw, dbl=dbl):
                        r = None
                        for i in range(8):
                            ds = [d for d in range(3) if 0 <= i + d - 1 < 8]
                            for n, d in enumerate(ds):
                                r = e.matmul(pd_t[dbl][:, i * 128:(i + 1) * 128],
                                             a_t[:, i + d - 1, cc * 128:(cc + 1) * 128],
                                             aw[:, i, d, :], start=(n == 0), stop=(n == len(ds) - 1))
                        return r
                    fw.op(PE, mm, reads=ab[0:8] + [awb], writes=[ps_b[2 * dbl], ps_b[2 * dbl + 1]])
                    evac(cc, a_t[:, 8 + cc, :], pd_t[dbl][:], [ps_b[2 * dbl], ps_b[2 * dbl + 1]],
                         [ab[8 + cc]])
                    if cc % 2 == 1:
                        ws.done(ak)
                pw, pwb, pk = ws.next(pool_w.rearrange("g (kk p) d -> p g kk d", p=128), (4, 2, 256))
                n = 0
                acc_begin()
                for dc in range(8):
                    g = dc // 2
                    for tt in range(2):
                        bank = n % 6
                        n += 1

                        def mm2(e, dc=dc, g=g, tt=tt, bank=bank):
                            r = None
                            for kk in range(2):
                                r = e.matmul(ps(bank), pw[:, g, kk, (dc % 2) * 128:(dc % 2) * 128 + 128],
                                             a_t[:, 8 + g * 2 + kk, tsl(tt)], start=(kk == 0), stop=(kk == 1))
                            return r
                        fw.op(PE, mm2, reads=[pwb, ab[8 + g * 2], ab[9 + g * 2]], writes=[ps_b[bank]])
                        acc_pe(3)
                        fw.op(DVE, lambda e, dc=dc, tt=tt, bank=bank: e.scalar_tensor_tensor(
                            out=x_t[:, dc, tsl(tt)], in0=ps(bank), scalar=mcol_t[:, dc:dc + 1],
                            in1=x_t[:, dc, tsl(tt)], op0=ALU.mult, op1=ALU.add),
                            reads=[ps_b[bank], mcol_b, xb[dc][tt]], writes=[xb[dc][tt]])
                        acc_x(dc, tt)
                acc_pe(0)
                ws.done(pk)

            def mixer_fnet(l):
                fw.barrier_bufs(ab)

                def pq(i, g):
                    return a_t[:, 2 * i + g // 2, (g % 2) * 512:(g % 2) * 512 + 512]
                n = 0
                for i in range(8):
                    for g in range(4):
                        bank = n % 8

                        def mm(e, i=i, g=g, bank=bank):
                            r = None
                            for kk in range(2):
                                r = e.matmul(ps(bank), h_t[:, g * 2 + kk, i * 128:(i + 1) * 128],
                                             cs_t[:, kk, :], start=(kk == 0), stop=(kk == 1))
                            return r
                        fw.op(PE, mm, reads=[hb[g * 2][i // 4], hb[g * 2 + 1][i // 4], cs_b],
                              writes=[ps_b[bank]])
                        evac(n, pq(i, g), ps(bank), [ps_b[bank]], [ab[2 * i + g // 2]])
                        n += 1
                for tt in range(2):
                    cw, cwb, ck = ws.next(dftC[:, :, tsl(tt)], (8, 512))
                    sw, swb, sk = ws.next(dftS[:, :, tsl(tt)], (8, 512))
                    for mc in range(8):
                        g, m2 = mc // 2, mc % 2
                        bank = n % 8

                        def mm(e, g=g, m2=m2, bank=bank, cw=cw, sw=sw):
                            r = None
                            for i in range(8):
                                r = e.matmul(ps(bank), pq(i, g)[:, m2 * 128:(m2 + 1) * 128], cw[:, i, :],
                                             start=(i == 0), stop=False)
                                r = e.matmul(ps(bank), pq(i, g)[:, 256 + m2 * 128:256 + (m2 + 1) * 128],
                                             sw[:, i, :], start=False, stop=(i == 7))
                            return r
                        fw.op(PE, mm, reads=ab[0:16] + [cwb, swb], writes=[ps_b[bank]])
                        evac(n, a_t[:, 16 + mc, tsl(tt)], ps(bank), [ps_b[bank]], [ab[16 + mc]])
                        n += 1
                    ws.done(ck)
                    ws.done(sk)
                out_linear(l, fnet_w, lambda kk, tt: a_t[:, 16 + kk, tsl(tt)],
                           lambda tt: ab[16:24], lambda oc: mod_t[l][:, 16 + oc:17 + oc])

            def mixer_sconv(l):
                fw.barrier_bufs(ab)
                fw.op(DVE, lambda e: e.tensor_scalar(out=mcol_t[:, 8:16], in0=pv("sw0", 0, KC),
                                                     scalar1=pv("flagneg"), scalar2=None, op0=ALU.mult),
                      reads=[pvec_b], writes=[mcol_b])
                fw.op(DVE, lambda e: e.tensor_scalar(out=mcol_t[:, 16:24], in0=pv("sw2", 0, KC),
                                                     scalar1=pv("flagneg"), scalar2=None, op0=ALU.mult),
                      reads=[pvec_b], writes=[mcol_b])
                wv = sconv_win.rearrange("(k p) n -> p k n", p=128)
                nd = 0
                for jg in range(2):
                    wl = [ws.next(wv[:, :, part * D + jg * 512:part * D + jg * 512 + 512], (KC, 512))
                          for part in range(3)]
                    for jj in range(4):
                        j = jg * 4 + jj
                        dbls = []
                        dbls = [None, None, None]
                        for part in (2, 1, 0):
                            dbl = nd % 4
                            nd += 1
                            dbls[part] = dbl
                            w, wb, _ = wl[part]
                            for tt in range(2):
                                bank = 2 * dbl + tt

                                def mm(e, w=w, jj=jj, tt=tt, bank=bank):
                                    r = None
                                    for kk in range(KC):
                                        r = e.matmul(ps(bank), w[:, kk, jj * 128:(jj + 1) * 128],
                                                     h_t[:, kk, tsl(tt)], start=(kk == 0), stop=(kk == KC - 1))
                                    return r
                                fw.op(PE, mm, reads=[wb] + [hb[kk][tt] for kk in range(KC)],
                                      writes=[ps_b[bank]])
                        gbB = [ps_b[2 * dbls[0]], ps_b[2 * dbls[0] + 1]]
                        gcB = [ps_b[2 * dbls[1]], ps_b[2 * dbls[1] + 1]]
                        uB = [ps_b[2 * dbls[2]], ps_b[2 * dbls[2] + 1]]
                        s = j % 2
                        v, vb = tA_t[s], tA_b[s]
                        t2, t2b = stg_t[s], stg_b[s]
                        fw.op(ACT, lambda e, v=v, d=dbls[2]: e.activation(out=v[:], in_=pd_t[d][:], func=AF.Copy),
                              reads=uB, writes=[vb])
                        fw.op(DVE, lambda e, v=v, d=dbls[1]: e.tensor_tensor(out=v[:], in0=pd_t[d][:], in1=v[:],
                                                                             op=ALU.mult),
                              reads=gcB + [vb], writes=[vb])
                        fw.op(ACT, lambda e, v=v, t2=t2, j=j: e.activation(out=t2[:], in_=v[:], func=AF.Copy,
                                                                             scale=pv("sw1", j)),
                              reads=[vb, pvec_b], writes=[t2b])
                        fw.op(DVE, lambda e, v=v, t2=t2, j=j: e.scalar_tensor_tensor(
                            out=t2[:, 1:T], in0=v[:, 0:T - 1], scalar=pv("sw0", j), in1=t2[:, 1:T],
                            op0=ALU.mult, op1=ALU.add), reads=[vb, pvec_b, t2b], writes=[t2b])
                        fw.op(DVE, lambda e, v=v, t2=t2, j=j: e.scalar_tensor_tensor(
                            out=t2[:, 0:T - 1], in0=v[:, 1:T], scalar=pv("sw2", j), in1=t2[:, 0:T - 1],
                            op0=ALU.mult, op1=ALU.add), reads=[vb, pvec_b, t2b], writes=[t2b])
                        conv_fix(DVE, t2, v, mcol_t[:, 8 + j:9 + j], mcol_t[:, 16 + j:17 + j],
                                 reads=[vb, mcol_b, t2b], writes=[t2b])
                        fw.op(DVE, lambda e, t2=t2, j=j, d=dbls[0]: e.tensor_tensor(
                            out=a_t[:, j, :], in0=pd_t[d][:], in1=t2[:], op=ALU.mult),
                            reads=gbB + [t2b], writes=[ab[j]])
                    for _, _, wk in wl:
                        ws.done(wk)
                out_linear(l, sconv_wout, lambda kk, tt: a_t[:, kk, tsl(tt)],
                           lambda tt: ab[0:8], lambda oc: mod_t[l][:, 16 + oc:17 + oc])


            def mixer_mla(l):
                SC = 1.0 / float(np.sqrt(96.0))
                arena = a_t[:].rearrange("p a b -> p (a b)")
                cqT = lambda c: a_t[:, c, :]
                ckvall = lambda c: arena[:, 3 * T + c * 1536:3 * T + (c + 1) * 1536]
                KR = arena[:, 6 * T:6 * T + 1536]
                QT = lambda r: a_t[:, 8 + r, :]
                KT = lambda r: arena[:, (11 + 2 * r) * T:(11 + 2 * r) * T + 1536]
                VP = lambda r: arena[:, (17 + 3 * r) * T:(17 + 3 * r) * T + 3072].rearrange(
                    "p (k x) -> p k x", k=12, x=256)
                PT = lambda r: arena[:, 23 * T + r * 512:23 * T + (r + 1) * 512]
                for i in range(2):
                    rr = fw.dma(SP, ropecs[:, i, :], ropeCS_d[:, i, :], dst=ropecs_b)
                    if MLA_SUB == 0.11 and not fw.dry:
                        fw.out_recs.append(rr)
                for c in range(2):
                    fw.dma(POOL, ckvall(c)[:, 0:512], cacheT_d[:, c, :], dst=ckvall_b[c])
                fw.dma(POOL, KR, krmask_d, dst=kr_b)
                for r in range(3):
                    fw.dma(POOL, QT(r), eq_d, dst=qt_b[r])
                for r in range(2):
                    fw.op(DVE, lambda e, r=r: e.memset(VP(r), 0.0), writes=[vp_b[r]])
                    fw.op(DVE, lambda e, r=r: e.memset(VP(r)[:, :, 64:65], 1.0), writes=[vp_b[r]])
                    fw.op(DVE, lambda e, r=r: e.memset(VP(r)[:, :, 128:129], 1.0), writes=[vp_b[r]])

                if MLA_SUB < 0.2:
                    return
                w, wb, wk = ws.next(wdq.rearrange("(k p) n -> p k n", p=128), (KC, 384))
                for tt in range(2):
                    banks = [(4 * tt + c) % 8 for c in range(3)]
                    for c in range(3):
                        def mm(e, c=c, tt=tt, bank=banks[c], w=w):
                            r = None
                            for kk in range(KC):
                                r = e.matmul(ps(bank), w[:, kk, c * 128:(c + 1) * 128], h_t[:, kk, tsl(tt)],
                                             start=(kk == 0), stop=(kk == KC - 1))
                            return r
                        fw.op(PE, mm, reads=[wb] + [hb[kk][tt] for kk in range(KC)], writes=[ps_b[banks[c]]])
                    rms_stats([(ps(banks[c]), [ps_b[banks[c]]]) for c in range(3)], 3, 1.0 / 384,
                              bank=(4 * tt + 3) % 8)
                    for c in range(3):
                        s = c % 2
                        fw.op(DVE, lambda e, s=s, bank=banks[c]: e.tensor_tensor(
                            out=xn_t[s][:], in0=ps(bank), in1=rstd_t[:], op=ALU.mult),
                            reads=[ps_b[banks[c]], rstd_b], writes=[xn_b[s]])
                        fw.op(ACT, lambda e, s=s, c=c, tt=tt: e.activation(
                            out=cqT(c)[:, tsl(tt)], in_=xn_t[s][:], func=AF.Copy, scale=pv("qnw", c)),
                            reads=[xn_b[s], pvec_b], writes=[cq_b[c]])
                ws.done(wk)

                if MLA_SUB < 0.5:
                    return
                w, wb, wk = ws.next(wdkv_aug.rearrange("(k p) n -> p k n", p=128), (KC, 448))
                for tt in range(2):
                    banks = [(5 * tt + i) % 8 for i in range(4)]
                    cols = [(0, 128), (128, 128), (256, 96), (352, 96)]
                    for i in range(4):
                        c0, m = cols[i]

                        def mm(e, c0=c0, m=m, tt=tt, bank=banks[i], w=w):
                            r = None
                            for kk in range(KC):
                                r = e.matmul(ps(bank)[0:m, :], w[:, kk, c0:c0 + m], h_t[:, kk, tsl(tt)],
                                             start=(kk == 0), stop=(kk == KC - 1))
                            return r
                        fw.op(PE, mm, reads=[wb] + [hb[kk][tt] for kk in range(KC)], writes=[ps_b[banks[i]]])
                    if MLA_SUB < 0.51:
                        continue
                    rms_stats([(ps(banks[c]), [ps_b[banks[c]]]) for c in range(2)], 2, 1.0 / 256,
                              bank=(5 * tt + 4) % 8)
                    if MLA_SUB < 0.52:
                        continue
                    for c in range(2):
                        s = c % 2
                        fw.op(DVE, lambda e, s=s, bank=banks[c]: e.tensor_tensor(
                            out=xn_t[s][:], in0=ps(bank), in1=rstd_t[:], op=ALU.mult),
                            reads=[ps_b[banks[c]], rstd_b], writes=[xn_b[s]])
                        fw.op(ACT, lambda e, s=s, c=c, tt=tt: e.activation(
                            out=ckvall(c)[:, 512 + tt * 512:1024 + tt * 512], in_=xn_t[s][:], func=AF.Copy,
                            scale=pv("kvnw", c)),
                            reads=[xn_b[s], pvec_b], writes=[ckvall_b[c]])
                        fw.op(DVE, lambda e, s=s, c=c, tt=tt: e.tensor_scalar(
                            out=tA_t[c][:, tsl(tt)], in0=xn_t[s][:], scalar1=pv("kvnw", c), scalar2=None,
                            op0=ALU.mult),
                            reads=[xn_b[s], pvec_b], writes=[tA_b[c]])
                    if MLA_SUB < 0.55:
                        continue
                    bA, bB = banks[2], banks[3]
                    fw.op(ACT, lambda e, tt=tt, bA=bA: e.activation(
                        out=stg_t[0][64:96, tsl(tt)], in_=ps(bA)[64:96, :], func=AF.Copy),
                        reads=[ps_b[bA]], writes=[stg_b[0]])
                    if MLA_SUB < 0.56:
                        continue
                    fw.op(DVE, lambda e, tt=tt, bA=bA: e.tensor_tensor(
                        out=xn_t[0][64:96, :], in0=ps(bA)[64:96, :], in1=ropecs[64:96, 0, tsl(tt)], op=ALU.mult),
                        reads=[ps_b[bA], ropecs_b], writes=[xn_b[0]])
                    if MLA_SUB < 0.57:
                        continue
                    fw.op(DVE, lambda e, tt=tt, bB=bB: e.tensor_tensor(
                        out=xn_t[1][64:96, :], in0=ps(bB)[64:96, :], in1=ropecs[64:96, 1, tsl(tt)], op=ALU.mult),
                        reads=[ps_b[bB], ropecs_b], writes=[xn_b[1]])
                    if MLA_SUB < 0.58:
                        continue
                    fw.op(DVE, lambda e, tt=tt: e.tensor_tensor(
                        out=KR[64:96, 512 + tt * 512:1024 + tt * 512], in0=xn_t[0][64:96, :],
                        in1=xn_t[1][64:96, :], op=ALU.add),
                        reads=[xn_b[0], xn_b[1]], writes=[kr_b])
                ws.done(wk)

                for i in range(8):
                    bank = i % 8

                    def tr(e, i=i, bank=bank):
                        r = None
                        for c in range(2):
                            r = e.transpose(ps(bank)[:, c * 128:(c + 1) * 128],
                                            tA_t[c][:, i * 128:(i + 1) * 128], ident[:])
                        r = e.transpose(ps(bank)[:, 256:384], stg_t[0][:, i * 128:(i + 1) * 128], ident[:])
                        return r
                    fw.op(PE, tr, reads=[tA_b[0], tA_b[1], stg_b[0], ident_b], writes=[ps_b[bank]])
                    fw.op(DVE, lambda e, i=i, bank=bank: e.tensor_copy(out=ckvst[:, i, :], in_=ps(bank)[:, 0:256]),
                          reads=[ps_b[bank]], writes=[ckvst_b])
                    fw.op(ACT, lambda e, i=i, bank=bank: e.activation(out=krst[:, i, :], in_=ps(bank)[:, 320:352],
                                                                     func=AF.Copy),
                          reads=[ps_b[bank]], writes=[krst_b])
                fw.dma(SP, ckv_o.rearrange("(i p) c -> p i c", p=128), ckvst[:], dst=None, src=[ckvst_b])
                fw.dma(SP, kr_o.rearrange("(i p) c -> p i c", p=128), krst[:], dst=None, src=[krst_b])

                for r in range(3):
                    fw.op(DVE, lambda e, r=r: e.tensor_copy(out=KT(r)[64:128, :], in_=KR[64:128, :]),
                          reads=[kr_b], writes=[kt_b[r]])

                fw.op(DVE, lambda e: e.memset(stg_t[1][:], 0.0), writes=[stg_b[1]])
                if MLA_SUB < 2:
                    return
                ukv_dram = wukv.rearrange("(k p) n -> p k n", p=128)
                wqv = wuq_aug.rearrange("(k p) n -> p k n", p=128)
                st = {'qw': None, 'ukv': None, 'n_o': 0, 'n_s': 0, 'n_p': 0}

                def prep_V(p):
                    vr = p % 2
                    if p % 2 == 0:
                        if st['ukv'] is not None:
                            ws.done(st['ukv'][2])
                        st['ukv'] = ws.next(ukv_dram, (2, 2048))
                    ukv, ukvb = st['ukv'][0], st['ukv'][1]
                    for ktg in range(3):
                        bank = 6 + ktg % 2

                        def mmv(e, ktg=ktg, bank=bank, p=p, ukv=ukv):
                            r = None
                            for j in range(4):
                                kt = ktg * 4 + j
                                for kc in range(2):
                                    rhs = ukv[:, kc, p * 256:(p + 1) * 256].rearrange("p (h x) -> p h x", h=2)[:, :, 64:128]
                                    r = e.matmul(ps(bank)[:, j * 128:(j + 1) * 128].rearrange("p (h x) -> p h x", h=2),
                                                 ckvall(kc)[:, kt * 128:(kt + 1) * 128], rhs,
                                                 start=(kc == 0), stop=(kc == 1))
                            return r
                        fw.op(PE, mmv, reads=[ukvb, ckvall_b[0], ckvall_b[1]], writes=[ps_b[bank]])
                        src = ps(bank).rearrange("p (j h x) -> p j h x", j=4, h=2, x=64)
                        fw.op(DVE, lambda e, src=src, ktg=ktg, vr=vr: e.tensor_copy(
                            out=VP(vr)[:, ktg * 4:ktg * 4 + 4, 0:64], in_=src[:, :, 0, :]),
                            reads=[ps_b[bank]], writes=[vp_b[vr]])
                        fw.op(ACT, lambda e, src=src, ktg=ktg, vr=vr: e.activation(
                            out=VP(vr)[:, ktg * 4:ktg * 4 + 4, 192:256], in_=src[:, :, 1, :], func=AF.Copy),
                            reads=[ps_b[bank]], writes=[vp_b[vr]])

                def prep_KQ(h):
                    r3 = h % 3
                    if h % 4 == 0:
                        if st['qw'] is not None:
                            ws.done(st['qw'][2])
                        st['qw'] = ws.next(wqv[:, :, (h // 4) * 768:(h // 4 + 1) * 768], (3, 768))
                    qw = st['qw']
                    ukv, ukvb = st['ukv'][0], st['ukv'][1]
                    for kt5 in range(3):
                        bank = 6 + kt5 % 2

                        def mmk(e, h=h, kt5=kt5, bank=bank, ukv=ukv):
                            r = None
                            for kc in range(2):
                                r = e.matmul(ps(bank)[0:64, :], ukv[:, kc, h * 128:h * 128 + 64],
                                             ckvall(kc)[:, kt5 * 512:(kt5 + 1) * 512],
                                             start=(kc == 0), stop=(kc == 1))
                            return r
                        fw.op(PE, mmk, reads=[ukvb, ckvall_b[0], ckvall_b[1]], writes=[ps_b[bank]])
                        evac(kt5, KT(r3)[0:64, kt5 * 512:(kt5 + 1) * 512], ps(bank)[0:64, :],
                             [ps_b[bank]], [kt_b[r3]])
                    qcol = (h % 4) * 192
                    for tt in range(2):
                        for which in range(2):
                            bank = 6 + which

                            def mmq(e, which=which, tt=tt, bank=bank, qcol=qcol, qwv=qw[0]):
                                r = None
                                for kc in range(3):
                                    r = e.matmul(ps(bank)[0:96, :],
                                                 qwv[:, kc, qcol + which * 96:qcol + which * 96 + 96],
                                                 cqT(kc)[:, tsl(tt)], start=(kc == 0), stop=(kc == 2))
                                return r
                            fw.op(PE, mmq, reads=[qw[1]] + cq_b, writes=[ps_b[bank]])
                        fw.op(ACT, lambda e, r3=r3, tt=tt: e.activation(
                            out=QT(r3)[0:64, tsl(tt)], in_=ps(6)[0:64, :], func=AF.Copy),
                            reads=[ps_b[6]], writes=[qt_b[r3]])
                        fw.op(DVE, lambda e, tt=tt: e.tensor_tensor(
                            out=xn_t[0][64:96, :], in0=ps(6)[64:96, :], in1=ropecs[64:96, 0, tsl(tt)],
                            op=ALU.mult), reads=[ps_b[6], ropecs_b], writes=[xn_b[0]])
                        fw.op(DVE, lambda e, tt=tt: e.tensor_tensor(
                            out=xn_t[1][64:96, :], in0=ps(7)[64:96, :], in1=ropecs[64:96, 1, tsl(tt)],
                            op=ALU.mult), reads=[ps_b[7], ropecs_b], writes=[xn_b[1]])
                        fw.op(DVE, lambda e, r3=r3, tt=tt: e.tensor_tensor(
                            out=QT(r3)[64:96, tsl(tt)], in0=xn_t[0][64:96, :], in1=xn_t[1][64:96, :],
                            op=ALU.add), reads=[xn_b[0], xn_b[1]], writes=[qt_b[r3]])

                def attend(h, tts, pending=None):
                    p, hh = h // 2, h % 2
                    vr = p % 2
                    r3 = h % 3
                    if hh == 0:
                        vcols, orows, srow, om = (0, 65), (0, 64), 64, 65
                    else:
                        vcols, orows, srow, om = (128, 256), (64, 128), 0, 128
                    for tt in (tts if MLA_SUB >= 3 else ()):
                        ob = 2 + st['n_o'] % 3
                        st['n_o'] += 1
                        pend = []
                        for kt in range(14):
                            if kt < 12:
                                sb_ = (0, 1, 5)[st['n_s'] % 3]
                                st['n_s'] += 1
                                fw.op(PE, lambda e, sb_=sb_, kt=kt, r3=r3, tt=tt: e.matmul(
                                    ps(sb_), KT(r3)[:, kt * 128:(kt + 1) * 128], QT(r3)[:, tsl(tt)],
                                    start=True, stop=True),
                                    reads=[kt_b[r3], qt_b[r3]], writes=[ps_b[sb_]])
                                pi = st['n_p'] % 4
                                st['n_p'] += 1
                                fw.op(ACT, lambda e, sb_=sb_, pi=pi: e.activation(
                                    out=PT(pi), in_=ps(sb_), func=AF.Exp, scale=SC),
                                    reads=[ps_b[sb_]], writes=[pt_b[pi]])
                            if kt >= 2:
                                pkt, ppi = pend.pop(0)
                                fw.op(PE, lambda e, ob=ob, om=om, vr=vr, pkt=pkt, ppi=ppi, vcols=vcols: e.matmul(
                                    ps(ob)[0:om, :], VP(vr)[:, pkt, vcols[0]:vcols[1]], PT(ppi),
                                    start=(pkt == 0), stop=(pkt == 11)),
                                    reads=[vp_b[vr], pt_b[ppi]], writes=[ps_b[ob]])
                            if kt < 12:
                                pend.append((kt, pi))
                            if kt == 8 and pending is not None:
                                pending()
                                pending = None
                        if MLA_SUB < 4:
                            continue
                        rs = stg_t[1][srow:srow + 1, 0:512] if hh == 0 else stg_t[1][srow:srow + 1, 512:1024]
                        fw.op(ACT, lambda e, rs=rs, ob=ob, srow=srow: e.activation(
                            out=rs, in_=ps(ob)[srow:srow + 1, :], func=AF.Ln), reads=[ps_b[ob]], writes=[stg_b[1]])
                        fw.op(ACT, lambda e, rs=rs: e.activation(out=rs, in_=rs, func=AF.Exp, scale=-1.0),
                              reads=[stg_b[1]], writes=[stg_b[1]])
                        return lambda ob=ob, tt=tt: norm_o(h, tt, ob)
                    return None

                def norm_o(h, tt, ob):
                    p, hh = h // 2, h % 2
                    if hh == 0:
                        orows, srow = (0, 64), 64
                    else:
                        orows, srow = (64, 128), 0
                    if True:
                        bb = 6 + st['n_o'] % 2
                        fw.op(PE, lambda e, bb=bb, hh=hh: e.matmul(
                            ps(bb), sel_t[:, hh, :], stg_t[1][:, hh * 512:(hh + 1) * 512],
                            start=True, stop=True),
                            reads=[stg_b[1], const_b], writes=[ps_b[bb]])
                        fw.op(ACT, lambda e, bb=bb: e.activation(out=rstd_t[:], in_=ps(bb), func=AF.Copy),
                              reads=[ps_b[bb]], writes=[rstd_b])
                        fw.op(DVE, lambda e, ob=ob, orows=orows, p=p, tt=tt: e.tensor_tensor(
                            out=h_t[orows[0]:orows[1], p, tsl(tt)], in0=ps(ob)[orows[0]:orows[1], :],
                            in1=rstd_t[orows[0]:orows[1], :], op=ALU.mult),
                            reads=[ps_b[ob], rstd_b], writes=[hb[p][tt]])

                prep_V(0)
                prep_KQ(0)
                pnd = None
                for h in range(16):
                    pnd = attend(h, (0,), pnd)
                    if h + 1 < 16:
                        if (h + 1) % 2 == 0:
                            prep_V((h + 1) // 2)
                        prep_KQ(h + 1)
                    pnd = attend(h, (1,), pnd)
                    if h % 2 == 1:
                        ada_block(0, 4 + h // 2, 6, 7)
                if pnd is not None:
                    pnd()
                ada_finish2(0)
                qw = st['qw']
                ws.done(qw[2])
                ws.done(st['ukv'][2])
                out_linear(l, wo, lambda kk, tt: h_t[:, kk, tsl(tt)],
                           lambda tt: [hb[kk][tt] for kk in range(KC)], lambda oc: mod_t[l][:, 16 + oc:17 + oc])

            def mixer(l):
                norm_mod(l, 1, pre=(l > 0))
                if l == 0 and stage >= 4:
                    mixer_mla(l)
                elif l == 1 and stage >= 3:
                    mixer_pool(l)
                elif l == 2 and stage >= 3:
                    mixer_fnet(l)
                elif l == 3 and stage >= 3:
                    mixer_sconv(l)
                fw.barrier_bufs(ab)

            if stage >= 2:
                for nb in range(4 if stage >= 4 else 12):
                    ada_block(0, nb, 4 + nb % 2, 6 + nb % 2)
                ada_finish1(0)
                if stage < 4:
                    ada_finish2(0)
                for l in range(DEPTH):
                    mixer(l)
                    norm_mod(l, 2, pre=(stage >= 4))
                    if l + 1 < DEPTH:
                        def hook(jg, l=l):
                            ada_block(l + 1, 2 * jg, 0, 1)
                            ada_block(l + 1, 2 * jg + 1, 2, 3)
                            if jg == 5:
                                ada_finish(l + 1)
                        ffn(l, hook)
                    else:
                        ffn(l)

            fo, _ = PV["fnw"]
            for tt in range(2):
                if stage >= 4:
                    rms_tail(1.0 / D, 6 + tt, rstd_t, rstd_b)
                else:
                    rms_stats([(x_t[:, c, tsl(tt)], [xb[c][tt]]) for c in range(KC)], KC, 1.0 / D,
                              bank=tt)
                for c in range(KC):
                    fw.op(DVE, lambda e, c=c, tt=tt: e.scalar_tensor_tensor(
                        out=x_t[:, c, tsl(tt)], in0=x_t[:, c, tsl(tt)],
                        scalar=pvec[:, fo + c:fo + c + 1], in1=rstd_t[:],
                        op0=ALU.mult, op1=ALU.mult),
                        reads=[xb[c][tt], rstd_b, pvec_b], writes=[xb[c][tt]])
            for i in range(8):
                s = i % 4
                tt = i // 4
                for half in range(2):
                    bank = 2 + (i * 2 + half) % 6

                    def mm(e, half=half, bank=bank, i=i):
                        r = None
                        for cc in range(4):
                            c = half * 4 + cc
                            r = e.transpose(ps(bank)[:, cc * 128:(cc + 1) * 128],
                                            x_t[:, c, i * 128:(i + 1) * 128], ident[:])
                        return r
                    fw.op(PE, mm, reads=[xb[c][tt] for c in range(half * 4, half * 4 + 4)] + [ident_b],
                          writes=[ps_b[bank]])
                    if half == 0:
                        fw.op(DVE, lambda e, s=s, bank=bank: e.tensor_copy(
                            out=stg4_t[s][:, 0:512], in_=ps(bank)),
                            reads=[ps_b[bank]], writes=[stg4_b[s]])
                    else:
                        fw.op(ACT, lambda e, s=s, bank=bank: e.activation(
                            out=stg4_t[s][:, 512:1024], in_=ps(bank), func=AF.Copy),
                            reads=[ps_b[bank]], writes=[stg4_b[s]])
                fw.dma(SP, yout[i * 128:(i + 1) * 128, :], stg4_t[s][:], dst=None, src=[stg4_b[s]])

        fw.dry = True
        emit()
        fw.dry = False
        ws.reset()
        emit()
        assert ws.consumed == len(ws.specs)

        final_waits = {}
        for sem, val in fw.out_recs:
            final_waits[sem] = max(final_waits.get(sem, 0), val)

        with nc.Block() as block:
            @block.sync
            def _(e):
                fw.replay(SP, e)
                for sem, val in final_waits.items():
                    e.wait_ge(sem, val)

            @block.tensor
            def _(e):
                fw.replay(PE, e)

            @block.scalar
            def _(e):
                fw.replay(ACT, e)

            @block.vector
            def _(e):
                fw.replay(DVE, e)

            @block.gpsimd
            def _(e):
                fw.replay(POOL, e)
    return nc


def _cols(v):
    v = np.asarray(v, np.float32)
    return np.ascontiguousarray(v.reshape(-1, 128).T)


def _make_pvec(inp, cond_vec, flagneg):
    pv = np.zeros((128, NPV), np.float32)

    def put(name, arr):
        o, n = PV[name]
        assert arr.shape == (128, n), (name, arr.shape, n)
        pv[:, o:o + n] = arr

    for l in range(DEPTH):
        put(f"n1w{l}", _cols(inp["norm1_w"][l]))
        put(f"n2w{l}", _cols(inp["norm2_w"][l]))
        put(f"fw0_{l}", _cols(inp["ffn_conv_w"][l, 0]))
        put(f"fw1_{l}", _cols(inp["ffn_conv_w"][l, 1]))
        put(f"fw2_{l}", _cols(inp["ffn_conv_w"][l, 2]))
        put(f"fb_{l}", _cols(inp["ffn_conv_b"][l]))
        put(f"adab{l}", _cols(inp["ada_b"][l]))
    put("fnw", _cols(inp["final_norm_w"]))
    put("cond", _cols(cond_vec))
    pv[:, PV["flagneg"][0]] = flagneg
    put("qnw", _cols(inp["mla_q_norm"][0]))
    put("kvnw", _cols(inp["mla_kv_norm"][0]))
    put("pscale", _cols(inp["pool_scale"][0]))
    put("sw0", _cols(inp["sconv_conv"][0, 0]))
    put("sw1", _cols(inp["sconv_conv"][0, 1]))
    put("sw2", _cols(inp["sconv_conv"][0, 2]))
    return pv


def _pool_tables(L):
    A = np.zeros((4, T, T), np.float64)
    wins = (2, 4, 8, 16)
    for g, w in enumerate(wins):
        for t in range(T):
            s0 = (t // L) * L
            tl = t - s0
            lo = max(tl - w // 2, 0)
            hi = min(tl + w - w // 2, L)
            A[g, t, s0 + lo:s0 + hi] = 1.0 / (hi - lo)
            A[g, t, t] -= 1.0
    out = np.zeros((128, 4, 8, 3, 128), np.float32)
    for g in range(4):
        for i in range(8):
            for d in range(3):
                ip = i + d - 1
                if 0 <= ip < 8:
                    out[:, g, i, d, :] = A[g, i * 128:(i + 1) * 128, ip * 128:(ip + 1) * 128].T
    return out.reshape(128, 4, 8 * 3 * 128).astype(ml_dtypes.bfloat16)


def _dft_tables(L):
    t = np.arange(T)
    same = (t[:, None] // L) == (t[None, :] // L)
    ang = 2.0 * np.pi * ((t[:, None] % L) * (t[None, :] % L) % L) / L
    nrm = 1.0 / np.sqrt(L * 256.0)
    C = np.where(same, np.cos(ang), 0.0) * nrm
    S = np.where(same, -np.sin(ang), 0.0) * nrm

    def lay(M):
        return np.ascontiguousarray(M.reshape(8, 128, T).transpose(1, 0, 2)).astype(ml_dtypes.bfloat16)
    c = np.arange(256)
    a2 = 2.0 * np.pi * ((c[:, None] * c[None, :]) % 256) / 256.0
    CS = np.concatenate([np.cos(a2), np.sin(a2)], axis=1)
    CS = np.ascontiguousarray(CS.reshape(2, 128, 512).transpose(1, 0, 2)).astype(ml_dtypes.bfloat16)
    return lay(C), lay(S), CS


_PERM = np.concatenate([np.arange(0, 32, 2), np.arange(1, 32, 2)])
_PERM_SW = np.concatenate([np.arange(1, 32, 2), np.arange(0, 32, 2)])


def _mla_weights(inp):
    wdkv = np.asarray(inp["mla_wdkv"][0], np.float32)
    aug = np.zeros((D, 448), np.float32)
    aug[:, 0:256] = wdkv[:, 0:256]
    aug[:, 320:352] = wdkv[:, 256 + _PERM]
    aug[:, 416:448] = wdkv[:, 256 + _PERM_SW]
    wuq = np.asarray(inp["mla_wuq"][0], np.float32).reshape(384, 16, 96)
    qa = np.zeros((384, 16, 192), np.float32)
    qa[:, :, 0:64] = wuq[:, :, 0:64]
    qa[:, :, 64:96] = wuq[:, :, 64 + _PERM]
    qa[:, :, 160:192] = wuq[:, :, 64 + _PERM_SW]
    return aug, np.ascontiguousarray(qa.reshape(384, 16 * 192))


def _rope_tables(kind):
    cs = np.zeros((128, 2, T), np.float32)
    if kind == "p":
        cs[64:96, 0, :] = 1.0
        return cs
    t = np.arange(T)
    r = (t // 64).astype(np.float32)
    col = (t % 64).astype(np.float32)
    inv = (np.float32(10000.0) ** (-np.arange(8, dtype=np.float32) / np.float32(8))).astype(np.float32)
    ang = np.concatenate([r[:, None] * inv, col[:, None] * inv], axis=-1).astype(np.float32)
    c, s = np.cos(ang).T, np.sin(ang).T
    cs[64:80, 0, :] = c
    cs[80:96, 0, :] = c
    cs[64:80, 1, :] = -s
    cs[80:96, 1, :] = s
    return cs


def _mask_tables(kind):
    NEG = -30000.0
    eq = np.zeros((128, T), np.float32)
    ek = np.zeros((128, 1536), np.float32)
    if kind == "p":
        seq = np.arange(T) // 256
        for r in range(4):
            eq[96 + r, :] = (seq == r)
            ek[96 + r, 0:512] = NEG
            ek[96 + r, 512:] = np.where(seq == r, 0.0, NEG)
    return eq, ek


def _core_roles():
    return [("s", 0), ("s", 1), ("p", 0), ("p", 1), ("p", 2), ("p", 3), ("p", 3), ("p", 3)]


_NC_CACHE = {}


def make_in_maps(inp):
    roles = _core_roles()
    ident = np.eye(128, dtype=np.float32)
    shared = {
        "ident": ident,
        "ada_w": np.ascontiguousarray(inp["ada_w"], dtype=np.float32),
        "ffn_up": np.ascontiguousarray(inp["ffn_up"], dtype=np.float32),
        "ffn_down": np.ascontiguousarray(inp["ffn_down"], dtype=np.float32),
        "pool_w": np.ascontiguousarray(inp["pool_w"][0], dtype=np.float32),
        "fnet_w": np.ascontiguousarray(inp["fnet_w"][0], dtype=np.float32),
        "sconv_win": np.ascontiguousarray(inp["sconv_win"][0], dtype=np.float32),
        "sconv_wout": np.ascontiguousarray(inp["sconv_wout"][0], dtype=np.float32),
        "identb": ident.astype(ml_dtypes.bfloat16),
        "wdq": np.ascontiguousarray(inp["mla_wdq"][0], dtype=np.float32),
        "wukv": np.ascontiguousarray(inp["mla_wukv"][0], dtype=np.float32),
        "wo": np.ascontiguousarray(inp["mla_wo"][0], dtype=np.float32),
    }
    shared["wdkv_aug"], shared["wuq_aug"] = _mla_weights(inp)
    tabs = {}
    for kind, L in (("s", 1024), ("p", 256)):
        C, S, CS = _dft_tables(L)
        eq, ek = _mask_tables(kind)
        tabs[kind] = {"poolA": _pool_tables(L), "dftC": C, "dftS": S, "dftCS": CS,
                      "ropeCS": _rope_tables(kind), "eq": eq, "_ek": ek}
    in_maps = []
    for kind, idx in roles:
        if kind == "s":
            xc = inp["x_sample"][idx]
            cond = inp["c"][idx]
            flag = 0.0
        else:
            xc = inp["x_prompt"][4 * idx:4 * idx + 4].reshape(T, D)
            cond = inp["c_ctx"]
            flag = -1.0
        m = dict(shared)
        m.update({a: b for a, b in tabs[kind].items() if not a.startswith("_")})
        krm = tabs[kind]["_ek"].copy()
        cT = np.zeros((128, 2, 512), np.float32)
        if kind == "s":
            cT[:] = np.asarray(inp["cache_ckv"][idx, 0], np.float32).T.reshape(2, 128, 512).transpose(1, 0, 2)
            krm[64:96, 0:512] = np.asarray(inp["cache_krope"][idx, 0], np.float32)[:, _PERM].T
        m["cacheT"] = cT
        m["krmask"] = krm
        m["xin"] = np.ascontiguousarray(xc, dtype=np.float32)
        m["pvec"] = _make_pvec(inp, cond, flag)
        in_maps.append(m)
    return in_maps


def kernel(**inputs):
    inp = {k: np.asarray(v) for k, v in inputs.items()}
    in_maps = make_in_maps(inp)
    if "nc" not in _NC_CACHE:
        _NC_CACHE["nc"] = build_program(STAGE)
    nc = _NC_CACHE["nc"]
    res = run_bass_kernel_spmd(nc, in_maps, core_ids=list(range(NCORES)))
    outs = res.results
    y_sample = np.stack([outs[0]["yout"], outs[1]["yout"]], axis=0).astype(np.float32)
    y_prompt = np.concatenate([outs[2 + g]["yout"].reshape(4, 256, D) for g in range(4)], axis=0)
    y_prompt = y_prompt.astype(np.float32)
    new_ckv = np.concatenate([outs[2 + g]["ckv_o"].reshape(4, 1, 256, 256) for g in range(4)], axis=0)
    krp = np.concatenate([outs[2 + g]["kr_o"].reshape(4, 1, 256, 32) for g in range(4)], axis=0)
    new_kr = np.empty_like(krp)
    new_kr[..., _PERM] = krp
    new_ckv = new_ckv.astype(np.float32)
    new_kr = new_kr.astype(np.float32)
    return (y_prompt, y_sample, new_ckv, new_kr)
```
